# Optimizing a Trainium2 kernel written in Bass

```python
import math
import jax
import jax.numpy as jnp
from jax import lax
import numpy as np

D_MODEL = 1024
BATCH = 4
SEQ = 4096
DEPTH = 4

GRID_W = 64
CTX_LEN = 256
N_MIXERS = 3
EPS = 1e-6
F32 = jnp.float32

M_EXPAND = 2
M_DI = M_EXPAND * D_MODEL
M_HEADDIM = 64
M_NH = M_DI // M_HEADDIM
M_NG = 8
M_HPG = M_NH // M_NG
M_NS = 128
M_CONV_W = 5
M_CHUNK = 128
M_CONV_CH = M_DI + 2 * M_NG * M_NS
M_IN = M_DI + M_CONV_CH + 2 * M_NH

A_HEADDIM = 64
A_NH = D_MODEL // A_HEADDIM
A_NKV = 4
A_REP = A_NH // A_NKV
A_W = A_NH * A_HEADDIM
A_KW = A_NKV * A_HEADDIM
A_IN = 2 * A_W + 2 * A_KW
A_QBLOCK = 128
ROPE_BASE = 10000.0

S_DI = D_MODEL
S_GROUP = 16
S_NG = S_DI // S_GROUP
S_P = 64

N_A = (DEPTH + N_MIXERS - 1) // N_MIXERS
N_B = (DEPTH + N_MIXERS - 2) // N_MIXERS
N_C = DEPTH // N_MIXERS

kernel_name = 'hybrid_ssd_gqa_s5_diffusion_trunk'


def rmsnorm(x, w):
    xf = x.astype(F32)
    y = xf * lax.rsqrt(jnp.mean(xf * xf, axis=-1, keepdims=True) + EPS)
    return (y * w.astype(F32)).astype(x.dtype)


def dwconv_centred(u, w, b):
    pad = w.shape[0] // 2
    out = lax.conv_general_dilated(u, w[:, None, :], window_strides=(1,), padding=[(pad, pad)],
                                   dimension_numbers=('NWC', 'WIO', 'NWC'),
                                   feature_group_count=u.shape[-1])
    return out + b


def ssd_scan(xs, dt, a, bm, cm, init_state, need_y=True):
    bsz, L = xs.shape[:2]
    nc = L // M_CHUNK

    def chunks(t):
        return t.astype(F32).reshape((bsz, nc, M_CHUNK) + t.shape[2:])

    xc, dtc, bc, cc = chunks(xs), chunks(dt), chunks(bm), chunks(cm)
    la_cum = jnp.cumsum(dtc * a, axis=2)
    to_end = jnp.exp(la_cum[:, :, -1:] - la_cum) * dtc
    states = jnp.einsum('bcsgn,bcsgr,bcsgrp->bcgrpn', bc, to_end, xc)
    chunk_decay = jnp.exp(la_cum[:, :, -1])

    def step(carry, inp):
        st, dec = inp
        return carry * dec[..., None, None] + st, carry

    final, prev = lax.scan(step, init_state.astype(F32),
                           (jnp.moveaxis(states, 1, 0), jnp.moveaxis(chunk_decay, 1, 0)))
    if not need_y:
        return None, final
    prev = jnp.moveaxis(prev, 0, 1)
    causal = jnp.tril(jnp.ones((M_CHUNK, M_CHUNK), dtype=bool))[:, :, None, None]
    seg = la_cum[:, :, :, None] - la_cum[:, :, None, :]
    decay = jnp.exp(jnp.where(causal, seg, -jnp.inf))
    cb = jnp.einsum('bclgn,bcsgn->bclsg', cc, bc)
    w = cb[..., None] * decay * dtc[:, :, None]
    y_diag = jnp.einsum('bclsgr,bcsgrp->bclgrp', w, xc)
    y_off = jnp.einsum('bclgn,bcgrpn,bclgr->bclgrp', cc, prev, jnp.exp(la_cum))
    return (y_diag + y_off).reshape(xs.shape), final


def mamba_mixer(h_ctx, h_lat, in_w, conv_w, conv_b, a_log, dt_bias, d_skip, norm_w, out_w, ctx_out):
    a = -jnp.exp(a_log.astype(F32)).reshape(2, M_NG, M_HPG)
    dtb = dt_bias.astype(F32).reshape(2, M_NG, M_HPG)

    def project(h):
        bsz, L, _ = h.shape
        z, xbc, dt = jnp.split(h @ in_w, [M_DI, M_DI + M_CONV_CH], axis=-1)
        xbc = jax.nn.silu(dwconv_centred(xbc, conv_w, conv_b))
        xs, bm, cm = jnp.split(xbc, [M_DI, M_DI + M_NG * M_NS], axis=-1)
        xs = xs.reshape(bsz, L, M_NG, M_HPG, M_HEADDIM)
        bm = bm.reshape(bsz, L, M_NG, M_NS)
        cm = cm.reshape(bsz, L, M_NG, M_NS)
        dt = jax.nn.softplus(dt.astype(F32).reshape(bsz, L, 2, M_NG, M_HPG) + dtb)
        return z, xs, bm, cm, dt

    zc, xc, bc, cc, dtc = project(h_ctx)
    zl, xl, bl, cl, dtl = project(h_lat)
    bsz = h_lat.shape[0]
    zero = jnp.zeros((bsz, M_NG, M_HPG, M_HEADDIM, M_NS), F32)
    rev = lambda t: jnp.flip(t, axis=1)
    yc_f, st_f = ssd_scan(xc, dtc[:, :, 0], a[0], bc, cc, zero, ctx_out)
    yl_f, _ = ssd_scan(xl, dtl[:, :, 0], a[0], bl, cl, st_f)
    yc_b, st_b = ssd_scan(rev(xc), rev(dtc[:, :, 1]), a[1], rev(bc), rev(cc), zero, ctx_out)
    yl_b, _ = ssd_scan(rev(xl), rev(dtl[:, :, 1]), a[1], rev(bl), rev(cl), st_b)
    dsk = d_skip.astype(F32).reshape(M_NG, M_HPG, 1)

    def finish(y, xs, z):
        b_, L = xs.shape[:2]
        y = (y + dsk * xs.astype(F32)).reshape(b_, L, M_DI).astype(z.dtype)
        return rmsnorm(y * jax.nn.silu(z), norm_w) @ out_w

    o_lat = finish(yl_f + rev(yl_b), xl, zl)
    o_ctx = finish(yc_f + rev(yc_b), xc, zc) if ctx_out else None
    return o_ctx, o_lat


def axial_rope_tables(L):
    rows = L // GRID_W
    r_idx, c_idx = jnp.meshgrid(jnp.arange(rows), jnp.arange(GRID_W), indexing='ij')
    r_idx = r_idx.reshape(-1).astype(F32)
    c_idx = c_idx.reshape(-1).astype(F32)
    half = A_HEADDIM // 2
    inv = ROPE_BASE ** (-jnp.arange(0, half, 2, dtype=F32) / half)
    ang = jnp.concatenate([r_idx[:, None] * inv, c_idx[:, None] * inv], axis=-1)
    return jnp.cos(ang), jnp.sin(ang)


def apply_rope(t, cos, sin):
    b_, L, h, d = t.shape
    q = d // 4
    tf = t.astype(F32).reshape(b_, L, h, 2, 2, q)
    t1, t2 = tf[..., 0, :], tf[..., 1, :]
    cs = cos.reshape(1, L, 1, 2, q)
    sn = sin.reshape(1, L, 1, 2, q)
    out = jnp.stack([t1 * cs - t2 * sn, t1 * sn + t2 * cs], axis=-2)
    return out.reshape(b_, L, h, d).astype(t.dtype)


def attend(q, k, v):
    bsz, lq = q.shape[:2]
    q = q.reshape(bsz, lq, A_NKV, A_REP, A_HEADDIM)
    s = jnp.einsum('bqgrd,bkgd->bgrqk', q, k).astype(F32) * (A_HEADDIM ** -0.5)
    p = jax.nn.softmax(s, axis=-1).astype(v.dtype)
    o = jnp.einsum('bgrqk,bkgd->bqgrd', p, v)
    return o.reshape(bsz, lq, A_W)


def attn_mixer(h_ctx, h_lat, in_w, q_norm, k_norm, out_w, ctx_out):
    def project(h):
        bsz, L, _ = h.shape
        q, k, v, g = jnp.split(h @ in_w, [A_W, A_W + A_KW, A_W + 2 * A_KW], axis=-1)
        q = rmsnorm(q.reshape(bsz, L, A_NH, A_HEADDIM), q_norm)
        k = rmsnorm(k.reshape(bsz, L, A_NKV, A_HEADDIM), k_norm)
        v = v.reshape(bsz, L, A_NKV, A_HEADDIM)
        return q, k, v, g

    qc, kc, vc, gc = project(h_ctx)
    ql, kl, vl, gl = project(h_lat)
    bsz, L = h_lat.shape[:2]
    cos, sin = axial_rope_tables(L)
    ql = apply_rope(ql, cos, sin)
    kl = apply_rope(kl, cos, sin)
    k_all = jnp.concatenate([kc, kl], axis=1)
    v_all = jnp.concatenate([vc, vl], axis=1)
    nblk = L // A_QBLOCK
    q_blocks = jnp.moveaxis(ql.reshape(bsz, nblk, A_QBLOCK, A_NH, A_HEADDIM), 1, 0)
    ol = lax.map(lambda qb: attend(qb, k_all, v_all), q_blocks)
    ol = jnp.moveaxis(ol, 0, 1).reshape(bsz, L, A_W)
    o_lat = (ol * jax.nn.silu(gl)) @ out_w
    o_ctx = (attend(qc, kc, vc) * jax.nn.silu(gc)) @ out_w if ctx_out else None
    return o_ctx, o_lat


def s5_discretize(lam_re, lam_im, log_step, b_re, b_im):
    lr, li = lam_re.astype(F32), lam_im.astype(F32)
    step = jnp.exp(log_step.astype(F32))[:, None]
    mag = jnp.exp(lr * step)
    abar_re, abar_im = mag * jnp.cos(li * step), mag * jnp.sin(li * step)
    den = lr * lr + li * li
    nr, ni = abar_re - 1.0, abar_im
    coef_re = ((nr * lr + ni * li) / den)[..., None]
    coef_im = ((ni * lr - nr * li) / den)[..., None]
    br, bi = b_re.astype(F32), b_im.astype(F32)
    return abar_re, abar_im, coef_re * br - coef_im * bi, coef_re * bi + coef_im * br


def complex_affine_combine(e1, e2):
    a1r, a1i, b1r, b1i = e1
    a2r, a2i, b2r, b2i = e2
    return (a1r * a2r - a1i * a2i, a1r * a2i + a1i * a2r,
            a2r * b1r - a2i * b1i + b2r, a2r * b1i + a2i * b1r + b2i)


def s5_scan(u, abar_re, abar_im, bbar_re, bbar_im, init_re, init_im):
    L = u.shape[1]
    bu_re = jnp.einsum('blgh,gph->lbgp', u, bbar_re)
    bu_im = jnp.einsum('blgh,gph->lbgp', u, bbar_im)
    bu_re = bu_re.at[0].add(abar_re * init_re - abar_im * init_im)
    bu_im = bu_im.at[0].add(abar_re * init_im + abar_im * init_re)
    a_re = jnp.broadcast_to(abar_re, (L, 1) + abar_re.shape)
    a_im = jnp.broadcast_to(abar_im, (L, 1) + abar_im.shape)
    _, _, x_re, x_im = lax.associative_scan(complex_affine_combine, (a_re, a_im, bu_re, bu_im), axis=0)
    return x_re, x_im


def s5_mixer(h_ctx, h_lat, in_w, lam_re, lam_im, log_step, b_re, b_im, c_re, c_im,
             d_skip, glu_w, glu_b, out_w, ctx_out):
    bsz = h_lat.shape[0]
    disc = [s5_discretize(lam_re[k], lam_im[k], log_step[k], b_re[k], b_im[k]) for k in range(2)]
    cr, ci = c_re.astype(F32), c_im.astype(F32)

    def readout(x_re, x_im, k):
        return jnp.einsum('lbgp,ghp->blgh', x_re, cr[k]) - jnp.einsum('lbgp,ghp->blgh', x_im, ci[k])

    uc, zc = jnp.split(h_ctx @ in_w, 2, axis=-1)
    ul, zl = jnp.split(h_lat @ in_w, 2, axis=-1)
    grp = lambda u: u.astype(F32).reshape(u.shape[0], u.shape[1], S_NG, S_GROUP)
    rev = lambda t: jnp.flip(t, axis=1)
    ucg, ulg = grp(uc), grp(ul)
    zero = jnp.zeros((bsz, S_NG, S_P), F32)
    xr, xi = s5_scan(ucg, *disc[0], zero, zero)
    yc = readout(xr, xi, 0) if ctx_out else None
    xr, xi = s5_scan(ulg, *disc[0], xr[-1], xi[-1])
    yl = readout(xr, xi, 0)
    xr, xi = s5_scan(rev(ucg), *disc[1], zero, zero)
    if ctx_out:
        yc = yc + rev(readout(xr, xi, 1))
    xr, xi = s5_scan(rev(ulg), *disc[1], xr[-1], xi[-1])
    yl = yl + rev(readout(xr, xi, 1))
    dsk = d_skip.astype(F32)

    def finish(y, u, z):
        b_, L = u.shape[:2]
        y = (y.reshape(b_, L, S_DI) + dsk * u.astype(F32)).astype(z.dtype)
        y = jax.nn.gelu(y)
        y = y * jax.nn.sigmoid(y @ glu_w + glu_b)
        return (y * jax.nn.silu(z)) @ out_w

    o_lat = finish(yl, ul, zl)
    o_ctx = finish(yc, uc, zc) if ctx_out else None
    return o_ctx, o_lat


def setup_inputs(seed: int = 0) -> dict:
    key = jax.random.key(seed)
    ks = iter(jax.random.split(key, 48))
    nrm = lambda shape, scale: jax.random.normal(next(ks), shape, F32) * scale
    gain = lambda shape: 1.0 + nrm(shape, 0.02)
    inp = {}
    inp['x'] = nrm((BATCH, SEQ, D_MODEL), 1.0)
    inp['c'] = nrm((BATCH, D_MODEL), 1.0)
    inp['ctx'] = nrm((BATCH, CTX_LEN, D_MODEL), 1.0)
    inp['c_ctx'] = nrm((D_MODEL,), 1.0)
    inp['norm_w'] = gain((DEPTH, D_MODEL))
    inp['mod_w'] = nrm((DEPTH, D_MODEL, 3 * D_MODEL), 0.5 * D_MODEL ** -0.5)
    inp['mod_b'] = nrm((DEPTH, 3 * D_MODEL), 0.02)
    inp['final_norm_w'] = gain((D_MODEL,))
    inp['m_in_w'] = nrm((N_A, D_MODEL, M_IN), D_MODEL ** -0.5)
    inp['m_conv_w'] = nrm((N_A, M_CONV_W, M_CONV_CH), M_CONV_W ** -0.5)
    inp['m_conv_b'] = nrm((N_A, M_CONV_CH), 0.02)
    inp['m_a_log'] = jnp.log(jax.random.uniform(next(ks), (N_A, 2, M_NH), F32, 1.0, 16.0))
    dt0 = jnp.exp(jax.random.uniform(next(ks), (N_A, 2, M_NH), F32, math.log(1e-3), math.log(1e-1)))
    inp['m_dt_bias'] = dt0 + jnp.log(-jnp.expm1(-dt0))
    inp['m_d'] = 1.0 + nrm((N_A, M_NH), 0.1)
    inp['m_norm_w'] = gain((N_A, M_DI))
    inp['m_out_w'] = nrm((N_A, M_DI, D_MODEL), M_DI ** -0.5)
    inp['a_in_w'] = nrm((N_B, D_MODEL, A_IN), D_MODEL ** -0.5)
    inp['a_q_norm'] = gain((N_B, A_HEADDIM))
    inp['a_k_norm'] = gain((N_B, A_HEADDIM))
    inp['a_out_w'] = nrm((N_B, A_W, D_MODEL), A_W ** -0.5)
    inp['s_in_w'] = nrm((N_C, D_MODEL, 2 * S_DI), D_MODEL ** -0.5)
    inp['s_lambda_re'] = -0.5 + nrm((N_C, 2, S_NG, S_P), 0.01)
    inp['s_lambda_im'] = math.pi * jnp.arange(S_P, dtype=F32) + nrm((N_C, 2, S_NG, S_P), 0.01)
    inp['s_log_step'] = jax.random.uniform(next(ks), (N_C, 2, S_NG), F32, math.log(1e-3), math.log(1e-1))
    inp['s_b_re'] = nrm((N_C, 2, S_NG, S_P, S_GROUP), (2 * S_GROUP) ** -0.5)
    inp['s_b_im'] = nrm((N_C, 2, S_NG, S_P, S_GROUP), (2 * S_GROUP) ** -0.5)
    inp['s_c_re'] = nrm((N_C, 2, S_NG, S_GROUP, S_P), S_P ** -0.5)
    inp['s_c_im'] = nrm((N_C, 2, S_NG, S_GROUP, S_P), S_P ** -0.5)
    inp['s_d'] = nrm((N_C, S_DI), 1.0)
    inp['s_glu_w'] = nrm((N_C, S_DI, S_DI), S_DI ** -0.5)
    inp['s_glu_b'] = nrm((N_C, S_DI), 0.02)
    inp['s_out_w'] = nrm((N_C, S_DI, D_MODEL), S_DI ** -0.5)
    return inp


def reference(x, c, ctx, c_ctx, norm_w, mod_w, mod_b, final_norm_w,
              m_in_w, m_conv_w, m_conv_b, m_a_log, m_dt_bias, m_d, m_norm_w, m_out_w,
              a_in_w, a_q_norm, a_k_norm, a_out_w,
              s_in_w, s_lambda_re, s_lambda_im, s_log_step, s_b_re, s_b_im, s_c_re, s_c_im,
              s_d, s_glu_w, s_glu_b, s_out_w):
    h_lat, h_ctx = x, ctx
    sc, scc = jax.nn.silu(c), jax.nn.silu(c_ctx)
    for i in range(DEPTH):
        kind, j = i % N_MIXERS, i // N_MIXERS
        ctx_out = i < DEPTH - 1
        sh, scl, gt = jnp.split(sc @ mod_w[i] + mod_b[i], 3, axis=-1)
        shc, sclc, gtc = jnp.split(scc @ mod_w[i] + mod_b[i], 3, axis=-1)
        in_lat = rmsnorm(h_lat, norm_w[i]) * (1.0 + scl[:, None]) + sh[:, None]
        in_ctx = rmsnorm(h_ctx, norm_w[i]) * (1.0 + sclc) + shc
        if kind == 0:
            o_ctx, o_lat = mamba_mixer(in_ctx, in_lat, m_in_w[j], m_conv_w[j], m_conv_b[j], m_a_log[j],
                                       m_dt_bias[j], m_d[j], m_norm_w[j], m_out_w[j], ctx_out)
        elif kind == 1:
            o_ctx, o_lat = attn_mixer(in_ctx, in_lat, a_in_w[j], a_q_norm[j], a_k_norm[j], a_out_w[j], ctx_out)
        else:
            o_ctx, o_lat = s5_mixer(in_ctx, in_lat, s_in_w[j], s_lambda_re[j], s_lambda_im[j], s_log_step[j],
                                    s_b_re[j], s_b_im[j], s_c_re[j], s_c_im[j], s_d[j], s_glu_w[j],
                                    s_glu_b[j], s_out_w[j], ctx_out)
        h_lat = h_lat + gt[:, None] * o_lat
        if ctx_out:
            h_ctx = h_ctx + gtc * o_ctx
    return rmsnorm(h_lat, final_norm_w)
```

```python
import os
import numpy as np
from contextlib import ExitStack
import ml_dtypes
import concourse.bass as bass
import concourse.mybir as mybir
from concourse.bass_utils import run_bass_kernel_spmd

F32 = mybir.dt.float32
BF16 = mybir.dt.bfloat16
I32 = mybir.dt.int32
AF = mybir.ActivationFunctionType
ALU = mybir.AluOpType
AX = mybir.AxisListType

D = 1024
NCTX = 256
NLAT = 4096
EPS = 1e-6
M_IN = 6208
PI = float(np.pi)


class Res:
    __slots__ = ("w", "r", "name")

    def __init__(self, name=""):
        self.w = {}
        self.r = {}
        self.name = name


class Sched:
    def __init__(self, nc, es):
        self.nc = nc
        self.eng = {"pe": nc.tensor, "act": nc.scalar, "dve": nc.vector, "pool": nc.gpsimd, "sp": nc.sync}
        self.sem = {}
        self.cnt = {}
        self.known = {e: {} for e in self.eng}
        for e in self.eng:
            self.sem[e] = es.enter_context(nc.semaphore("s_" + e))
            self.cnt[e] = 0
        self.NDS = 8
        self.dslot = {}
        for q in ("sp", "pool"):
            for i in range(self.NDS):
                k = "d_%s%d" % (q, i)
                self.sem[k] = es.enter_context(nc.semaphore(k))
                self.cnt[k] = 0
            self.dslot[q] = 0
        self.ninst = 0

    def _wait(self, e, evs):
        kn = self.known[e]
        for k, v in evs.items():
            if v <= 0 or (e == "pe" and k == "pe") or kn.get(k, 0) >= v:
                continue
            self.eng[e].wait_ge(self.sem[k], v)
            kn[k] = v

    @staticmethod
    def _deps(r, w, wa):
        evs = {}

        def add(d):
            for k, v in d.items():
                if evs.get(k, 0) < v:
                    evs[k] = v
        for x in r:
            add(x.w)
        for x in w:
            add(x.w)
            add(x.r)
        for x in wa:
            add(x.r)
        return evs

    @staticmethod
    def _commit(k, v, r, w, wa):
        for x in r:
            if x.r.get(k, 0) < v:
                x.r[k] = v
        for x in w:
            if x.w.get(k, 0) < v:
                x.w[k] = v
        for x in wa:
            if x.w.get(k, 0) < v:
                x.w[k] = v

    def op(self, e, fn, r=(), w=(), wa=()):
        self._wait(e, self._deps(r, w, wa))
        ins = fn(self.eng[e])
        self.cnt[e] += 1
        ins.then_inc(self.sem[e], 1)
        self._commit(e, self.cnt[e], r, w, wa)
        self.ninst += 1
        return ins

    def dma(self, q, out, in_, r=(), w=(), wa=(), **kw):
        i = self.dslot[q]
        self.dslot[q] = (i + 1) % self.NDS
        k = "d_%s%d" % (q, i)
        evs = self._deps(r, w, wa)
        evs[k] = max(evs.get(k, 0), self.cnt[k])
        self._wait(q, evs)
        ins = self.eng[q].dma_start(out=out, in_=in_, **kw)
        self.cnt[k] += 16
        ins.then_inc(self.sem[k], 16)
        self._commit(k, self.cnt[k], r, w, wa)
        self.ninst += 1
        return ins

    def barrier(self):
        evs = {k: v for k, v in self.cnt.items() if v > 0}
        for e in self.eng:
            self._wait(e, dict(evs))


class RR:
    def __init__(self, tiles):
        self.t = tiles
        self.r = [Res() for _ in tiles]
        self.i = 0

    def next(self):
        i = self.i
        self.i = (i + 1) % len(self.t)
        return self.t[i], self.r[i]


def build_program(nlat=NLAT, layers=(0, 1, 2, 3)):
    T = NCTX + nlat
    NTT = T // 128
    BLKS = [(0, NCTX)] + [(NCTX + 512 * i, 512) for i in range(nlat // 512)]
    nc = bass.Bass("TRN2", target_bir_lowering=False)
    es = ExitStack()
    es.enter_context(nc.allow_low_precision("bf16 matmul operands, fp32 accumulation"))
    S = Sched(nc, es)
    uid = [0]

    def mk(stack):
        def sb(shape, dt=F32, name="t"):
            uid[0] += 1
            return stack.enter_context(nc.sbuf_tensor("%s_%d" % (name, uid[0]), list(shape), dt))

        def ps(shape, dt=F32, name="p"):
            uid[0] += 1
            return stack.enter_context(nc.psum_tensor("%s_%d" % (name, uid[0]), list(shape), dt))
        return sb, ps

    def din(name, shape, dt=F32):
        return nc.dram_tensor(name, list(shape), dt, kind="ExternalInput")

    def dscr(name, shape, dt=F32):
        return nc.dram_tensor(name, list(shape), dt)

    V = lambda fn, r=(), w=(), wa=(): S.op("dve", fn, r, w, wa)
    A = lambda fn, r=(), w=(), wa=(): S.op("act", fn, r, w, wa)
    G = lambda fn, r=(), w=(), wa=(): S.op("pool", fn, r, w, wa)
    M = lambda fn, r=(), w=(), wa=(): S.op("pe", fn, r, w, wa)

    x_in = din("x", [nlat, D])
    ctx_in = din("ctx", [NCTX, D])
    cc_in = din("cc", [128, 8, 2])
    ident_in = din("ident", [128, 128])
    fnw_in = din("fnw", [128, 8])
    out_t = nc.dram_tensor("out", [nlat, D], F32, kind="ExternalOutput")
    L = {}
    for i in layers:
        L[i] = dict(normw=din("normw%d" % i, [128, 8]), modw=din("modw%d" % i, [D, 3 * D]), modb=din("modb%d" % i, [128, 24]))
        kind, j = i % 3, i // 3
        if kind == 0:
            L[i].update(inw=din("m_in_w%d" % j, [D, M_IN]), convw=din("m_convw%d" % j, [128, 32, 5]), convb=din("m_convb%d" % j, [128, 32]),
                        alog=din("m_alog%d" % j, [64, 1]), dtb=din("m_dtb%d" % j, [64, 1]), dvec=din("m_dvec%d" % j, [2048]),
                        mnw=din("m_normw%d" % j, [2048]), outw=din("m_out_w%d" % j, [2048, D]))
        elif kind == 1:
            L[i].update(inw=din("a_in_w", [D, 2560]), qkw=din("a_qkw", [128, 2]), outw=din("a_out_w", [D, D]),
                        rope=din("a_rope", [2, 128, nlat]), perm=din("a_perm", [128, 128]), bones=din("a_bones", [128, 128]))
        else:
            L[i].update(inw=din("s_in_w", [D, 2048]), lam=din("s_lam", [128, 3, 64]), brt=din("s_brt", [2, 128, 2, 32, 128]),
                        crp=din("s_crp", [2, 128, 2, 32, 32]), sd=din("s_sd", [128, 8]), gluw=din("s_glu_w", [D, D]),
                        glub=din("s_glub", [128, 8]), outw=din("s_out_w", [D, D]))
    masks_in = din("masks", [2, 128, 128]) if any(i % 3 == 0 for i in layers) else None

    hT = dscr("hT", [D, T])
    hT_ap = hT.ap()
    r_hT = {(c, tt): Res() for c in range(8) for tt in range(NTT)}

    def hres(c, t0, nt):
        return [r_hT[(c, tt)] for tt in range(t0 // 128, (t0 + nt + 127) // 128)]

    def hres_all(t0, nt):
        out = []
        for c in range(8):
            out += hres(c, t0, nt)
        return out

    gsb, gps = mk(es)
    ident = gsb([128, 128], F32, "ident")
    r_const = Res("const")
    S.dma("sp", ident[:], ident_in.ap(), w=[r_const])
    identb = gsb([128, 128], BF16, "identb")
    ones_bf = gsb([128, 128], BF16, "ones")
    G(lambda e: e.memset(ones_bf[:], 1.0), wa=[r_const])
    V(lambda e: e.tensor_copy(identb[:], ident[:]), r=[r_const], wa=[r_const])
    fnw = gsb([128, 8], F32, "fnw")
    S.dma("sp", fnw[:], fnw_in.ap(), wa=[r_const])
    cc = gsb([128, 8, 2], F32, "cc")
    S.dma("sp", cc[:], cc_in.ap(), wa=[r_const])
    scs = gsb([128, 8, 2], F32, "scs")
    A(lambda e: e.activation(scs[:], cc[:], AF.Silu), r=[r_const], wa=[r_const])
    mod_sc = gsb([128, 8, 2], F32, "mod_sc")
    mod_bi = gsb([128, 8, 2], F32, "mod_bi")
    mod_gt = gsb([128, 8, 2], F32, "mod_gt")
    r_mod = Res("mod")
    S.barrier()

    with ExitStack() as ph:
        sb, ps = mk(ph)
        xin = RR([sb([128, D], F32, "xin") for _ in range(2)])
        tp = RR([ps([128, 512], F32, "tp") for _ in range(2)])
        xo = RR([sb([128, 8, 128], F32, "xo") for _ in range(2)])
        for tt in range(NTT):
            xt, r_xt = xin.next()
            src = ctx_in.ap()[tt * 128:(tt + 1) * 128, :] if tt < 2 else x_in.ap()[(tt - 2) * 128:(tt - 1) * 128, :]
            S.dma("sp", xt[:], src, w=[r_xt])
            ot, r_ot = xo.next()
            for half in range(2):
                pt, r_pt = tp.next()
                for j in range(4):
                    c = half * 4 + j
                    M(lambda e: e.transpose(pt[:, j * 128:(j + 1) * 128], xt[:, c * 128:(c + 1) * 128], ident[:]),
                      r=[r_xt, r_const], w=[r_pt] if j == 0 else [], wa=[r_pt] if j else [])
                dst = ot[:, half * 4:(half + 1) * 4, :]
                if half:
                    A(lambda e: e.copy(dst, pt[:].rearrange("p (j t) -> p j t", j=4)), r=[r_pt], wa=[r_ot])
                else:
                    V(lambda e: e.tensor_copy(dst, pt[:].rearrange("p (j t) -> p j t", j=4)), r=[r_pt], w=[r_ot])
            S.dma("pool", hT_ap[:, tt * 128:(tt + 1) * 128].rearrange("(c p) t -> p c t", p=128), ot[:], r=[r_ot],
                  wa=[r_hT[(c, tt)] for c in range(8)])
        S.barrier()

    def pre_pass(lay, sb, ps, inT, r_inT, final=False):
        if not final:
            mw = RR([sb([128, 8, 512], F32, "modw") for _ in range(2)])
            mp = RR([ps([128, 512], F32, "modp") for _ in range(2)])
            modT = sb([128, 24, 2], F32, "modT")
            r_modT = Res()
            modb = sb([128, 24], F32, "modb")
            normw = sb([128, 8], F32, "normw")
            r_small = Res()
            S.dma("sp", modb[:], lay["modb"].ap(), w=[r_small])
            S.dma("sp", normw[:], lay["normw"].ap(), wa=[r_small])
            for cg in range(6):
                wt, r_wt = mw.next()
                S.dma("sp", wt[:], lay["modw"].ap()[:, cg * 512:(cg + 1) * 512].rearrange("(k p) n -> p k n", p=128), w=[r_wt])
                for c4 in range(4):
                    pt, r_pt = mp.next()
                    for k in range(8):
                        M(lambda e: e.matmul(pt[:, 0:2], wt[:, k, c4 * 128:(c4 + 1) * 128], scs[:, k, :], start=(k == 0), stop=(k == 7)),
                          r=[r_wt, r_const], w=[r_pt] if k == 0 else [], wa=[r_pt] if k else [])
                    col = cg * 4 + c4
                    V(lambda e: e.tensor_scalar(modT[:, col, :], pt[:, 0:2], modb[:, col:col + 1], None, ALU.add),
                      r=[r_pt, r_small], wa=[r_modT])
            V(lambda e: e.tensor_scalar(mod_sc[:], modT[:, 8:16, :], 1.0, None, ALU.add), r=[r_modT], w=[r_mod])
            V(lambda e: e.tensor_tensor(mod_sc[:], mod_sc[:], normw[:].unsqueeze(2).broadcast_to([128, 8, 2]), ALU.mult), r=[r_small], w=[r_mod])
            V(lambda e: e.tensor_copy(mod_bi[:], modT[:, 0:8, :]), r=[r_modT], w=[r_mod])
            V(lambda e: e.tensor_copy(mod_gt[:], modT[:, 16:24, :]), r=[r_modT], w=[r_mod])
        hb = RR([sb([128, 8, 512], F32, "hb") for _ in range(2)])
        sq = RR([sb([128, 8, 512], BF16, "sq") for _ in range(2)])
        ssp = RR([ps([128, 512], F32, "ssp") for _ in range(2)])
        rstd = RR([sb([128, 512], F32, "rstd") for _ in range(2)])
        tmp = RR([sb([128, 512], F32, "ntmp") for _ in range(3)])
        if final:
            hn = RR([sb([128, 8, 512], F32, "hn") for _ in range(2)])
            tp = RR([ps([128, 512], F32, "ftp") for _ in range(2)])
            ot = RR([sb([128, D], F32, "fot") for _ in range(2)])
            r_out = Res()
        for (t0, nt) in BLKS:
            j = 1 if t0 < NCTX else 0
            if final and j == 1:
                continue
            h, r_h = hb.next()
            S.dma("sp", h[:, :, 0:nt], hT_ap[:, t0:t0 + nt].rearrange("(c p) t -> p c t", p=128), r=hres_all(t0, nt), w=[r_h])
            q, r_q = sq.next()
            A(lambda e: e.activation(q[:, :, 0:nt], h[:, :, 0:nt], AF.Square), r=[r_h], w=[r_q])
            sp_, r_sp = ssp.next()
            for c in range(8):
                M(lambda e: e.matmul(sp_[:, 0:nt], ones_bf[:], q[:, c, 0:nt], start=(c == 0), stop=(c == 7)),
                  r=[r_q, r_const], w=[r_sp] if c == 0 else [], wa=[r_sp] if c else [])
            rs, r_rs = rstd.next()
            A(lambda e: e.activation(rs[:, 0:nt], sp_[:, 0:nt], AF.Sqrt, bias=EPS, scale=1.0 / D), r=[r_sp], w=[r_rs])
            V(lambda e: e.reciprocal(rs[:, 0:nt], rs[:, 0:nt]), w=[r_rs])
            if not final:
                for c in range(8):
                    tm, r_tm = tmp.next()
                    V(lambda e: e.tensor_tensor(tm[:, 0:nt], h[:, c, 0:nt], rs[:, 0:nt], ALU.mult), r=[r_h, r_rs], w=[r_tm])
                    A(lambda e: e.activation(inT[:, c, t0:t0 + nt], tm[:, 0:nt], AF.Identity, bias=mod_bi[:, c, j:j + 1], scale=mod_sc[:, c, j:j + 1]),
                      r=[r_tm, r_mod], wa=[r_inT])
            else:
                hn_, r_hn = hn.next()
                for c in range(8):
                    V(lambda e: e.scalar_tensor_tensor(hn_[:, c, 0:nt], h[:, c, 0:nt], fnw[:, c:c + 1], rs[:, 0:nt], ALU.mult, ALU.mult),
                      r=[r_h, r_rs, r_const], w=[r_hn] if c == 0 else [], wa=[r_hn] if c else [])
                for tl in range(nt // 128):
                    o, r_o = ot.next()
                    for half in range(2):
                        pt, r_pt = tp.next()
                        for jj in range(4):
                            c = half * 4 + jj
                            M(lambda e: e.transpose(pt[:, jj * 128:(jj + 1) * 128], hn_[:, c, tl * 128:(tl + 1) * 128], ident[:]),
                              r=[r_hn, r_const], w=[r_pt] if jj == 0 else [], wa=[r_pt] if jj else [])
                        if half:
                            A(lambda e: e.copy(o[:, 512:1024], pt[:]), r=[r_pt], wa=[r_o])
                        else:
                            V(lambda e: e.tensor_copy(o[:, 0:512], pt[:]), r=[r_pt], w=[r_o])
                    row = t0 - NCTX + tl * 128
                    S.dma("pool", out_t.ap()[row:row + 128, :], o[:], r=[r_o], wa=[r_out])
        if final:
            evs = dict(r_out.w)
            S._wait("sp", evs)

    def linear_fm(sb, ps, act, r_act, KC, W_ap, col_chunks, epi, blks=None, t_off=0):
        wts = RR([sb([128, KC, 128], BF16, "lw") for _ in range(3)])
        pts = RR([ps([128, 512], F32, "lp") for _ in range(2)])
        for ci, (c0, ncol) in enumerate(col_chunks):
            wt, r_wt = wts.next()
            S.dma("pool", wt[:, :, 0:ncol], W_ap[:, c0:c0 + ncol].rearrange("(k p) n -> p k n", p=128), w=[r_wt])
            for (t0, nt) in (blks or BLKS):
                pt, r_pt = pts.next()
                for k in range(KC):
                    M(lambda e: e.matmul(pt[0:ncol, 0:nt], wt[:, k, 0:ncol], act[:, k, t0 - t_off:t0 - t_off + nt], start=(k == 0), stop=(k == KC - 1)),
                      r=[r_wt, r_act], w=[r_pt] if k == 0 else [], wa=[r_pt] if k else [])
                epi(ci, c0, ncol, t0, nt, pt, r_pt)

    def linear_tm(sb, ps, act, r_act, KC, W_ap, c0, ncols, epi):
        wts = RR([sb([128, KC, 512], BF16, "lwt") for _ in range(2)])
        pts = RR([ps([128, 512], F32, "lpt") for _ in range(2)])
        for g0 in range(0, ncols, 512):
            n = min(512, ncols - g0)
            wt, r_wt = wts.next()
            S.dma("pool", wt[:, :, 0:n], W_ap[:, c0 + g0:c0 + g0 + n].rearrange("(k p) n -> p k n", p=128), w=[r_wt])
            for tt in range(NTT):
                pt, r_pt = pts.next()
                for k in range(KC):
                    M(lambda e: e.matmul(pt[:, 0:n], act[:, k, tt * 128:(tt + 1) * 128], wt[:, k, 0:n], start=(k == 0), stop=(k == KC - 1)),
                      r=[r_wt, r_act], w=[r_pt] if k == 0 else [], wa=[r_pt] if k else [])
                epi(g0, n, tt, pt, r_pt)

    def make_resid_epi(sb):
        hts = RR([sb([128, 512], F32, "rh") for _ in range(3)])

        def epi(ci, c0, ncol, t0, nt, pt, r_pt):
            c = c0 // 128
            j = 1 if t0 < NCTX else 0
            ht, r_ht = hts.next()
            S.dma("sp", ht[:, 0:nt], hT_ap[c * 128:(c + 1) * 128, t0:t0 + nt], r=hres(c, t0, nt), w=[r_ht])
            V(lambda e: e.scalar_tensor_tensor(ht[:, 0:nt], pt[:, 0:nt], mod_gt[:, c, j:j + 1], ht[:, 0:nt], ALU.mult, ALU.add),
              r=[r_pt, r_mod], w=[r_ht])
            S.dma("pool", hT_ap[c * 128:(c + 1) * 128, t0:t0 + nt], ht[:, 0:nt], r=[r_ht], wa=hres(c, t0, nt))
        return epi

    def mamba_layer(lay):
        x_tm = dscr("x_tm%d" % uid[0], [T, 2048], BF16)
        B_tm = dscr("B_tm%d" % uid[0], [T, 1024], BF16)
        BT_d = dscr("BT_d%d" % uid[0], [8, 128, T], BF16)
        CT_d = dscr("CT_d%d" % uid[0], [8, 128, T], BF16)
        sz_tm = dscr("sz_tm%d" % uid[0], [T, 2048], BF16)
        laT_d = dscr("laT_d%d" % uid[0], [64, T], F32)
        ltot_d = dscr("ltot_d%d" % uid[0], [NTT, 64], F32)
        Yacc = dscr("Yacc%d" % uid[0], [T, 2048], F32)
        uid[0] += 1
        r_xtm, r_Btm, r_BT, r_CT, r_sz, r_laT, r_ltot = Res(), Res(), Res(), Res(), Res(), Res(), Res()
        r_Y = [Res() for _ in range(NTT)]
        with ExitStack() as lst:
            lsb, lps = mk(lst)
            la_tm = lsb([128, NTT, 64], F32, "la_tm")
            dt_tm = lsb([128, NTT, 64], F32, "dt_tm")
            LTB = lsb([128, NTT, 64], F32, "LTB")
            r_tabs = Res()
            with ExitStack() as st1:
                sb1, ps1 = mk(st1)
                inT = sb1([128, 8, T], BF16, "inT")
                r_inT = Res()
                with ExitStack() as ph:
                    sb, ps = mk(ph)
                    pre_pass(lay, sb, ps, inT, r_inT)
                    S.barrier()
                with ExitStack() as ph:
                    sb, ps = mk(ph)
                    convw = sb([128, 32, 5], F32, "convw")
                    convb = sb([128, 32], F32, "convb")
                    r_cv = Res()
                    S.dma("sp", convw[:], lay["convw"].ap(), w=[r_cv])
                    S.dma("sp", convb[:], lay["convb"].ap(), wa=[r_cv])
                    xr = sb([128, T + 8], F32, "xr")
                    r_xr = Res()
                    G(lambda e: e.memset(xr[:], 0.0), w=[r_xr])
                    acc = sb([128, T], F32, "cacc")
                    r_acc = Res()
                    xo = RR([sb([128, T], BF16, "cxo") for _ in range(2)])
                    tps = RR([ps([128, 512], BF16, "ctp") for _ in range(2)])
                    tos = RR([sb([128, 512], BF16, "cto") for _ in range(3)])
                    state = {}

                    def epi_xbc(ci, c0, ncol, t0, nt, pt, r_pt):
                        off = 2 if t0 < NCTX else 6
                        if ci % 2:
                            A(lambda e: e.copy(xr[:, t0 + off:t0 + off + nt], pt[:, 0:nt]), r=[r_pt], wa=[r_xr])
                        else:
                            V(lambda e: e.tensor_copy(xr[:, t0 + off:t0 + off + nt], pt[:, 0:nt]), r=[r_pt], wa=[r_xr])
                        if t0 + nt < T:
                            return
                        segs = [(0, NCTX, 0), (NCTX, nlat, 4)]
                        for (s0, sn, dl) in segs:
                            A(lambda e: e.activation(acc[:, s0:s0 + sn], xr[:, s0 + dl:s0 + dl + sn], AF.Identity,
                                                     bias=convb[:, ci:ci + 1], scale=convw[:, ci, 0:1]), r=[r_xr, r_cv], wa=[r_acc])
                            for k in range(1, 5):
                                V(lambda e: e.scalar_tensor_tensor(acc[:, s0:s0 + sn], xr[:, s0 + dl + k:s0 + dl + k + sn], convw[:, ci, k:k + 1],
                                                                   acc[:, s0:s0 + sn], ALU.mult, ALU.add), r=[r_xr, r_cv], w=[r_acc])
                        o, r_o = xo.next()
                        A(lambda e: e.activation(o[:], acc[:], AF.Silu), r=[r_acc], w=[r_o])
                        if ci < 24:
                            dst, col0, r_d = (x_tm, ci * 128, r_xtm) if ci < 16 else (B_tm, (ci - 16) * 128, r_Btm)
                            for t4 in range(0, NTT, 4):
                                n4 = min(4, NTT - t4)
                                tp_, r_tp = tps.next()
                                for q in range(n4):
                                    M(lambda e: e.transpose(tp_[:, q * 128:(q + 1) * 128], o[:, (t4 + q) * 128:(t4 + q + 1) * 128], identb[:]),
                                      r=[r_o, r_const], w=[r_tp] if q == 0 else [], wa=[r_tp] if q else [])
                                to, r_to = tos.next()
                                A(lambda e: e.copy(to[:, 0:n4 * 128], tp_[:, 0:n4 * 128]), r=[r_tp], w=[r_to])
                                S.dma("sp", dst.ap()[t4 * 128:(t4 + n4) * 128, col0:col0 + 128].rearrange("(q p) c -> p q c", p=128),
                                      to[:, 0:n4 * 128].rearrange("p (q c) -> p q c", q=n4), r=[r_to], wa=[r_d])
                        if ci >= 16:
                            gg = (ci - 16) % 8
                            dd, r_dd = (BT_d, r_BT) if ci < 24 else (CT_d, r_CT)
                            S.dma("sp", dd.ap()[gg], o[:], r=[r_o], wa=[r_dd])

                    linear_fm(sb, ps, inT, r_inT, 8, lay["inw"].ap(), [(2048 + 128 * i, 128) for i in range(32)], epi_xbc)
                    S.barrier()
                with ExitStack() as ph:
                    sb, ps = mk(ph)
                    dtT = sb([64, NTT, 128], F32, "dtT")
                    dA = sb([64, NTT, 128], F32, "dA")
                    laP = sb([64, NTT, 128], F32, "laP")
                    laT = sb([64, NTT, 128], F32, "laT")
                    rp = sb([64, NTT, 128], F32, "rp")
                    r_dt, r_dA, r_laP, r_laTs, r_rp = Res(), Res(), Res(), Res(), Res()
                    sm = sb([64, 4], F32, "dtsm")
                    r_sm = Res()
                    S.dma("sp", sm[:, 0:1], lay["alog"].ap(), w=[r_sm])
                    S.dma("sp", sm[:, 1:2], lay["dtb"].ap(), wa=[r_sm])
                    A(lambda e: e.activation(sm[:, 2:3], sm[:, 0:1], AF.Exp), r=[r_sm], wa=[r_sm])
                    V(lambda e: e.tensor_scalar(sm[:, 3:4], sm[:, 2:3], -1.0, None, ALU.mult), r=[r_sm], wa=[r_sm])
                    G(lambda e: e.memset(rp[:], 1.0), w=[r_rp])
                    G(lambda e: e.memset(rp[:, :, 0:1], 0.0), w=[r_rp])
                    dtf = dtT[:].rearrange("p c l -> p (c l)")

                    def epi_dt(ci, c0, ncol, t0, nt, pt, r_pt):
                        A(lambda e: e.activation(dtf[:, t0:t0 + nt], pt[0:64, 0:nt], AF.Exp, bias=sm[:, 1:2], scale=1.0), r=[r_pt, r_sm], wa=[r_dt])
                    linear_fm(sb, ps, inT, r_inT, 8, lay["inw"].ap(), [(6144, 64)], epi_dt)
                    A(lambda e: e.activation(dtf, dtf, AF.Ln, bias=1.0, scale=1.0), w=[r_dt])
                    V(lambda e: e.tensor_scalar(dA[:], dtT[:], sm[:, 3:4], None, ALU.mult), r=[r_dt, r_sm], w=[r_dA])
                    V(lambda e: e.tensor_tensor_scan(laP[:].rearrange("p c l -> p (c l)"), rp[:].rearrange("p c l -> p (c l)"),
                                                     dA[:].rearrange("p c l -> p (c l)"), 0.0, ALU.mult, ALU.add), r=[r_rp, r_dA], w=[r_laP])
                    V(lambda e: e.tensor_copy(laT[0:32], laP[0:32]), r=[r_laP], w=[r_laTs])
                    V(lambda e: e.tensor_tensor(laT[32:64], dA[32:64], laP[32:64], ALU.subtract), r=[r_laP, r_dA], wa=[r_laTs])
                    V(lambda e: e.tensor_tensor(laT[32:64], laT[32:64], laP[32:64, :, 127:128].broadcast_to([32, NTT, 128]), ALU.add), r=[r_laP], w=[r_laTs])
                    S.dma("sp", laT_d.ap(), laT[:].rearrange("p c l -> p (c l)"), r=[r_laTs], w=[r_laT])
                    tpp = RR([ps([128, 64], F32, "dtp") for _ in range(2)])
                    for tt in range(NTT):
                        for (src, r_src, dst) in ((laT, r_laTs, la_tm), (dtT, r_dt, dt_tm)):
                            tp_, r_tp = tpp.next()
                            M(lambda e: e.transpose(tp_[:], src[:, tt, :], ident[0:64, 0:64]), r=[r_src, r_const], w=[r_tp])
                            V(lambda e: e.tensor_copy(dst[:, tt, :], tp_[:]), r=[r_tp], wa=[r_tabs])
                    S.dma("sp", ltot_d.ap()[:, 0:32], la_tm[127:128, :, 0:32], r=[r_tabs], w=[r_ltot])
                    S.dma("sp", ltot_d.ap()[:, 32:64], la_tm[0:1, :, 32:64], r=[r_tabs], wa=[r_ltot])
                    S.dma("sp", LTB[:].rearrange("p c h -> p (c h)"), bass.AP(ltot_d, 0, [[0, 128], [1, NTT * 64]]), r=[r_ltot], wa=[r_tabs])
                    S.barrier()
                with ExitStack() as ph:
                    sb, ps = mk(ph)
                    zo = RR([sb([128, 512], BF16, "zo") for _ in range(3)])

                    def epi_z(g0, n, tt, pt, r_pt):
                        o, r_o = zo.next()
                        A(lambda e: e.activation(o[:, 0:n], pt[:, 0:n], AF.Silu), r=[r_pt], w=[r_o])
                        S.dma("sp", sz_tm.ap()[tt * 128:(tt + 1) * 128, g0:g0 + n], o[:, 0:n], r=[r_o], wa=[r_sz])
                    linear_tm(sb, ps, inT, r_inT, 8, lay["inw"].ap(), 0, 2048, epi_z)
                    S.barrier()
            with ExitStack() as ph:
                sb, ps = mk(ph)
                masks = sb([128, 2, 128], F32, "masks")
                r_mk = Res()
                S.dma("sp", masks[:], masks_in.ap().rearrange("d s l -> s d l"), w=[r_mk])
                xt_p = RR([sb([128, 2048], BF16, "sx") for _ in range(2)])
                bt_p = RR([sb([128, 1024], BF16, "sB") for _ in range(2)])
                BTs_p = RR([sb([128, 8, 128], BF16, "sBT") for _ in range(2)])
                CTs_p = RR([sb([128, 8, 128], BF16, "sCT") for _ in range(2)])
                LaB_p = RR([sb([128, 32, 128], F32, "sLaB") for _ in range(2)])
                dmat = sb([128, 32, 128], F32, "dmat")
                r_dmat = Res()
                decay = sb([128, 32, 128], BF16, "decay")
                r_decay = Res()
                wT = sb([128, 32, 128], BF16, "wT")
                r_wT = Res()
                CBm = sb([128, 8, 128], BF16, "CBm")
                r_CBm = Res()
                xdt = sb([128, 2048], BF16, "xdt")
                r_xdt = Res()
                xw = sb([128, 2048], BF16, "xw")
                r_xw = Res()
                sml = sb([128, 4, 32], F32, "ssml")
                r_sml = Res()
                ST = sb([128, 2048], F32, "ST")
                r_ST = Res()
                prevb = sb([128, 2048], BF16, "prevb")
                r_prevb = Res()
                ysb = sb([128, 2048], F32, "ysb")
                r_ysb = Res()
                yin_p = RR([sb([128, 2048], F32, "yin") for _ in range(2)])
                cbp = ps([128, 8, 128], F32, "cbp")
                r_cbp = Res()
                ydp = ps([128, 1024], F32, "ydp")
                r_ydp = Res()
                yop = ps([128, 1024], F32, "yop")
                r_yop = Res()
                stp = ps([128, 1024], F32, "stp")
                r_stp = Res()
                for dr in range(2):
                    order = list(range(NTT)) if dr == 0 else [1, 0] + list(range(NTT - 1, 1, -1))
                    V(lambda e: e.memset(ST[:], 0.0), w=[r_ST])
                    hc = dr * 32
                    for c in order:
                        tok = slice(c * 128, (c + 1) * 128)
                        xt, r_xt = xt_p.next()
                        S.dma("sp", xt[:], x_tm.ap()[tok, :], r=[r_xtm], w=[r_xt])
                        bt, r_bt = bt_p.next()
                        S.dma("sp", bt[:], B_tm.ap()[tok, :], r=[r_Btm], w=[r_bt])
                        BTs, r_BTs = BTs_p.next()
                        S.dma("sp", BTs[:], BT_d.ap()[:, :, tok].rearrange("g n t -> n g t"), r=[r_BT], w=[r_BTs])
                        CTs, r_CTs = CTs_p.next()
                        S.dma("sp", CTs[:], CT_d.ap()[:, :, tok].rearrange("g n t -> n g t"), r=[r_CT], w=[r_CTs])
                        LaB, r_LaB = LaB_p.next()
                        S.dma("sp", LaB[:], bass.AP(laT_d, hc * T + c * 128, [[0, 128], [T, 32], [1, 128]]), r=[r_laT], w=[r_LaB])
                        la_c = la_tm[:, c, hc:hc + 32]
                        A(lambda e: e.activation(sml[:, 0, :], la_c, AF.Exp), r=[r_tabs], w=[r_sml])
                        V(lambda e: e.tensor_tensor(sml[:, 3, :], LTB[:, c, hc:hc + 32], la_c, ALU.subtract), r=[r_tabs], w=[r_sml])
                        V(lambda e: e.tensor_single_scalar(sml[:, 3, :], sml[:, 3, :], 0.0, ALU.min), w=[r_sml])
                        A(lambda e: e.activation(sml[:, 1, :], sml[:, 3, :], AF.Exp), w=[r_sml])
                        A(lambda e: e.activation(sml[:, 2, :], LTB[:, c, hc:hc + 32], AF.Exp), r=[r_tabs], w=[r_sml])
                        for g in range(8):
                            M(lambda e: e.matmul(cbp[:, g, :], BTs[:, g, :], CTs[:, g, :], start=True, stop=True), r=[r_BTs, r_CTs],
                              w=[r_cbp] if g == 0 else [], wa=[r_cbp] if g else [])
                        V(lambda e: e.tensor_tensor(CBm[:], cbp[:], masks[:, dr:dr + 1, :].broadcast_to([128, 8, 128]), ALU.mult),
                          r=[r_cbp, r_mk], w=[r_CBm])
                        for h in range(32):
                            V(lambda e: e.tensor_scalar(dmat[:, h, :], LaB[:, h, :], la_tm[:, c, hc + h:hc + h + 1], 0.0, ALU.subtract, ALU.min),
                              r=[r_LaB, r_tabs], w=[r_dmat] if h == 0 else [], wa=[r_dmat] if h else [])
                        A(lambda e: e.activation(decay[:], dmat[:], AF.Exp), r=[r_dmat], w=[r_decay])
                        V(lambda e: e.tensor_tensor(wT[:].rearrange("p (g h) l -> p g h l", g=8), decay[:].rearrange("p (g h) l -> p g h l", g=8),
                                                    CBm[:].unsqueeze(2).broadcast_to([128, 8, 4, 128]), ALU.mult), r=[r_decay, r_CBm], w=[r_wT])
                        V(lambda e: e.tensor_tensor(xdt[:].rearrange("p (h q) -> p h q", h=32), xt[:].rearrange("p (h q) -> p h q", h=32),
                                                    dt_tm[:, c, hc:hc + 32].unsqueeze(2).broadcast_to([128, 32, 64]), ALU.mult), r=[r_xt, r_tabs], w=[r_xdt])
                        V(lambda e: e.tensor_tensor(xw[:].rearrange("p (h q) -> p h q", h=32), xdt[:].rearrange("p (h q) -> p h q", h=32),
                                                    sml[:, 1, :].unsqueeze(2).broadcast_to([128, 32, 64]), ALU.mult), r=[r_xdt, r_sml], w=[r_xw])
                        A(lambda e: e.copy(prevb[:], ST[:]), r=[r_ST], w=[r_prevb])
                        if dr == 1:
                            yin, r_yin = yin_p.next()
                            S.dma("sp", yin[:], Yacc.ap()[tok, :], r=[r_Y[c]], w=[r_yin])
                        for gh in range(2):
                            cs = slice(gh * 1024, (gh + 1) * 1024)
                            for hl in range(16):
                                h = gh * 16 + hl
                                M(lambda e: e.matmul(ydp[:, hl * 64:(hl + 1) * 64], wT[:, h, :], xdt[:, h * 64:(h + 1) * 64], start=True, stop=True),
                                  r=[r_wT, r_xdt], w=[r_ydp] if hl == 0 else [], wa=[r_ydp] if hl else [])
                            for gl in range(4):
                                g = gh * 4 + gl
                                M(lambda e: e.matmul(yop[:, gl * 256:(gl + 1) * 256], CTs[:, g, :], prevb[:, g * 256:(g + 1) * 256], start=True, stop=True),
                                  r=[r_CTs, r_prevb], w=[r_yop] if gl == 0 else [], wa=[r_yop] if gl else [])
                            for gl in range(4):
                                g = gh * 4 + gl
                                M(lambda e: e.matmul(stp[:, gl * 256:(gl + 1) * 256], bt[:, g * 128:(g + 1) * 128], xw[:, g * 256:(g + 1) * 256], start=True, stop=True),
                                  r=[r_bt, r_xw], w=[r_stp] if gl == 0 else [], wa=[r_stp] if gl else [])
                            if dr == 1:
                                V(lambda e: e.tensor_tensor(ysb[:, cs], ydp[:], yin[:, cs], ALU.add), r=[r_ydp, r_yin], w=[r_ysb] if gh == 0 else [], wa=[r_ysb] if gh else [])
                            else:
                                A(lambda e: e.copy(ysb[:, cs], ydp[:]), r=[r_ydp], w=[r_ysb] if gh == 0 else [], wa=[r_ysb] if gh else [])
                            for hl in range(16):
                                h = gh * 16 + hl
                                V(lambda e: e.scalar_tensor_tensor(ysb[:, h * 64:(h + 1) * 64], yop[:, hl * 64:(hl + 1) * 64], sml[:, 0, h:h + 1],
                                                                   ysb[:, h * 64:(h + 1) * 64], ALU.mult, ALU.add), r=[r_yop, r_sml], w=[r_ysb])
                            V(lambda e: e.tensor_tensor(ST[:, cs].rearrange("p (h q) -> p h q", h=16), ST[:, cs].rearrange("p (h q) -> p h q", h=16),
                                                        sml[:, 2, gh * 16:(gh + 1) * 16].unsqueeze(2).broadcast_to([128, 16, 64]), ALU.mult),
                              r=[r_sml, r_prevb], w=[r_ST])
                            V(lambda e: e.tensor_tensor(ST[:, cs], ST[:, cs], stp[:], ALU.add), r=[r_stp], w=[r_ST])
                        S.dma("pool", Yacc.ap()[tok, :], ysb[:], r=[r_ysb], w=[r_Y[c]])
                S.barrier()
            with ExitStack() as ph:
                sb, ps = mk(ph)
                dvec = sb([128, 2048], F32, "dvec")
                mnw = sb([128, 2048], F32, "mnw")
                r_dv = Res()
                S.dma("sp", dvec[:], bass.AP(lay["dvec"], 0, [[0, 128], [1, 2048]]), w=[r_dv])
                S.dma("sp", mnw[:], bass.AP(lay["mnw"], 0, [[0, 128], [1, 2048]]), wa=[r_dv])
                ow = sb([128, 16, D], BF16, "ow")
                r_ow = Res()
                for k4 in range(4):
                    S.dma("pool", ow[:, k4 * 4:(k4 + 1) * 4, :], lay["outw"].ap()[k4 * 512:(k4 + 1) * 512, :].rearrange("(k p) n -> p k n", p=128),
                          w=[r_ow] if k4 == 0 else [], wa=[r_ow] if k4 else [])
                y_p = RR([sb([128, 2048], F32, "ty") for _ in range(2)])
                x_p = RR([sb([128, 2048], BF16, "tx") for _ in range(2)])
                z_p = RR([sb([128, 2048], BF16, "tz") for _ in range(2)])
                g_p = RR([sb([128, 2048], F32, "tg") for _ in range(2)])
                gb_p = RR([sb([128, 2048], BF16, "tgb") for _ in range(2)])
                junk = sb([128, 2048], BF16, "tjunk")
                r_junk = Res()
                ss_p = RR([sb([128, 2], F32, "tss") for _ in range(2)])
                gT_p = RR([sb([128, 16, 128], BF16, "tgT") for _ in range(2)])
                tp_p = RR([ps([128, 512], BF16, "ttp") for _ in range(2)])
                op_p = RR([ps([128, 1024], F32, "top") for _ in range(2)])
                ht_p = RR([sb([128, 8, 128], F32, "tht") for _ in range(2)])
                for tt in range(NTT):
                    tok = slice(tt * 128, (tt + 1) * 128)
                    j = 1 if tt < 2 else 0
                    y, r_y = y_p.next()
                    S.dma("sp", y[:], Yacc.ap()[tok, :], r=[r_Y[tt]], w=[r_y])
                    xt, r_xt = x_p.next()
                    S.dma("sp", xt[:], x_tm.ap()[tok, :], r=[r_xtm], w=[r_xt])
                    zt, r_zt = z_p.next()
                    S.dma("sp", zt[:], sz_tm.ap()[tok, :], r=[r_sz], w=[r_zt])
                    gt_, r_g = g_p.next()
                    V(lambda e: e.tensor_tensor(gt_[:], xt[:], dvec[:], ALU.mult), r=[r_xt, r_dv], w=[r_g])
                    V(lambda e: e.tensor_tensor(gt_[:], gt_[:], y[:], ALU.add), r=[r_y], w=[r_g])
                    V(lambda e: e.tensor_tensor(gt_[:], gt_[:], zt[:], ALU.mult), r=[r_zt], w=[r_g])
                    ss, r_ss = ss_p.next()
                    A(lambda e: e.activation(junk[:], gt_[:], AF.Square, accum_out=ss[:, 0:1]), r=[r_g], w=[r_junk, r_ss])
                    A(lambda e: e.activation(ss[:, 1:2], ss[:, 0:1], AF.Sqrt, bias=EPS, scale=1.0 / 2048), w=[r_ss])
                    V(lambda e: e.reciprocal(ss[:, 1:2], ss[:, 1:2]), w=[r_ss])
                    gb, r_gb = gb_p.next()
                    V(lambda e: e.scalar_tensor_tensor(gb[:], gt_[:], ss[:, 1:2], mnw[:], ALU.mult, ALU.mult), r=[r_g, r_ss, r_dv], w=[r_gb])
                    gT, r_gT = gT_p.next()
                    for k4 in range(4):
                        tp_, r_tp = tp_p.next()
                        for q in range(4):
                            k = k4 * 4 + q
                            M(lambda e: e.transpose(tp_[:, q * 128:(q + 1) * 128], gb[:, k * 128:(k + 1) * 128], identb[:]),
                              r=[r_gb, r_const], w=[r_tp] if q == 0 else [], wa=[r_tp] if q else [])
                        A(lambda e: e.copy(gT[:, k4 * 4:(k4 + 1) * 4, :], tp_[:].rearrange("p (q t) -> p q t", q=4)), r=[r_tp],
                          w=[r_gT] if k4 == 0 else [], wa=[r_gT] if k4 else [])
                    op, r_op = op_p.next()
                    for dc in range(8):
                        for k in range(16):
                            M(lambda e: e.matmul(op[:, dc * 128:(dc + 1) * 128], ow[:, k, dc * 128:(dc + 1) * 128], gT[:, k, :], start=(k == 0), stop=(k == 15)),
                              r=[r_ow, r_gT], w=[r_op] if (dc == 0 and k == 0) else [], wa=[] if (dc == 0 and k == 0) else [r_op])
                    ht, r_ht = ht_p.next()
                    hr = [r_hT[(c, tt)] for c in range(8)]
                    S.dma("sp", ht[:], hT_ap[:, tok].rearrange("(c p) t -> p c t", p=128), r=hr, w=[r_ht])
                    for dc in range(8):
                        V(lambda e: e.scalar_tensor_tensor(ht[:, dc, :], op[:, dc * 128:(dc + 1) * 128], mod_gt[:, dc, j:j + 1], ht[:, dc, :], ALU.mult, ALU.add),
                          r=[r_op, r_mod], w=[r_ht])
                    S.dma("pool", hT_ap[:, tok].rearrange("(c p) t -> p c t", p=128), ht[:], r=[r_ht], wa=hr)
                S.barrier()

    def attn_layer(lay):
        qT_d = dscr("qT_d", [D, T], BF16)
        kT_d = dscr("kT_d", [256, T], BF16)
        v_tm = dscr("v_tm", [T, 256], BF16)
        sgT_d = dscr("sgT_d", [D, T], BF16)
        oT_d = dscr("oT_d", [D, T], BF16)
        r_q, r_k, r_v, r_sg, r_o = Res(), Res(), Res(), Res(), Res()
        with ExitStack() as st1:
            sb1, ps1 = mk(st1)
            inT = sb1([128, 8, T], BF16, "inT")
            r_inT = Res()
            with ExitStack() as ph:
                sb, ps = mk(ph)
                pre_pass(lay, sb, ps, inT, r_inT)
                S.barrier()
            with ExitStack() as ph:
                sb, ps = mk(ph)
                rope = sb([128, 2, nlat], F32, "rope")
                r_cst = Res()
                S.dma("sp", rope[:], lay["rope"].ap().rearrange("a p t -> p a t"), w=[r_cst])
                qkw = sb([128, 2], F32, "qkw")
                S.dma("sp", qkw[:], lay["qkw"].ap(), wa=[r_cst])
                permb = sb([128, 128], BF16, "permb")
                S.dma("pool", permb[:], lay["perm"].ap(), wa=[r_cst])
                bones = sb([128, 128], BF16, "bones")
                S.dma("pool", bones[:], lay["bones"].ap(), wa=[r_cst])
                sq_p = RR([sb([128, 512], BF16, "asq") for _ in range(2)])
                ss_p = RR([ps([128, 512], F32, "ass") for _ in range(2)])
                rs_p = RR([sb([128, 512], F32, "ars") for _ in range(2)])
                qn_p = RR([sb([128, 512], F32, "aqn") for _ in range(2)])
                qb_p = RR([sb([128, 512], BF16, "aqb") for _ in range(2)])
                rot_p = RR([ps([128, 512], F32, "arot") for _ in range(2)])
                t1_p = RR([sb([128, 512], F32, "at1") for _ in range(2)])
                t2_p = RR([sb([128, 512], F32, "at2") for _ in range(2)])
                qo_p = RR([sb([128, 512], BF16, "aqo") for _ in range(3)])

                def epi_qkg(ci, c0, ncol, t0, nt, pt, r_pt):
                    if c0 >= 1536:
                        o, r_o_ = qo_p.next()
                        A(lambda e: e.activation(o[:, 0:nt], pt[:, 0:nt], AF.Silu), r=[r_pt], w=[r_o_])
                        cg = (c0 - 1536) // 128
                        S.dma("sp", sgT_d.ap()[cg * 128:(cg + 1) * 128, t0:t0 + nt], o[:, 0:nt], r=[r_o_], wa=[r_sg])
                        return
                    isq = c0 < 1024
                    wcol = 0 if isq else 1
                    sq, r_sq = sq_p.next()
                    A(lambda e: e.activation(sq[:, 0:nt], pt[:, 0:nt], AF.Square), r=[r_pt], w=[r_sq])
                    ss, r_ss = ss_p.next()
                    M(lambda e: e.matmul(ss[:, 0:nt], bones[:], sq[:, 0:nt], start=True, stop=True), r=[r_sq, r_cst], w=[r_ss])
                    rs, r_rs = rs_p.next()
                    A(lambda e: e.activation(rs[:, 0:nt], ss[:, 0:nt], AF.Sqrt, bias=EPS, scale=1.0 / 64), r=[r_ss], w=[r_rs])
                    V(lambda e: e.reciprocal(rs[:, 0:nt], rs[:, 0:nt]), w=[r_rs])
                    o, r_o_ = qo_p.next()
                    if t0 < NCTX:
                        V(lambda e: e.scalar_tensor_tensor(o[:, 0:nt], pt[:, 0:nt], qkw[:, wcol:wcol + 1], rs[:, 0:nt], ALU.mult, ALU.mult),
                          r=[r_pt, r_rs, r_cst], w=[r_o_])
                    else:
                        qn, r_qn = qn_p.next()
                        V(lambda e: e.scalar_tensor_tensor(qn[:, 0:nt], pt[:, 0:nt], qkw[:, wcol:wcol + 1], rs[:, 0:nt], ALU.mult, ALU.mult),
                          r=[r_pt, r_rs, r_cst], w=[r_qn])
                        qb, r_qb = qb_p.next()
                        A(lambda e: e.copy(qb[:, 0:nt], qn[:, 0:nt]), r=[r_qn], w=[r_qb])
                        rot, r_rot = rot_p.next()
                        M(lambda e: e.matmul(rot[:, 0:nt], permb[:], qb[:, 0:nt], start=True, stop=True), r=[r_qb, r_cst], w=[r_rot])
                        l0 = t0 - NCTX
                        t1, r_t1 = t1_p.next()
                        G(lambda e: e.tensor_tensor(t1[:, 0:nt], qn[:, 0:nt], rope[:, 0, l0:l0 + nt], ALU.mult), r=[r_qn, r_cst], w=[r_t1])
                        t2, r_t2 = t2_p.next()
                        V(lambda e: e.tensor_tensor(t2[:, 0:nt], rot[:, 0:nt], rope[:, 1, l0:l0 + nt], ALU.mult), r=[r_rot, r_cst], w=[r_t2])
                        V(lambda e: e.tensor_tensor(o[:, 0:nt], t1[:, 0:nt], t2[:, 0:nt], ALU.add), r=[r_t1, r_t2], w=[r_o_])
                    if isq:
                        S.dma("sp", qT_d.ap()[c0:c0 + 128, t0:t0 + nt], o[:, 0:nt], r=[r_o_], wa=[r_q])
                    else:
                        S.dma("sp", kT_d.ap()[c0 - 1024:c0 - 1024 + 128, t0:t0 + nt], o[:, 0:nt], r=[r_o_], wa=[r_k])

                cols = [(128 * i, 128) for i in range(10)] + [(1536 + 128 * i, 128) for i in range(8)]
                linear_fm(sb, ps, inT, r_inT, 8, lay["inw"].ap(), cols, epi_qkg)
                vo_p = RR([sb([128, 256], BF16, "avo") for _ in range(3)])

                def epi_v(g0, n, tt, pt, r_pt):
                    o, r_o_ = vo_p.next()
                    V(lambda e: e.tensor_copy(o[:, 0:n], pt[:, 0:n]), r=[r_pt], w=[r_o_])
                    S.dma("sp", v_tm.ap()[tt * 128:(tt + 1) * 128, :], o[:, 0:n], r=[r_o_], wa=[r_v])
                linear_tm(sb, ps, inT, r_inT, 8, lay["inw"].ap(), 1280, 256, epi_v)
                S.barrier()
        with ExitStack() as ph:
            sb, ps = mk(ph)
            onesf = sb([128, 64], F32, "aones")
            r_on = Res()
            G(lambda e: e.memset(onesf[:], 1.0), w=[r_on])
            Vg = sb([128, NTT, 65], BF16, "Vg")
            r_Vg = Res()
            G(lambda e: e.memset(Vg[:], 1.0), w=[r_Vg])
            kk_p = RR([sb([128, T], BF16, "kk") for _ in range(2)])
            qc_p = RR([sb([128, T], BF16, "qc") for _ in range(2)])
            sg_p = RR([sb([64, T], BF16, "sgh") for _ in range(2)])
            sp_p = RR([ps([128, 512], F32, "asp") for _ in range(3)])
            P_p = RR([sb([128, 512], BF16, "aP") for _ in range(3)])
            oa_p = RR([ps([128, 512], F32, "aoa") for _ in range(2)])
            bc_p = RR([ps([64, 512], F32, "abc") for _ in range(2)])
            osb_p = RR([sb([128, 512], F32, "aosb") for _ in range(2)])
            o1_p = RR([sb([64, 512], F32, "ao1") for _ in range(2)])
            og_p = RR([sb([64, 512], BF16, "aog") for _ in range(2)])
            for gk in range(4):
                kk, r_kk = kk_p.next()
                S.dma("sp", kk[0:64, :], kT_d.ap()[gk * 64:(gk + 1) * 64, :], r=[r_k], w=[r_kk])
                S.dma("sp", kk[64:128, :], kT_d.ap()[gk * 64:(gk + 1) * 64, :], r=[r_k], wa=[r_kk])
                S.dma("sp", Vg[:, :, 0:64], v_tm.ap()[:, gk * 64:(gk + 1) * 64].rearrange("(t p) d -> p t d", p=128), r=[r_v], w=[r_Vg])
                for qc in (2 * gk, 2 * gk + 1):
                    qt, r_qt = qc_p.next()
                    S.dma("sp", qt[:], qT_d.ap()[qc * 128:(qc + 1) * 128, :], r=[r_q], w=[r_qt])
                    for hh in range(2):
                        h = 2 * qc + hh
                        pr = slice(64 * hh, 64 * hh + 64)
                        sgh, r_sgh = sg_p.next()
                        S.dma("sp", sgh[:], sgT_d.ap()[h * 64:(h + 1) * 64, :], r=[r_sg], w=[r_sgh])
                        for (t0, nt) in BLKS:
                            ktiles = [0, 1] if t0 < NCTX else list(range(NTT))
                            oa, r_oa = oa_p.next()
                            for ki, kt in enumerate(ktiles):
                                sp_, r_sp = sp_p.next()
                                M(lambda e: e.matmul(sp_[:, 0:nt], kk[pr, kt * 128:(kt + 1) * 128], qt[pr, t0:t0 + nt], start=True, stop=True),
                                  r=[r_kk, r_qt], w=[r_sp])
                                P, r_P = P_p.next()
                                A(lambda e: e.activation(P[:, 0:nt], sp_[:, 0:nt], AF.Exp, bias=-8.0, scale=0.125), r=[r_sp], w=[r_P])
                                M(lambda e: e.matmul(oa[0:65, 0:nt], Vg[:, kt, :], P[:, 0:nt], start=(ki == 0), stop=(ki == len(ktiles) - 1)),
                                  r=[r_Vg, r_P], w=[r_oa] if ki == 0 else [], wa=[r_oa] if ki else [])
                            osb, r_osb = osb_p.next()
                            V(lambda e: e.tensor_copy(osb[0:65, 0:nt], oa[0:65, 0:nt]), r=[r_oa], w=[r_osb])
                            V(lambda e: e.reciprocal(osb[64:65, 0:nt], osb[64:65, 0:nt]), w=[r_osb])
                            bc, r_bc = bc_p.next()
                            M(lambda e: e.matmul(bc[:, 0:nt], onesf[64:65, :], osb[64:65, 0:nt], start=True, stop=True), r=[r_osb, r_on], w=[r_bc])
                            o1, r_o1 = o1_p.next()
                            V(lambda e: e.tensor_tensor(o1[:, 0:nt], osb[0:64, 0:nt], bc[:, 0:nt], ALU.mult), r=[r_osb, r_bc], w=[r_o1])
                            og, r_og = og_p.next()
                            G(lambda e: e.tensor_tensor(og[:, 0:nt], o1[:, 0:nt], sgh[:, t0:t0 + nt], ALU.mult), r=[r_o1, r_sgh], w=[r_og])
                            S.dma("sp", oT_d.ap()[h * 64:(h + 1) * 64, t0:t0 + nt], og[:, 0:nt], r=[r_og], wa=[r_o])
            S.barrier()
        with ExitStack() as ph:
            sb, ps = mk(ph)
            oT = sb([128, 8, T], BF16, "oT")
            r_oT = Res()
            S.dma("sp", oT[:], oT_d.ap().rearrange("(c p) t -> p c t", p=128), r=[r_o], w=[r_oT])
            linear_fm(sb, ps, oT, r_oT, 8, lay["outw"].ap(), [(128 * i, 128) for i in range(8)], make_resid_epi(sb))
            S.barrier()

    def s5_layer(lay):
        uT_d = dscr("uT_d", [D, T], BF16)
        szT_d = dscr("szT_d", [D, T], BF16)
        gT_d = dscr("gT_d", [D, T], BF16)
        y2T_d = dscr("y2T_d", [D, T], BF16)
        r_u, r_sz, r_g, r_y2 = Res(), Res(), Res(), Res()
        NLV = 1
        while (1 << (NLV - 1)) < T:
            NLV += 1
        with ExitStack() as st1:
            sb1, ps1 = mk(st1)
            inT = sb1([128, 8, T], BF16, "inT")
            r_inT = Res()
            with ExitStack() as ph:
                sb, ps = mk(ph)
                pre_pass(lay, sb, ps, inT, r_inT)
                S.barrier()
            with ExitStack() as ph:
                sb, ps = mk(ph)
                uo_p = RR([sb([128, 512], BF16, "suo") for _ in range(3)])

                def epi_uz(ci, c0, ncol, t0, nt, pt, r_pt):
                    o, r_o_ = uo_p.next()
                    if c0 < 1024:
                        V(lambda e: e.tensor_copy(o[:, 0:nt], pt[:, 0:nt]), r=[r_pt], w=[r_o_])
                        S.dma("sp", uT_d.ap()[c0:c0 + 128, t0:t0 + nt], o[:, 0:nt], r=[r_o_], wa=[r_u])
                    else:
                        A(lambda e: e.activation(o[:, 0:nt], pt[:, 0:nt], AF.Silu), r=[r_pt], w=[r_o_])
                        S.dma("sp", szT_d.ap()[c0 - 1024:c0 - 1024 + 128, t0:t0 + nt], o[:, 0:nt], r=[r_o_], wa=[r_sz])
                linear_fm(sb, ps, inT, r_inT, 8, lay["inw"].ap(), [(128 * i, 128) for i in range(16)], epi_uz)
                S.barrier()
        with ExitStack() as ph:
            sb, ps = mk(ph)
            lam = sb([128, 3, 64], F32, "lam")
            r_t = Res()
            S.dma("sp", lam[:], lay["lam"].ap(), w=[r_t])
            tb = sb([128, 16, 64], F32, "stb")
            tbi = sb([128, 64], I32, "stbi")
            coef = sb([128, 3, 64], F32, "coef")
            pw = sb([128, 64, NLV, 3], F32, "pw")
            lr, li, ls = lam[:, 0, :], lam[:, 1, :], lam[:, 2, :]
            X = lambda i: tb[:, i, :]

            def vt(fn):
                V(fn, w=[r_t])

            def at(fn):
                A(fn, w=[r_t])
            at(lambda e: e.activation(X(0), ls, AF.Exp))
            vt(lambda e: e.tensor_tensor(X(1), lr, X(0), ALU.mult))
            at(lambda e: e.activation(X(2), X(1), AF.Exp))
            vt(lambda e: e.tensor_tensor(X(3), li, X(0), ALU.mult))

            def sin_of(dst, src, shift):
                vt(lambda e: e.tensor_scalar(X(4), src, shift, 1.0 / (2 * PI), ALU.add, ALU.mult))
                vt(lambda e: e.tensor_copy(tbi[:], X(4)))
                vt(lambda e: e.tensor_copy(X(5), tbi[:]))
                vt(lambda e: e.tensor_scalar(X(4), src, shift, None, ALU.add))
                vt(lambda e: e.scalar_tensor_tensor(X(4), X(5), -2 * PI, X(4), ALU.mult, ALU.add))
                vt(lambda e: e.tensor_single_scalar(X(5), X(4), PI, ALU.is_gt))
                vt(lambda e: e.scalar_tensor_tensor(X(4), X(5), -2 * PI, X(4), ALU.mult, ALU.add))
                vt(lambda e: e.tensor_single_scalar(X(5), X(4), -PI, ALU.is_lt))
                vt(lambda e: e.scalar_tensor_tensor(X(4), X(5), 2 * PI, X(4), ALU.mult, ALU.add))
                at(lambda e: e.activation(dst, X(4), AF.Sin))
            sin_of(X(6), X(3), 0.0)
            sin_of(X(7), X(3), PI / 2)
            vt(lambda e: e.tensor_tensor(X(8), X(2), X(7), ALU.mult))
            vt(lambda e: e.tensor_tensor(X(9), X(2), X(6), ALU.mult))
            vt(lambda e: e.tensor_tensor(X(10), lr, lr, ALU.mult))
            vt(lambda e: e.tensor_tensor(X(11), li, li, ALU.mult))
            vt(lambda e: e.tensor_tensor(X(10), X(10), X(11), ALU.add))
            vt(lambda e: e.reciprocal(X(10), X(10)))
            vt(lambda e: e.tensor_scalar(X(11), X(8), -1.0, None, ALU.add))
            vt(lambda e: e.tensor_tensor(X(12), X(11), lr, ALU.mult))
            vt(lambda e: e.tensor_tensor(X(13), X(9), li, ALU.mult))
            vt(lambda e: e.tensor_tensor(X(12), X(12), X(13), ALU.add))
            vt(lambda e: e.tensor_tensor(coef[:, 0, :], X(12), X(10), ALU.mult))
            vt(lambda e: e.tensor_tensor(X(12), X(9), lr, ALU.mult))
            vt(lambda e: e.tensor_tensor(X(13), X(11), li, ALU.mult))
            vt(lambda e: e.tensor_tensor(X(12), X(12), X(13), ALU.subtract))
            vt(lambda e: e.tensor_tensor(coef[:, 1, :], X(12), X(10), ALU.mult))
            vt(lambda e: e.tensor_scalar(coef[:, 2, :], coef[:, 1, :], -1.0, None, ALU.mult))
            vt(lambda e: e.tensor_copy(pw[:, :, 0, 0], X(8)))
            vt(lambda e: e.tensor_copy(pw[:, :, 0, 1], X(9)))
            for lv in range(NLV):
                vt(lambda e: e.tensor_scalar(pw[:, :, lv, 2], pw[:, :, lv, 1], -1.0, None, ALU.mult))
                if lv + 1 < NLV:
                    vt(lambda e: e.tensor_tensor(X(12), pw[:, :, lv, 0], pw[:, :, lv, 0], ALU.mult))
                    vt(lambda e: e.tensor_tensor(X(13), pw[:, :, lv, 1], pw[:, :, lv, 1], ALU.mult))
                    vt(lambda e: e.tensor_tensor(pw[:, :, lv + 1, 0], X(12), X(13), ALU.subtract))
                    vt(lambda e: e.tensor_tensor(X(12), pw[:, :, lv, 0], pw[:, :, lv, 1], ALU.mult))
                    vt(lambda e: e.tensor_scalar(pw[:, :, lv + 1, 1], X(12), 2.0, None, ALU.mult))
            brt = sb([128, 2, 2, 32, 128], BF16, "brt")
            for q in range(2):
                for k in range(2):
                    for j8 in range(4):
                        S.dma("pool", brt[:, q, k, j8 * 8:(j8 + 1) * 8, :], lay["brt"].ap()[q, :, k, j8 * 8:(j8 + 1) * 8, :], wa=[r_t])
            crp = sb([128, 2, 2, 32, 32], BF16, "crp")
            S.dma("pool", crp[:, 0], lay["crp"].ap()[0], wa=[r_t])
            S.dma("pool", crp[:, 1], lay["crp"].ap()[1], wa=[r_t])
            at(lambda e: e.mul(crp[:, 1], crp[:, 1], -1.0))
            sd = sb([128, 8], F32, "sd")
            S.dma("sp", sd[:], lay["sd"].ap(), wa=[r_t])
            Xr = sb([128, T], F32, "Xr")
            Xi = sb([128, T], F32, "Xi")
            Yr = sb([128, T], F32, "Yr")
            Yi = sb([128, T], F32, "Yi")
            r_X, r_Yb = Res(), Res()
            xb = [[sb([128, T], BF16, "xb%d%d" % (k, q)) for q in range(2)] for k in range(2)]
            r_xb = [Res(), Res()]
            uc_p = RR([sb([128, T], BF16, "suc") for _ in range(2)])
            p12_p = RR([ps([128, 512], F32, "sp12") for _ in range(4)])
            tmp_p = RR([sb([128, 512], F32, "stmp") for _ in range(2)])
            yp_p = RR([ps([128, 512], F32, "syp") for _ in range(2)])
            yv = sb([128, T], F32, "yv")
            ga = Yr
            r_yv, r_ga = Res(), r_Yb
            go_p = RR([sb([128, T], BF16, "sgo") for _ in range(1)])

            def sview(t, off, step, a0, cnt, mult):
                s0 = off + a0 * step
                st_ = mult * step
                return t[:, s0:s0 + (cnt - 1) * st_ + 1:st_]

            def scan(col, k):
                outr, outi = xb[k]

                def rec(tr, ti, off, step, n, lv, yoff, top):
                    if n == 1:
                        if top:
                            A(lambda e: e.copy(outr[:, 0:1], tr[:, off:off + 1]), r=[r_X], w=[r_xb[k]])
                            A(lambda e: e.copy(outi[:, 0:1], ti[:, off:off + 1]), r=[r_X], w=[r_xb[k]])
                        return
                    m = n // 2
                    ne = n - m
                    ar, ai, nai = pw[:, col, lv, 0:1], pw[:, col, lv, 1:2], pw[:, col, lv, 2:3]
                    rs_ = [r_X, r_Yb, r_t]
                    Ev = lambda t, a0, cnt: sview(t, off, step, 2 * a0, cnt, 2)
                    Ov = lambda t, a0, cnt: sview(t, off, step, 2 * a0 + 1, cnt, 2)
                    yr, yi = Yr[:, yoff:yoff + m], Yi[:, yoff:yoff + m]
                    V(lambda e: e.scalar_tensor_tensor(yr, Ev(tr, 0, m), ar, Ov(tr, 0, m), ALU.mult, ALU.add), r=rs_, w=[r_Yb])
                    V(lambda e: e.scalar_tensor_tensor(yr, Ev(ti, 0, m), nai, yr, ALU.mult, ALU.add), r=rs_, w=[r_Yb])
                    V(lambda e: e.scalar_tensor_tensor(yi, Ev(ti, 0, m), ar, Ov(ti, 0, m), ALU.mult, ALU.add), r=rs_, w=[r_Yb])
                    V(lambda e: e.scalar_tensor_tensor(yi, Ev(tr, 0, m), ai, yi, ALU.mult, ALU.add), r=rs_, w=[r_Yb])
                    rec(Yr, Yi, yoff, 1, m, lv + 1, yoff + m, False)
                    ne1 = ne - 1
                    zr, zi = Yr[:, yoff:yoff + ne1], Yi[:, yoff:yoff + ne1]
                    if top:
                        A(lambda e: e.copy(sview(outr, 0, 1, 1, m, 2), yr), r=[r_Yb], w=[r_xb[k]])
                        A(lambda e: e.copy(sview(outi, 0, 1, 1, m, 2), yi), r=[r_Yb], w=[r_xb[k]])
                        A(lambda e: e.copy(outr[:, 0:1], tr[:, off:off + 1]), r=[r_X], w=[r_xb[k]])
                        A(lambda e: e.copy(outi[:, 0:1], ti[:, off:off + 1]), r=[r_X], w=[r_xb[k]])
                        if ne1 > 0:
                            V(lambda e: e.scalar_tensor_tensor(Ev(tr, 1, ne1), zr, ar, Ev(tr, 1, ne1), ALU.mult, ALU.add), r=rs_, w=[r_X])
                            V(lambda e: e.scalar_tensor_tensor(sview(outr, 0, 1, 2, ne1, 2), zi, nai, Ev(tr, 1, ne1), ALU.mult, ALU.add), r=rs_, w=[r_xb[k]])
                            V(lambda e: e.scalar_tensor_tensor(Ev(ti, 1, ne1), zi, ar, Ev(ti, 1, ne1), ALU.mult, ALU.add), r=rs_, w=[r_X])
                            V(lambda e: e.scalar_tensor_tensor(sview(outi, 0, 1, 2, ne1, 2), zr, ai, Ev(ti, 1, ne1), ALU.mult, ALU.add), r=rs_, w=[r_xb[k]])
                    else:
                        wres = [r_Yb] if tr is Yr else [r_X]
                        A(lambda e: e.copy(Ov(tr, 0, m), yr), r=[r_Yb], w=wres)
                        A(lambda e: e.copy(Ov(ti, 0, m), yi), r=[r_Yb], w=wres)
                        if ne1 > 0:
                            V(lambda e: e.scalar_tensor_tensor(Ev(tr, 1, ne1), zr, ar, Ev(tr, 1, ne1), ALU.mult, ALU.add), r=rs_, w=wres)
                            V(lambda e: e.scalar_tensor_tensor(Ev(tr, 1, ne1), zi, nai, Ev(tr, 1, ne1), ALU.mult, ALU.add), r=rs_, w=wres)
                            V(lambda e: e.scalar_tensor_tensor(Ev(ti, 1, ne1), zi, ar, Ev(ti, 1, ne1), ALU.mult, ALU.add), r=rs_, w=wres)
                            V(lambda e: e.scalar_tensor_tensor(Ev(ti, 1, ne1), zr, ai, Ev(ti, 1, ne1), ALU.mult, ALU.add), r=rs_, w=wres)
                rec(Xr, Xi, 0, 1, T, 0, 0, True)

            def bwd_pos(t0, nt):
                return (NCTX - t0 - nt) if t0 < NCTX else (NCTX + T - t0 - nt)

            uc = None
            SK = os.environ.get("S5_SKIP", "")
            for j in range(32 if "L" not in SK else 0):
                cj, jm = j // 4, j % 4
                pr = slice(32 * jm, 32 * jm + 32)
                if jm == 0:
                    uc, r_uc = uc_p.next()
                    S.dma("sp", uc[:], uT_d.ap()[cj * 128:(cj + 1) * 128, :], r=[r_u], w=[r_uc])
                for k in range(2):
                    col = k * 32 + j
                    for (t0, nt) in BLKS:
                        if k == 0:
                            i0 = t0
                            uv = uc[:, t0:t0 + nt]
                        else:
                            i0 = bwd_pos(t0, nt)
                            uv = uc[:, t0:t0 + nt][:, ::-1]
                        if "m" in SK:
                            continue
                        p1, r_p1 = p12_p.next()
                        p2, r_p2 = p12_p.next()
                        M(lambda e: e.matmul(p1[:, 0:nt], brt[:, 0, k, j, :], uv, start=True, stop=True), r=[r_uc, r_t], w=[r_p1])
                        M(lambda e: e.matmul(p2[:, 0:nt], brt[:, 1, k, j, :], uv, start=True, stop=True), r=[r_uc, r_t], w=[r_p2])
                        if "e" in SK:
                            continue
                        tm, r_tm = tmp_p.next()
                        if "a" not in SK:
                            A(lambda e: e.activation(tm[:, 0:nt], p2[:, 0:nt], AF.Identity, scale=coef[:, 2, col:col + 1]), r=[r_t], w=[r_tm, r_p2])
                        if "v" not in SK:
                            V(lambda e: e.scalar_tensor_tensor(Xr[:, i0:i0 + nt], p1[:, 0:nt], coef[:, 0, col:col + 1], tm[:, 0:nt], ALU.mult, ALU.add),
                              r=[r_tm, r_t], w=[r_X, r_p1])
                        tm2, r_tm2 = tmp_p.next()
                        if "a" not in SK:
                            A(lambda e: e.activation(tm2[:, 0:nt], p1[:, 0:nt], AF.Identity, scale=coef[:, 1, col:col + 1]), r=[r_t], w=[r_tm2, r_p1])
                        if "v" not in SK:
                            V(lambda e: e.scalar_tensor_tensor(Xi[:, i0:i0 + nt], p2[:, 0:nt], coef[:, 0, col:col + 1], tm2[:, 0:nt], ALU.mult, ALU.add),
                              r=[r_tm2, r_t], w=[r_X, r_p2])
                    if "s" not in SK:
                        scan(col, k)
                for (t0, nt) in (BLKS if "r" not in SK else []):
                    yp, r_yp = yp_p.next()
                    i0 = bwd_pos(t0, nt)
                    rv = (lambda a: a[:, ::-1]) if not os.environ.get("S5_NOREV") else (lambda a: a)
                    ops = [(crp[:, 0, 0, j, :], xb[0][0][:, t0:t0 + nt], r_xb[0]), (crp[:, 1, 0, j, :], xb[0][1][:, t0:t0 + nt], r_xb[0]),
                           (crp[:, 0, 1, j, :], rv(xb[1][0][:, i0:i0 + nt]), r_xb[1]), (crp[:, 1, 1, j, :], rv(xb[1][1][:, i0:i0 + nt]), r_xb[1])]
                    for qi, (lh, rh, rr) in enumerate(ops):
                        M(lambda e: e.matmul(yp[pr, 0:nt], lh, rh, start=(qi == 0), stop=(qi == 3), tile_position=(0, 32 * jm)), r=[rr, r_t],
                          w=[r_yp] if qi == 0 else [], wa=[r_yp] if qi else [])
                    V(lambda e: e.scalar_tensor_tensor(yv[pr, t0:t0 + nt], uc[pr, t0:t0 + nt], sd[pr, cj:cj + 1], yp[pr, 0:nt], ALU.mult, ALU.add),
                      r=[r_yp, r_uc, r_t], w=[r_yv] if (jm == 0 and t0 == 0) else [], wa=[] if (jm == 0 and t0 == 0) else [r_yv])
                if jm == 3 and "g" not in SK:
                    A(lambda e: e.activation(ga[:], yv[:], AF.Square), r=[r_yv], w=[r_ga])
                    V(lambda e: e.tensor_scalar(ga[:], ga[:], 0.044715, 1.0, ALU.mult, ALU.add), w=[r_ga])
                    V(lambda e: e.tensor_tensor(ga[:], ga[:], yv[:], ALU.mult), r=[r_yv], w=[r_ga])
                    A(lambda e: e.activation(ga[:], ga[:], AF.Sigmoid, scale=1.5957691216057308), w=[r_ga])
                    go, r_go = go_p.next()
                    V(lambda e: e.tensor_tensor(go[:], ga[:], yv[:], ALU.mult), r=[r_ga, r_yv], w=[r_go])
                    S.dma("sp", gT_d.ap()[cj * 128:(cj + 1) * 128, :], go[:], r=[r_go], wa=[r_g])
            S.barrier()
        with ExitStack() as ph:
            sb, ps = mk(ph)
            gT = sb([128, 8, T], BF16, "gT")
            r_gT = Res()
            S.dma("sp", gT[:], gT_d.ap().rearrange("(c p) t -> p c t", p=128), r=[r_g], w=[r_gT])
            glub = sb([128, 8], F32, "glub")
            r_gb = Res()
            S.dma("sp", glub[:], lay["glub"].ap(), w=[r_gb])
            sig_p = RR([sb([128, 512], F32, "ssig") for _ in range(2)])
            szt_p = RR([sb([128, 512], BF16, "sszt") for _ in range(2)])
            y2_p = RR([sb([128, 512], BF16, "sy2") for _ in range(3)])

            def epi_glu(ci, c0, ncol, t0, nt, pt, r_pt):
                sg, r_sg_ = sig_p.next()
                A(lambda e: e.activation(sg[:, 0:nt], pt[:, 0:nt], AF.Sigmoid, bias=glub[:, ci:ci + 1], scale=1.0), r=[r_pt, r_gb], w=[r_sg_])
                szt, r_szt = szt_p.next()
                S.dma("sp", szt[:, 0:nt], szT_d.ap()[c0:c0 + 128, t0:t0 + nt], r=[r_sz], w=[r_szt])
                V(lambda e: e.tensor_tensor(sg[:, 0:nt], sg[:, 0:nt], gT[:, ci, t0:t0 + nt], ALU.mult), r=[r_gT], w=[r_sg_])
                y2, r_y2_ = y2_p.next()
                V(lambda e: e.tensor_tensor(y2[:, 0:nt], sg[:, 0:nt], szt[:, 0:nt], ALU.mult), r=[r_sg_, r_szt], w=[r_y2_])
                S.dma("sp", y2T_d.ap()[c0:c0 + 128, t0:t0 + nt], y2[:, 0:nt], r=[r_y2_], wa=[r_y2])
            linear_fm(sb, ps, gT, r_gT, 8, lay["gluw"].ap(), [(128 * i, 128) for i in range(8)], epi_glu)
            S.barrier()
        with ExitStack() as ph:
            sb, ps = mk(ph)
            y2 = sb([128, 8, T], BF16, "y2r")
            r_y2r = Res()
            S.dma("sp", y2[:], y2T_d.ap().rearrange("(c p) t -> p c t", p=128), r=[r_y2], w=[r_y2r])
            linear_fm(sb, ps, y2, r_y2r, 8, lay["outw"].ap(), [(128 * i, 128) for i in range(8)], make_resid_epi(sb))
            S.barrier()

    for i in layers:
        kind = i % 3
        if kind == 0:
            mamba_layer(L[i])
        elif kind == 1:
            attn_layer(L[i])
        else:
            s5_layer(L[i])

    with ExitStack() as ph:
        sb, ps = mk(ph)
        pre_pass(None, sb, ps, None, None, final=True)
    S.barrier()
    es.close()
    nc._ninst = S.ninst
    return nc


def prep_inputs(inputs, b, nlat=NLAT, layers=(0, 1, 2, 3)):
    f = lambda a: np.ascontiguousarray(np.asarray(a, dtype=np.float32))
    chunked = lambda v, n: f(np.asarray(v, np.float32).reshape(n, 128).T)
    m = {}
    m["x"] = f(inputs["x"][b][:nlat])
    m["ctx"] = f(inputs["ctx"][b])
    m["cc"] = f(np.stack([chunked(inputs["c"][b], 8), chunked(inputs["c_ctx"], 8)], axis=-1))
    m["ident"] = np.eye(128, dtype=np.float32)
    m["fnw"] = chunked(inputs["final_norm_w"], 8)
    has_m = False
    for i in layers:
        m["normw%d" % i] = chunked(inputs["norm_w"][i], 8)
        m["modw%d" % i] = f(inputs["mod_w"][i])
        m["modb%d" % i] = chunked(inputs["mod_b"][i], 24)
        kind, j = i % 3, i // 3
        if kind == 0:
            has_m = True
            m["m_in_w%d" % j] = f(inputs["m_in_w"][j])
            cw = np.asarray(inputs["m_conv_w"][j], np.float32)
            m["m_convw%d" % j] = f(cw.reshape(5, 32, 128).transpose(2, 1, 0))
            m["m_convb%d" % j] = chunked(inputs["m_conv_b"][j], 32)
            m["m_alog%d" % j] = f(np.asarray(inputs["m_a_log"][j], np.float32).reshape(64, 1))
            m["m_dtb%d" % j] = f(np.asarray(inputs["m_dt_bias"][j], np.float32).reshape(64, 1))
            m["m_dvec%d" % j] = f(np.repeat(np.asarray(inputs["m_d"][j], np.float32), 64))
            m["m_normw%d" % j] = f(inputs["m_norm_w"][j])
            m["m_out_w%d" % j] = f(inputs["m_out_w"][j])
        elif kind == 1:
            m["a_in_w"] = f(inputs["a_in_w"][0])
            m["a_out_w"] = f(inputs["a_out_w"][0])
            m["a_qkw"] = f(np.stack([np.tile(np.asarray(inputs["a_q_norm"][0], np.float32), 2),
                                     np.tile(np.asarray(inputs["a_k_norm"][0], np.float32), 2)], axis=1))
            grid_w = 64
            pos = np.arange(nlat)
            r_idx, c_idx = (pos // grid_w).astype(np.float32), (pos % grid_w).astype(np.float32)
            inv = (10000.0 ** (-np.arange(0, 32, 2, dtype=np.float32) / 32)).astype(np.float32)
            dd = np.arange(128) % 64
            ax, part, ii = dd // 32, (dd % 32) // 16, dd % 16
            ang = np.where(ax[:, None] == 0, r_idx[None, :], c_idx[None, :]).astype(np.float32) * inv[ii][:, None]
            m["a_rope"] = f(np.stack([np.cos(ang), np.sin(ang)]))
            perm = np.zeros((128, 128), np.float32)
            for dcol in range(128):
                if part[dcol] == 0:
                    perm[dcol + 16, dcol] = -1.0
                else:
                    perm[dcol - 16, dcol] = 1.0
            m["a_perm"] = perm
            bo = np.zeros((128, 128), np.float32)
            bo[:64, :64] = 1.0
            bo[64:, 64:] = 1.0
            m["a_bones"] = bo
        else:
            m["s_in_w"] = f(inputs["s_in_w"][0])
            m["s_glu_w"] = f(inputs["s_glu_w"][0])
            m["s_out_w"] = f(inputs["s_out_w"][0])
            m["s_sd"] = chunked(inputs["s_d"][0], 8)
            m["s_glub"] = chunked(inputs["s_glu_b"][0], 8)
            lre = np.asarray(inputs["s_lambda_re"][0], np.float32)
            lim = np.asarray(inputs["s_lambda_im"][0], np.float32)
            lst = np.asarray(inputs["s_log_step"][0], np.float32)

            def pair_layout(a):
                a = a.reshape(2, 32, 2, 64)
                return a.transpose(2, 3, 0, 1).reshape(128, 64)
            lam = np.stack([pair_layout(lre), pair_layout(lim),
                            pair_layout(np.broadcast_to(lst[:, :, None], (2, 64, 64)))], axis=1)
            m["s_lam"] = f(lam)
            brt = np.zeros((2, 128, 2, 32, 128), np.float32)
            crp = np.zeros((2, 128, 2, 32, 32), np.float32)
            for q, (bsrc, csrc) in enumerate(((inputs["s_b_re"][0], inputs["s_c_re"][0]), (inputs["s_b_im"][0], inputs["s_c_im"][0]))):
                bsrc = np.asarray(bsrc, np.float32)
                csrc = np.asarray(csrc, np.float32)
                for k in range(2):
                    for j in range(32):
                        for gl in range(2):
                            g_ = 2 * j + gl
                            r0 = 32 * (j % 4) + 16 * gl
                            brt[q, r0:r0 + 16, k, j, gl * 64:(gl + 1) * 64] = bsrc[k, g_].T
                            crp[q, gl * 64:(gl + 1) * 64, k, j, 16 * gl:16 * gl + 16] = csrc[k, g_].T
            m["s_brt"] = brt
            m["s_crp"] = crp
    if has_m:
        up = np.triu(np.ones((128, 128), np.float32))
        m["masks"] = f(np.stack([up, up.T]))
    return m


def kernel(**inputs):
    nc = build_program()
    in_maps = [prep_inputs(inputs, core % 4) for core in range(4)]
    in_maps = in_maps + in_maps
    res = run_bass_kernel_spmd(nc, in_maps, core_ids=list(range(8)))
    out = np.stack([np.asarray(res.results[b]["out"], dtype=np.float32) for b in range(4)], axis=0)
    return out
```

```python
import os
import numpy as np
from contextlib import ExitStack
import ml_dtypes
import concourse.bass as bass
import concourse.mybir as mybir
from concourse.bass_utils import run_bass_kernel_spmd

F32 = mybir.dt.float32
BF16 = mybir.dt.bfloat16
I32 = mybir.dt.int32
AF = mybir.ActivationFunctionType
ALU = mybir.AluOpType
AX = mybir.AxisListType

D = 1024
NCTX = 256
NLAT = 4096
EPS = 1e-6
M_IN = 6208
PI = float(np.pi)


class Res:
    __slots__ = ("w", "r", "name")

    def __init__(self, name=""):
        self.w = {}
        self.r = {}
        self.name = name


class Sched:
    def __init__(self, nc, es):
        self.nc = nc
        self.eng = {"pe": nc.tensor, "act": nc.scalar, "dve": nc.vector, "pool": nc.gpsimd, "sp": nc.sync}
        self.sem = {}
        self.cnt = {}
        self.known = {e: {} for e in self.eng}
        for e in self.eng:
            self.sem[e] = es.enter_context(nc.semaphore("s_" + e))
            self.cnt[e] = 0
        self.NDS = 8
        self.dslot = {}
        for q in ("sp", "pool"):
            for i in range(self.NDS):
                k = "d_%s%d" % (q, i)
                self.sem[k] = es.enter_context(nc.semaphore(k))
                self.cnt[k] = 0
            self.dslot[q] = 0
        self.ninst = 0

    def _wait(self, e, evs):
        kn = self.known[e]
        for k, v in evs.items():
            if v <= 0 or (e == "pe" and k == "pe") or kn.get(k, 0) >= v:
                continue
            self.eng[e].wait_ge(self.sem[k], v)
            kn[k] = v

    @staticmethod
    def _deps(r, w, wa):
        evs = {}

        def add(d):
            for k, v in d.items():
                if evs.get(k, 0) < v:
                    evs[k] = v
        for x in r:
            add(x.w)
        for x in w:
            add(x.w)
            add(x.r)
        for x in wa:
            add(x.r)
        return evs

    @staticmethod
    def _commit(k, v, r, w, wa):
        for x in r:
            if x.r.get(k, 0) < v:
                x.r[k] = v
        for x in w:
            if x.w.get(k, 0) < v:
                x.w[k] = v
        for x in wa:
            if x.w.get(k, 0) < v:
                x.w[k] = v

    def op(self, e, fn, r=(), w=(), wa=(), inc=True):
        self._wait(e, self._deps(r, w, wa))
        ins = fn(self.eng[e])
        if inc:
            self.cnt[e] += 1
            ins.then_inc(self.sem[e], 1)
            self._commit(e, self.cnt[e], r, w, wa)
        else:
            self._commit(e, self.cnt[e] + 1, r, w, wa)
        self.ninst += 1
        return ins

    def dma(self, q, out, in_, r=(), w=(), wa=(), **kw):
        i = self.dslot[q]
        self.dslot[q] = (i + 1) % self.NDS
        k = "d_%s%d" % (q, i)
        evs = self._deps(r, w, wa)
        evs[k] = max(evs.get(k, 0), self.cnt[k])
        self._wait(q, evs)
        ins = self.eng[q].dma_start(out=out, in_=in_, **kw)
        self.cnt[k] += 16
        ins.then_inc(self.sem[k], 16)
        self._commit(k, self.cnt[k], r, w, wa)
        self.ninst += 1
        return ins

    def barrier(self):
        evs = {k: v for k, v in self.cnt.items() if v > 0}
        for e in self.eng:
            self._wait(e, dict(evs))


class RR:
    def __init__(self, tiles):
        self.t = tiles
        self.r = [Res() for _ in tiles]
        self.i = 0

    def next(self):
        i = self.i
        self.i = (i + 1) % len(self.t)
        return self.t[i], self.r[i]


def build_program(nlat=NLAT, layers=(0, 1, 2, 3)):
    T = NCTX + nlat
    NTT = T // 128
    BLKS = [(0, NCTX)] + [(NCTX + 512 * i, 512) for i in range(nlat // 512)]
    nc = bass.Bass("TRN2", target_bir_lowering=False)
    es = ExitStack()
    es.enter_context(nc.allow_low_precision("bf16 matmul operands, fp32 accumulation"))
    S = Sched(nc, es)
    uid = [0]

    def mk(stack):
        def sb(shape, dt=F32, name="t"):
            uid[0] += 1
            return stack.enter_context(nc.sbuf_tensor("%s_%d" % (name, uid[0]), list(shape), dt))

        def ps(shape, dt=F32, name="p"):
            uid[0] += 1
            return stack.enter_context(nc.psum_tensor("%s_%d" % (name, uid[0]), list(shape), dt))
        return sb, ps

    def din(name, shape, dt=F32):
        return nc.dram_tensor(name, list(shape), dt, kind="ExternalInput")

    def dscr(name, shape, dt=F32):
        return nc.dram_tensor(name, list(shape), dt)

    V = lambda fn, r=(), w=(), wa=(): S.op("dve", fn, r, w, wa)
    A = lambda fn, r=(), w=(), wa=(): S.op("act", fn, r, w, wa)
    G = lambda fn, r=(), w=(), wa=(): S.op("pool", fn, r, w, wa)
    M = lambda fn, r=(), w=(), wa=(), inc=True: S.op("pe", fn, r, w, wa, inc)

    x_in = din("x", [nlat, D])
    ctx_in = din("ctx", [NCTX, D])
    cc_in = din("cc", [128, 8, 2])
    ident_in = din("ident", [128, 128])
    fnw_in = din("fnw", [128, 8])
    out_t = nc.dram_tensor("out", [nlat, D], F32, kind="ExternalOutput")
    L = {}
    for i in layers:
        L[i] = dict(normw=din("normw%d" % i, [128, 8]), modw=din("modw%d" % i, [D, 3 * D]), modb=din("modb%d" % i, [128, 24]))
        kind, j = i % 3, i // 3
        if kind == 0:
            L[i].update(inw=din("m_in_w%d" % j, [D, M_IN]), convw=din("m_convw%d" % j, [128, 32, 5]), convb=din("m_convb%d" % j, [128, 32]),
                        alog=din("m_alog%d" % j, [64, 1]), dtb=din("m_dtb%d" % j, [64, 1]), dvec=din("m_dvec%d" % j, [2048]),
                        mnw=din("m_normw%d" % j, [2048]), outw=din("m_out_w%d" % j, [2048, D]))
        elif kind == 1:
            L[i].update(inw=din("a_in_w", [D, 2560]), qkw=din("a_qkw", [128, 2]), outw=din("a_out_w", [D, D]),
                        rope=din("a_rope", [2, 128, nlat]), perm=din("a_perm", [128, 128]), bones=din("a_bones", [128, 128]))
        else:
            L[i].update(inw=din("s_in_w", [D, 2048]), lam=din("s_lam", [128, 3, 64]), brt=din("s_brt", [2, 128, 2, 32, 128]),
                        crp=din("s_crp", [2, 128, 2, 32, 32]), sd=din("s_sd", [128, 8]), gluw=din("s_glu_w", [D, D]),
                        glub=din("s_glub", [128, 8]), outw=din("s_out_w", [D, D]))
    masks_in = din("masks", [2, 128, 128]) if any(i % 3 == 0 for i in layers) else None

    hT = dscr("hT", [D, T])
    hT_ap = hT.ap()
    r_hT = {(c, tt): Res() for c in range(8) for tt in range(NTT)}

    def hres(c, t0, nt):
        return [r_hT[(c, tt)] for tt in range(t0 // 128, (t0 + nt + 127) // 128)]

    def hres_all(t0, nt):
        out = []
        for c in range(8):
            out += hres(c, t0, nt)
        return out

    gsb, gps = mk(es)
    ident = gsb([128, 128], F32, "ident")
    r_const = Res("const")
    S.dma("sp", ident[:], ident_in.ap(), w=[r_const])
    identb = gsb([128, 128], BF16, "identb")
    ones_bf = gsb([128, 128], BF16, "ones")
    G(lambda e: e.memset(ones_bf[:], 1.0), wa=[r_const])
    V(lambda e: e.tensor_copy(identb[:], ident[:]), r=[r_const], wa=[r_const])
    fnw = gsb([128, 8], F32, "fnw")
    S.dma("sp", fnw[:], fnw_in.ap(), wa=[r_const])
    cc = gsb([128, 8, 2], F32, "cc")
    S.dma("sp", cc[:], cc_in.ap(), wa=[r_const])
    scs = gsb([128, 8, 2], F32, "scs")
    A(lambda e: e.activation(scs[:], cc[:], AF.Silu), r=[r_const], wa=[r_const])
    mod_sc = gsb([128, 8, 2], F32, "mod_sc")
    mod_bi = gsb([128, 8, 2], F32, "mod_bi")
    mod_gt = gsb([128, 8, 2], F32, "mod_gt")
    r_mod = Res("mod")
    S.barrier()

    with ExitStack() as ph:
        sb, ps = mk(ph)
        xin = RR([sb([128, D], F32, "xin") for _ in range(2)])
        tp = RR([ps([128, 512], F32, "tp") for _ in range(2)])
        xo = RR([sb([128, 8, 128], F32, "xo") for _ in range(2)])
        for tt in range(NTT):
            xt, r_xt = xin.next()
            src = ctx_in.ap()[tt * 128:(tt + 1) * 128, :] if tt < 2 else x_in.ap()[(tt - 2) * 128:(tt - 1) * 128, :]
            S.dma("sp", xt[:], src, w=[r_xt])
            ot, r_ot = xo.next()
            for half in range(2):
                pt, r_pt = tp.next()
                for j in range(4):
                    c = half * 4 + j
                    M(lambda e: e.transpose(pt[:, j * 128:(j + 1) * 128], xt[:, c * 128:(c + 1) * 128], ident[:]),
                      r=[r_xt, r_const], w=[r_pt] if j == 0 else [], wa=[r_pt] if j else [])
                dst = ot[:, half * 4:(half + 1) * 4, :]
                if half:
                    A(lambda e: e.copy(dst, pt[:].rearrange("p (j t) -> p j t", j=4)), r=[r_pt], wa=[r_ot])
                else:
                    V(lambda e: e.tensor_copy(dst, pt[:].rearrange("p (j t) -> p j t", j=4)), r=[r_pt], w=[r_ot])
            S.dma("pool", hT_ap[:, tt * 128:(tt + 1) * 128].rearrange("(c p) t -> p c t", p=128), ot[:], r=[r_ot],
                  wa=[r_hT[(c, tt)] for c in range(8)])
        S.barrier()

    def pre_pass(lay, sb, ps, inT, r_inT, final=False):
        if not final:
            mw = RR([sb([128, 8, 512], F32, "modw") for _ in range(2)])
            mp = RR([ps([128, 512], F32, "modp") for _ in range(2)])
            modT = sb([128, 24, 2], F32, "modT")
            r_modT = Res()
            modb = sb([128, 24], F32, "modb")
            normw = sb([128, 8], F32, "normw")
            r_small = Res()
            S.dma("sp", modb[:], lay["modb"].ap(), w=[r_small])
            S.dma("sp", normw[:], lay["normw"].ap(), wa=[r_small])
            for cg in range(6):
                wt, r_wt = mw.next()
                S.dma("sp", wt[:], lay["modw"].ap()[:, cg * 512:(cg + 1) * 512].rearrange("(k p) n -> p k n", p=128), w=[r_wt])
                for c4 in range(4):
                    pt, r_pt = mp.next()
                    for k in range(8):
                        M(lambda e: e.matmul(pt[:, 0:2], wt[:, k, c4 * 128:(c4 + 1) * 128], scs[:, k, :], start=(k == 0), stop=(k == 7)),
                          r=[r_wt, r_const], w=[r_pt] if k == 0 else [], wa=[r_pt] if k else [])
                    col = cg * 4 + c4
                    V(lambda e: e.tensor_scalar(modT[:, col, :], pt[:, 0:2], modb[:, col:col + 1], None, ALU.add),
                      r=[r_pt, r_small], wa=[r_modT])
            V(lambda e: e.tensor_scalar(mod_sc[:], modT[:, 8:16, :], 1.0, None, ALU.add), r=[r_modT], w=[r_mod])
            V(lambda e: e.tensor_tensor(mod_sc[:], mod_sc[:], normw[:].unsqueeze(2).broadcast_to([128, 8, 2]), ALU.mult), r=[r_small], w=[r_mod])
            V(lambda e: e.tensor_copy(mod_bi[:], modT[:, 0:8, :]), r=[r_modT], w=[r_mod])
            V(lambda e: e.tensor_copy(mod_gt[:], modT[:, 16:24, :]), r=[r_modT], w=[r_mod])
        hb = RR([sb([128, 8, 512], F32, "hb") for _ in range(2)])
        sq = RR([sb([128, 8, 512], BF16, "sq") for _ in range(2)])
        ssp = RR([ps([128, 512], F32, "ssp") for _ in range(2)])
        rstd = RR([sb([128, 512], F32, "rstd") for _ in range(2)])
        tmp = RR([sb([128, 512], F32, "ntmp") for _ in range(3)])
        if final:
            hn = RR([sb([128, 8, 512], F32, "hn") for _ in range(2)])
            tp = RR([ps([128, 512], F32, "ftp") for _ in range(2)])
            ot = RR([sb([128, D], F32, "fot") for _ in range(2)])
            r_out = Res()
        for (t0, nt) in BLKS:
            j = 1 if t0 < NCTX else 0
            if final and j == 1:
                continue
            h, r_h = hb.next()
            S.dma("sp", h[:, :, 0:nt], hT_ap[:, t0:t0 + nt].rearrange("(c p) t -> p c t", p=128), r=hres_all(t0, nt), w=[r_h])
            q, r_q = sq.next()
            A(lambda e: e.activation(q[:, :, 0:nt], h[:, :, 0:nt], AF.Square), r=[r_h], w=[r_q])
            sp_, r_sp = ssp.next()
            for c in range(8):
                M(lambda e: e.matmul(sp_[:, 0:nt], ones_bf[:], q[:, c, 0:nt], start=(c == 0), stop=(c == 7)),
                  r=[r_q, r_const], w=[r_sp] if c == 0 else [], wa=[r_sp] if c else [], inc=(c == 7))
            rs, r_rs = rstd.next()
            A(lambda e: e.activation(rs[:, 0:nt], sp_[:, 0:nt], AF.Sqrt, bias=EPS, scale=1.0 / D), r=[r_sp], w=[r_rs])
            V(lambda e: e.reciprocal(rs[:, 0:nt], rs[:, 0:nt]), w=[r_rs])
            if not final:
                for c in range(8):
                    tm, r_tm = tmp.next()
                    V(lambda e: e.tensor_tensor(tm[:, 0:nt], h[:, c, 0:nt], rs[:, 0:nt], ALU.mult), r=[r_h, r_rs], w=[r_tm])
                    A(lambda e: e.activation(inT[:, c, t0:t0 + nt], tm[:, 0:nt], AF.Identity, bias=mod_bi[:, c, j:j + 1], scale=mod_sc[:, c, j:j + 1]),
                      r=[r_tm, r_mod], wa=[r_inT])
            else:
                hn_, r_hn = hn.next()
                for c in range(8):
                    V(lambda e: e.scalar_tensor_tensor(hn_[:, c, 0:nt], h[:, c, 0:nt], fnw[:, c:c + 1], rs[:, 0:nt], ALU.mult, ALU.mult),
                      r=[r_h, r_rs, r_const], w=[r_hn] if c == 0 else [], wa=[r_hn] if c else [])
                for tl in range(nt // 128):
                    o, r_o = ot.next()
                    for half in range(2):
                        pt, r_pt = tp.next()
                        for jj in range(4):
                            c = half * 4 + jj
                            M(lambda e: e.transpose(pt[:, jj * 128:(jj + 1) * 128], hn_[:, c, tl * 128:(tl + 1) * 128], ident[:]),
                              r=[r_hn, r_const], w=[r_pt] if jj == 0 else [], wa=[r_pt] if jj else [])
                        if half:
                            A(lambda e: e.copy(o[:, 512:1024], pt[:]), r=[r_pt], wa=[r_o])
                        else:
                            V(lambda e: e.tensor_copy(o[:, 0:512], pt[:]), r=[r_pt], w=[r_o])
                    row = t0 - NCTX + tl * 128
                    S.dma("pool", out_t.ap()[row:row + 128, :], o[:], r=[r_o], wa=[r_out])
        if final:
            evs = dict(r_out.w)
            S._wait("sp", evs)

    def linear_fm(sb, ps, act, r_act, KC, W_ap, col_chunks, epi, blks=None, t_off=0):
        wts = RR([sb([128, KC, 128], BF16, "lw") for _ in range(3)])
        pts = RR([ps([128, 512], F32, "lp") for _ in range(2)])
        for ci, (c0, ncol) in enumerate(col_chunks):
            wt, r_wt = wts.next()
            S.dma("pool", wt[:, :, 0:ncol], W_ap[:, c0:c0 + ncol].rearrange("(k p) n -> p k n", p=128), w=[r_wt])
            for (t0, nt) in (blks or BLKS):
                pt, r_pt = pts.next()
                for k in range(KC):
                    M(lambda e: e.matmul(pt[0:ncol, 0:nt], wt[:, k, 0:ncol], act[:, k, t0 - t_off:t0 - t_off + nt], start=(k == 0), stop=(k == KC - 1)),
                      r=[r_wt, r_act], w=[r_pt] if k == 0 else [], wa=[r_pt] if k else [], inc=(k == KC - 1))
                epi(ci, c0, ncol, t0, nt, pt, r_pt)

    def linear_tm(sb, ps, act, r_act, KC, W_ap, c0, ncols, epi):
        wts = RR([sb([128, KC, 512], BF16, "lwt") for _ in range(2)])
        pts = RR([ps([128, 512], F32, "lpt") for _ in range(2)])
        for g0 in range(0, ncols, 512):
            n = min(512, ncols - g0)
            wt, r_wt = wts.next()
            S.dma("pool", wt[:, :, 0:n], W_ap[:, c0 + g0:c0 + g0 + n].rearrange("(k p) n -> p k n", p=128), w=[r_wt])
            for tt in range(NTT):
                pt, r_pt = pts.next()
                for k in range(KC):
                    M(lambda e: e.matmul(pt[:, 0:n], act[:, k, tt * 128:(tt + 1) * 128], wt[:, k, 0:n], start=(k == 0), stop=(k == KC - 1)),
                      r=[r_wt, r_act], w=[r_pt] if k == 0 else [], wa=[r_pt] if k else [], inc=(k == KC - 1))
                epi(g0, n, tt, pt, r_pt)

    def make_resid_epi(sb):
        hts = RR([sb([128, 512], F32, "rh") for _ in range(3)])

        def epi(ci, c0, ncol, t0, nt, pt, r_pt):
            c = c0 // 128
            j = 1 if t0 < NCTX else 0
            ht, r_ht = hts.next()
            S.dma("sp", ht[:, 0:nt], hT_ap[c * 128:(c + 1) * 128, t0:t0 + nt], r=hres(c, t0, nt), w=[r_ht])
            V(lambda e: e.scalar_tensor_tensor(ht[:, 0:nt], pt[:, 0:nt], mod_gt[:, c, j:j + 1], ht[:, 0:nt], ALU.mult, ALU.add),
              r=[r_pt, r_mod], w=[r_ht])
            S.dma("pool", hT_ap[c * 128:(c + 1) * 128, t0:t0 + nt], ht[:, 0:nt], r=[r_ht], wa=hres(c, t0, nt))
        return epi

    def mamba_layer(lay):
        x_tm = dscr("x_tm%d" % uid[0], [T, 2048], BF16)
        B_tm = dscr("B_tm%d" % uid[0], [T, 1024], BF16)
        BT_d = dscr("BT_d%d" % uid[0], [8, 128, T], BF16)
        CT_d = dscr("CT_d%d" % uid[0], [8, 128, T], BF16)
        sz_tm = dscr("sz_tm%d" % uid[0], [T, 2048], BF16)
        laT_d = dscr("laT_d%d" % uid[0], [64, T], F32)
        ltot_d = dscr("ltot_d%d" % uid[0], [NTT, 64], F32)
        Yacc = dscr("Yacc%d" % uid[0], [T, 2048], F32)
        uid[0] += 1
        r_xtm, r_Btm, r_BT, r_CT, r_sz, r_laT, r_ltot = Res(), Res(), Res(), Res(), Res(), Res(), Res()
        r_Y = [Res() for _ in range(NTT)]
        with ExitStack() as lst:
            lsb, lps = mk(lst)
            la_tm = lsb([128, NTT, 64], F32, "la_tm")
            dt_tm = lsb([128, NTT, 64], F32, "dt_tm")
            LTB = lsb([128, NTT, 64], F32, "LTB")
            r_tabs = Res()
            with ExitStack() as st1:
                sb1, ps1 = mk(st1)
                inT = sb1([128, 8, T], BF16, "inT")
                r_inT = Res()
                with ExitStack() as ph:
                    sb, ps = mk(ph)
                    pre_pass(lay, sb, ps, inT, r_inT)
                    S.barrier()
                with ExitStack() as ph:
                    sb, ps = mk(ph)
                    convw = sb([128, 32, 5], F32, "convw")
                    convb = sb([128, 32], F32, "convb")
                    r_cv = Res()
                    S.dma("sp", convw[:], lay["convw"].ap(), w=[r_cv])
                    S.dma("sp", convb[:], lay["convb"].ap(), wa=[r_cv])
                    xr = sb([128, T + 8], F32, "xr")
                    r_xr = Res()
                    G(lambda e: e.memset(xr[:], 0.0), w=[r_xr])
                    acc = sb([128, T], F32, "cacc")
                    r_acc = Res()
                    xo = RR([sb([128, T], BF16, "cxo") for _ in range(2)])
                    tps = RR([ps([128, 512], BF16, "ctp") for _ in range(2)])
                    tos = RR([sb([128, 512], BF16, "cto") for _ in range(3)])
                    state = {}

                    def epi_xbc(ci, c0, ncol, t0, nt, pt, r_pt):
                        off = 2 if t0 < NCTX else 6
                        if ci % 2:
                            A(lambda e: e.copy(xr[:, t0 + off:t0 + off + nt], pt[:, 0:nt]), r=[r_pt], wa=[r_xr])
                        else:
                            V(lambda e: e.tensor_copy(xr[:, t0 + off:t0 + off + nt], pt[:, 0:nt]), r=[r_pt], wa=[r_xr])
                        if t0 + nt < T:
                            return
                        segs = [(0, NCTX, 0), (NCTX, nlat, 4)]
                        for (s0, sn, dl) in segs:
                            A(lambda e: e.activation(acc[:, s0:s0 + sn], xr[:, s0 + dl:s0 + dl + sn], AF.Identity,
                                                     bias=convb[:, ci:ci + 1], scale=convw[:, ci, 0:1]), r=[r_xr, r_cv], wa=[r_acc])
                            for k in range(1, 5):
                                V(lambda e: e.scalar_tensor_tensor(acc[:, s0:s0 + sn], xr[:, s0 + dl + k:s0 + dl + k + sn], convw[:, ci, k:k + 1],
                                                                   acc[:, s0:s0 + sn], ALU.mult, ALU.add), r=[r_xr, r_cv], w=[r_acc])
                        o, r_o = xo.next()
                        A(lambda e: e.activation(o[:], acc[:], AF.Silu), r=[r_acc], w=[r_o])
                        if ci < 24:
                            dst, col0, r_d = (x_tm, ci * 128, r_xtm) if ci < 16 else (B_tm, (ci - 16) * 128, r_Btm)
                            for t4 in range(0, NTT, 4):
                                n4 = min(4, NTT - t4)
                                tp_, r_tp = tps.next()
                                for q in range(n4):
                                    M(lambda e: e.transpose(tp_[:, q * 128:(q + 1) * 128], o[:, (t4 + q) * 128:(t4 + q + 1) * 128], identb[:]),
                                      r=[r_o, r_const], w=[r_tp] if q == 0 else [], wa=[r_tp] if q else [])
                                to, r_to = tos.next()
                                A(lambda e: e.copy(to[:, 0:n4 * 128], tp_[:, 0:n4 * 128]), r=[r_tp], w=[r_to])
                                S.dma("sp", dst.ap()[t4 * 128:(t4 + n4) * 128, col0:col0 + 128].rearrange("(q p) c -> p q c", p=128),
                                      to[:, 0:n4 * 128].rearrange("p (q c) -> p q c", q=n4), r=[r_to], wa=[r_d])
                        if ci >= 16:
                            gg = (ci - 16) % 8
                            dd, r_dd = (BT_d, r_BT) if ci < 24 else (CT_d, r_CT)
                            S.dma("sp", dd.ap()[gg], o[:], r=[r_o], wa=[r_dd])

                    linear_fm(sb, ps, inT, r_inT, 8, lay["inw"].ap(), [(2048 + 128 * i, 128) for i in range(32)], epi_xbc)
                    S.barrier()
                with ExitStack() as ph:
                    sb, ps = mk(ph)
                    dtT = sb([64, NTT, 128], F32, "dtT")
                    dA = sb([64, NTT, 128], F32, "dA")
                    laP = sb([64, NTT, 128], F32, "laP")
                    laT = sb([64, NTT, 128], F32, "laT")
                    rp = sb([64, NTT, 128], F32, "rp")
                    r_dt, r_dA, r_laP, r_laTs, r_rp = Res(), Res(), Res(), Res(), Res()
                    sm = sb([64, 4], F32, "dtsm")
                    r_sm = Res()
                    S.dma("sp", sm[:, 0:1], lay["alog"].ap(), w=[r_sm])
                    S.dma("sp", sm[:, 1:2], lay["dtb"].ap(), wa=[r_sm])
                    A(lambda e: e.activation(sm[:, 2:3], sm[:, 0:1], AF.Exp), r=[r_sm], wa=[r_sm])
                    V(lambda e: e.tensor_scalar(sm[:, 3:4], sm[:, 2:3], -1.0, None, ALU.mult), r=[r_sm], wa=[r_sm])
                    G(lambda e: e.memset(rp[:], 1.0), w=[r_rp])
                    G(lambda e: e.memset(rp[:, :, 0:1], 0.0), w=[r_rp])
                    dtf = dtT[:].rearrange("p c l -> p (c l)")

                    def epi_dt(ci, c0, ncol, t0, nt, pt, r_pt):
                        A(lambda e: e.activation(dtf[:, t0:t0 + nt], pt[0:64, 0:nt], AF.Exp, bias=sm[:, 1:2], scale=1.0), r=[r_pt, r_sm], wa=[r_dt])
                    linear_fm(sb, ps, inT, r_inT, 8, lay["inw"].ap(), [(6144, 64)], epi_dt)
                    A(lambda e: e.activation(dtf, dtf, AF.Ln, bias=1.0, scale=1.0), w=[r_dt])
                    V(lambda e: e.tensor_scalar(dA[:], dtT[:], sm[:, 3:4], None, ALU.mult), r=[r_dt, r_sm], w=[r_dA])
                    V(lambda e: e.tensor_tensor_scan(laP[:].rearrange("p c l -> p (c l)"), rp[:].rearrange("p c l -> p (c l)"),
                                                     dA[:].rearrange("p c l -> p (c l)"), 0.0, ALU.mult, ALU.add), r=[r_rp, r_dA], w=[r_laP])
                    V(lambda e: e.tensor_copy(laT[0:32], laP[0:32]), r=[r_laP], w=[r_laTs])
                    V(lambda e: e.tensor_tensor(laT[32:64], dA[32:64], laP[32:64], ALU.subtract), r=[r_laP, r_dA], wa=[r_laTs])
                    V(lambda e: e.tensor_tensor(laT[32:64], laT[32:64], laP[32:64, :, 127:128].broadcast_to([32, NTT, 128]), ALU.add), r=[r_laP], w=[r_laTs])
                    S.dma("sp", laT_d.ap(), laT[:].rearrange("p c l -> p (c l)"), r=[r_laTs], w=[r_laT])
                    tpp = RR([ps([128, 64], F32, "dtp") for _ in range(2)])
                    for tt in range(NTT):
                        for (src, r_src, dst) in ((laT, r_laTs, la_tm), (dtT, r_dt, dt_tm)):
                            tp_, r_tp = tpp.next()
                            M(lambda e: e.transpose(tp_[:], src[:, tt, :], ident[0:64, 0:64]), r=[r_src, r_const], w=[r_tp])
                            V(lambda e: e.tensor_copy(dst[:, tt, :], tp_[:]), r=[r_tp], wa=[r_tabs])
                    S.dma("sp", ltot_d.ap()[:, 0:32], la_tm[127:128, :, 0:32], r=[r_tabs], w=[r_ltot])
                    S.dma("sp", ltot_d.ap()[:, 32:64], la_tm[0:1, :, 32:64], r=[r_tabs], wa=[r_ltot])
                    S.dma("sp", LTB[:].rearrange("p c h -> p (c h)"), bass.AP(ltot_d, 0, [[0, 128], [1, NTT * 64]]), r=[r_ltot], wa=[r_tabs])
                    S.barrier()
                with ExitStack() as ph:
                    sb, ps = mk(ph)
                    zo = RR([sb([128, 512], BF16, "zo") for _ in range(3)])

                    def epi_z(g0, n, tt, pt, r_pt):
                        o, r_o = zo.next()
                        A(lambda e: e.activation(o[:, 0:n], pt[:, 0:n], AF.Silu), r=[r_pt], w=[r_o])
                        S.dma("sp", sz_tm.ap()[tt * 128:(tt + 1) * 128, g0:g0 + n], o[:, 0:n], r=[r_o], wa=[r_sz])
                    linear_tm(sb, ps, inT, r_inT, 8, lay["inw"].ap(), 0, 2048, epi_z)
                    S.barrier()
            with ExitStack() as ph:
                sb, ps = mk(ph)
                masks = sb([128, 2, 128], F32, "masks")
                r_mk = Res()
                S.dma("sp", masks[:], masks_in.ap().rearrange("d s l -> s d l"), w=[r_mk])
                xt_p = RR([sb([128, 2048], BF16, "sx") for _ in range(2)])
                bt_p = RR([sb([128, 1024], BF16, "sB") for _ in range(2)])
                BTs_p = RR([sb([128, 8, 128], BF16, "sBT") for _ in range(2)])
                CTs_p = RR([sb([128, 8, 128], BF16, "sCT") for _ in range(2)])
                LaB_p = RR([sb([128, 32, 128], F32, "sLaB") for _ in range(2)])
                dmat = sb([128, 32, 128], F32, "dmat")
                r_dmat = Res()
                decay = sb([128, 32, 128], BF16, "decay")
                r_decay = Res()
                wT = sb([128, 32, 128], BF16, "wT")
                r_wT = Res()
                CBm = sb([128, 8, 128], BF16, "CBm")
                r_CBm = Res()
                xdt = sb([128, 2048], BF16, "xdt")
                r_xdt = Res()
                xw = sb([128, 2048], BF16, "xw")
                r_xw = Res()
                sml = sb([128, 4, 32], F32, "ssml")
                r_sml = Res()
                ST = sb([128, 2048], F32, "ST")
                r_ST = Res()
                prevb = sb([128, 2048], BF16, "prevb")
                r_prevb = Res()
                ysb = sb([128, 2048], F32, "ysb")
                r_ysb = Res()
                yin_p = RR([sb([128, 2048], F32, "yin") for _ in range(2)])
                cbp = ps([128, 8, 128], F32, "cbp")
                r_cbp = Res()
                ydp = ps([128, 1024], F32, "ydp")
                r_ydp = Res()
                yop = ps([128, 1024], F32, "yop")
                r_yop = Res()
                stp = ps([128, 1024], F32, "stp")
                r_stp = Res()
                for dr in range(2):
                    order = list(range(NTT)) if dr == 0 else [1, 0] + list(range(NTT - 1, 1, -1))
                    V(lambda e: e.memset(ST[:], 0.0), w=[r_ST])
                    hc = dr * 32
                    for c in order:
                        tok = slice(c * 128, (c + 1) * 128)
                        xt, r_xt = xt_p.next()
                        S.dma("sp", xt[:], x_tm.ap()[tok, :], r=[r_xtm], w=[r_xt])
                        bt, r_bt = bt_p.next()
                        S.dma("sp", bt[:], B_tm.ap()[tok, :], r=[r_Btm], w=[r_bt])
                        BTs, r_BTs = BTs_p.next()
                        S.dma("sp", BTs[:], BT_d.ap()[:, :, tok].rearrange("g n t -> n g t"), r=[r_BT], w=[r_BTs])
                        CTs, r_CTs = CTs_p.next()
                        S.dma("sp", CTs[:], CT_d.ap()[:, :, tok].rearrange("g n t -> n g t"), r=[r_CT], w=[r_CTs])
                        LaB, r_LaB = LaB_p.next()
                        S.dma("sp", LaB[:], bass.AP(laT_d, hc * T + c * 128, [[0, 128], [T, 32], [1, 128]]), r=[r_laT], w=[r_LaB])
                        la_c = la_tm[:, c, hc:hc + 32]
                        A(lambda e: e.activation(sml[:, 0, :], la_c, AF.Exp), r=[r_tabs], w=[r_sml])
                        V(lambda e: e.tensor_tensor(sml[:, 3, :], LTB[:, c, hc:hc + 32], la_c, ALU.subtract), r=[r_tabs], w=[r_sml])
                        V(lambda e: e.tensor_single_scalar(sml[:, 3, :], sml[:, 3, :], 0.0, ALU.min), w=[r_sml])
                        A(lambda e: e.activation(sml[:, 1, :], sml[:, 3, :], AF.Exp), w=[r_sml])
                        A(lambda e: e.activation(sml[:, 2, :], LTB[:, c, hc:hc + 32], AF.Exp), r=[r_tabs], w=[r_sml])
                        for g in range(8):
                            M(lambda e: e.matmul(cbp[:, g, :], BTs[:, g, :], CTs[:, g, :], start=True, stop=True), r=[r_BTs, r_CTs],
                              w=[r_cbp] if g == 0 else [], wa=[r_cbp] if g else [], inc=(g == 7))
                        V(lambda e: e.tensor_tensor(CBm[:], cbp[:], masks[:, dr:dr + 1, :].broadcast_to([128, 8, 128]), ALU.mult),
                          r=[r_cbp, r_mk], w=[r_CBm])
                        for h in range(32):
                            V(lambda e: e.tensor_scalar(dmat[:, h, :], LaB[:, h, :], la_tm[:, c, hc + h:hc + h + 1], 0.0, ALU.subtract, ALU.min),
                              r=[r_LaB, r_tabs], w=[r_dmat] if h == 0 else [], wa=[r_dmat] if h else [])
                        A(lambda e: e.activation(decay[:], dmat[:], AF.Exp), r=[r_dmat], w=[r_decay])
                        V(lambda e: e.tensor_tensor(wT[:].rearrange("p (g h) l -> p g h l", g=8), decay[:].rearrange("p (g h) l -> p g h l", g=8),
                                                    CBm[:].unsqueeze(2).broadcast_to([128, 8, 4, 128]), ALU.mult), r=[r_decay, r_CBm], w=[r_wT])
                        V(lambda e: e.tensor_tensor(xdt[:].rearrange("p (h q) -> p h q", h=32), xt[:].rearrange("p (h q) -> p h q", h=32),
                                                    dt_tm[:, c, hc:hc + 32].unsqueeze(2).broadcast_to([128, 32, 64]), ALU.mult), r=[r_xt, r_tabs], w=[r_xdt])
                        V(lambda e: e.tensor_tensor(xw[:].rearrange("p (h q) -> p h q", h=32), xdt[:].rearrange("p (h q) -> p h q", h=32),
                                                    sml[:, 1, :].unsqueeze(2).broadcast_to([128, 32, 64]), ALU.mult), r=[r_xdt, r_sml], w=[r_xw])
                        A(lambda e: e.copy(prevb[:], ST[:]), r=[r_ST], w=[r_prevb])
                        if dr == 1:
                            yin, r_yin = yin_p.next()
                            S.dma("sp", yin[:], Yacc.ap()[tok, :], r=[r_Y[c]], w=[r_yin])
                        for gh in range(2):
                            cs = slice(gh * 1024, (gh + 1) * 1024)
                            for hl in range(16):
                                h = gh * 16 + hl
                                M(lambda e: e.matmul(ydp[:, hl * 64:(hl + 1) * 64], wT[:, h, :], xdt[:, h * 64:(h + 1) * 64], start=True, stop=True),
                                  r=[r_wT, r_xdt], w=[r_ydp] if hl == 0 else [], wa=[r_ydp] if hl else [], inc=(hl == 15))
                            for gl in range(4):
                                g = gh * 4 + gl
                                M(lambda e: e.matmul(yop[:, gl * 256:(gl + 1) * 256], CTs[:, g, :], prevb[:, g * 256:(g + 1) * 256], start=True, stop=True),
                                  r=[r_CTs, r_prevb], w=[r_yop] if gl == 0 else [], wa=[r_yop] if gl else [], inc=(gl == 3))
                            for gl in range(4):
                                g = gh * 4 + gl
                                M(lambda e: e.matmul(stp[:, gl * 256:(gl + 1) * 256], bt[:, g * 128:(g + 1) * 128], xw[:, g * 256:(g + 1) * 256], start=True, stop=True),
                                  r=[r_bt, r_xw], w=[r_stp] if gl == 0 else [], wa=[r_stp] if gl else [], inc=(gl == 3))
                            if dr == 1:
                                V(lambda e: e.tensor_tensor(ysb[:, cs], ydp[:], yin[:, cs], ALU.add), r=[r_ydp, r_yin], w=[r_ysb] if gh == 0 else [], wa=[r_ysb] if gh else [])
                            else:
                                A(lambda e: e.copy(ysb[:, cs], ydp[:]), r=[r_ydp], w=[r_ysb] if gh == 0 else [], wa=[r_ysb] if gh else [])
                            for hl in range(16):
                                h = gh * 16 + hl
                                V(lambda e: e.scalar_tensor_tensor(ysb[:, h * 64:(h + 1) * 64], yop[:, hl * 64:(hl + 1) * 64], sml[:, 0, h:h + 1],
                                                                   ysb[:, h * 64:(h + 1) * 64], ALU.mult, ALU.add), r=[r_yop, r_sml], w=[r_ysb])
                            V(lambda e: e.tensor_tensor(ST[:, cs].rearrange("p (h q) -> p h q", h=16), ST[:, cs].rearrange("p (h q) -> p h q", h=16),
                                                        sml[:, 2, gh * 16:(gh + 1) * 16].unsqueeze(2).broadcast_to([128, 16, 64]), ALU.mult),
                              r=[r_sml, r_prevb], w=[r_ST])
                            V(lambda e: e.tensor_tensor(ST[:, cs], ST[:, cs], stp[:], ALU.add), r=[r_stp], w=[r_ST])
                        S.dma("pool", Yacc.ap()[tok, :], ysb[:], r=[r_ysb], w=[r_Y[c]])
                S.barrier()
            with ExitStack() as ph:
                sb, ps = mk(ph)
                dvec = sb([128, 2048], F32, "dvec")
                mnw = sb([128, 2048], F32, "mnw")
                r_dv = Res()
                S.dma("sp", dvec[:], bass.AP(lay["dvec"], 0, [[0, 128], [1, 2048]]), w=[r_dv])
                S.dma("sp", mnw[:], bass.AP(lay["mnw"], 0, [[0, 128], [1, 2048]]), wa=[r_dv])
                ow = sb([128, 16, D], BF16, "ow")
                r_ow = Res()
                for k4 in range(4):
                    S.dma("pool", ow[:, k4 * 4:(k4 + 1) * 4, :], lay["outw"].ap()[k4 * 512:(k4 + 1) * 512, :].rearrange("(k p) n -> p k n", p=128),
                          w=[r_ow] if k4 == 0 else [], wa=[r_ow] if k4 else [])
                y_p = RR([sb([128, 2048], F32, "ty") for _ in range(2)])
                x_p = RR([sb([128, 2048], BF16, "tx") for _ in range(2)])
                z_p = RR([sb([128, 2048], BF16, "tz") for _ in range(2)])
                g_p = RR([sb([128, 2048], F32, "tg") for _ in range(2)])
                gb_p = RR([sb([128, 2048], BF16, "tgb") for _ in range(2)])
                junk = sb([128, 2048], BF16, "tjunk")
                r_junk = Res()
                ss_p = RR([sb([128, 2], F32, "tss") for _ in range(2)])
                gT_p = RR([sb([128, 16, 128], BF16, "tgT") for _ in range(2)])
                tp_p = RR([ps([128, 512], BF16, "ttp") for _ in range(2)])
                op_p = RR([ps([128, 1024], F32, "top") for _ in range(2)])
                ht_p = RR([sb([128, 8, 128], F32, "tht") for _ in range(2)])
                for tt in range(NTT):
                    tok = slice(tt * 128, (tt + 1) * 128)
                    j = 1 if tt < 2 else 0
                    y, r_y = y_p.next()
                    S.dma("sp", y[:], Yacc.ap()[tok, :], r=[r_Y[tt]], w=[r_y])
                    xt, r_xt = x_p.next()
                    S.dma("sp", xt[:], x_tm.ap()[tok, :], r=[r_xtm], w=[r_xt])
                    zt, r_zt = z_p.next()
                    S.dma("sp", zt[:], sz_tm.ap()[tok, :], r=[r_sz], w=[r_zt])
                    gt_, r_g = g_p.next()
                    V(lambda e: e.tensor_tensor(gt_[:], xt[:], dvec[:], ALU.mult), r=[r_xt, r_dv], w=[r_g])
                    V(lambda e: e.tensor_tensor(gt_[:], gt_[:], y[:], ALU.add), r=[r_y], w=[r_g])
                    V(lambda e: e.tensor_tensor(gt_[:], gt_[:], zt[:], ALU.mult), r=[r_zt], w=[r_g])
                    ss, r_ss = ss_p.next()
                    A(lambda e: e.activation(junk[:], gt_[:], AF.Square, accum_out=ss[:, 0:1]), r=[r_g], w=[r_junk, r_ss])
                    A(lambda e: e.activation(ss[:, 1:2], ss[:, 0:1], AF.Sqrt, bias=EPS, scale=1.0 / 2048), w=[r_ss])
                    V(lambda e: e.reciprocal(ss[:, 1:2], ss[:, 1:2]), w=[r_ss])
                    gb, r_gb = gb_p.next()
                    V(lambda e: e.scalar_tensor_tensor(gb[:], gt_[:], ss[:, 1:2], mnw[:], ALU.mult, ALU.mult), r=[r_g, r_ss, r_dv], w=[r_gb])
                    gT, r_gT = gT_p.next()
                    for k4 in range(4):
                        tp_, r_tp = tp_p.next()
                        for q in range(4):
                            k = k4 * 4 + q
                            M(lambda e: e.transpose(tp_[:, q * 128:(q + 1) * 128], gb[:, k * 128:(k + 1) * 128], identb[:]),
                              r=[r_gb, r_const], w=[r_tp] if q == 0 else [], wa=[r_tp] if q else [], inc=(q == 3))
                        A(lambda e: e.copy(gT[:, k4 * 4:(k4 + 1) * 4, :], tp_[:].rearrange("p (q t) -> p q t", q=4)), r=[r_tp],
                          w=[r_gT] if k4 == 0 else [], wa=[r_gT] if k4 else [])
                    op, r_op = op_p.next()
                    for dc in range(8):
                        for k in range(16):
                            M(lambda e: e.matmul(op[:, dc * 128:(dc + 1) * 128], ow[:, k, dc * 128:(dc + 1) * 128], gT[:, k, :], start=(k == 0), stop=(k == 15)),
                              r=[r_ow, r_gT], w=[r_op] if (dc == 0 and k == 0) else [], wa=[] if (dc == 0 and k == 0) else [r_op], inc=(dc == 7 and k == 15))
                    ht, r_ht = ht_p.next()
                    hr = [r_hT[(c, tt)] for c in range(8)]
                    S.dma("sp", ht[:], hT_ap[:, tok].rearrange("(c p) t -> p c t", p=128), r=hr, w=[r_ht])
                    for dc in range(8):
                        V(lambda e: e.scalar_tensor_tensor(ht[:, dc, :], op[:, dc * 128:(dc + 1) * 128], mod_gt[:, dc, j:j + 1], ht[:, dc, :], ALU.mult, ALU.add),
                          r=[r_op, r_mod], w=[r_ht])
                    S.dma("pool", hT_ap[:, tok].rearrange("(c p) t -> p c t", p=128), ht[:], r=[r_ht], wa=hr)
                S.barrier()

    def attn_layer(lay):
        qT_d = dscr("qT_d", [D, T], BF16)
        kT_d = dscr("kT_d", [256, T], BF16)
        v_tm = dscr("v_tm", [T, 256], BF16)
        sgT_d = dscr("sgT_d", [D, T], BF16)
        oT_d = dscr("oT_d", [D, T], BF16)
        r_q, r_k, r_v, r_sg, r_o = Res(), Res(), Res(), Res(), Res()
        with ExitStack() as st1:
            sb1, ps1 = mk(st1)
            inT = sb1([128, 8, T], BF16, "inT")
            r_inT = Res()
            with ExitStack() as ph:
                sb, ps = mk(ph)
                pre_pass(lay, sb, ps, inT, r_inT)
                S.barrier()
            with ExitStack() as ph:
                sb, ps = mk(ph)
                rope = sb([128, 2, nlat], F32, "rope")
                r_cst = Res()
                S.dma("sp", rope[:], lay["rope"].ap().rearrange("a p t -> p a t"), w=[r_cst])
                qkw = sb([128, 2], F32, "qkw")
                S.dma("sp", qkw[:], lay["qkw"].ap(), wa=[r_cst])
                permb = sb([128, 128], BF16, "permb")
                S.dma("pool", permb[:], lay["perm"].ap(), wa=[r_cst])
                bones = sb([128, 128], BF16, "bones")
                S.dma("pool", bones[:], lay["bones"].ap(), wa=[r_cst])
                sq_p = RR([sb([128, 512], BF16, "asq") for _ in range(2)])
                ss_p = RR([ps([128, 512], F32, "ass") for _ in range(2)])
                rs_p = RR([sb([128, 512], F32, "ars") for _ in range(2)])
                qn_p = RR([sb([128, 512], F32, "aqn") for _ in range(2)])
                qb_p = RR([sb([128, 512], BF16, "aqb") for _ in range(2)])
                rot_p = RR([ps([128, 512], F32, "arot") for _ in range(2)])
                t1_p = RR([sb([128, 512], F32, "at1") for _ in range(2)])
                t2_p = RR([sb([128, 512], F32, "at2") for _ in range(2)])
                qo_p = RR([sb([128, 512], BF16, "aqo") for _ in range(3)])

                def epi_qkg(ci, c0, ncol, t0, nt, pt, r_pt):
                    if c0 >= 1536:
                        o, r_o_ = qo_p.next()
                        A(lambda e: e.activation(o[:, 0:nt], pt[:, 0:nt], AF.Silu), r=[r_pt], w=[r_o_])
                        cg = (c0 - 1536) // 128
                        S.dma("sp", sgT_d.ap()[cg * 128:(cg + 1) * 128, t0:t0 + nt], o[:, 0:nt], r=[r_o_], wa=[r_sg])
                        return
                    isq = c0 < 1024
                    wcol = 0 if isq else 1
                    sq, r_sq = sq_p.next()
                    A(lambda e: e.activation(sq[:, 0:nt], pt[:, 0:nt], AF.Square), r=[r_pt], w=[r_sq])
                    ss, r_ss = ss_p.next()
                    M(lambda e: e.matmul(ss[:, 0:nt], bones[:], sq[:, 0:nt], start=True, stop=True), r=[r_sq, r_cst], w=[r_ss])
                    rs, r_rs = rs_p.next()
                    A(lambda e: e.activation(rs[:, 0:nt], ss[:, 0:nt], AF.Sqrt, bias=EPS, scale=1.0 / 64), r=[r_ss], w=[r_rs])
                    V(lambda e: e.reciprocal(rs[:, 0:nt], rs[:, 0:nt]), w=[r_rs])
                    o, r_o_ = qo_p.next()
                    if t0 < NCTX:
                        V(lambda e: e.scalar_tensor_tensor(o[:, 0:nt], pt[:, 0:nt], qkw[:, wcol:wcol + 1], rs[:, 0:nt], ALU.mult, ALU.mult),
                          r=[r_pt, r_rs, r_cst], w=[r_o_])
                    else:
                        qn, r_qn = qn_p.next()
                        V(lambda e: e.scalar_tensor_tensor(qn[:, 0:nt], pt[:, 0:nt], qkw[:, wcol:wcol + 1], rs[:, 0:nt], ALU.mult, ALU.mult),
                          r=[r_pt, r_rs, r_cst], w=[r_qn])
                        qb, r_qb = qb_p.next()
                        A(lambda e: e.copy(qb[:, 0:nt], qn[:, 0:nt]), r=[r_qn], w=[r_qb])
                        rot, r_rot = rot_p.next()
                        M(lambda e: e.matmul(rot[:, 0:nt], permb[:], qb[:, 0:nt], start=True, stop=True), r=[r_qb, r_cst], w=[r_rot])
                        l0 = t0 - NCTX
                        t1, r_t1 = t1_p.next()
                        G(lambda e: e.tensor_tensor(t1[:, 0:nt], qn[:, 0:nt], rope[:, 0, l0:l0 + nt], ALU.mult), r=[r_qn, r_cst], w=[r_t1])
                        t2, r_t2 = t2_p.next()
                        V(lambda e: e.tensor_tensor(t2[:, 0:nt], rot[:, 0:nt], rope[:, 1, l0:l0 + nt], ALU.mult), r=[r_rot, r_cst], w=[r_t2])
                        V(lambda e: e.tensor_tensor(o[:, 0:nt], t1[:, 0:nt], t2[:, 0:nt], ALU.add), r=[r_t1, r_t2], w=[r_o_])
                    if isq:
                        S.dma("sp", qT_d.ap()[c0:c0 + 128, t0:t0 + nt], o[:, 0:nt], r=[r_o_], wa=[r_q])
                    else:
                        S.dma("sp", kT_d.ap()[c0 - 1024:c0 - 1024 + 128, t0:t0 + nt], o[:, 0:nt], r=[r_o_], wa=[r_k])

                cols = [(128 * i, 128) for i in range(10)] + [(1536 + 128 * i, 128) for i in range(8)]
                linear_fm(sb, ps, inT, r_inT, 8, lay["inw"].ap(), cols, epi_qkg)
                vo_p = RR([sb([128, 256], BF16, "avo") for _ in range(3)])

                def epi_v(g0, n, tt, pt, r_pt):
                    o, r_o_ = vo_p.next()
                    V(lambda e: e.tensor_copy(o[:, 0:n], pt[:, 0:n]), r=[r_pt], w=[r_o_])
                    S.dma("sp", v_tm.ap()[tt * 128:(tt + 1) * 128, :], o[:, 0:n], r=[r_o_], wa=[r_v])
                linear_tm(sb, ps, inT, r_inT, 8, lay["inw"].ap(), 1280, 256, epi_v)
                S.barrier()
        with ExitStack() as ph:
            sb, ps = mk(ph)
            onesf = sb([128, 64], F32, "aones")
            r_on = Res()
            G(lambda e: e.memset(onesf[:], 1.0), w=[r_on])
            Vg = sb([128, NTT, 65], BF16, "Vg")
            r_Vg = Res()
            G(lambda e: e.memset(Vg[:], 1.0), w=[r_Vg])
            kk_p = RR([sb([128, T], BF16, "kk") for _ in range(2)])
            qc_p = RR([sb([128, T], BF16, "qc") for _ in range(2)])
            sg_p = RR([sb([64, T], BF16, "sgh") for _ in range(2)])
            sp_p = RR([ps([128, 512], F32, "asp") for _ in range(4)])
            P_p = RR([sb([128, 512], BF16, "aP") for _ in range(4)])
            oa_p = RR([ps([128, 512], F32, "aoa") for _ in range(2)])
            bc_p = RR([ps([64, 512], F32, "abc") for _ in range(2)])
            osb_p = RR([sb([128, 512], F32, "aosb") for _ in range(2)])
            o1_p = RR([sb([64, 512], F32, "ao1") for _ in range(2)])
            og_p = RR([sb([64, 512], BF16, "aog") for _ in range(2)])
            for gk in range(4):
                kk, r_kk = kk_p.next()
                S.dma("sp", kk[0:64, :], kT_d.ap()[gk * 64:(gk + 1) * 64, :], r=[r_k], w=[r_kk])
                S.dma("sp", kk[64:128, :], kT_d.ap()[gk * 64:(gk + 1) * 64, :], r=[r_k], wa=[r_kk])
                S.dma("sp", Vg[:, :, 0:64], v_tm.ap()[:, gk * 64:(gk + 1) * 64].rearrange("(t p) d -> p t d", p=128), r=[r_v], w=[r_Vg])
                for qc in (2 * gk, 2 * gk + 1):
                    qt, r_qt = qc_p.next()
                    S.dma("sp", qt[:], qT_d.ap()[qc * 128:(qc + 1) * 128, :], r=[r_q], w=[r_qt])
                    for hh in range(2):
                        h = 2 * qc + hh
                        pr = slice(64 * hh, 64 * hh + 64)
                        sgh, r_sgh = sg_p.next()
                        S.dma("sp", sgh[:], sgT_d.ap()[h * 64:(h + 1) * 64, :], r=[r_sg], w=[r_sgh])
                        tasks = []
                        for (t0, nt) in BLKS:
                            ktiles = [0, 1] if t0 < NCTX else list(range(NTT))
                            for ki, kt in enumerate(ktiles):
                                tasks.append((t0, nt, ki, kt, len(ktiles)))
                        spq = {}
                        cur = {}
                        deferred = []

                        def emit_qk(ti):
                            t0, nt, ki, kt, nk = tasks[ti]
                            sp_, r_sp = sp_p.next()
                            M(lambda e: e.matmul(sp_[:, 0:nt], kk[pr, kt * 128:(kt + 1) * 128], qt[pr, t0:t0 + nt], start=True, stop=True),
                              r=[r_kk, r_qt], w=[r_sp])
                            spq[ti] = (sp_, r_sp)

                        def finalize_pe(args):
                            (t0, nt, osb, r_osb) = args
                            bc, r_bc = bc_p.next()
                            M(lambda e: e.matmul(bc[:, 0:nt], onesf[64:65, :], osb[64:65, 0:nt], start=True, stop=True), r=[r_osb, r_on], w=[r_bc])
                            o1, r_o1 = o1_p.next()
                            V(lambda e: e.tensor_tensor(o1[:, 0:nt], osb[0:64, 0:nt], bc[:, 0:nt], ALU.mult), r=[r_osb, r_bc], w=[r_o1])
                            og, r_og = og_p.next()
                            G(lambda e: e.tensor_tensor(og[:, 0:nt], o1[:, 0:nt], sgh[:, t0:t0 + nt], ALU.mult), r=[r_o1, r_sgh], w=[r_og])
                            S.dma("sp", oT_d.ap()[h * 64:(h + 1) * 64, t0:t0 + nt], og[:, 0:nt], r=[r_og], wa=[r_o])

                        LOOK = 2
                        for ti in range(min(LOOK, len(tasks))):
                            emit_qk(ti)
                        for ti in range(len(tasks)):
                            t0, nt, ki, kt, nk = tasks[ti]
                            sp_, r_sp = spq.pop(ti)
                            if ki == 0:
                                cur["oa"] = oa_p.next()
                            oa, r_oa = cur["oa"]
                            P, r_P = P_p.next()
                            A(lambda e: e.activation(P[:, 0:nt], sp_[:, 0:nt], AF.Exp, bias=-8.0, scale=0.125), r=[r_sp], w=[r_P])
                            M(lambda e: e.matmul(oa[0:65, 0:nt], Vg[:, kt, :], P[:, 0:nt], start=(ki == 0), stop=(ki == nk - 1)),
                              r=[r_Vg, r_P], w=[r_oa] if ki == 0 else [], wa=[r_oa] if ki else [], inc=(ki == nk - 1))
                            if ti + LOOK < len(tasks):
                                emit_qk(ti + LOOK)
                            deferred = [(n - 1, a) for (n, a) in deferred]
                            while deferred and deferred[0][0] <= 0:
                                finalize_pe(deferred.pop(0)[1])
                            if ki == nk - 1:
                                osb, r_osb = osb_p.next()
                                V(lambda e: e.tensor_copy(osb[0:65, 0:nt], oa[0:65, 0:nt]), r=[r_oa], w=[r_osb])
                                V(lambda e: e.reciprocal(osb[64:65, 0:nt], osb[64:65, 0:nt]), w=[r_osb])
                                deferred.append((4, (t0, nt, osb, r_osb)))
                        for (_, a) in deferred:
                            finalize_pe(a)
            S.barrier()
        with ExitStack() as ph:
            sb, ps = mk(ph)
            oT = sb([128, 8, T], BF16, "oT")
            r_oT = Res()
            S.dma("sp", oT[:], oT_d.ap().rearrange("(c p) t -> p c t", p=128), r=[r_o], w=[r_oT])
            linear_fm(sb, ps, oT, r_oT, 8, lay["outw"].ap(), [(128 * i, 128) for i in range(8)], make_resid_epi(sb))
            S.barrier()

    def s5_layer(lay):
        uT_d = dscr("uT_d", [D, T], BF16)
        szT_d = dscr("szT_d", [D, T], BF16)
        gT_d = dscr("gT_d", [D, T], BF16)
        y2T_d = dscr("y2T_d", [D, T], BF16)
        r_u, r_sz, r_g, r_y2 = Res(), Res(), Res(), Res()
        NLV = 1
        while (1 << (NLV - 1)) < T:
            NLV += 1
        with ExitStack() as st1:
            sb1, ps1 = mk(st1)
            inT = sb1([128, 8, T], BF16, "inT")
            r_inT = Res()
            with ExitStack() as ph:
                sb, ps = mk(ph)
                pre_pass(lay, sb, ps, inT, r_inT)
                S.barrier()
            with ExitStack() as ph:
                sb, ps = mk(ph)
                uo_p = RR([sb([128, 512], BF16, "suo") for _ in range(3)])

                def epi_uz(ci, c0, ncol, t0, nt, pt, r_pt):
                    o, r_o_ = uo_p.next()
                    if c0 < 1024:
                        V(lambda e: e.tensor_copy(o[:, 0:nt], pt[:, 0:nt]), r=[r_pt], w=[r_o_])
                        S.dma("sp", uT_d.ap()[c0:c0 + 128, t0:t0 + nt], o[:, 0:nt], r=[r_o_], wa=[r_u])
                    else:
                        A(lambda e: e.activation(o[:, 0:nt], pt[:, 0:nt], AF.Silu), r=[r_pt], w=[r_o_])
                        S.dma("sp", szT_d.ap()[c0 - 1024:c0 - 1024 + 128, t0:t0 + nt], o[:, 0:nt], r=[r_o_], wa=[r_sz])
                linear_fm(sb, ps, inT, r_inT, 8, lay["inw"].ap(), [(128 * i, 128) for i in range(16)], epi_uz)
                S.barrier()
        with ExitStack() as ph:
            sb, ps = mk(ph)
            lam = sb([128, 3, 64], F32, "lam")
            r_t = Res()
            S.dma("sp", lam[:], lay["lam"].ap(), w=[r_t])
            tb = sb([128, 16, 64], F32, "stb")
            tbi = sb([128, 64], I32, "stbi")
            coef = sb([128, 3, 64], F32, "coef")
            pw = sb([128, 64, NLV, 3], F32, "pw")
            lr, li, ls = lam[:, 0, :], lam[:, 1, :], lam[:, 2, :]
            X = lambda i: tb[:, i, :]

            def vt(fn):
                V(fn, w=[r_t])

            def at(fn):
                A(fn, w=[r_t])
            at(lambda e: e.activation(X(0), ls, AF.Exp))
            vt(lambda e: e.tensor_tensor(X(1), lr, X(0), ALU.mult))
            at(lambda e: e.activation(X(2), X(1), AF.Exp))
            vt(lambda e: e.tensor_tensor(X(3), li, X(0), ALU.mult))

            def sin_of(dst, src, shift):
                vt(lambda e: e.tensor_scalar(X(4), src, shift, 1.0 / (2 * PI), ALU.add, ALU.mult))
                vt(lambda e: e.tensor_copy(tbi[:], X(4)))
                vt(lambda e: e.tensor_copy(X(5), tbi[:]))
                vt(lambda e: e.tensor_scalar(X(4), src, shift, None, ALU.add))
                vt(lambda e: e.scalar_tensor_tensor(X(4), X(5), -2 * PI, X(4), ALU.mult, ALU.add))
                vt(lambda e: e.tensor_single_scalar(X(5), X(4), PI, ALU.is_gt))
                vt(lambda e: e.scalar_tensor_tensor(X(4), X(5), -2 * PI, X(4), ALU.mult, ALU.add))
                vt(lambda e: e.tensor_single_scalar(X(5), X(4), -PI, ALU.is_lt))
                vt(lambda e: e.scalar_tensor_tensor(X(4), X(5), 2 * PI, X(4), ALU.mult, ALU.add))
                at(lambda e: e.activation(dst, X(4), AF.Sin))
            sin_of(X(6), X(3), 0.0)
            sin_of(X(7), X(3), PI / 2)
            vt(lambda e: e.tensor_tensor(X(8), X(2), X(7), ALU.mult))
            vt(lambda e: e.tensor_tensor(X(9), X(2), X(6), ALU.mult))
            vt(lambda e: e.tensor_tensor(X(10), lr, lr, ALU.mult))
            vt(lambda e: e.tensor_tensor(X(11), li, li, ALU.mult))
            vt(lambda e: e.tensor_tensor(X(10), X(10), X(11), ALU.add))
            vt(lambda e: e.reciprocal(X(10), X(10)))
            vt(lambda e: e.tensor_scalar(X(11), X(8), -1.0, None, ALU.add))
            vt(lambda e: e.tensor_tensor(X(12), X(11), lr, ALU.mult))
            vt(lambda e: e.tensor_tensor(X(13), X(9), li, ALU.mult))
            vt(lambda e: e.tensor_tensor(X(12), X(12), X(13), ALU.add))
            vt(lambda e: e.tensor_tensor(coef[:, 0, :], X(12), X(10), ALU.mult))
            vt(lambda e: e.tensor_tensor(X(12), X(9), lr, ALU.mult))
            vt(lambda e: e.tensor_tensor(X(13), X(11), li, ALU.mult))
            vt(lambda e: e.tensor_tensor(X(12), X(12), X(13), ALU.subtract))
            vt(lambda e: e.tensor_tensor(coef[:, 1, :], X(12), X(10), ALU.mult))
            vt(lambda e: e.tensor_scalar(coef[:, 2, :], coef[:, 1, :], -1.0, None, ALU.mult))
            vt(lambda e: e.tensor_copy(pw[:, :, 0, 0], X(8)))
            vt(lambda e: e.tensor_copy(pw[:, :, 0, 1], X(9)))
            for lv in range(NLV):
                vt(lambda e: e.tensor_scalar(pw[:, :, lv, 2], pw[:, :, lv, 1], -1.0, None, ALU.mult))
                if lv + 1 < NLV:
                    vt(lambda e: e.tensor_tensor(X(12), pw[:, :, lv, 0], pw[:, :, lv, 0], ALU.mult))
                    vt(lambda e: e.tensor_tensor(X(13), pw[:, :, lv, 1], pw[:, :, lv, 1], ALU.mult))
                    vt(lambda e: e.tensor_tensor(pw[:, :, lv + 1, 0], X(12), X(13), ALU.subtract))
                    vt(lambda e: e.tensor_tensor(X(12), pw[:, :, lv, 0], pw[:, :, lv, 1], ALU.mult))
                    vt(lambda e: e.tensor_scalar(pw[:, :, lv + 1, 1], X(12), 2.0, None, ALU.mult))
            brt = sb([128, 2, 2, 32, 128], BF16, "brt")
            for q in range(2):
                for k in range(2):
                    for j8 in range(4):
                        S.dma("pool", brt[:, q, k, j8 * 8:(j8 + 1) * 8, :], lay["brt"].ap()[q, :, k, j8 * 8:(j8 + 1) * 8, :], wa=[r_t])
            crp = sb([128, 2, 2, 32, 32], BF16, "crp")
            S.dma("pool", crp[:, 0], lay["crp"].ap()[0], wa=[r_t])
            S.dma("pool", crp[:, 1], lay["crp"].ap()[1], wa=[r_t])
            at(lambda e: e.mul(crp[:, 1], crp[:, 1], -1.0))
            sd = sb([128, 8], F32, "sd")
            S.dma("sp", sd[:], lay["sd"].ap(), wa=[r_t])
            Xr = sb([128, T], F32, "Xr")
            Xi = sb([128, T], F32, "Xi")
            Yr = sb([128, T], F32, "Yr")
            Yi = sb([128, T], F32, "Yi")
            r_X, r_Yb = Res(), Res()
            xb = [[sb([128, T], BF16, "xb%d%d" % (k, q)) for q in range(2)] for k in range(2)]
            r_xb = [Res(), Res()]
            uc_p = RR([sb([128, T], BF16, "suc") for _ in range(2)])
            p12_p = RR([ps([128, 512], F32, "sp12") for _ in range(4)])
            tmp_p = RR([sb([128, 512], F32, "stmp") for _ in range(2)])
            yp_p = RR([ps([128, 512], F32, "syp") for _ in range(2)])
            yv = sb([128, T], F32, "yv")
            ga = Yr
            r_yv, r_ga = Res(), r_Yb
            go_p = RR([sb([128, T], BF16, "sgo") for _ in range(1)])

            def sview(t, off, step, a0, cnt, mult):
                s0 = off + a0 * step
                st_ = mult * step
                return t[:, s0:s0 + (cnt - 1) * st_ + 1:st_]

            def scan(col, k):
                outr, outi = xb[k]

                def rec(tr, ti, off, step, n, lv, yoff, top):
                    if n == 1:
                        if top:
                            A(lambda e: e.copy(outr[:, 0:1], tr[:, off:off + 1]), r=[r_X], w=[r_xb[k]])
                            A(lambda e: e.copy(outi[:, 0:1], ti[:, off:off + 1]), r=[r_X], w=[r_xb[k]])
                        return
                    m = n // 2
                    ne = n - m
                    ar, ai, nai = pw[:, col, lv, 0:1], pw[:, col, lv, 1:2], pw[:, col, lv, 2:3]
                    rs_ = [r_X, r_Yb, r_t]
                    Ev = lambda t, a0, cnt: sview(t, off, step, 2 * a0, cnt, 2)
                    Ov = lambda t, a0, cnt: sview(t, off, step, 2 * a0 + 1, cnt, 2)
                    yr, yi = Yr[:, yoff:yoff + m], Yi[:, yoff:yoff + m]
                    V(lambda e: e.scalar_tensor_tensor(yr, Ev(tr, 0, m), ar, Ov(tr, 0, m), ALU.mult, ALU.add), r=rs_, w=[r_Yb])
                    V(lambda e: e.scalar_tensor_tensor(yr, Ev(ti, 0, m), nai, yr, ALU.mult, ALU.add), r=rs_, w=[r_Yb])
                    V(lambda e: e.scalar_tensor_tensor(yi, Ev(ti, 0, m), ar, Ov(ti, 0, m), ALU.mult, ALU.add), r=rs_, w=[r_Yb])
                    V(lambda e: e.scalar_tensor_tensor(yi, Ev(tr, 0, m), ai, yi, ALU.mult, ALU.add), r=rs_, w=[r_Yb])
                    rec(Yr, Yi, yoff, 1, m, lv + 1, yoff + m, False)
                    ne1 = ne - 1
                    zr, zi = Yr[:, yoff:yoff + ne1], Yi[:, yoff:yoff + ne1]
                    if top:
                        A(lambda e: e.copy(sview(outr, 0, 1, 1, m, 2), yr), r=[r_Yb], w=[r_xb[k]])
                        A(lambda e: e.copy(sview(outi, 0, 1, 1, m, 2), yi), r=[r_Yb], w=[r_xb[k]])
                        A(lambda e: e.copy(outr[:, 0:1], tr[:, off:off + 1]), r=[r_X], w=[r_xb[k]])
                        A(lambda e: e.copy(outi[:, 0:1], ti[:, off:off + 1]), r=[r_X], w=[r_xb[k]])
                        if ne1 > 0:
                            V(lambda e: e.scalar_tensor_tensor(Ev(tr, 1, ne1), zr, ar, Ev(tr, 1, ne1), ALU.mult, ALU.add), r=rs_, w=[r_X])
                            V(lambda e: e.scalar_tensor_tensor(sview(outr, 0, 1, 2, ne1, 2), zi, nai, Ev(tr, 1, ne1), ALU.mult, ALU.add), r=rs_, w=[r_xb[k]])
                            V(lambda e: e.scalar_tensor_tensor(Ev(ti, 1, ne1), zi, ar, Ev(ti, 1, ne1), ALU.mult, ALU.add), r=rs_, w=[r_X])
                            V(lambda e: e.scalar_tensor_tensor(sview(outi, 0, 1, 2, ne1, 2), zr, ai, Ev(ti, 1, ne1), ALU.mult, ALU.add), r=rs_, w=[r_xb[k]])
                    else:
                        wres = [r_Yb] if tr is Yr else [r_X]
                        A(lambda e: e.copy(Ov(tr, 0, m), yr), r=[r_Yb], w=wres)
                        A(lambda e: e.copy(Ov(ti, 0, m), yi), r=[r_Yb], w=wres)
                        if ne1 > 0:
                            V(lambda e: e.scalar_tensor_tensor(Ev(tr, 1, ne1), zr, ar, Ev(tr, 1, ne1), ALU.mult, ALU.add), r=rs_, w=wres)
                            V(lambda e: e.scalar_tensor_tensor(Ev(tr, 1, ne1), zi, nai, Ev(tr, 1, ne1), ALU.mult, ALU.add), r=rs_, w=wres)
                            V(lambda e: e.scalar_tensor_tensor(Ev(ti, 1, ne1), zi, ar, Ev(ti, 1, ne1), ALU.mult, ALU.add), r=rs_, w=wres)
                            V(lambda e: e.scalar_tensor_tensor(Ev(ti, 1, ne1), zr, ai, Ev(ti, 1, ne1), ALU.mult, ALU.add), r=rs_, w=wres)
                rec(Xr, Xi, 0, 1, T, 0, 0, True)

            def bwd_pos(t0, nt):
                return (NCTX - t0 - nt) if t0 < NCTX else (NCTX + T - t0 - nt)

            uc = None
            SK = os.environ.get("S5_SKIP", "")
            for j in range(32 if "L" not in SK else 0):
                cj, jm = j // 4, j % 4
                pr = slice(32 * jm, 32 * jm + 32)
                if jm == 0:
                    uc, r_uc = uc_p.next()
                    S.dma("sp", uc[:], uT_d.ap()[cj * 128:(cj + 1) * 128, :], r=[r_u], w=[r_uc])
                for k in range(2):
                    col = k * 32 + j
                    for (t0, nt) in BLKS:
                        if k == 0:
                            i0 = t0
                            uv = uc[:, t0:t0 + nt]
                        else:
                            i0 = bwd_pos(t0, nt)
                            uv = uc[:, t0:t0 + nt][:, ::-1]
                        if "m" in SK:
                            continue
                        p1, r_p1 = p12_p.next()
                        p2, r_p2 = p12_p.next()
                        M(lambda e: e.matmul(p1[:, 0:nt], brt[:, 0, k, j, :], uv, start=True, stop=True), r=[r_uc, r_t], w=[r_p1])
                        M(lambda e: e.matmul(p2[:, 0:nt], brt[:, 1, k, j, :], uv, start=True, stop=True), r=[r_uc, r_t], w=[r_p2])
                        if "e" in SK:
                            continue
                        tm, r_tm = tmp_p.next()
                        if "a" not in SK:
                            A(lambda e: e.activation(tm[:, 0:nt], p2[:, 0:nt], AF.Identity, scale=coef[:, 2, col:col + 1]), r=[r_t], w=[r_tm, r_p2])
                        if "v" not in SK:
                            V(lambda e: e.scalar_tensor_tensor(Xr[:, i0:i0 + nt], p1[:, 0:nt], coef[:, 0, col:col + 1], tm[:, 0:nt], ALU.mult, ALU.add),
                              r=[r_tm, r_t], w=[r_X, r_p1])
                        tm2, r_tm2 = tmp_p.next()
                        if "a" not in SK:
                            A(lambda e: e.activation(tm2[:, 0:nt], p1[:, 0:nt], AF.Identity, scale=coef[:, 1, col:col + 1]), r=[r_t], w=[r_tm2, r_p1])
                        if "v" not in SK:
                            V(lambda e: e.scalar_tensor_tensor(Xi[:, i0:i0 + nt], p2[:, 0:nt], coef[:, 0, col:col + 1], tm2[:, 0:nt], ALU.mult, ALU.add),
                              r=[r_tm2, r_t], w=[r_X, r_p2])
                    if "s" not in SK:
                        scan(col, k)
                for (t0, nt) in (BLKS if "r" not in SK else []):
                    yp, r_yp = yp_p.next()
                    i0 = bwd_pos(t0, nt)
                    rv = (lambda a: a[:, ::-1]) if not os.environ.get("S5_NOREV") else (lambda a: a)
                    ops = [(crp[:, 0, 0, j, :], xb[0][0][:, t0:t0 + nt], r_xb[0]), (crp[:, 1, 0, j, :], xb[0][1][:, t0:t0 + nt], r_xb[0]),
                           (crp[:, 0, 1, j, :], rv(xb[1][0][:, i0:i0 + nt]), r_xb[1]), (crp[:, 1, 1, j, :], rv(xb[1][1][:, i0:i0 + nt]), r_xb[1])]
                    for qi, (lh, rh, rr) in enumerate(ops):
                        M(lambda e: e.matmul(yp[pr, 0:nt], lh, rh, start=(qi == 0), stop=(qi == 3), tile_position=(0, 32 * jm)), r=[rr, r_t],
                          w=[r_yp] if qi == 0 else [], wa=[r_yp] if qi else [])
                    V(lambda e: e.scalar_tensor_tensor(yv[pr, t0:t0 + nt], uc[pr, t0:t0 + nt], sd[pr, cj:cj + 1], yp[pr, 0:nt], ALU.mult, ALU.add),
                      r=[r_yp, r_uc, r_t], w=[r_yv] if (jm == 0 and t0 == 0) else [], wa=[] if (jm == 0 and t0 == 0) else [r_yv])
                if jm == 3 and "g" not in SK:
                    A(lambda e: e.activation(ga[:], yv[:], AF.Square), r=[r_yv], w=[r_ga])
                    V(lambda e: e.tensor_scalar(ga[:], ga[:], 0.044715, 1.0, ALU.mult, ALU.add), w=[r_ga])
                    V(lambda e: e.tensor_tensor(ga[:], ga[:], yv[:], ALU.mult), r=[r_yv], w=[r_ga])
                    A(lambda e: e.activation(ga[:], ga[:], AF.Sigmoid, scale=1.5957691216057308), w=[r_ga])
                    go, r_go = go_p.next()
                    V(lambda e: e.tensor_tensor(go[:], ga[:], yv[:], ALU.mult), r=[r_ga, r_yv], w=[r_go])
                    S.dma("sp", gT_d.ap()[cj * 128:(cj + 1) * 128, :], go[:], r=[r_go], wa=[r_g])
            S.barrier()
        with ExitStack() as ph:
            sb, ps = mk(ph)
            gT = sb([128, 8, T], BF16, "gT")
            r_gT = Res()
            S.dma("sp", gT[:], gT_d.ap().rearrange("(c p) t -> p c t", p=128), r=[r_g], w=[r_gT])
            glub = sb([128, 8], F32, "glub")
            r_gb = Res()
            S.dma("sp", glub[:], lay["glub"].ap(), w=[r_gb])
            sig_p = RR([sb([128, 512], F32, "ssig") for _ in range(2)])
            szt_p = RR([sb([128, 512], BF16, "sszt") for _ in range(2)])
            y2_p = RR([sb([128, 512], BF16, "sy2") for _ in range(3)])

            def epi_glu(ci, c0, ncol, t0, nt, pt, r_pt):
                sg, r_sg_ = sig_p.next()
                A(lambda e: e.activation(sg[:, 0:nt], pt[:, 0:nt], AF.Sigmoid, bias=glub[:, ci:ci + 1], scale=1.0), r=[r_pt, r_gb], w=[r_sg_])
                szt, r_szt = szt_p.next()
                S.dma("sp", szt[:, 0:nt], szT_d.ap()[c0:c0 + 128, t0:t0 + nt], r=[r_sz], w=[r_szt])
                V(lambda e: e.tensor_tensor(sg[:, 0:nt], sg[:, 0:nt], gT[:, ci, t0:t0 + nt], ALU.mult), r=[r_gT], w=[r_sg_])
                y2, r_y2_ = y2_p.next()
                V(lambda e: e.tensor_tensor(y2[:, 0:nt], sg[:, 0:nt], szt[:, 0:nt], ALU.mult), r=[r_sg_, r_szt], w=[r_y2_])
                S.dma("sp", y2T_d.ap()[c0:c0 + 128, t0:t0 + nt], y2[:, 0:nt], r=[r_y2_], wa=[r_y2])
            linear_fm(sb, ps, gT, r_gT, 8, lay["gluw"].ap(), [(128 * i, 128) for i in range(8)], epi_glu)
            S.barrier()
        with ExitStack() as ph:
            sb, ps = mk(ph)
            y2 = sb([128, 8, T], BF16, "y2r")
            r_y2r = Res()
            S.dma("sp", y2[:], y2T_d.ap().rearrange("(c p) t -> p c t", p=128), r=[r_y2], w=[r_y2r])
            linear_fm(sb, ps, y2, r_y2r, 8, lay["outw"].ap(), [(128 * i, 128) for i in range(8)], make_resid_epi(sb))
            S.barrier()

    for i in layers:
        kind = i % 3
        if kind == 0:
            mamba_layer(L[i])
        elif kind == 1:
            attn_layer(L[i])
        else:
            s5_layer(L[i])

    with ExitStack() as ph:
        sb, ps = mk(ph)
        pre_pass(None, sb, ps, None, None, final=True)
    S.barrier()
    es.close()
    nc._ninst = S.ninst
    return nc


def prep_inputs(inputs, b, nlat=NLAT, layers=(0, 1, 2, 3)):
    f = lambda a: np.ascontiguousarray(np.asarray(a, dtype=np.float32))
    chunked = lambda v, n: f(np.asarray(v, np.float32).reshape(n, 128).T)
    m = {}
    m["x"] = f(inputs["x"][b][:nlat])
    m["ctx"] = f(inputs["ctx"][b])
    m["cc"] = f(np.stack([chunked(inputs["c"][b], 8), chunked(inputs["c_ctx"], 8)], axis=-1))
    m["ident"] = np.eye(128, dtype=np.float32)
    m["fnw"] = chunked(inputs["final_norm_w"], 8)
    has_m = False
    for i in layers:
        m["normw%d" % i] = chunked(inputs["norm_w"][i], 8)
        m["modw%d" % i] = f(inputs["mod_w"][i])
        m["modb%d" % i] = chunked(inputs["mod_b"][i], 24)
        kind, j = i % 3, i // 3
        if kind == 0:
            has_m = True
            m["m_in_w%d" % j] = f(inputs["m_in_w"][j])
            cw = np.asarray(inputs["m_conv_w"][j], np.float32)
            m["m_convw%d" % j] = f(cw.reshape(5, 32, 128).transpose(2, 1, 0))
            m["m_convb%d" % j] = chunked(inputs["m_conv_b"][j], 32)
            m["m_alog%d" % j] = f(np.asarray(inputs["m_a_log"][j], np.float32).reshape(64, 1))
            m["m_dtb%d" % j] = f(np.asarray(inputs["m_dt_bias"][j], np.float32).reshape(64, 1))
            m["m_dvec%d" % j] = f(np.repeat(np.asarray(inputs["m_d"][j], np.float32), 64))
            m["m_normw%d" % j] = f(inputs["m_norm_w"][j])
            m["m_out_w%d" % j] = f(inputs["m_out_w"][j])
        elif kind == 1:
            m["a_in_w"] = f(inputs["a_in_w"][0])
            m["a_out_w"] = f(inputs["a_out_w"][0])
            m["a_qkw"] = f(np.stack([np.tile(np.asarray(inputs["a_q_norm"][0], np.float32), 2),
                                     np.tile(np.asarray(inputs["a_k_norm"][0], np.float32), 2)], axis=1))
            grid_w = 64
            pos = np.arange(nlat)
            r_idx, c_idx = (pos // grid_w).astype(np.float32), (pos % grid_w).astype(np.float32)
            inv = (10000.0 ** (-np.arange(0, 32, 2, dtype=np.float32) / 32)).astype(np.float32)
            dd = np.arange(128) % 64
            ax, part, ii = dd // 32, (dd % 32) // 16, dd % 16
            ang = np.where(ax[:, None] == 0, r_idx[None, :], c_idx[None, :]).astype(np.float32) * inv[ii][:, None]
            m["a_rope"] = f(np.stack([np.cos(ang), np.sin(ang)]))
            perm = np.zeros((128, 128), np.float32)
            for dcol in range(128):
                if part[dcol] == 0:
                    perm[dcol + 16, dcol] = -1.0
                else:
                    perm[dcol - 16, dcol] = 1.0
            m["a_perm"] = perm
            bo = np.zeros((128, 128), np.float32)
            bo[:64, :64] = 1.0
            bo[64:, 64:] = 1.0
            m["a_bones"] = bo
        else:
            m["s_in_w"] = f(inputs["s_in_w"][0])
            m["s_glu_w"] = f(inputs["s_glu_w"][0])
            m["s_out_w"] = f(inputs["s_out_w"][0])
            m["s_sd"] = chunked(inputs["s_d"][0], 8)
            m["s_glub"] = chunked(inputs["s_glu_b"][0], 8)
            lre = np.asarray(inputs["s_lambda_re"][0], np.float32)
            lim = np.asarray(inputs["s_lambda_im"][0], np.float32)
            lst = np.asarray(inputs["s_log_step"][0], np.float32)

            def pair_layout(a):
                a = a.reshape(2, 32, 2, 64)
                return a.transpose(2, 3, 0, 1).reshape(128, 64)
            lam = np.stack([pair_layout(lre), pair_layout(lim),
                            pair_layout(np.broadcast_to(lst[:, :, None], (2, 64, 64)))], axis=1)
            m["s_lam"] = f(lam)
            brt = np.zeros((2, 128, 2, 32, 128), np.float32)
            crp = np.zeros((2, 128, 2, 32, 32), np.float32)
            for q, (bsrc, csrc) in enumerate(((inputs["s_b_re"][0], inputs["s_c_re"][0]), (inputs["s_b_im"][0], inputs["s_c_im"][0]))):
                bsrc = np.asarray(bsrc, np.float32)
                csrc = np.asarray(csrc, np.float32)
                for k in range(2):
                    for j in range(32):
                        for gl in range(2):
                            g_ = 2 * j + gl
                            r0 = 32 * (j % 4) + 16 * gl
                            brt[q, r0:r0 + 16, k, j, gl * 64:(gl + 1) * 64] = bsrc[k, g_].T
                            crp[q, gl * 64:(gl + 1) * 64, k, j, 16 * gl:16 * gl + 16] = csrc[k, g_].T
            m["s_brt"] = brt
            m["s_crp"] = crp
    if has_m:
        up = np.triu(np.ones((128, 128), np.float32))
        m["masks"] = f(np.stack([up, up.T]))
    return m


ACTIVE_CORES = (0, 1, 4, 5)


def kernel(**inputs):
    nc = build_program()
    real = [prep_inputs(inputs, b) for b in range(4)]
    big = ("x", "ctx", "cc", "modw", "m_in_w", "m_out_w", "a_in_w", "a_out_w", "s_in_w", "s_glu_w", "s_out_w", "s_brt", "s_crp")
    idle = {k: (np.zeros_like(v) if k.startswith(big) else v) for k, v in real[0].items()}
    in_maps = [idle] * 8
    for b, core in enumerate(ACTIVE_CORES):
        in_maps[core] = real[b]
    res = run_bass_kernel_spmd(nc, in_maps, core_ids=list(range(8)))
    out = np.stack([np.asarray(res.results[core]["out"], dtype=np.float32) for core in ACTIVE_CORES], axis=0)
    return out
```

```python
import os
import numpy as np
from contextlib import ExitStack
import ml_dtypes
import concourse.bass as bass
import concourse.mybir as mybir
from concourse.bass_utils import run_bass_kernel_spmd

F32 = mybir.dt.float32
BF16 = mybir.dt.bfloat16
I32 = mybir.dt.int32
AF = mybir.ActivationFunctionType
ALU = mybir.AluOpType
AX = mybir.AxisListType

D = 1024
NCTX = 256
NLAT = 4096
EPS = 1e-6
M_IN = 6208
PI = float(np.pi)


class Res:
    __slots__ = ("w", "r", "name")

    def __init__(self, name=""):
        self.w = {}
        self.r = {}
        self.name = name


class Sched:
    def __init__(self, nc, es):
        self.nc = nc
        self.eng = {"pe": nc.tensor, "act": nc.scalar, "dve": nc.vector, "pool": nc.gpsimd, "sp": nc.sync}
        self.sem = {}
        self.cnt = {}
        self.known = {e: {} for e in self.eng}
        for e in self.eng:
            self.sem[e] = es.enter_context(nc.semaphore("s_" + e))
            self.cnt[e] = 0
        self.NDS = 8
        self.dslot = {}
        for q in ("sp", "pool"):
            for i in range(self.NDS):
                k = "d_%s%d" % (q, i)
                self.sem[k] = es.enter_context(nc.semaphore(k))
                self.cnt[k] = 0
            self.dslot[q] = 0
        self.ninst = 0

    def _wait(self, e, evs):
        kn = self.known[e]
        for k, v in evs.items():
            if v <= 0 or (e == "pe" and k == "pe") or kn.get(k, 0) >= v:
                continue
            self.eng[e].wait_ge(self.sem[k], v)
            kn[k] = v

    @staticmethod
    def _deps(r, w, wa):
        evs = {}

        def add(d):
            for k, v in d.items():
                if evs.get(k, 0) < v:
                    evs[k] = v
        for x in r:
            add(x.w)
        for x in w:
            add(x.w)
            add(x.r)
        for x in wa:
            add(x.r)
        return evs

    @staticmethod
    def _commit(k, v, r, w, wa):
        for x in r:
            if x.r.get(k, 0) < v:
                x.r[k] = v
        for x in w:
            if x.w.get(k, 0) < v:
                x.w[k] = v
        for x in wa:
            if x.w.get(k, 0) < v:
                x.w[k] = v

    def op(self, e, fn, r=(), w=(), wa=(), inc=True):
        self._wait(e, self._deps(r, w, wa))
        ins = fn(self.eng[e])
        if inc:
            self.cnt[e] += 1
            ins.then_inc(self.sem[e], 1)
            self._commit(e, self.cnt[e], r, w, wa)
        else:
            self._commit(e, self.cnt[e] + 1, r, w, wa)
        self.ninst += 1
        return ins

    def dma(self, q, out, in_, r=(), w=(), wa=(), **kw):
        i = self.dslot[q]
        self.dslot[q] = (i + 1) % self.NDS
        k = "d_%s%d" % (q, i)
        evs = self._deps(r, w, wa)
        evs[k] = max(evs.get(k, 0), self.cnt[k])
        self._wait(q, evs)
        ins = self.eng[q].dma_start(out=out, in_=in_, **kw)
        self.cnt[k] += 16
        ins.then_inc(self.sem[k], 16)
        self._commit(k, self.cnt[k], r, w, wa)
        self.ninst += 1
        return ins

    def barrier(self):
        evs = {k: v for k, v in self.cnt.items() if v > 0}
        for e in self.eng:
            self._wait(e, dict(evs))


class RR:
    def __init__(self, tiles):
        self.t = tiles
        self.r = [Res() for _ in tiles]
        self.i = 0

    def next(self):
        i = self.i
        self.i = (i + 1) % len(self.t)
        return self.t[i], self.r[i]


def build_program(nlat=NLAT, layers=(0, 1, 2, 3)):
    T = NCTX + nlat
    NTT = T // 128
    BLKS = [(0, NCTX)] + [(NCTX + 512 * i, 512) for i in range(nlat // 512)]
    nc = bass.Bass("TRN2", target_bir_lowering=False)
    es = ExitStack()
    es.enter_context(nc.allow_low_precision("bf16 matmul operands, fp32 accumulation"))
    S = Sched(nc, es)
    uid = [0]

    def mk(stack):
        def sb(shape, dt=F32, name="t"):
            uid[0] += 1
            return stack.enter_context(nc.sbuf_tensor("%s_%d" % (name, uid[0]), list(shape), dt))

        def ps(shape, dt=F32, name="p"):
            uid[0] += 1
            return stack.enter_context(nc.psum_tensor("%s_%d" % (name, uid[0]), list(shape), dt))
        return sb, ps

    def din(name, shape, dt=F32):
        return nc.dram_tensor(name, list(shape), dt, kind="ExternalInput")

    def dscr(name, shape, dt=F32):
        return nc.dram_tensor(name, list(shape), dt)

    V = lambda fn, r=(), w=(), wa=(): S.op("dve", fn, r, w, wa)
    A = lambda fn, r=(), w=(), wa=(): S.op("act", fn, r, w, wa)
    G = lambda fn, r=(), w=(), wa=(): S.op("pool", fn, r, w, wa)
    M = lambda fn, r=(), w=(), wa=(), inc=True: S.op("pe", fn, r, w, wa, inc)

    x_in = din("x", [nlat, D])
    ctx_in = din("ctx", [NCTX, D])
    cc_in = din("cc", [128, 8, 2])
    ident_in = din("ident", [128, 128])
    fnw_in = din("fnw", [128, 8])
    out_t = nc.dram_tensor("out", [nlat, D], F32, kind="ExternalOutput")
    L = {}
    for i in layers:
        L[i] = dict(normw=din("normw%d" % i, [128, 8]), modw=din("modw%d" % i, [D, 3 * D]), modb=din("modb%d" % i, [128, 24]))
        kind, j = i % 3, i // 3
        if kind == 0:
            L[i].update(inw=din("m_in_w%d" % j, [D, M_IN]), convw=din("m_convw%d" % j, [128, 32, 5]), convb=din("m_convb%d" % j, [128, 32]),
                        alog=din("m_alog%d" % j, [64, 1]), dtb=din("m_dtb%d" % j, [64, 1]), dvec=din("m_dvec%d" % j, [2048]),
                        mnw=din("m_normw%d" % j, [2048]), outw=din("m_out_w%d" % j, [2048, D]))
        elif kind == 1:
            L[i].update(inw=din("a_in_w", [D, 2560]), qkw=din("a_qkw", [128, 2]), outw=din("a_out_w", [D, D]),
                        rope=din("a_rope", [2, 128, nlat]), perm=din("a_perm", [128, 128]), bones=din("a_bones", [128, 128]))
        else:
            L[i].update(inw=din("s_in_w", [D, 2048]), lam=din("s_lam", [128, 3, 64]), brt=din("s_brt", [2, 128, 2, 32, 128]),
                        crp=din("s_crp", [2, 128, 2, 32, 32]), sd=din("s_sd", [128, 8]), gluw=din("s_glu_w", [D, D]),
                        glub=din("s_glub", [128, 8]), outw=din("s_out_w", [D, D]))
    masks_in = din("masks", [2, 128, 128]) if any(i % 3 == 0 for i in layers) else None

    hT = dscr("hT", [D, T])
    hT_ap = hT.ap()
    r_hT = {(c, tt): Res() for c in range(8) for tt in range(NTT)}

    def hres(c, t0, nt):
        return [r_hT[(c, tt)] for tt in range(t0 // 128, (t0 + nt + 127) // 128)]

    def hres_all(t0, nt):
        out = []
        for c in range(8):
            out += hres(c, t0, nt)
        return out

    gsb, gps = mk(es)
    ident = gsb([128, 128], F32, "ident")
    r_const = Res("const")
    S.dma("sp", ident[:], ident_in.ap(), w=[r_const])
    identb = gsb([128, 128], BF16, "identb")
    ones_bf = gsb([128, 128], BF16, "ones")
    G(lambda e: e.memset(ones_bf[:], 1.0), wa=[r_const])
    V(lambda e: e.tensor_copy(identb[:], ident[:]), r=[r_const], wa=[r_const])
    fnw = gsb([128, 8], F32, "fnw")
    S.dma("sp", fnw[:], fnw_in.ap(), wa=[r_const])
    cc = gsb([128, 8, 2], F32, "cc")
    S.dma("sp", cc[:], cc_in.ap(), wa=[r_const])
    scs = gsb([128, 8, 2], F32, "scs")
    A(lambda e: e.activation(scs[:], cc[:], AF.Silu), r=[r_const], wa=[r_const])
    mod_sc = gsb([128, 8, 2], F32, "mod_sc")
    mod_bi = gsb([128, 8, 2], F32, "mod_bi")
    mod_gt = gsb([128, 8, 2], F32, "mod_gt")
    r_mod = Res("mod")
    S.barrier()

    with ExitStack() as ph:
        sb, ps = mk(ph)
        xin = RR([sb([128, D], F32, "xin") for _ in range(2)])
        tp = RR([ps([128, 512], F32, "tp") for _ in range(2)])
        xo = RR([sb([128, 8, 128], F32, "xo") for _ in range(2)])
        for tt in range(NTT):
            xt, r_xt = xin.next()
            src = ctx_in.ap()[tt * 128:(tt + 1) * 128, :] if tt < 2 else x_in.ap()[(tt - 2) * 128:(tt - 1) * 128, :]
            S.dma("sp", xt[:], src, w=[r_xt])
            ot, r_ot = xo.next()
            for half in range(2):
                pt, r_pt = tp.next()
                for j in range(4):
                    c = half * 4 + j
                    M(lambda e: e.transpose(pt[:, j * 128:(j + 1) * 128], xt[:, c * 128:(c + 1) * 128], ident[:]),
                      r=[r_xt, r_const], w=[r_pt] if j == 0 else [], wa=[r_pt] if j else [])
                dst = ot[:, half * 4:(half + 1) * 4, :]
                if half:
                    A(lambda e: e.copy(dst, pt[:].rearrange("p (j t) -> p j t", j=4)), r=[r_pt], wa=[r_ot])
                else:
                    V(lambda e: e.tensor_copy(dst, pt[:].rearrange("p (j t) -> p j t", j=4)), r=[r_pt], w=[r_ot])
            S.dma("pool", hT_ap[:, tt * 128:(tt + 1) * 128].rearrange("(c p) t -> p c t", p=128), ot[:], r=[r_ot],
                  wa=[r_hT[(c, tt)] for c in range(8)])
        S.barrier()

    def pre_pass(lay, sb, ps, inT, r_inT, final=False):
        if not final:
            mw = RR([sb([128, 8, 512], F32, "modw") for _ in range(2)])
            mp = RR([ps([128, 512], F32, "modp") for _ in range(2)])
            modT = sb([128, 24, 2], F32, "modT")
            r_modT = Res()
            modb = sb([128, 24], F32, "modb")
            normw = sb([128, 8], F32, "normw")
            r_small = Res()
            S.dma("sp", modb[:], lay["modb"].ap(), w=[r_small])
            S.dma("sp", normw[:], lay["normw"].ap(), wa=[r_small])
            for cg in range(6):
                wt, r_wt = mw.next()
                S.dma("sp", wt[:], lay["modw"].ap()[:, cg * 512:(cg + 1) * 512].rearrange("(k p) n -> p k n", p=128), w=[r_wt])
                for c4 in range(4):
                    pt, r_pt = mp.next()
                    for k in range(8):
                        M(lambda e: e.matmul(pt[:, 0:2], wt[:, k, c4 * 128:(c4 + 1) * 128], scs[:, k, :], start=(k == 0), stop=(k == 7)),
                          r=[r_wt, r_const], w=[r_pt] if k == 0 else [], wa=[r_pt] if k else [])
                    col = cg * 4 + c4
                    V(lambda e: e.tensor_scalar(modT[:, col, :], pt[:, 0:2], modb[:, col:col + 1], None, ALU.add),
                      r=[r_pt, r_small], wa=[r_modT])
            V(lambda e: e.tensor_scalar(mod_sc[:], modT[:, 8:16, :], 1.0, None, ALU.add), r=[r_modT], w=[r_mod])
            V(lambda e: e.tensor_tensor(mod_sc[:], mod_sc[:], normw[:].unsqueeze(2).broadcast_to([128, 8, 2]), ALU.mult), r=[r_small], w=[r_mod])
            V(lambda e: e.tensor_copy(mod_bi[:], modT[:, 0:8, :]), r=[r_modT], w=[r_mod])
            V(lambda e: e.tensor_copy(mod_gt[:], modT[:, 16:24, :]), r=[r_modT], w=[r_mod])
        hb = RR([sb([128, 8, 512], F32, "hb") for _ in range(2)])
        sq = RR([sb([128, 8, 512], BF16, "sq") for _ in range(2)])
        ssp = RR([ps([128, 512], F32, "ssp") for _ in range(2)])
        rstd = RR([sb([128, 512], F32, "rstd") for _ in range(2)])
        tmp = RR([sb([128, 512], F32, "ntmp") for _ in range(3)])
        if final:
            hn = RR([sb([128, 8, 512], F32, "hn") for _ in range(2)])
            tp = RR([ps([128, 512], F32, "ftp") for _ in range(2)])
            ot = RR([sb([128, D], F32, "fot") for _ in range(2)])
            r_out = Res()
        for (t0, nt) in BLKS:
            j = 1 if t0 < NCTX else 0
            if final and j == 1:
                continue
            h, r_h = hb.next()
            S.dma("sp", h[:, :, 0:nt], hT_ap[:, t0:t0 + nt].rearrange("(c p) t -> p c t", p=128), r=hres_all(t0, nt), w=[r_h])
            q, r_q = sq.next()
            A(lambda e: e.activation(q[:, :, 0:nt], h[:, :, 0:nt], AF.Square), r=[r_h], w=[r_q])
            sp_, r_sp = ssp.next()
            for c in range(8):
                M(lambda e: e.matmul(sp_[:, 0:nt], ones_bf[:], q[:, c, 0:nt], start=(c == 0), stop=(c == 7)),
                  r=[r_q, r_const], w=[r_sp] if c == 0 else [], wa=[r_sp] if c else [], inc=(c == 7))
            rs, r_rs = rstd.next()
            A(lambda e: e.activation(rs[:, 0:nt], sp_[:, 0:nt], AF.Sqrt, bias=EPS, scale=1.0 / D), r=[r_sp], w=[r_rs])
            V(lambda e: e.reciprocal(rs[:, 0:nt], rs[:, 0:nt]), w=[r_rs])
            if not final:
                for c in range(8):
                    tm, r_tm = tmp.next()
                    V(lambda e: e.tensor_tensor(tm[:, 0:nt], h[:, c, 0:nt], rs[:, 0:nt], ALU.mult), r=[r_h, r_rs], w=[r_tm])
                    A(lambda e: e.activation(inT[:, c, t0:t0 + nt], tm[:, 0:nt], AF.Identity, bias=mod_bi[:, c, j:j + 1], scale=mod_sc[:, c, j:j + 1]),
                      r=[r_tm, r_mod], wa=[r_inT])
            else:
                hn_, r_hn = hn.next()
                for c in range(8):
                    V(lambda e: e.scalar_tensor_tensor(hn_[:, c, 0:nt], h[:, c, 0:nt], fnw[:, c:c + 1], rs[:, 0:nt], ALU.mult, ALU.mult),
                      r=[r_h, r_rs, r_const], w=[r_hn] if c == 0 else [], wa=[r_hn] if c else [])
                for tl in range(nt // 128):
                    o, r_o = ot.next()
                    for half in range(2):
                        pt, r_pt = tp.next()
                        for jj in range(4):
                            c = half * 4 + jj
                            M(lambda e: e.transpose(pt[:, jj * 128:(jj + 1) * 128], hn_[:, c, tl * 128:(tl + 1) * 128], ident[:]),
                              r=[r_hn, r_const], w=[r_pt] if jj == 0 else [], wa=[r_pt] if jj else [])
                        if half:
                            A(lambda e: e.copy(o[:, 512:1024], pt[:]), r=[r_pt], wa=[r_o])
                        else:
                            V(lambda e: e.tensor_copy(o[:, 0:512], pt[:]), r=[r_pt], w=[r_o])
                    row = t0 - NCTX + tl * 128
                    S.dma("pool", out_t.ap()[row:row + 128, :], o[:], r=[r_o], wa=[r_out])
        if final:
            evs = dict(r_out.w)
            S._wait("sp", evs)

    def linear_fm(sb, ps, act, r_act, KC, W_ap, col_chunks, epi, blks=None, t_off=0):
        wts = RR([sb([128, KC, 128], BF16, "lw") for _ in range(3)])
        pts = RR([ps([128, 512], F32, "lp") for _ in range(2)])
        for ci, (c0, ncol) in enumerate(col_chunks):
            wt, r_wt = wts.next()
            S.dma("pool", wt[:, :, 0:ncol], W_ap[:, c0:c0 + ncol].rearrange("(k p) n -> p k n", p=128), w=[r_wt])
            for (t0, nt) in (blks or BLKS):
                pt, r_pt = pts.next()
                for k in range(KC):
                    M(lambda e: e.matmul(pt[0:ncol, 0:nt], wt[:, k, 0:ncol], act[:, k, t0 - t_off:t0 - t_off + nt], start=(k == 0), stop=(k == KC - 1)),
                      r=[r_wt, r_act], w=[r_pt] if k == 0 else [], wa=[r_pt] if k else [], inc=(k == KC - 1))
                epi(ci, c0, ncol, t0, nt, pt, r_pt)

    def linear_tm(sb, ps, act, r_act, KC, W_ap, c0, ncols, epi):
        wts = RR([sb([128, KC, 512], BF16, "lwt") for _ in range(2)])
        pts = RR([ps([128, 512], F32, "lpt") for _ in range(2)])
        for g0 in range(0, ncols, 512):
            n = min(512, ncols - g0)
            wt, r_wt = wts.next()
            S.dma("pool", wt[:, :, 0:n], W_ap[:, c0 + g0:c0 + g0 + n].rearrange("(k p) n -> p k n", p=128), w=[r_wt])
            for tt in range(NTT):
                pt, r_pt = pts.next()
                for k in range(KC):
                    M(lambda e: e.matmul(pt[:, 0:n], act[:, k, tt * 128:(tt + 1) * 128], wt[:, k, 0:n], start=(k == 0), stop=(k == KC - 1)),
                      r=[r_wt, r_act], w=[r_pt] if k == 0 else [], wa=[r_pt] if k else [], inc=(k == KC - 1))
                epi(g0, n, tt, pt, r_pt)

    def make_resid_epi(sb):
        hts = RR([sb([128, 512], F32, "rh") for _ in range(3)])

        def epi(ci, c0, ncol, t0, nt, pt, r_pt):
            c = c0 // 128
            j = 1 if t0 < NCTX else 0
            ht, r_ht = hts.next()
            S.dma("sp", ht[:, 0:nt], hT_ap[c * 128:(c + 1) * 128, t0:t0 + nt], r=hres(c, t0, nt), w=[r_ht])
            V(lambda e: e.scalar_tensor_tensor(ht[:, 0:nt], pt[:, 0:nt], mod_gt[:, c, j:j + 1], ht[:, 0:nt], ALU.mult, ALU.add),
              r=[r_pt, r_mod], w=[r_ht])
            S.dma("pool", hT_ap[c * 128:(c + 1) * 128, t0:t0 + nt], ht[:, 0:nt], r=[r_ht], wa=hres(c, t0, nt))
        return epi

    def mamba_layer(lay):
        x_tm = dscr("x_tm%d" % uid[0], [T, 2048], BF16)
        B_tm = dscr("B_tm%d" % uid[0], [T, 1024], BF16)
        BT_d = dscr("BT_d%d" % uid[0], [8, 128, T], BF16)
        CT_d = dscr("CT_d%d" % uid[0], [8, 128, T], BF16)
        sz_tm = dscr("sz_tm%d" % uid[0], [T, 2048], BF16)
        laT_d = dscr("laT_d%d" % uid[0], [64, T], F32)
        ltot_d = dscr("ltot_d%d" % uid[0], [NTT, 64], F32)
        Yacc = dscr("Yacc%d" % uid[0], [T, 2048], F32)
        uid[0] += 1
        r_xtm, r_Btm, r_BT, r_CT, r_sz, r_laT, r_ltot = Res(), Res(), Res(), Res(), Res(), Res(), Res()
        r_Y = [Res() for _ in range(NTT)]
        with ExitStack() as lst:
            lsb, lps = mk(lst)
            la_tm = lsb([128, NTT, 64], F32, "la_tm")
            dt_tm = lsb([128, NTT, 64], F32, "dt_tm")
            LTB = lsb([128, NTT, 64], F32, "LTB")
            r_tabs = Res()
            with ExitStack() as st1:
                sb1, ps1 = mk(st1)
                inT = sb1([128, 8, T], BF16, "inT")
                r_inT = Res()
                with ExitStack() as ph:
                    sb, ps = mk(ph)
                    pre_pass(lay, sb, ps, inT, r_inT)
                    S.barrier()
                with ExitStack() as ph:
                    sb, ps = mk(ph)
                    convw = sb([128, 32, 5], F32, "convw")
                    convb = sb([128, 32], F32, "convb")
                    r_cv = Res()
                    S.dma("sp", convw[:], lay["convw"].ap(), w=[r_cv])
                    S.dma("sp", convb[:], lay["convb"].ap(), wa=[r_cv])
                    xr = sb([128, T + 8], F32, "xr")
                    r_xr = Res()
                    G(lambda e: e.memset(xr[:], 0.0), w=[r_xr])
                    acc = sb([128, T], F32, "cacc")
                    r_acc = Res()
                    xo = RR([sb([128, T], BF16, "cxo") for _ in range(2)])
                    tps = RR([ps([128, 512], BF16, "ctp") for _ in range(2)])
                    tos = RR([sb([128, 512], BF16, "cto") for _ in range(3)])
                    state = {}

                    def epi_xbc(ci, c0, ncol, t0, nt, pt, r_pt):
                        off = 2 if t0 < NCTX else 6
                        if ci % 2:
                            A(lambda e: e.copy(xr[:, t0 + off:t0 + off + nt], pt[:, 0:nt]), r=[r_pt], wa=[r_xr])
                        else:
                            V(lambda e: e.tensor_copy(xr[:, t0 + off:t0 + off + nt], pt[:, 0:nt]), r=[r_pt], wa=[r_xr])
                        if t0 + nt < T:
                            return
                        segs = [(0, NCTX, 0), (NCTX, nlat, 4)]
                        for (s0, sn, dl) in segs:
                            A(lambda e: e.activation(acc[:, s0:s0 + sn], xr[:, s0 + dl:s0 + dl + sn], AF.Identity,
                                                     bias=convb[:, ci:ci + 1], scale=convw[:, ci, 0:1]), r=[r_xr, r_cv], wa=[r_acc])
                            for k in range(1, 5):
                                V(lambda e: e.scalar_tensor_tensor(acc[:, s0:s0 + sn], xr[:, s0 + dl + k:s0 + dl + k + sn], convw[:, ci, k:k + 1],
                                                                   acc[:, s0:s0 + sn], ALU.mult, ALU.add), r=[r_xr, r_cv], w=[r_acc])
                        o, r_o = xo.next()
                        A(lambda e: e.activation(o[:], acc[:], AF.Silu), r=[r_acc], w=[r_o])
                        if ci < 24:
                            dst, col0, r_d = (x_tm, ci * 128, r_xtm) if ci < 16 else (B_tm, (ci - 16) * 128, r_Btm)
                            for t4 in range(0, NTT, 4):
                                n4 = min(4, NTT - t4)
                                tp_, r_tp = tps.next()
                                for q in range(n4):
                                    M(lambda e: e.transpose(tp_[:, q * 128:(q + 1) * 128], o[:, (t4 + q) * 128:(t4 + q + 1) * 128], identb[:]),
                                      r=[r_o, r_const], w=[r_tp] if q == 0 else [], wa=[r_tp] if q else [])
                                to, r_to = tos.next()
                                A(lambda e: e.copy(to[:, 0:n4 * 128], tp_[:, 0:n4 * 128]), r=[r_tp], w=[r_to])
                                S.dma("sp", dst.ap()[t4 * 128:(t4 + n4) * 128, col0:col0 + 128].rearrange("(q p) c -> p q c", p=128),
                                      to[:, 0:n4 * 128].rearrange("p (q c) -> p q c", q=n4), r=[r_to], wa=[r_d])
                        if ci >= 16:
                            gg = (ci - 16) % 8
                            dd, r_dd = (BT_d, r_BT) if ci < 24 else (CT_d, r_CT)
                            S.dma("sp", dd.ap()[gg], o[:], r=[r_o], wa=[r_dd])

                    linear_fm(sb, ps, inT, r_inT, 8, lay["inw"].ap(), [(2048 + 128 * i, 128) for i in range(32)], epi_xbc)
                    S.barrier()
                with ExitStack() as ph:
                    sb, ps = mk(ph)
                    dtT = sb([64, NTT, 128], F32, "dtT")
                    dA = sb([64, NTT, 128], F32, "dA")
                    laP = sb([64, NTT, 128], F32, "laP")
                    laT = sb([64, NTT, 128], F32, "laT")
                    rp = sb([64, NTT, 128], F32, "rp")
                    r_dt, r_dA, r_laP, r_laTs, r_rp = Res(), Res(), Res(), Res(), Res()
                    sm = sb([64, 4], F32, "dtsm")
                    r_sm = Res()
                    S.dma("sp", sm[:, 0:1], lay["alog"].ap(), w=[r_sm])
                    S.dma("sp", sm[:, 1:2], lay["dtb"].ap(), wa=[r_sm])
                    A(lambda e: e.activation(sm[:, 2:3], sm[:, 0:1], AF.Exp), r=[r_sm], wa=[r_sm])
                    V(lambda e: e.tensor_scalar(sm[:, 3:4], sm[:, 2:3], -1.0, None, ALU.mult), r=[r_sm], wa=[r_sm])
                    G(lambda e: e.memset(rp[:], 1.0), w=[r_rp])
                    G(lambda e: e.memset(rp[:, :, 0:1], 0.0), w=[r_rp])
                    dtf = dtT[:].rearrange("p c l -> p (c l)")

                    def epi_dt(ci, c0, ncol, t0, nt, pt, r_pt):
                        A(lambda e: e.activation(dtf[:, t0:t0 + nt], pt[0:64, 0:nt], AF.Exp, bias=sm[:, 1:2], scale=1.0), r=[r_pt, r_sm], wa=[r_dt])
                    linear_fm(sb, ps, inT, r_inT, 8, lay["inw"].ap(), [(6144, 64)], epi_dt)
                    A(lambda e: e.activation(dtf, dtf, AF.Ln, bias=1.0, scale=1.0), w=[r_dt])
                    V(lambda e: e.tensor_scalar(dA[:], dtT[:], sm[:, 3:4], None, ALU.mult), r=[r_dt, r_sm], w=[r_dA])
                    V(lambda e: e.tensor_tensor_scan(laP[:].rearrange("p c l -> p (c l)"), rp[:].rearrange("p c l -> p (c l)"),
                                                     dA[:].rearrange("p c l -> p (c l)"), 0.0, ALU.mult, ALU.add), r=[r_rp, r_dA], w=[r_laP])
                    V(lambda e: e.tensor_copy(laT[0:32], laP[0:32]), r=[r_laP], w=[r_laTs])
                    V(lambda e: e.tensor_tensor(laT[32:64], dA[32:64], laP[32:64], ALU.subtract), r=[r_laP, r_dA], wa=[r_laTs])
                    V(lambda e: e.tensor_tensor(laT[32:64], laT[32:64], laP[32:64, :, 127:128].broadcast_to([32, NTT, 128]), ALU.add), r=[r_laP], w=[r_laTs])
                    S.dma("sp", laT_d.ap(), laT[:].rearrange("p c l -> p (c l)"), r=[r_laTs], w=[r_laT])
                    tpp = RR([ps([128, 64], F32, "dtp") for _ in range(2)])
                    for tt in range(NTT):
                        for (src, r_src, dst) in ((laT, r_laTs, la_tm), (dtT, r_dt, dt_tm)):
                            tp_, r_tp = tpp.next()
                            M(lambda e: e.transpose(tp_[:], src[:, tt, :], ident[0:64, 0:64]), r=[r_src, r_const], w=[r_tp])
                            V(lambda e: e.tensor_copy(dst[:, tt, :], tp_[:]), r=[r_tp], wa=[r_tabs])
                    S.dma("sp", ltot_d.ap()[:, 0:32], la_tm[127:128, :, 0:32], r=[r_tabs], w=[r_ltot])
                    S.dma("sp", ltot_d.ap()[:, 32:64], la_tm[0:1, :, 32:64], r=[r_tabs], wa=[r_ltot])
                    S.dma("sp", LTB[:].rearrange("p c h -> p (c h)"), bass.AP(ltot_d, 0, [[0, 128], [1, NTT * 64]]), r=[r_ltot], wa=[r_tabs])
                    S.barrier()
                with ExitStack() as ph:
                    sb, ps = mk(ph)
                    zo = RR([sb([128, 512], BF16, "zo") for _ in range(3)])

                    def epi_z(g0, n, tt, pt, r_pt):
                        o, r_o = zo.next()
                        A(lambda e: e.activation(o[:, 0:n], pt[:, 0:n], AF.Silu), r=[r_pt], w=[r_o])
                        S.dma("sp", sz_tm.ap()[tt * 128:(tt + 1) * 128, g0:g0 + n], o[:, 0:n], r=[r_o], wa=[r_sz])
                    linear_tm(sb, ps, inT, r_inT, 8, lay["inw"].ap(), 0, 2048, epi_z)
                    S.barrier()
            with ExitStack() as ph:
                sb, ps = mk(ph)
                masks = sb([128, 2, 128], F32, "masks")
                r_mk = Res()
                S.dma("sp", masks[:], masks_in.ap().rearrange("d s l -> s d l"), w=[r_mk])
                xt_p = RR([sb([128, 2048], BF16, "sx") for _ in range(2)])
                bt_p = RR([sb([128, 1024], BF16, "sB") for _ in range(2)])
                BTs_p = RR([sb([128, 8, 128], BF16, "sBT") for _ in range(2)])
                CTs_p = RR([sb([128, 8, 128], BF16, "sCT") for _ in range(2)])
                LaB_p = RR([sb([128, 32, 128], F32, "sLaB") for _ in range(2)])
                dmat = sb([128, 32, 128], F32, "dmat")
                r_dmat = Res()
                decay = sb([128, 32, 128], BF16, "decay")
                r_decay = Res()
                wT = sb([128, 32, 128], BF16, "wT")
                r_wT = Res()
                CBm = sb([128, 8, 128], BF16, "CBm")
                r_CBm = Res()
                xdt = sb([128, 2048], BF16, "xdt")
                r_xdt = Res()
                xw = sb([128, 2048], BF16, "xw")
                r_xw = Res()
                sml = sb([128, 4, 32], F32, "ssml")
                r_sml = Res()
                ST = sb([128, 2048], F32, "ST")
                r_ST = Res()
                prevb = sb([128, 2048], BF16, "prevb")
                r_prevb = Res()
                ysb = sb([128, 2048], F32, "ysb")
                r_ysb = Res()
                yin_p = RR([sb([128, 2048], F32, "yin") for _ in range(2)])
                cbp = ps([128, 8, 128], F32, "cbp")
                r_cbp = Res()
                ydp = ps([128, 1024], F32, "ydp")
                r_ydp = Res()
                yop = ps([128, 1024], F32, "yop")
                r_yop = Res()
                stp = ps([128, 1024], F32, "stp")
                r_stp = Res()
                for dr in range(2):
                    order = list(range(NTT)) if dr == 0 else [1, 0] + list(range(NTT - 1, 1, -1))
                    V(lambda e: e.memset(ST[:], 0.0), w=[r_ST])
                    hc = dr * 32
                    for c in order:
                        tok = slice(c * 128, (c + 1) * 128)
                        xt, r_xt = xt_p.next()
                        S.dma("sp", xt[:], x_tm.ap()[tok, :], r=[r_xtm], w=[r_xt])
                        bt, r_bt = bt_p.next()
                        S.dma("sp", bt[:], B_tm.ap()[tok, :], r=[r_Btm], w=[r_bt])
                        BTs, r_BTs = BTs_p.next()
                        S.dma("sp", BTs[:], BT_d.ap()[:, :, tok].rearrange("g n t -> n g t"), r=[r_BT], w=[r_BTs])
                        CTs, r_CTs = CTs_p.next()
                        S.dma("sp", CTs[:], CT_d.ap()[:, :, tok].rearrange("g n t -> n g t"), r=[r_CT], w=[r_CTs])
                        LaB, r_LaB = LaB_p.next()
                        S.dma("sp", LaB[:], bass.AP(laT_d, hc * T + c * 128, [[0, 128], [T, 32], [1, 128]]), r=[r_laT], w=[r_LaB])
                        la_c = la_tm[:, c, hc:hc + 32]
                        A(lambda e: e.activation(sml[:, 0, :], la_c, AF.Exp), r=[r_tabs], w=[r_sml])
                        V(lambda e: e.tensor_tensor(sml[:, 3, :], LTB[:, c, hc:hc + 32], la_c, ALU.subtract), r=[r_tabs], w=[r_sml])
                        V(lambda e: e.tensor_single_scalar(sml[:, 3, :], sml[:, 3, :], 0.0, ALU.min), w=[r_sml])
                        A(lambda e: e.activation(sml[:, 1, :], sml[:, 3, :], AF.Exp), w=[r_sml])
                        A(lambda e: e.activation(sml[:, 2, :], LTB[:, c, hc:hc + 32], AF.Exp), r=[r_tabs], w=[r_sml])
                        for g in range(8):
                            M(lambda e: e.matmul(cbp[:, g, :], BTs[:, g, :], CTs[:, g, :], start=True, stop=True), r=[r_BTs, r_CTs],
                              w=[r_cbp] if g == 0 else [], wa=[r_cbp] if g else [], inc=(g == 7))
                        V(lambda e: e.tensor_tensor(CBm[:], cbp[:], masks[:, dr:dr + 1, :].broadcast_to([128, 8, 128]), ALU.mult),
                          r=[r_cbp, r_mk], w=[r_CBm])
                        for h in range(32):
                            V(lambda e: e.tensor_scalar(dmat[:, h, :], LaB[:, h, :], la_tm[:, c, hc + h:hc + h + 1], 0.0, ALU.subtract, ALU.min),
                              r=[r_LaB, r_tabs], w=[r_dmat] if h == 0 else [], wa=[r_dmat] if h else [])
                        A(lambda e: e.activation(decay[:], dmat[:], AF.Exp), r=[r_dmat], w=[r_decay])
                        V(lambda e: e.tensor_tensor(wT[:].rearrange("p (g h) l -> p g h l", g=8), decay[:].rearrange("p (g h) l -> p g h l", g=8),
                                                    CBm[:].unsqueeze(2).broadcast_to([128, 8, 4, 128]), ALU.mult), r=[r_decay, r_CBm], w=[r_wT])
                        V(lambda e: e.tensor_tensor(xdt[:].rearrange("p (h q) -> p h q", h=32), xt[:].rearrange("p (h q) -> p h q", h=32),
                                                    dt_tm[:, c, hc:hc + 32].unsqueeze(2).broadcast_to([128, 32, 64]), ALU.mult), r=[r_xt, r_tabs], w=[r_xdt])
                        V(lambda e: e.tensor_tensor(xw[:].rearrange("p (h q) -> p h q", h=32), xdt[:].rearrange("p (h q) -> p h q", h=32),
                                                    sml[:, 1, :].unsqueeze(2).broadcast_to([128, 32, 64]), ALU.mult), r=[r_xdt, r_sml], w=[r_xw])
                        A(lambda e: e.copy(prevb[:], ST[:]), r=[r_ST], w=[r_prevb])
                        if dr == 1:
                            yin, r_yin = yin_p.next()
                            S.dma("sp", yin[:], Yacc.ap()[tok, :], r=[r_Y[c]], w=[r_yin])
                        for gh in range(2):
                            cs = slice(gh * 1024, (gh + 1) * 1024)
                            for hl in range(16):
                                h = gh * 16 + hl
                                M(lambda e: e.matmul(ydp[:, hl * 64:(hl + 1) * 64], wT[:, h, :], xdt[:, h * 64:(h + 1) * 64], start=True, stop=True),
                                  r=[r_wT, r_xdt], w=[r_ydp] if hl == 0 else [], wa=[r_ydp] if hl else [], inc=(hl == 15))
                            for gl in range(4):
                                g = gh * 4 + gl
                                M(lambda e: e.matmul(yop[:, gl * 256:(gl + 1) * 256], CTs[:, g, :], prevb[:, g * 256:(g + 1) * 256], start=True, stop=True),
                                  r=[r_CTs, r_prevb], w=[r_yop] if gl == 0 else [], wa=[r_yop] if gl else [], inc=(gl == 3))
                            for gl in range(4):
                                g = gh * 4 + gl
                                M(lambda e: e.matmul(stp[:, gl * 256:(gl + 1) * 256], bt[:, g * 128:(g + 1) * 128], xw[:, g * 256:(g + 1) * 256], start=True, stop=True),
                                  r=[r_bt, r_xw], w=[r_stp] if gl == 0 else [], wa=[r_stp] if gl else [], inc=(gl == 3))
                            if dr == 1:
                                V(lambda e: e.tensor_tensor(ysb[:, cs], ydp[:], yin[:, cs], ALU.add), r=[r_ydp, r_yin], w=[r_ysb] if gh == 0 else [], wa=[r_ysb] if gh else [])
                            else:
                                A(lambda e: e.copy(ysb[:, cs], ydp[:]), r=[r_ydp], w=[r_ysb] if gh == 0 else [], wa=[r_ysb] if gh else [])
                            for hl in range(16):
                                h = gh * 16 + hl
                                V(lambda e: e.scalar_tensor_tensor(ysb[:, h * 64:(h + 1) * 64], yop[:, hl * 64:(hl + 1) * 64], sml[:, 0, h:h + 1],
                                                                   ysb[:, h * 64:(h + 1) * 64], ALU.mult, ALU.add), r=[r_yop, r_sml], w=[r_ysb])
                            V(lambda e: e.tensor_tensor(ST[:, cs].rearrange("p (h q) -> p h q", h=16), ST[:, cs].rearrange("p (h q) -> p h q", h=16),
                                                        sml[:, 2, gh * 16:(gh + 1) * 16].unsqueeze(2).broadcast_to([128, 16, 64]), ALU.mult),
                              r=[r_sml, r_prevb], w=[r_ST])
                            V(lambda e: e.tensor_tensor(ST[:, cs], ST[:, cs], stp[:], ALU.add), r=[r_stp], w=[r_ST])
                        S.dma("pool", Yacc.ap()[tok, :], ysb[:], r=[r_ysb], w=[r_Y[c]])
                S.barrier()
            with ExitStack() as ph:
                sb, ps = mk(ph)
                dvec = sb([128, 2048], F32, "dvec")
                mnw = sb([128, 2048], F32, "mnw")
                r_dv = Res()
                S.dma("sp", dvec[:], bass.AP(lay["dvec"], 0, [[0, 128], [1, 2048]]), w=[r_dv])
                S.dma("sp", mnw[:], bass.AP(lay["mnw"], 0, [[0, 128], [1, 2048]]), wa=[r_dv])
                ow = sb([128, 16, D], BF16, "ow")
                r_ow = Res()
                for k4 in range(4):
                    S.dma("pool", ow[:, k4 * 4:(k4 + 1) * 4, :], lay["outw"].ap()[k4 * 512:(k4 + 1) * 512, :].rearrange("(k p) n -> p k n", p=128),
                          w=[r_ow] if k4 == 0 else [], wa=[r_ow] if k4 else [])
                y_p = RR([sb([128, 2048], F32, "ty") for _ in range(2)])
                x_p = RR([sb([128, 2048], BF16, "tx") for _ in range(2)])
                z_p = RR([sb([128, 2048], BF16, "tz") for _ in range(2)])
                g_p = RR([sb([128, 2048], F32, "tg") for _ in range(2)])
                gb_p = RR([sb([128, 2048], BF16, "tgb") for _ in range(2)])
                junk = sb([128, 2048], BF16, "tjunk")
                r_junk = Res()
                ss_p = RR([sb([128, 2], F32, "tss") for _ in range(2)])
                gT_p = RR([sb([128, 16, 512], BF16, "tgT") for _ in range(2)])
                tp_p = RR([ps([128, 512], BF16, "ttp") for _ in range(2)])
                op_p = RR([ps([128, 512], F32, "top") for _ in range(3)])
                ht_p = RR([sb([128, 8, 512], F32, "tht") for _ in range(2)])
                groups = [(0, 2)] + [(2 + 4 * i, 4) for i in range((NTT - 2) // 4)]
                for (tt0, ng) in groups:
                    j = 1 if tt0 < 2 else 0
                    t0, nt = tt0 * 128, ng * 128
                    gT, r_gT = gT_p.next()
                    for q4 in range(ng):
                        tt = tt0 + q4
                        tok = slice(tt * 128, (tt + 1) * 128)
                        y, r_y = y_p.next()
                        S.dma("sp", y[:], Yacc.ap()[tok, :], r=[r_Y[tt]], w=[r_y])
                        xt, r_xt = x_p.next()
                        S.dma("sp", xt[:], x_tm.ap()[tok, :], r=[r_xtm], w=[r_xt])
                        zt, r_zt = z_p.next()
                        S.dma("sp", zt[:], sz_tm.ap()[tok, :], r=[r_sz], w=[r_zt])
                        gt_, r_g = g_p.next()
                        V(lambda e: e.tensor_tensor(gt_[:], xt[:], dvec[:], ALU.mult), r=[r_xt, r_dv], w=[r_g])
                        V(lambda e: e.tensor_tensor(gt_[:], gt_[:], y[:], ALU.add), r=[r_y], w=[r_g])
                        V(lambda e: e.tensor_tensor(gt_[:], gt_[:], zt[:], ALU.mult), r=[r_zt], w=[r_g])
                        ss, r_ss = ss_p.next()
                        A(lambda e: e.activation(junk[:], gt_[:], AF.Square, accum_out=ss[:, 0:1]), r=[r_g], w=[r_junk, r_ss])
                        A(lambda e: e.activation(ss[:, 1:2], ss[:, 0:1], AF.Sqrt, bias=EPS, scale=1.0 / 2048), w=[r_ss])
                        V(lambda e: e.reciprocal(ss[:, 1:2], ss[:, 1:2]), w=[r_ss])
                        gb, r_gb = gb_p.next()
                        V(lambda e: e.scalar_tensor_tensor(gb[:], gt_[:], ss[:, 1:2], mnw[:], ALU.mult, ALU.mult), r=[r_g, r_ss, r_dv], w=[r_gb])
                        for k4 in range(4):
                            tp_, r_tp = tp_p.next()
                            for q in range(4):
                                k = k4 * 4 + q
                                M(lambda e: e.transpose(tp_[:, q * 128:(q + 1) * 128], gb[:, k * 128:(k + 1) * 128], identb[:]),
                                  r=[r_gb, r_const], w=[r_tp] if q == 0 else [], wa=[r_tp] if q else [], inc=(q == 3))
                            A(lambda e: e.copy(gT[:, k4 * 4:(k4 + 1) * 4, q4 * 128:(q4 + 1) * 128], tp_[:].rearrange("p (q t) -> p q t", q=4)), r=[r_tp],
                              w=[r_gT] if (k4 == 0 and q4 == 0) else [], wa=[] if (k4 == 0 and q4 == 0) else [r_gT])
                    ht, r_ht = ht_p.next()
                    hr = [r_hT[(c, tt)] for c in range(8) for tt in range(tt0, tt0 + ng)]
                    S.dma("sp", ht[:, :, 0:nt], hT_ap[:, t0:t0 + nt].rearrange("(c p) t -> p c t", p=128), r=hr, w=[r_ht])
                    for dc in range(8):
                        op, r_op = op_p.next()
                        for k in range(16):
                            M(lambda e: e.matmul(op[:, 0:nt], ow[:, k, dc * 128:(dc + 1) * 128], gT[:, k, 0:nt], start=(k == 0), stop=(k == 15)),
                              r=[r_ow, r_gT], w=[r_op] if k == 0 else [], wa=[r_op] if k else [], inc=(k == 15))
                        V(lambda e: e.scalar_tensor_tensor(ht[:, dc, 0:nt], op[:, 0:nt], mod_gt[:, dc, j:j + 1], ht[:, dc, 0:nt], ALU.mult, ALU.add),
                          r=[r_op, r_mod], w=[r_ht])
                    S.dma("pool", hT_ap[:, t0:t0 + nt].rearrange("(c p) t -> p c t", p=128), ht[:, :, 0:nt], r=[r_ht], wa=hr)
                S.barrier()

    def attn_layer(lay):
        qT_d = dscr("qT_d", [D, T], BF16)
        kT_d = dscr("kT_d", [256, T], BF16)
        v_tm = dscr("v_tm", [T, 256], BF16)
        sgT_d = dscr("sgT_d", [D, T], BF16)
        oT_d = dscr("oT_d", [D, T], BF16)
        r_q, r_k, r_v, r_sg, r_o = Res(), Res(), Res(), Res(), Res()
        with ExitStack() as st1:
            sb1, ps1 = mk(st1)
            inT = sb1([128, 8, T], BF16, "inT")
            r_inT = Res()
            with ExitStack() as ph:
                sb, ps = mk(ph)
                pre_pass(lay, sb, ps, inT, r_inT)
                S.barrier()
            with ExitStack() as ph:
                sb, ps = mk(ph)
                rope = sb([128, 2, nlat], F32, "rope")
                r_cst = Res()
                S.dma("sp", rope[:], lay["rope"].ap().rearrange("a p t -> p a t"), w=[r_cst])
                qkw = sb([128, 2], F32, "qkw")
                S.dma("sp", qkw[:], lay["qkw"].ap(), wa=[r_cst])
                permb = sb([128, 128], BF16, "permb")
                S.dma("pool", permb[:], lay["perm"].ap(), wa=[r_cst])
                bones = sb([128, 128], BF16, "bones")
                S.dma("pool", bones[:], lay["bones"].ap(), wa=[r_cst])
                sq_p = RR([sb([128, 512], BF16, "asq") for _ in range(2)])
                ss_p = RR([ps([128, 512], F32, "ass") for _ in range(2)])
                rs_p = RR([sb([128, 512], F32, "ars") for _ in range(2)])
                qn_p = RR([sb([128, 512], F32, "aqn") for _ in range(2)])
                qb_p = RR([sb([128, 512], BF16, "aqb") for _ in range(2)])
                rot_p = RR([ps([128, 512], F32, "arot") for _ in range(2)])
                t1_p = RR([sb([128, 512], F32, "at1") for _ in range(2)])
                t2_p = RR([sb([128, 512], F32, "at2") for _ in range(2)])
                qo_p = RR([sb([128, 512], BF16, "aqo") for _ in range(3)])

                def epi_qkg(ci, c0, ncol, t0, nt, pt, r_pt):
                    if c0 >= 1536:
                        o, r_o_ = qo_p.next()
                        A(lambda e: e.activation(o[:, 0:nt], pt[:, 0:nt], AF.Silu), r=[r_pt], w=[r_o_])
                        cg = (c0 - 1536) // 128
                        S.dma("sp", sgT_d.ap()[cg * 128:(cg + 1) * 128, t0:t0 + nt], o[:, 0:nt], r=[r_o_], wa=[r_sg])
                        return
                    isq = c0 < 1024
                    wcol = 0 if isq else 1
                    sq, r_sq = sq_p.next()
                    A(lambda e: e.activation(sq[:, 0:nt], pt[:, 0:nt], AF.Square), r=[r_pt], w=[r_sq])
                    ss, r_ss = ss_p.next()
                    M(lambda e: e.matmul(ss[:, 0:nt], bones[:], sq[:, 0:nt], start=True, stop=True), r=[r_sq, r_cst], w=[r_ss])
                    rs, r_rs = rs_p.next()
                    A(lambda e: e.activation(rs[:, 0:nt], ss[:, 0:nt], AF.Sqrt, bias=EPS, scale=1.0 / 64), r=[r_ss], w=[r_rs])
                    V(lambda e: e.reciprocal(rs[:, 0:nt], rs[:, 0:nt]), w=[r_rs])
                    o, r_o_ = qo_p.next()
                    if t0 < NCTX:
                        V(lambda e: e.scalar_tensor_tensor(o[:, 0:nt], pt[:, 0:nt], qkw[:, wcol:wcol + 1], rs[:, 0:nt], ALU.mult, ALU.mult),
                          r=[r_pt, r_rs, r_cst], w=[r_o_])
                    else:
                        qn, r_qn = qn_p.next()
                        V(lambda e: e.scalar_tensor_tensor(qn[:, 0:nt], pt[:, 0:nt], qkw[:, wcol:wcol + 1], rs[:, 0:nt], ALU.mult, ALU.mult),
                          r=[r_pt, r_rs, r_cst], w=[r_qn])
                        qb, r_qb = qb_p.next()
                        A(lambda e: e.copy(qb[:, 0:nt], qn[:, 0:nt]), r=[r_qn], w=[r_qb])
                        rot, r_rot = rot_p.next()
                        M(lambda e: e.matmul(rot[:, 0:nt], permb[:], qb[:, 0:nt], start=True, stop=True), r=[r_qb, r_cst], w=[r_rot])
                        l0 = t0 - NCTX
                        t1, r_t1 = t1_p.next()
                        G(lambda e: e.tensor_tensor(t1[:, 0:nt], qn[:, 0:nt], rope[:, 0, l0:l0 + nt], ALU.mult), r=[r_qn, r_cst], w=[r_t1])
                        t2, r_t2 = t2_p.next()
                        V(lambda e: e.tensor_tensor(t2[:, 0:nt], rot[:, 0:nt], rope[:, 1, l0:l0 + nt], ALU.mult), r=[r_rot, r_cst], w=[r_t2])
                        V(lambda e: e.tensor_tensor(o[:, 0:nt], t1[:, 0:nt], t2[:, 0:nt], ALU.add), r=[r_t1, r_t2], w=[r_o_])
                    if isq:
                        S.dma("sp", qT_d.ap()[c0:c0 + 128, t0:t0 + nt], o[:, 0:nt], r=[r_o_], wa=[r_q])
                    else:
                        S.dma("sp", kT_d.ap()[c0 - 1024:c0 - 1024 + 128, t0:t0 + nt], o[:, 0:nt], r=[r_o_], wa=[r_k])

                cols = [(128 * i, 128) for i in range(10)] + [(1536 + 128 * i, 128) for i in range(8)]
                linear_fm(sb, ps, inT, r_inT, 8, lay["inw"].ap(), cols, epi_qkg)
                vo_p = RR([sb([128, 256], BF16, "avo") for _ in range(3)])

                def epi_v(g0, n, tt, pt, r_pt):
                    o, r_o_ = vo_p.next()
                    V(lambda e: e.tensor_copy(o[:, 0:n], pt[:, 0:n]), r=[r_pt], w=[r_o_])
                    S.dma("sp", v_tm.ap()[tt * 128:(tt + 1) * 128, :], o[:, 0:n], r=[r_o_], wa=[r_v])
                linear_tm(sb, ps, inT, r_inT, 8, lay["inw"].ap(), 1280, 256, epi_v)
                S.barrier()
        with ExitStack() as ph:
            sb, ps = mk(ph)
            onesf = sb([128, 64], F32, "aones")
            r_on = Res()
            G(lambda e: e.memset(onesf[:], 1.0), w=[r_on])
            Vg = sb([128, NTT, 65], BF16, "Vg")
            r_Vg = Res()
            G(lambda e: e.memset(Vg[:], 1.0), w=[r_Vg])
            kk_p = RR([sb([128, T], BF16, "kk") for _ in range(2)])
            qc_p = RR([sb([128, T], BF16, "qc") for _ in range(2)])
            sg_p = RR([sb([64, T], BF16, "sgh") for _ in range(2)])
            sp_p = RR([ps([128, 512], F32, "asp") for _ in range(4)])
            P_p = RR([sb([128, 512], BF16, "aP") for _ in range(4)])
            oa_p = RR([ps([128, 512], F32, "aoa") for _ in range(2)])
            bc_p = RR([ps([64, 512], F32, "abc") for _ in range(2)])
            osb_p = RR([sb([128, 512], F32, "aosb") for _ in range(2)])
            o1_p = RR([sb([64, 512], F32, "ao1") for _ in range(2)])
            og_p = RR([sb([64, 512], BF16, "aog") for _ in range(2)])
            for gk in range(4):
                kk, r_kk = kk_p.next()
                S.dma("sp", kk[0:64, :], kT_d.ap()[gk * 64:(gk + 1) * 64, :], r=[r_k], w=[r_kk])
                S.dma("sp", kk[64:128, :], kT_d.ap()[gk * 64:(gk + 1) * 64, :], r=[r_k], wa=[r_kk])
                S.dma("sp", Vg[:, :, 0:64], v_tm.ap()[:, gk * 64:(gk + 1) * 64].rearrange("(t p) d -> p t d", p=128), r=[r_v], w=[r_Vg])
                for qc in (2 * gk, 2 * gk + 1):
                    qt, r_qt = qc_p.next()
                    S.dma("sp", qt[:], qT_d.ap()[qc * 128:(qc + 1) * 128, :], r=[r_q], w=[r_qt])
                    for hh in range(2):
                        h = 2 * qc + hh
                        pr = slice(64 * hh, 64 * hh + 64)
                        sgh, r_sgh = sg_p.next()
                        S.dma("sp", sgh[:], sgT_d.ap()[h * 64:(h + 1) * 64, :], r=[r_sg], w=[r_sgh])
                        tasks = []
                        for (t0, nt) in BLKS:
                            ktiles = [0, 1] if t0 < NCTX else list(range(NTT))
                            for ki, kt in enumerate(ktiles):
                                tasks.append((t0, nt, ki, kt, len(ktiles)))
                        spq = {}
                        cur = {}
                        deferred = []

                        def emit_qk(ti):
                            t0, nt, ki, kt, nk = tasks[ti]
                            sp_, r_sp = sp_p.next()
                            M(lambda e: e.matmul(sp_[:, 0:nt], kk[pr, kt * 128:(kt + 1) * 128], qt[pr, t0:t0 + nt], start=True, stop=True),
                              r=[r_kk, r_qt], w=[r_sp])
                            spq[ti] = (sp_, r_sp)

                        def finalize_pe(args):
                            (t0, nt, osb, r_osb) = args
                            bc, r_bc = bc_p.next()
                            M(lambda e: e.matmul(bc[:, 0:nt], onesf[64:65, :], osb[64:65, 0:nt], start=True, stop=True), r=[r_osb, r_on], w=[r_bc])
                            o1, r_o1 = o1_p.next()
                            V(lambda e: e.tensor_tensor(o1[:, 0:nt], osb[0:64, 0:nt], bc[:, 0:nt], ALU.mult), r=[r_osb, r_bc], w=[r_o1])
                            og, r_og = og_p.next()
                            G(lambda e: e.tensor_tensor(og[:, 0:nt], o1[:, 0:nt], sgh[:, t0:t0 + nt], ALU.mult), r=[r_o1, r_sgh], w=[r_og])
                            S.dma("sp", oT_d.ap()[h * 64:(h + 1) * 64, t0:t0 + nt], og[:, 0:nt], r=[r_og], wa=[r_o])

                        LOOK = 2
                        for ti in range(min(LOOK, len(tasks))):
                            emit_qk(ti)
                        for ti in range(len(tasks)):
                            t0, nt, ki, kt, nk = tasks[ti]
                            sp_, r_sp = spq.pop(ti)
                            if ki == 0:
                                cur["oa"] = oa_p.next()
                            oa, r_oa = cur["oa"]
                            P, r_P = P_p.next()
                            A(lambda e: e.activation(P[:, 0:nt], sp_[:, 0:nt], AF.Exp, bias=-8.0, scale=0.125), r=[r_sp], w=[r_P])
                            M(lambda e: e.matmul(oa[0:65, 0:nt], Vg[:, kt, :], P[:, 0:nt], start=(ki == 0), stop=(ki == nk - 1)),
                              r=[r_Vg, r_P], w=[r_oa] if ki == 0 else [], wa=[r_oa] if ki else [], inc=(ki == nk - 1))
                            if ti + LOOK < len(tasks):
                                emit_qk(ti + LOOK)
                            deferred = [(n - 1, a) for (n, a) in deferred]
                            while deferred and deferred[0][0] <= 0:
                                finalize_pe(deferred.pop(0)[1])
                            if ki == nk - 1:
                                osb, r_osb = osb_p.next()
                                V(lambda e: e.tensor_copy(osb[0:65, 0:nt], oa[0:65, 0:nt]), r=[r_oa], w=[r_osb])
                                V(lambda e: e.reciprocal(osb[64:65, 0:nt], osb[64:65, 0:nt]), w=[r_osb])
                                deferred.append((4, (t0, nt, osb, r_osb)))
                        for (_, a) in deferred:
                            finalize_pe(a)
            S.barrier()
        with ExitStack() as ph:
            sb, ps = mk(ph)
            oT = sb([128, 8, T], BF16, "oT")
            r_oT = Res()
            S.dma("sp", oT[:], oT_d.ap().rearrange("(c p) t -> p c t", p=128), r=[r_o], w=[r_oT])
            linear_fm(sb, ps, oT, r_oT, 8, lay["outw"].ap(), [(128 * i, 128) for i in range(8)], make_resid_epi(sb))
            S.barrier()

    def s5_layer(lay):
        uT_d = dscr("uT_d", [D, T], BF16)
        szT_d = dscr("szT_d", [D, T], BF16)
        gT_d = dscr("gT_d", [D, T], BF16)
        y2T_d = dscr("y2T_d", [D, T], BF16)
        r_u, r_sz, r_g, r_y2 = Res(), Res(), Res(), Res()
        NLV = 1
        while (1 << (NLV - 1)) < T:
            NLV += 1
        with ExitStack() as st1:
            sb1, ps1 = mk(st1)
            inT = sb1([128, 8, T], BF16, "inT")
            r_inT = Res()
            with ExitStack() as ph:
                sb, ps = mk(ph)
                pre_pass(lay, sb, ps, inT, r_inT)
                S.barrier()
            with ExitStack() as ph:
                sb, ps = mk(ph)
                uo_p = RR([sb([128, 512], BF16, "suo") for _ in range(3)])

                def epi_uz(ci, c0, ncol, t0, nt, pt, r_pt):
                    o, r_o_ = uo_p.next()
                    if c0 < 1024:
                        V(lambda e: e.tensor_copy(o[:, 0:nt], pt[:, 0:nt]), r=[r_pt], w=[r_o_])
                        S.dma("sp", uT_d.ap()[c0:c0 + 128, t0:t0 + nt], o[:, 0:nt], r=[r_o_], wa=[r_u])
                    else:
                        A(lambda e: e.activation(o[:, 0:nt], pt[:, 0:nt], AF.Silu), r=[r_pt], w=[r_o_])
                        S.dma("sp", szT_d.ap()[c0 - 1024:c0 - 1024 + 128, t0:t0 + nt], o[:, 0:nt], r=[r_o_], wa=[r_sz])
                linear_fm(sb, ps, inT, r_inT, 8, lay["inw"].ap(), [(128 * i, 128) for i in range(16)], epi_uz)
                S.barrier()
        with ExitStack() as ph:
            sb, ps = mk(ph)
            lam = sb([128, 3, 64], F32, "lam")
            r_t = Res()
            S.dma("sp", lam[:], lay["lam"].ap(), w=[r_t])
            tb = sb([128, 16, 64], F32, "stb")
            tbi = sb([128, 64], I32, "stbi")
            coef = sb([128, 3, 64], F32, "coef")
            pw = sb([128, 64, NLV, 3], F32, "pw")
            lr, li, ls = lam[:, 0, :], lam[:, 1, :], lam[:, 2, :]
            X = lambda i: tb[:, i, :]

            def vt(fn):
                V(fn, w=[r_t])

            def at(fn):
                A(fn, w=[r_t])
            at(lambda e: e.activation(X(0), ls, AF.Exp))
            vt(lambda e: e.tensor_tensor(X(1), lr, X(0), ALU.mult))
            at(lambda e: e.activation(X(2), X(1), AF.Exp))
            vt(lambda e: e.tensor_tensor(X(3), li, X(0), ALU.mult))

            def sin_of(dst, src, shift):
                vt(lambda e: e.tensor_scalar(X(4), src, shift, 1.0 / (2 * PI), ALU.add, ALU.mult))
                vt(lambda e: e.tensor_copy(tbi[:], X(4)))
                vt(lambda e: e.tensor_copy(X(5), tbi[:]))
                vt(lambda e: e.tensor_scalar(X(4), src, shift, None, ALU.add))
                vt(lambda e: e.scalar_tensor_tensor(X(4), X(5), -2 * PI, X(4), ALU.mult, ALU.add))
                vt(lambda e: e.tensor_single_scalar(X(5), X(4), PI, ALU.is_gt))
                vt(lambda e: e.scalar_tensor_tensor(X(4), X(5), -2 * PI, X(4), ALU.mult, ALU.add))
                vt(lambda e: e.tensor_single_scalar(X(5), X(4), -PI, ALU.is_lt))
                vt(lambda e: e.scalar_tensor_tensor(X(4), X(5), 2 * PI, X(4), ALU.mult, ALU.add))
                at(lambda e: e.activation(dst, X(4), AF.Sin))
            sin_of(X(6), X(3), 0.0)
            sin_of(X(7), X(3), PI / 2)
            vt(lambda e: e.tensor_tensor(X(8), X(2), X(7), ALU.mult))
            vt(lambda e: e.tensor_tensor(X(9), X(2), X(6), ALU.mult))
            vt(lambda e: e.tensor_tensor(X(10), lr, lr, ALU.mult))
            vt(lambda e: e.tensor_tensor(X(11), li, li, ALU.mult))
            vt(lambda e: e.tensor_tensor(X(10), X(10), X(11), ALU.add))
            vt(lambda e: e.reciprocal(X(10), X(10)))
            vt(lambda e: e.tensor_scalar(X(11), X(8), -1.0, None, ALU.add))
            vt(lambda e: e.tensor_tensor(X(12), X(11), lr, ALU.mult))
            vt(lambda e: e.tensor_tensor(X(13), X(9), li, ALU.mult))
            vt(lambda e: e.tensor_tensor(X(12), X(12), X(13), ALU.add))
            vt(lambda e: e.tensor_tensor(coef[:, 0, :], X(12), X(10), ALU.mult))
            vt(lambda e: e.tensor_tensor(X(12), X(9), lr, ALU.mult))
            vt(lambda e: e.tensor_tensor(X(13), X(11), li, ALU.mult))
            vt(lambda e: e.tensor_tensor(X(12), X(12), X(13), ALU.subtract))
            vt(lambda e: e.tensor_tensor(coef[:, 1, :], X(12), X(10), ALU.mult))
            vt(lambda e: e.tensor_scalar(coef[:, 2, :], coef[:, 1, :], -1.0, None, ALU.mult))
            vt(lambda e: e.tensor_copy(pw[:, :, 0, 0], X(8)))
            vt(lambda e: e.tensor_copy(pw[:, :, 0, 1], X(9)))
            for lv in range(NLV):
                vt(lambda e: e.tensor_scalar(pw[:, :, lv, 2], pw[:, :, lv, 1], -1.0, None, ALU.mult))
                if lv + 1 < NLV:
                    vt(lambda e: e.tensor_tensor(X(12), pw[:, :, lv, 0], pw[:, :, lv, 0], ALU.mult))
                    vt(lambda e: e.tensor_tensor(X(13), pw[:, :, lv, 1], pw[:, :, lv, 1], ALU.mult))
                    vt(lambda e: e.tensor_tensor(pw[:, :, lv + 1, 0], X(12), X(13), ALU.subtract))
                    vt(lambda e: e.tensor_tensor(X(12), pw[:, :, lv, 0], pw[:, :, lv, 1], ALU.mult))
                    vt(lambda e: e.tensor_scalar(pw[:, :, lv + 1, 1], X(12), 2.0, None, ALU.mult))
            brt = sb([128, 2, 2, 32, 128], BF16, "brt")
            for q in range(2):
                for k in range(2):
                    for j8 in range(4):
                        S.dma("pool", brt[:, q, k, j8 * 8:(j8 + 1) * 8, :], lay["brt"].ap()[q, :, k, j8 * 8:(j8 + 1) * 8, :], wa=[r_t])
            crp = sb([128, 2, 2, 32, 32], BF16, "crp")
            S.dma("pool", crp[:, 0], lay["crp"].ap()[0], wa=[r_t])
            S.dma("pool", crp[:, 1], lay["crp"].ap()[1], wa=[r_t])
            at(lambda e: e.mul(crp[:, 1], crp[:, 1], -1.0))
            sd = sb([128, 8], F32, "sd")
            S.dma("sp", sd[:], lay["sd"].ap(), wa=[r_t])
            Xr = sb([128, T], F32, "Xr")
            Xi = sb([128, T], F32, "Xi")
            Yr = sb([128, T], F32, "Yr")
            Yi = sb([128, T], F32, "Yi")
            r_X, r_Yb, r_Yi = Res(), Res(), Res()
            xb = [[sb([128, T], BF16, "xb%d%d" % (k, q)) for q in range(2)] for k in range(2)]
            r_xb = [Res(), Res()]
            uc_p = RR([sb([128, T], BF16, "suc") for _ in range(2)])
            p12_p = RR([ps([128, 512], F32, "sp12") for _ in range(4)])
            tmp_p = RR([sb([128, 512], F32, "stmp") for _ in range(2)])
            yp_p = RR([ps([128, 512], F32, "syp") for _ in range(2)])
            yv = sb([128, T], F32, "yv")
            ga = Yr
            r_yv, r_ga = Res(), r_Yb
            go_p = RR([sb([128, T], BF16, "sgo") for _ in range(1)])

            def sview(t, off, step, a0, cnt, mult):
                s0 = off + a0 * step
                st_ = mult * step
                return t[:, s0:s0 + (cnt - 1) * st_ + 1:st_]

            def scan(col, k):
                outr, outi = xb[k]

                def rec(tr, ti, off, step, n, lv, yoff, top):
                    if n == 1:
                        if top:
                            A(lambda e: e.copy(outr[:, 0:1], tr[:, off:off + 1]), r=[r_X], w=[r_xb[k]])
                            A(lambda e: e.copy(outi[:, 0:1], ti[:, off:off + 1]), r=[r_X], w=[r_xb[k]])
                        return
                    m = n // 2
                    ne = n - m
                    ar, ai, nai = pw[:, col, lv, 0:1], pw[:, col, lv, 1:2], pw[:, col, lv, 2:3]
                    rs_ = [r_X, r_Yb, r_t]
                    Ev = lambda t, a0, cnt: sview(t, off, step, 2 * a0, cnt, 2)
                    Ov = lambda t, a0, cnt: sview(t, off, step, 2 * a0 + 1, cnt, 2)
                    yr, yi = Yr[:, yoff:yoff + m], Yi[:, yoff:yoff + m]
                    V(lambda e: e.scalar_tensor_tensor(yr, Ev(tr, 0, m), ar, Ov(tr, 0, m), ALU.mult, ALU.add), r=rs_, w=[r_Yb])
                    V(lambda e: e.scalar_tensor_tensor(yi, Ev(ti, 0, m), ar, Ov(ti, 0, m), ALU.mult, ALU.add), r=rs_, wa=[r_Yi])
                    V(lambda e: e.scalar_tensor_tensor(yr, Ev(ti, 0, m), nai, yr, ALU.mult, ALU.add), r=rs_, w=[r_Yb])
                    V(lambda e: e.scalar_tensor_tensor(yi, Ev(tr, 0, m), ai, yi, ALU.mult, ALU.add), r=rs_, w=[r_Yb, r_Yi])
                    rec(Yr, Yi, yoff, 1, m, lv + 1, yoff + m, False)
                    ne1 = ne - 1
                    zr, zi = Yr[:, yoff:yoff + ne1], Yi[:, yoff:yoff + ne1]
                    if top:
                        A(lambda e: e.copy(sview(outr, 0, 1, 1, m, 2), yr), r=[r_Yb], w=[r_xb[k]])
                        A(lambda e: e.copy(sview(outi, 0, 1, 1, m, 2), yi), r=[r_Yb], w=[r_xb[k]])
                        A(lambda e: e.copy(outr[:, 0:1], tr[:, off:off + 1]), r=[r_X], w=[r_xb[k]])
                        A(lambda e: e.copy(outi[:, 0:1], ti[:, off:off + 1]), r=[r_X], w=[r_xb[k]])
                        if ne1 > 0:
                            V(lambda e: e.scalar_tensor_tensor(Ev(tr, 1, ne1), zr, ar, Ev(tr, 1, ne1), ALU.mult, ALU.add), r=rs_, w=[r_X])
                            V(lambda e: e.scalar_tensor_tensor(sview(outr, 0, 1, 2, ne1, 2), zi, nai, Ev(tr, 1, ne1), ALU.mult, ALU.add), r=rs_, w=[r_xb[k]])
                            V(lambda e: e.scalar_tensor_tensor(Ev(ti, 1, ne1), zi, ar, Ev(ti, 1, ne1), ALU.mult, ALU.add), r=rs_, w=[r_X])
                            V(lambda e: e.scalar_tensor_tensor(sview(outi, 0, 1, 2, ne1, 2), zr, ai, Ev(ti, 1, ne1), ALU.mult, ALU.add), r=rs_, w=[r_xb[k]])
                    else:
                        wres = [r_Yb] if tr is Yr else [r_X]
                        A(lambda e: e.copy(Ov(tr, 0, m), yr), r=[r_Yb], w=wres)
                        A(lambda e: e.copy(Ov(ti, 0, m), yi), r=[r_Yb], w=wres)
                        if ne1 > 0:
                            V(lambda e: e.scalar_tensor_tensor(Ev(tr, 1, ne1), zr, ar, Ev(tr, 1, ne1), ALU.mult, ALU.add), r=rs_, w=wres)
                            V(lambda e: e.scalar_tensor_tensor(Ev(tr, 1, ne1), zi, nai, Ev(tr, 1, ne1), ALU.mult, ALU.add), r=rs_, w=wres)
                            V(lambda e: e.scalar_tensor_tensor(Ev(ti, 1, ne1), zi, ar, Ev(ti, 1, ne1), ALU.mult, ALU.add), r=rs_, w=wres)
                            V(lambda e: e.scalar_tensor_tensor(Ev(ti, 1, ne1), zr, ai, Ev(ti, 1, ne1), ALU.mult, ALU.add), r=rs_, w=wres)
                rec(Xr, Xi, 0, 1, T, 0, 0, True)

            def bwd_pos(t0, nt):
                return (NCTX - t0 - nt) if t0 < NCTX else (NCTX + T - t0 - nt)

            uc = None
            SK = os.environ.get("S5_SKIP", "")
            for j in range(32 if "L" not in SK else 0):
                cj, jm = j // 4, j % 4
                pr = slice(32 * jm, 32 * jm + 32)
                if jm == 0:
                    uc, r_uc = uc_p.next()
                    S.dma("sp", uc[:], uT_d.ap()[cj * 128:(cj + 1) * 128, :], r=[r_u], w=[r_uc])
                for k in range(2):
                    col = k * 32 + j
                    for (t0, nt) in BLKS:
                        if k == 0:
                            i0 = t0
                            uv = uc[:, t0:t0 + nt]
                        else:
                            i0 = bwd_pos(t0, nt)
                            uv = uc[:, t0:t0 + nt][:, ::-1]
                        if "m" in SK:
                            continue
                        p1, r_p1 = p12_p.next()
                        p2, r_p2 = p12_p.next()
                        M(lambda e: e.matmul(p1[:, 0:nt], brt[:, 0, k, j, :], uv, start=True, stop=True), r=[r_uc, r_t], w=[r_p1])
                        M(lambda e: e.matmul(p2[:, 0:nt], brt[:, 1, k, j, :], uv, start=True, stop=True), r=[r_uc, r_t], w=[r_p2])
                        if "e" in SK:
                            continue
                        tm, r_tm = tmp_p.next()
                        if "a" not in SK:
                            A(lambda e: e.activation(tm[:, 0:nt], p2[:, 0:nt], AF.Identity, scale=coef[:, 2, col:col + 1]), r=[r_t], w=[r_tm, r_p2])
                        if "v" not in SK:
                            V(lambda e: e.scalar_tensor_tensor(Xr[:, i0:i0 + nt], p1[:, 0:nt], coef[:, 0, col:col + 1], tm[:, 0:nt], ALU.mult, ALU.add),
                              r=[r_tm, r_t], w=[r_X, r_p1])
                        tm2, r_tm2 = tmp_p.next()
                        if "a" not in SK:
                            A(lambda e: e.activation(tm2[:, 0:nt], p1[:, 0:nt], AF.Identity, scale=coef[:, 1, col:col + 1]), r=[r_t], w=[r_tm2, r_p1])
                        if "v" not in SK:
                            V(lambda e: e.scalar_tensor_tensor(Xi[:, i0:i0 + nt], p2[:, 0:nt], coef[:, 0, col:col + 1], tm2[:, 0:nt], ALU.mult, ALU.add),
                              r=[r_tm2, r_t], w=[r_X, r_p2])
                    if "s" not in SK:
                        scan(col, k)
                for (t0, nt) in (BLKS if "r" not in SK else []):
                    yp, r_yp = yp_p.next()
                    i0 = bwd_pos(t0, nt)
                    rv = (lambda a: a[:, ::-1]) if not os.environ.get("S5_NOREV") else (lambda a: a)
                    ops = [(crp[:, 0, 0, j, :], xb[0][0][:, t0:t0 + nt], r_xb[0]), (crp[:, 1, 0, j, :], xb[0][1][:, t0:t0 + nt], r_xb[0]),
                           (crp[:, 0, 1, j, :], rv(xb[1][0][:, i0:i0 + nt]), r_xb[1]), (crp[:, 1, 1, j, :], rv(xb[1][1][:, i0:i0 + nt]), r_xb[1])]
                    for qi, (lh, rh, rr) in enumerate(ops):
                        M(lambda e: e.matmul(yp[pr, 0:nt], lh, rh, start=(qi == 0), stop=(qi == 3), tile_position=(0, 32 * jm)), r=[rr, r_t],
                          w=[r_yp] if qi == 0 else [], wa=[r_yp] if qi else [])
                    V(lambda e: e.scalar_tensor_tensor(yv[pr, t0:t0 + nt], uc[pr, t0:t0 + nt], sd[pr, cj:cj + 1], yp[pr, 0:nt], ALU.mult, ALU.add),
                      r=[r_yp, r_uc, r_t], w=[r_yv] if (jm == 0 and t0 == 0) else [], wa=[] if (jm == 0 and t0 == 0) else [r_yv])
                if jm == 3 and "g" not in SK:
                    A(lambda e: e.activation(ga[:], yv[:], AF.Square), r=[r_yv], w=[r_ga])
                    V(lambda e: e.tensor_scalar(ga[:], ga[:], 0.044715, 1.0, ALU.mult, ALU.add), w=[r_ga])
                    V(lambda e: e.tensor_tensor(ga[:], ga[:], yv[:], ALU.mult), r=[r_yv], w=[r_ga])
                    A(lambda e: e.activation(ga[:], ga[:], AF.Sigmoid, scale=1.5957691216057308), w=[r_ga])
                    go, r_go = go_p.next()
                    V(lambda e: e.tensor_tensor(go[:], ga[:], yv[:], ALU.mult), r=[r_ga, r_yv], w=[r_go])
                    S.dma("sp", gT_d.ap()[cj * 128:(cj + 1) * 128, :], go[:], r=[r_go], wa=[r_g])
            S.barrier()
        with ExitStack() as ph:
            sb, ps = mk(ph)
            gT = sb([128, 8, T], BF16, "gT")
            r_gT = Res()
            S.dma("sp", gT[:], gT_d.ap().rearrange("(c p) t -> p c t", p=128), r=[r_g], w=[r_gT])
            glub = sb([128, 8], F32, "glub")
            r_gb = Res()
            S.dma("sp", glub[:], lay["glub"].ap(), w=[r_gb])
            sig_p = RR([sb([128, 512], F32, "ssig") for _ in range(2)])
            szt_p = RR([sb([128, 512], BF16, "sszt") for _ in range(2)])
            y2_p = RR([sb([128, 512], BF16, "sy2") for _ in range(3)])

            def epi_glu(ci, c0, ncol, t0, nt, pt, r_pt):
                sg, r_sg_ = sig_p.next()
                A(lambda e: e.activation(sg[:, 0:nt], pt[:, 0:nt], AF.Sigmoid, bias=glub[:, ci:ci + 1], scale=1.0), r=[r_pt, r_gb], w=[r_sg_])
                szt, r_szt = szt_p.next()
                S.dma("sp", szt[:, 0:nt], szT_d.ap()[c0:c0 + 128, t0:t0 + nt], r=[r_sz], w=[r_szt])
                V(lambda e: e.tensor_tensor(sg[:, 0:nt], sg[:, 0:nt], gT[:, ci, t0:t0 + nt], ALU.mult), r=[r_gT], w=[r_sg_])
                y2, r_y2_ = y2_p.next()
                V(lambda e: e.tensor_tensor(y2[:, 0:nt], sg[:, 0:nt], szt[:, 0:nt], ALU.mult), r=[r_sg_, r_szt], w=[r_y2_])
                S.dma("sp", y2T_d.ap()[c0:c0 + 128, t0:t0 + nt], y2[:, 0:nt], r=[r_y2_], wa=[r_y2])
            linear_fm(sb, ps, gT, r_gT, 8, lay["gluw"].ap(), [(128 * i, 128) for i in range(8)], epi_glu)
            S.barrier()
        with ExitStack() as ph:
            sb, ps = mk(ph)
            y2 = sb([128, 8, T], BF16, "y2r")
            r_y2r = Res()
            S.dma("sp", y2[:], y2T_d.ap().rearrange("(c p) t -> p c t", p=128), r=[r_y2], w=[r_y2r])
            linear_fm(sb, ps, y2, r_y2r, 8, lay["outw"].ap(), [(128 * i, 128) for i in range(8)], make_resid_epi(sb))
            S.barrier()

    for i in layers:
        kind = i % 3
        if kind == 0:
            mamba_layer(L[i])
        elif kind == 1:
            attn_layer(L[i])
        else:
            s5_layer(L[i])

    with ExitStack() as ph:
        sb, ps = mk(ph)
        pre_pass(None, sb, ps, None, None, final=True)
    S.barrier()
    es.close()
    nc._ninst = S.ninst
    return nc


def prep_inputs(inputs, b, nlat=NLAT, layers=(0, 1, 2, 3)):
    f = lambda a: np.ascontiguousarray(np.asarray(a, dtype=np.float32))
    chunked = lambda v, n: f(np.asarray(v, np.float32).reshape(n, 128).T)
    m = {}
    m["x"] = f(inputs["x"][b][:nlat])
    m["ctx"] = f(inputs["ctx"][b])
    m["cc"] = f(np.stack([chunked(inputs["c"][b], 8), chunked(inputs["c_ctx"], 8)], axis=-1))
    m["ident"] = np.eye(128, dtype=np.float32)
    m["fnw"] = chunked(inputs["final_norm_w"], 8)
    has_m = False
    for i in layers:
        m["normw%d" % i] = chunked(inputs["norm_w"][i], 8)
        m["modw%d" % i] = f(inputs["mod_w"][i])
        m["modb%d" % i] = chunked(inputs["mod_b"][i], 24)
        kind, j = i % 3, i // 3
        if kind == 0:
            has_m = True
            m["m_in_w%d" % j] = f(inputs["m_in_w"][j])
            cw = np.asarray(inputs["m_conv_w"][j], np.float32)
            m["m_convw%d" % j] = f(cw.reshape(5, 32, 128).transpose(2, 1, 0))
            m["m_convb%d" % j] = chunked(inputs["m_conv_b"][j], 32)
            m["m_alog%d" % j] = f(np.asarray(inputs["m_a_log"][j], np.float32).reshape(64, 1))
            m["m_dtb%d" % j] = f(np.asarray(inputs["m_dt_bias"][j], np.float32).reshape(64, 1))
            m["m_dvec%d" % j] = f(np.repeat(np.asarray(inputs["m_d"][j], np.float32), 64))
            m["m_normw%d" % j] = f(inputs["m_norm_w"][j])
            m["m_out_w%d" % j] = f(inputs["m_out_w"][j])
        elif kind == 1:
            m["a_in_w"] = f(inputs["a_in_w"][0])
            m["a_out_w"] = f(inputs["a_out_w"][0])
            m["a_qkw"] = f(np.stack([np.tile(np.asarray(inputs["a_q_norm"][0], np.float32), 2),
                                     np.tile(np.asarray(inputs["a_k_norm"][0], np.float32), 2)], axis=1))
            grid_w = 64
            pos = np.arange(nlat)
            r_idx, c_idx = (pos // grid_w).astype(np.float32), (pos % grid_w).astype(np.float32)
            inv = (10000.0 ** (-np.arange(0, 32, 2, dtype=np.float32) / 32)).astype(np.float32)
            dd = np.arange(128) % 64
            ax, part, ii = dd // 32, (dd % 32) // 16, dd % 16
            ang = np.where(ax[:, None] == 0, r_idx[None, :], c_idx[None, :]).astype(np.float32) * inv[ii][:, None]
            m["a_rope"] = f(np.stack([np.cos(ang), np.sin(ang)]))
            perm = np.zeros((128, 128), np.float32)
            for dcol in range(128):
                if part[dcol] == 0:
                    perm[dcol + 16, dcol] = -1.0
                else:
                    perm[dcol - 16, dcol] = 1.0
            m["a_perm"] = perm
            bo = np.zeros((128, 128), np.float32)
            bo[:64, :64] = 1.0
            bo[64:, 64:] = 1.0
            m["a_bones"] = bo
        else:
            m["s_in_w"] = f(inputs["s_in_w"][0])
            m["s_glu_w"] = f(inputs["s_glu_w"][0])
            m["s_out_w"] = f(inputs["s_out_w"][0])
            m["s_sd"] = chunked(inputs["s_d"][0], 8)
            m["s_glub"] = chunked(inputs["s_glu_b"][0], 8)
            lre = np.asarray(inputs["s_lambda_re"][0], np.float32)
            lim = np.asarray(inputs["s_lambda_im"][0], np.float32)
            lst = np.asarray(inputs["s_log_step"][0], np.float32)

            def pair_layout(a):
                a = a.reshape(2, 32, 2, 64)
                return a.transpose(2, 3, 0, 1).reshape(128, 64)
            lam = np.stack([pair_layout(lre), pair_layout(lim),
                            pair_layout(np.broadcast_to(lst[:, :, None], (2, 64, 64)))], axis=1)
            m["s_lam"] = f(lam)
            brt = np.zeros((2, 128, 2, 32, 128), np.float32)
            crp = np.zeros((2, 128, 2, 32, 32), np.float32)
            for q, (bsrc, csrc) in enumerate(((inputs["s_b_re"][0], inputs["s_c_re"][0]), (inputs["s_b_im"][0], inputs["s_c_im"][0]))):
                bsrc = np.asarray(bsrc, np.float32)
                csrc = np.asarray(csrc, np.float32)
                for k in range(2):
                    for j in range(32):
                        for gl in range(2):
                            g_ = 2 * j + gl
                            r0 = 32 * (j % 4) + 16 * gl
                            brt[q, r0:r0 + 16, k, j, gl * 64:(gl + 1) * 64] = bsrc[k, g_].T
                            crp[q, gl * 64:(gl + 1) * 64, k, j, 16 * gl:16 * gl + 16] = csrc[k, g_].T
            m["s_brt"] = brt
            m["s_crp"] = crp
    if has_m:
        up = np.triu(np.ones((128, 128), np.float32))
        m["masks"] = f(np.stack([up, up.T]))
    return m


ACTIVE_CORES = (0, 1, 4, 5)


def kernel(**inputs):
    nc = build_program()
    real = [prep_inputs(inputs, b) for b in range(4)]
    big = ("x", "ctx", "cc", "modw", "m_in_w", "m_out_w", "a_in_w", "a_out_w", "s_in_w", "s_glu_w", "s_out_w", "s_brt", "s_crp")
    idle = {k: (np.zeros_like(v) if k.startswith(big) else v) for k, v in real[0].items()}
    in_maps = [idle] * 8
    for b, core in enumerate(ACTIVE_CORES):
        in_maps[core] = real[b]
    res = run_bass_kernel_spmd(nc, in_maps, core_ids=list(range(8)))
    out = np.stack([np.asarray(res.results[core]["out"], dtype=np.float32) for core in ACTIVE_CORES], axis=0)
    return out
```

```python
import os
import numpy as np
from contextlib import ExitStack
import ml_dtypes
import concourse.bass as bass
import concourse.mybir as mybir
from concourse.bass_utils import run_bass_kernel_spmd

F32 = mybir.dt.float32
BF16 = mybir.dt.bfloat16
I32 = mybir.dt.int32
AF = mybir.ActivationFunctionType
ALU = mybir.AluOpType
AX = mybir.AxisListType

D = 1024
NCTX = 256
NLAT = 4096
EPS = 1e-6
M_IN = 6208
PI = float(np.pi)


class Res:
    __slots__ = ("w", "r", "name")

    def __init__(self, name=""):
        self.w = {}
        self.r = {}
        self.name = name


class Sched:
    def __init__(self, nc, es):
        self.nc = nc
        self.eng = {"pe": nc.tensor, "act": nc.scalar, "dve": nc.vector, "pool": nc.gpsimd, "sp": nc.sync}
        self.sem = {}
        self.cnt = {}
        self.known = {e: {} for e in self.eng}
        for e in self.eng:
            self.sem[e] = es.enter_context(nc.semaphore("s_" + e))
            self.cnt[e] = 0
        self.NDS = 8
        self.dslot = {}
        for q in ("sp", "pool"):
            for i in range(self.NDS):
                k = "d_%s%d" % (q, i)
                self.sem[k] = es.enter_context(nc.semaphore(k))
                self.cnt[k] = 0
            self.dslot[q] = 0
        self.ninst = 0

    def _wait(self, e, evs):
        kn = self.known[e]
        for k, v in evs.items():
            if v <= 0 or (e == "pe" and k == "pe") or kn.get(k, 0) >= v:
                continue
            self.eng[e].wait_ge(self.sem[k], v)
            kn[k] = v

    @staticmethod
    def _deps(r, w, wa):
        evs = {}

        def add(d):
            for k, v in d.items():
                if evs.get(k, 0) < v:
                    evs[k] = v
        for x in r:
            add(x.w)
        for x in w:
            add(x.w)
            add(x.r)
        for x in wa:
            add(x.r)
        return evs

    @staticmethod
    def _commit(k, v, r, w, wa):
        for x in r:
            if x.r.get(k, 0) < v:
                x.r[k] = v
        for x in w:
            if x.w.get(k, 0) < v:
                x.w[k] = v
        for x in wa:
            if x.w.get(k, 0) < v:
                x.w[k] = v

    def op(self, e, fn, r=(), w=(), wa=(), inc=True):
        self._wait(e, self._deps(r, w, wa))
        ins = fn(self.eng[e])
        if inc:
            self.cnt[e] += 1
            ins.then_inc(self.sem[e], 1)
            self._commit(e, self.cnt[e], r, w, wa)
        else:
            self._commit(e, self.cnt[e] + 1, r, w, wa)
        self.ninst += 1
        return ins

    def dma(self, q, out, in_, r=(), w=(), wa=(), **kw):
        i = self.dslot[q]
        self.dslot[q] = (i + 1) % self.NDS
        k = "d_%s%d" % (q, i)
        evs = self._deps(r, w, wa)
        evs[k] = max(evs.get(k, 0), self.cnt[k])
        self._wait(q, evs)
        ins = self.eng[q].dma_start(out=out, in_=in_, **kw)
        self.cnt[k] += 16
        ins.then_inc(self.sem[k], 16)
        self._commit(k, self.cnt[k], r, w, wa)
        self.ninst += 1
        return ins

    def barrier(self):
        evs = {k: v for k, v in self.cnt.items() if v > 0}
        for e in self.eng:
            self._wait(e, dict(evs))


class RR:
    def __init__(self, tiles):
        self.t = tiles
        self.r = [Res() for _ in tiles]
        self.i = 0

    def next(self):
        i = self.i
        self.i = (i + 1) % len(self.t)
        return self.t[i], self.r[i]


def build_program(nlat=NLAT, layers=(0, 1, 2, 3)):
    T = NCTX + nlat
    NTT = T // 128
    BLKS = [(0, NCTX)] + [(NCTX + 512 * i, 512) for i in range(nlat // 512)]
    nc = bass.Bass("TRN2", target_bir_lowering=False)
    es = ExitStack()
    es.enter_context(nc.allow_low_precision("bf16 matmul operands, fp32 accumulation"))
    S = Sched(nc, es)
    uid = [0]

    def mk(stack):
        def sb(shape, dt=F32, name="t"):
            uid[0] += 1
            return stack.enter_context(nc.sbuf_tensor("%s_%d" % (name, uid[0]), list(shape), dt))

        def ps(shape, dt=F32, name="p"):
            uid[0] += 1
            return stack.enter_context(nc.psum_tensor("%s_%d" % (name, uid[0]), list(shape), dt))
        return sb, ps

    def din(name, shape, dt=F32):
        return nc.dram_tensor(name, list(shape), dt, kind="ExternalInput")

    def dscr(name, shape, dt=F32):
        return nc.dram_tensor(name, list(shape), dt)

    V = lambda fn, r=(), w=(), wa=(): S.op("dve", fn, r, w, wa)
    A = lambda fn, r=(), w=(), wa=(): S.op("act", fn, r, w, wa)
    G = lambda fn, r=(), w=(), wa=(): S.op("pool", fn, r, w, wa)
    M = lambda fn, r=(), w=(), wa=(), inc=True: S.op("pe", fn, r, w, wa, inc)

    x_in = din("x", [nlat, D])
    ctx_in = din("ctx", [NCTX, D])
    cc_in = din("cc", [128, 8, 2])
    ident_in = din("ident", [128, 128])
    fnw_in = din("fnw", [128, 8])
    out_t = nc.dram_tensor("out", [nlat, D], F32, kind="ExternalOutput")
    L = {}
    for i in layers:
        L[i] = dict(normw=din("normw%d" % i, [128, 8]), modw=din("modw%d" % i, [D, 3 * D]), modb=din("modb%d" % i, [128, 24]))
        kind, j = i % 3, i // 3
        if kind == 0:
            L[i].update(inw=din("m_in_w%d" % j, [D, M_IN]), convw=din("m_convw%d" % j, [128, 32, 5]), convb=din("m_convb%d" % j, [128, 32]),
                        alog=din("m_alog%d" % j, [64, 1]), dtb=din("m_dtb%d" % j, [64, 1]), dvec=din("m_dvec%d" % j, [2048]),
                        mnw=din("m_normw%d" % j, [2048]), outw=din("m_out_w%d" % j, [2048, D]))
        elif kind == 1:
            L[i].update(inw=din("a_in_w", [D, 2560]), qkw=din("a_qkw", [128, 2]), outw=din("a_out_w", [D, D]),
                        rope=din("a_rope", [2, 128, nlat]), perm=din("a_perm", [128, 128]), bones=din("a_bones", [128, 128]))
        else:
            L[i].update(inw=din("s_in_w", [D, 2048]), lam=din("s_lam", [128, 3, 64]), brt=din("s_brt", [2, 128, 2, 32, 128]),
                        crp=din("s_crp", [2, 128, 2, 32, 32]), sd=din("s_sd", [128, 8]), gluw=din("s_glu_w", [D, D]),
                        glub=din("s_glub", [128, 8]), outw=din("s_out_w", [D, D]))
    masks_in = din("masks", [2, 128, 128]) if any(i % 3 == 0 for i in layers) else None

    hT = dscr("hT", [D, T])
    hT_ap = hT.ap()
    r_hT = {(c, tt): Res() for c in range(8) for tt in range(NTT)}

    def hres(c, t0, nt):
        return [r_hT[(c, tt)] for tt in range(t0 // 128, (t0 + nt + 127) // 128)]

    def hres_all(t0, nt):
        out = []
        for c in range(8):
            out += hres(c, t0, nt)
        return out

    gsb, gps = mk(es)
    ident = gsb([128, 128], F32, "ident")
    r_const = Res("const")
    S.dma("sp", ident[:], ident_in.ap(), w=[r_const])
    identb = gsb([128, 128], BF16, "identb")
    ones_bf = gsb([128, 128], BF16, "ones")
    G(lambda e: e.memset(ones_bf[:], 1.0), wa=[r_const])
    V(lambda e: e.tensor_copy(identb[:], ident[:]), r=[r_const], wa=[r_const])
    fnw = gsb([128, 8], F32, "fnw")
    S.dma("sp", fnw[:], fnw_in.ap(), wa=[r_const])
    cc = gsb([128, 8, 2], F32, "cc")
    S.dma("sp", cc[:], cc_in.ap(), wa=[r_const])
    scs = gsb([128, 8, 2], F32, "scs")
    A(lambda e: e.activation(scs[:], cc[:], AF.Silu), r=[r_const], wa=[r_const])
    mod_sc = gsb([128, 8, 2], F32, "mod_sc")
    mod_bi = gsb([128, 8, 2], F32, "mod_bi")
    mod_gt = gsb([128, 8, 2], F32, "mod_gt")
    r_mod = Res("mod")
    S.barrier()

    with ExitStack() as ph:
        sb, ps = mk(ph)
        xin = RR([sb([128, D], F32, "xin") for _ in range(2)])
        tp = RR([ps([128, 512], F32, "tp") for _ in range(2)])
        xo = RR([sb([128, 8, 128], F32, "xo") for _ in range(2)])
        for tt in range(NTT):
            xt, r_xt = xin.next()
            src = ctx_in.ap()[tt * 128:(tt + 1) * 128, :] if tt < 2 else x_in.ap()[(tt - 2) * 128:(tt - 1) * 128, :]
            S.dma("sp", xt[:], src, w=[r_xt])
            ot, r_ot = xo.next()
            for half in range(2):
                pt, r_pt = tp.next()
                for j in range(4):
                    c = half * 4 + j
                    M(lambda e: e.transpose(pt[:, j * 128:(j + 1) * 128], xt[:, c * 128:(c + 1) * 128], ident[:]),
                      r=[r_xt, r_const], w=[r_pt] if j == 0 else [], wa=[r_pt] if j else [])
                dst = ot[:, half * 4:(half + 1) * 4, :]
                if half:
                    A(lambda e: e.copy(dst, pt[:].rearrange("p (j t) -> p j t", j=4)), r=[r_pt], wa=[r_ot])
                else:
                    V(lambda e: e.tensor_copy(dst, pt[:].rearrange("p (j t) -> p j t", j=4)), r=[r_pt], w=[r_ot])
            S.dma("pool", hT_ap[:, tt * 128:(tt + 1) * 128].rearrange("(c p) t -> p c t", p=128), ot[:], r=[r_ot],
                  wa=[r_hT[(c, tt)] for c in range(8)])
        S.barrier()

    def pre_pass(lay, sb, ps, inT, r_inT, final=False):
        if not final:
            mw = RR([sb([128, 8, 512], F32, "modw") for _ in range(2)])
            mp = RR([ps([128, 512], F32, "modp") for _ in range(2)])
            modT = sb([128, 24, 2], F32, "modT")
            r_modT = Res()
            modb = sb([128, 24], F32, "modb")
            normw = sb([128, 8], F32, "normw")
            r_small = Res()
            S.dma("sp", modb[:], lay["modb"].ap(), w=[r_small])
            S.dma("sp", normw[:], lay["normw"].ap(), wa=[r_small])
            for cg in range(6):
                wt, r_wt = mw.next()
                S.dma("sp", wt[:], lay["modw"].ap()[:, cg * 512:(cg + 1) * 512].rearrange("(k p) n -> p k n", p=128), w=[r_wt])
                for c4 in range(4):
                    pt, r_pt = mp.next()
                    for k in range(8):
                        M(lambda e: e.matmul(pt[:, 0:2], wt[:, k, c4 * 128:(c4 + 1) * 128], scs[:, k, :], start=(k == 0), stop=(k == 7)),
                          r=[r_wt, r_const], w=[r_pt] if k == 0 else [], wa=[r_pt] if k else [])
                    col = cg * 4 + c4
                    V(lambda e: e.tensor_scalar(modT[:, col, :], pt[:, 0:2], modb[:, col:col + 1], None, ALU.add),
                      r=[r_pt, r_small], wa=[r_modT])
            V(lambda e: e.tensor_scalar(mod_sc[:], modT[:, 8:16, :], 1.0, None, ALU.add), r=[r_modT], w=[r_mod])
            V(lambda e: e.tensor_tensor(mod_sc[:], mod_sc[:], normw[:].unsqueeze(2).broadcast_to([128, 8, 2]), ALU.mult), r=[r_small], w=[r_mod])
            V(lambda e: e.tensor_copy(mod_bi[:], modT[:, 0:8, :]), r=[r_modT], w=[r_mod])
            V(lambda e: e.tensor_copy(mod_gt[:], modT[:, 16:24, :]), r=[r_modT], w=[r_mod])
        hb = RR([sb([128, 8, 512], F32, "hb") for _ in range(2)])
        sq = RR([sb([128, 8, 512], BF16, "sq") for _ in range(2)])
        ssp = RR([ps([128, 512], F32, "ssp") for _ in range(2)])
        rstd = RR([sb([128, 512], F32, "rstd") for _ in range(2)])
        tmp = RR([sb([128, 512], F32, "ntmp") for _ in range(3)])
        if final:
            hn = RR([sb([128, 8, 512], F32, "hn") for _ in range(2)])
            tp = RR([ps([128, 512], F32, "ftp") for _ in range(2)])
            ot = RR([sb([128, D], F32, "fot") for _ in range(2)])
            r_out = Res()
        for (t0, nt) in BLKS:
            j = 1 if t0 < NCTX else 0
            if final and j == 1:
                continue
            h, r_h = hb.next()
            S.dma("sp", h[:, :, 0:nt], hT_ap[:, t0:t0 + nt].rearrange("(c p) t -> p c t", p=128), r=hres_all(t0, nt), w=[r_h])
            q, r_q = sq.next()
            A(lambda e: e.activation(q[:, :, 0:nt], h[:, :, 0:nt], AF.Square), r=[r_h], w=[r_q])
            sp_, r_sp = ssp.next()
            for c in range(8):
                M(lambda e: e.matmul(sp_[:, 0:nt], ones_bf[:], q[:, c, 0:nt], start=(c == 0), stop=(c == 7)),
                  r=[r_q, r_const], w=[r_sp] if c == 0 else [], wa=[r_sp] if c else [], inc=(c == 7))
            rs, r_rs = rstd.next()
            A(lambda e: e.activation(rs[:, 0:nt], sp_[:, 0:nt], AF.Sqrt, bias=EPS, scale=1.0 / D), r=[r_sp], w=[r_rs])
            V(lambda e: e.reciprocal(rs[:, 0:nt], rs[:, 0:nt]), w=[r_rs])
            if not final:
                for c in range(8):
                    tm, r_tm = tmp.next()
                    V(lambda e: e.tensor_tensor(tm[:, 0:nt], h[:, c, 0:nt], rs[:, 0:nt], ALU.mult), r=[r_h, r_rs], w=[r_tm])
                    A(lambda e: e.activation(inT[:, c, t0:t0 + nt], tm[:, 0:nt], AF.Identity, bias=mod_bi[:, c, j:j + 1], scale=mod_sc[:, c, j:j + 1]),
                      r=[r_tm, r_mod], wa=[r_inT])
            else:
                hn_, r_hn = hn.next()
                for c in range(8):
                    V(lambda e: e.scalar_tensor_tensor(hn_[:, c, 0:nt], h[:, c, 0:nt], fnw[:, c:c + 1], rs[:, 0:nt], ALU.mult, ALU.mult),
                      r=[r_h, r_rs, r_const], w=[r_hn] if c == 0 else [], wa=[r_hn] if c else [])
                for tl in range(nt // 128):
                    o, r_o = ot.next()
                    for half in range(2):
                        pt, r_pt = tp.next()
                        for jj in range(4):
                            c = half * 4 + jj
                            M(lambda e: e.transpose(pt[:, jj * 128:(jj + 1) * 128], hn_[:, c, tl * 128:(tl + 1) * 128], ident[:]),
                              r=[r_hn, r_const], w=[r_pt] if jj == 0 else [], wa=[r_pt] if jj else [])
                        if half:
                            A(lambda e: e.copy(o[:, 512:1024], pt[:]), r=[r_pt], wa=[r_o])
                        else:
                            V(lambda e: e.tensor_copy(o[:, 0:512], pt[:]), r=[r_pt], w=[r_o])
                    row = t0 - NCTX + tl * 128
                    S.dma("pool", out_t.ap()[row:row + 128, :], o[:], r=[r_o], wa=[r_out])
        if final:
            evs = dict(r_out.w)
            S._wait("sp", evs)

    def linear_fm(sb, ps, act, r_act, KC, W_ap, col_chunks, epi, blks=None, t_off=0):
        wts = RR([sb([128, KC, 128], BF16, "lw") for _ in range(3)])
        pts = RR([ps([128, 512], F32, "lp") for _ in range(2)])
        for ci, (c0, ncol) in enumerate(col_chunks):
            wt, r_wt = wts.next()
            S.dma("pool", wt[:, :, 0:ncol], W_ap[:, c0:c0 + ncol].rearrange("(k p) n -> p k n", p=128), w=[r_wt])
            for (t0, nt) in (blks or BLKS):
                pt, r_pt = pts.next()
                for k in range(KC):
                    M(lambda e: e.matmul(pt[0:ncol, 0:nt], wt[:, k, 0:ncol], act[:, k, t0 - t_off:t0 - t_off + nt], start=(k == 0), stop=(k == KC - 1)),
                      r=[r_wt, r_act], w=[r_pt] if k == 0 else [], wa=[r_pt] if k else [], inc=(k == KC - 1))
                epi(ci, c0, ncol, t0, nt, pt, r_pt)

    def linear_tm(sb, ps, act, r_act, KC, W_ap, c0, ncols, epi):
        wts = RR([sb([128, KC, 512], BF16, "lwt") for _ in range(2)])
        pts = RR([ps([128, 512], F32, "lpt") for _ in range(2)])
        for g0 in range(0, ncols, 512):
            n = min(512, ncols - g0)
            wt, r_wt = wts.next()
            S.dma("pool", wt[:, :, 0:n], W_ap[:, c0 + g0:c0 + g0 + n].rearrange("(k p) n -> p k n", p=128), w=[r_wt])
            for tt in range(NTT):
                pt, r_pt = pts.next()
                for k in range(KC):
                    M(lambda e: e.matmul(pt[:, 0:n], act[:, k, tt * 128:(tt + 1) * 128], wt[:, k, 0:n], start=(k == 0), stop=(k == KC - 1)),
                      r=[r_wt, r_act], w=[r_pt] if k == 0 else [], wa=[r_pt] if k else [], inc=(k == KC - 1))
                epi(g0, n, tt, pt, r_pt)

    def make_resid_epi(sb):
        hts = RR([sb([128, 512], F32, "rh") for _ in range(3)])

        def epi(ci, c0, ncol, t0, nt, pt, r_pt):
            c = c0 // 128
            j = 1 if t0 < NCTX else 0
            ht, r_ht = hts.next()
            S.dma("sp", ht[:, 0:nt], hT_ap[c * 128:(c + 1) * 128, t0:t0 + nt], r=hres(c, t0, nt), w=[r_ht])
            V(lambda e: e.scalar_tensor_tensor(ht[:, 0:nt], pt[:, 0:nt], mod_gt[:, c, j:j + 1], ht[:, 0:nt], ALU.mult, ALU.add),
              r=[r_pt, r_mod], w=[r_ht])
            S.dma("pool", hT_ap[c * 128:(c + 1) * 128, t0:t0 + nt], ht[:, 0:nt], r=[r_ht], wa=hres(c, t0, nt))
        return epi

    def mamba_layer(lay):
        x_tm = dscr("x_tm%d" % uid[0], [T, 2048], BF16)
        B_tm = dscr("B_tm%d" % uid[0], [T, 1024], BF16)
        BT_d = dscr("BT_d%d" % uid[0], [8, 128, T], BF16)
        CT_d = dscr("CT_d%d" % uid[0], [8, 128, T], BF16)
        sz_tm = dscr("sz_tm%d" % uid[0], [T, 2048], BF16)
        laT_d = dscr("laT_d%d" % uid[0], [64, T], F32)
        ltot_d = dscr("ltot_d%d" % uid[0], [NTT, 64], F32)
        Yacc = dscr("Yacc%d" % uid[0], [T, 2048], F32)
        uid[0] += 1
        r_xtm, r_Btm, r_BT, r_CT, r_sz, r_laT, r_ltot = Res(), Res(), Res(), Res(), Res(), Res(), Res()
        r_Y = [Res() for _ in range(NTT)]
        with ExitStack() as lst:
            lsb, lps = mk(lst)
            la_tm = lsb([128, NTT, 64], F32, "la_tm")
            dt_tm = lsb([128, NTT, 64], F32, "dt_tm")
            LTB = lsb([128, NTT, 64], F32, "LTB")
            r_tabs = Res()
            with ExitStack() as st1:
                sb1, ps1 = mk(st1)
                inT = sb1([128, 8, T], BF16, "inT")
                r_inT = Res()
                with ExitStack() as ph:
                    sb, ps = mk(ph)
                    pre_pass(lay, sb, ps, inT, r_inT)
                    S.barrier()
                with ExitStack() as ph:
                    sb, ps = mk(ph)
                    convw = sb([128, 32, 5], F32, "convw")
                    convb = sb([128, 32], F32, "convb")
                    r_cv = Res()
                    S.dma("sp", convw[:], lay["convw"].ap(), w=[r_cv])
                    S.dma("sp", convb[:], lay["convb"].ap(), wa=[r_cv])
                    xr = sb([128, T + 8], F32, "xr")
                    r_xr = Res()
                    G(lambda e: e.memset(xr[:], 0.0), w=[r_xr])
                    acc = sb([128, T], F32, "cacc")
                    r_acc = Res()
                    xo = RR([sb([128, T], BF16, "cxo") for _ in range(2)])
                    tps = RR([ps([128, 512], BF16, "ctp") for _ in range(2)])
                    tos = RR([sb([128, 512], BF16, "cto") for _ in range(3)])
                    state = {}

                    def epi_xbc(ci, c0, ncol, t0, nt, pt, r_pt):
                        off = 2 if t0 < NCTX else 6
                        if ci % 2:
                            A(lambda e: e.copy(xr[:, t0 + off:t0 + off + nt], pt[:, 0:nt]), r=[r_pt], wa=[r_xr])
                        else:
                            V(lambda e: e.tensor_copy(xr[:, t0 + off:t0 + off + nt], pt[:, 0:nt]), r=[r_pt], wa=[r_xr])
                        if t0 + nt < T:
                            return
                        segs = [(0, NCTX, 0), (NCTX, nlat, 4)]
                        for (s0, sn, dl) in segs:
                            A(lambda e: e.activation(acc[:, s0:s0 + sn], xr[:, s0 + dl:s0 + dl + sn], AF.Identity,
                                                     bias=convb[:, ci:ci + 1], scale=convw[:, ci, 0:1]), r=[r_xr, r_cv], wa=[r_acc])
                            for k in range(1, 5):
                                V(lambda e: e.scalar_tensor_tensor(acc[:, s0:s0 + sn], xr[:, s0 + dl + k:s0 + dl + k + sn], convw[:, ci, k:k + 1],
                                                                   acc[:, s0:s0 + sn], ALU.mult, ALU.add), r=[r_xr, r_cv], w=[r_acc])
                        o, r_o = xo.next()
                        A(lambda e: e.activation(o[:], acc[:], AF.Silu), r=[r_acc], w=[r_o])
                        if ci < 24:
                            dst, col0, r_d = (x_tm, ci * 128, r_xtm) if ci < 16 else (B_tm, (ci - 16) * 128, r_Btm)
                            for t4 in range(0, NTT, 4):
                                n4 = min(4, NTT - t4)
                                tp_, r_tp = tps.next()
                                for q in range(n4):
                                    M(lambda e: e.transpose(tp_[:, q * 128:(q + 1) * 128], o[:, (t4 + q) * 128:(t4 + q + 1) * 128], identb[:]),
                                      r=[r_o, r_const], w=[r_tp] if q == 0 else [], wa=[r_tp] if q else [])
                                to, r_to = tos.next()
                                A(lambda e: e.copy(to[:, 0:n4 * 128], tp_[:, 0:n4 * 128]), r=[r_tp], w=[r_to])
                                S.dma("sp", dst.ap()[t4 * 128:(t4 + n4) * 128, col0:col0 + 128].rearrange("(q p) c -> p q c", p=128),
                                      to[:, 0:n4 * 128].rearrange("p (q c) -> p q c", q=n4), r=[r_to], wa=[r_d])
                        if ci >= 16:
                            gg = (ci - 16) % 8
                            dd, r_dd = (BT_d, r_BT) if ci < 24 else (CT_d, r_CT)
                            S.dma("sp", dd.ap()[gg], o[:], r=[r_o], wa=[r_dd])

                    linear_fm(sb, ps, inT, r_inT, 8, lay["inw"].ap(), [(2048 + 128 * i, 128) for i in range(32)], epi_xbc)
                    S.barrier()
                with ExitStack() as ph:
                    sb, ps = mk(ph)
                    dtT = sb([64, NTT, 128], F32, "dtT")
                    dA = sb([64, NTT, 128], F32, "dA")
                    laP = sb([64, NTT, 128], F32, "laP")
                    laT = sb([64, NTT, 128], F32, "laT")
                    rp = sb([64, NTT, 128], F32, "rp")
                    r_dt, r_dA, r_laP, r_laTs, r_rp = Res(), Res(), Res(), Res(), Res()
                    sm = sb([64, 4], F32, "dtsm")
                    r_sm = Res()
                    S.dma("sp", sm[:, 0:1], lay["alog"].ap(), w=[r_sm])
                    S.dma("sp", sm[:, 1:2], lay["dtb"].ap(), wa=[r_sm])
                    A(lambda e: e.activation(sm[:, 2:3], sm[:, 0:1], AF.Exp), r=[r_sm], wa=[r_sm])
                    V(lambda e: e.tensor_scalar(sm[:, 3:4], sm[:, 2:3], -1.0, None, ALU.mult), r=[r_sm], wa=[r_sm])
                    G(lambda e: e.memset(rp[:], 1.0), w=[r_rp])
                    G(lambda e: e.memset(rp[:, :, 0:1], 0.0), w=[r_rp])
                    dtf = dtT[:].rearrange("p c l -> p (c l)")

                    def epi_dt(ci, c0, ncol, t0, nt, pt, r_pt):
                        A(lambda e: e.activation(dtf[:, t0:t0 + nt], pt[0:64, 0:nt], AF.Exp, bias=sm[:, 1:2], scale=1.0), r=[r_pt, r_sm], wa=[r_dt])
                    linear_fm(sb, ps, inT, r_inT, 8, lay["inw"].ap(), [(6144, 64)], epi_dt)
                    A(lambda e: e.activation(dtf, dtf, AF.Ln, bias=1.0, scale=1.0), w=[r_dt])
                    V(lambda e: e.tensor_scalar(dA[:], dtT[:], sm[:, 3:4], None, ALU.mult), r=[r_dt, r_sm], w=[r_dA])
                    V(lambda e: e.tensor_tensor_scan(laP[:].rearrange("p c l -> p (c l)"), rp[:].rearrange("p c l -> p (c l)"),
                                                     dA[:].rearrange("p c l -> p (c l)"), 0.0, ALU.mult, ALU.add), r=[r_rp, r_dA], w=[r_laP])
                    V(lambda e: e.tensor_copy(laT[0:32], laP[0:32]), r=[r_laP], w=[r_laTs])
                    V(lambda e: e.tensor_tensor(laT[32:64], dA[32:64], laP[32:64], ALU.subtract), r=[r_laP, r_dA], wa=[r_laTs])
                    V(lambda e: e.tensor_tensor(laT[32:64], laT[32:64], laP[32:64, :, 127:128].broadcast_to([32, NTT, 128]), ALU.add), r=[r_laP], w=[r_laTs])
                    S.dma("sp", laT_d.ap(), laT[:].rearrange("p c l -> p (c l)"), r=[r_laTs], w=[r_laT])
                    tpp = RR([ps([128, 64], F32, "dtp") for _ in range(2)])
                    for tt in range(NTT):
                        for (src, r_src, dst) in ((laT, r_laTs, la_tm), (dtT, r_dt, dt_tm)):
                            tp_, r_tp = tpp.next()
                            M(lambda e: e.transpose(tp_[:], src[:, tt, :], ident[0:64, 0:64]), r=[r_src, r_const], w=[r_tp])
                            V(lambda e: e.tensor_copy(dst[:, tt, :], tp_[:]), r=[r_tp], wa=[r_tabs])
                    S.dma("sp", ltot_d.ap()[:, 0:32], la_tm[127:128, :, 0:32], r=[r_tabs], w=[r_ltot])
                    S.dma("sp", ltot_d.ap()[:, 32:64], la_tm[0:1, :, 32:64], r=[r_tabs], wa=[r_ltot])
                    S.dma("sp", LTB[:].rearrange("p c h -> p (c h)"), bass.AP(ltot_d, 0, [[0, 128], [1, NTT * 64]]), r=[r_ltot], wa=[r_tabs])
                    S.barrier()
                with ExitStack() as ph:
                    sb, ps = mk(ph)
                    zo = RR([sb([128, 512], BF16, "zo") for _ in range(3)])

                    def epi_z(g0, n, tt, pt, r_pt):
                        o, r_o = zo.next()
                        A(lambda e: e.activation(o[:, 0:n], pt[:, 0:n], AF.Silu), r=[r_pt], w=[r_o])
                        S.dma("sp", sz_tm.ap()[tt * 128:(tt + 1) * 128, g0:g0 + n], o[:, 0:n], r=[r_o], wa=[r_sz])
                    linear_tm(sb, ps, inT, r_inT, 8, lay["inw"].ap(), 0, 2048, epi_z)
                    S.barrier()
            with ExitStack() as ph:
                sb, ps = mk(ph)
                masks = sb([128, 2, 128], F32, "masks")
                r_mk = Res()
                S.dma("sp", masks[:], masks_in.ap().rearrange("d s l -> s d l"), w=[r_mk])
                xt_p = RR([sb([128, 2048], BF16, "sx") for _ in range(2)])
                bt_p = RR([sb([128, 1024], BF16, "sB") for _ in range(2)])
                BTs_p = RR([sb([128, 8, 128], BF16, "sBT") for _ in range(2)])
                CTs_p = RR([sb([128, 8, 128], BF16, "sCT") for _ in range(2)])
                LaB_p = RR([sb([128, 32, 128], F32, "sLaB") for _ in range(2)])
                dmat = sb([128, 32, 128], F32, "dmat")
                r_dmat = Res()
                decay = sb([128, 32, 128], BF16, "decay")
                r_decay = Res()
                wT = sb([128, 32, 128], BF16, "wT")
                r_wT = Res()
                CBm = sb([128, 8, 128], BF16, "CBm")
                r_CBm = Res()
                xdt = sb([128, 2048], BF16, "xdt")
                r_xdt = Res()
                xw = sb([128, 2048], BF16, "xw")
                r_xw = Res()
                sml = sb([128, 4, 32], F32, "ssml")
                r_sml = Res()
                ST = sb([128, 2048], F32, "ST")
                r_ST = Res()
                prevb = sb([128, 2048], BF16, "prevb")
                r_prevb = Res()
                ysb = sb([128, 2048], F32, "ysb")
                r_ysb = Res()
                eyo = sb([128, 1024], F32, "eyo")
                r_eyo = Res()
                yin_p = RR([sb([128, 2048], F32, "yin") for _ in range(2)])
                cbp = ps([128, 8, 128], F32, "cbp")
                r_cbp = Res()
                ydp = ps([128, 1024], F32, "ydp")
                r_ydp = Res()
                yop = ps([128, 1024], F32, "yop")
                r_yop = Res()
                stp = ps([128, 1024], F32, "stp")
                r_stp = Res()
                for dr in range(2):
                    order = list(range(NTT)) if dr == 0 else [1, 0] + list(range(NTT - 1, 1, -1))
                    V(lambda e: e.memset(ST[:], 0.0), w=[r_ST])
                    hc = dr * 32
                    for c in order:
                        tok = slice(c * 128, (c + 1) * 128)
                        xt, r_xt = xt_p.next()
                        S.dma("sp", xt[:], x_tm.ap()[tok, :], r=[r_xtm], w=[r_xt])
                        bt, r_bt = bt_p.next()
                        S.dma("sp", bt[:], B_tm.ap()[tok, :], r=[r_Btm], w=[r_bt])
                        BTs, r_BTs = BTs_p.next()
                        S.dma("sp", BTs[:], BT_d.ap()[:, :, tok].rearrange("g n t -> n g t"), r=[r_BT], w=[r_BTs])
                        CTs, r_CTs = CTs_p.next()
                        S.dma("sp", CTs[:], CT_d.ap()[:, :, tok].rearrange("g n t -> n g t"), r=[r_CT], w=[r_CTs])
                        LaB, r_LaB = LaB_p.next()
                        S.dma("sp", LaB[:], bass.AP(laT_d, hc * T + c * 128, [[0, 128], [T, 32], [1, 128]]), r=[r_laT], w=[r_LaB])
                        la_c = la_tm[:, c, hc:hc + 32]
                        A(lambda e: e.activation(sml[:, 0, :], la_c, AF.Exp), r=[r_tabs], w=[r_sml])
                        V(lambda e: e.tensor_tensor(sml[:, 3, :], LTB[:, c, hc:hc + 32], la_c, ALU.subtract), r=[r_tabs], w=[r_sml])
                        V(lambda e: e.tensor_single_scalar(sml[:, 3, :], sml[:, 3, :], 0.0, ALU.min), w=[r_sml])
                        A(lambda e: e.activation(sml[:, 1, :], sml[:, 3, :], AF.Exp), w=[r_sml])
                        A(lambda e: e.activation(sml[:, 2, :], LTB[:, c, hc:hc + 32], AF.Exp), r=[r_tabs], w=[r_sml])
                        for g in range(8):
                            M(lambda e: e.matmul(cbp[:, g, :], BTs[:, g, :], CTs[:, g, :], start=True, stop=True), r=[r_BTs, r_CTs],
                              w=[r_cbp] if g == 0 else [], wa=[r_cbp] if g else [], inc=(g == 7))
                        V(lambda e: e.tensor_tensor(CBm[:], cbp[:], masks[:, dr:dr + 1, :].broadcast_to([128, 8, 128]), ALU.mult),
                          r=[r_cbp, r_mk], w=[r_CBm])
                        for h in range(32):
                            V(lambda e: e.tensor_scalar(dmat[:, h, :], LaB[:, h, :], la_tm[:, c, hc + h:hc + h + 1], 0.0, ALU.subtract, ALU.min),
                              r=[r_LaB, r_tabs], w=[r_dmat] if h == 0 else [], wa=[r_dmat] if h else [])
                        A(lambda e: e.activation(decay[:], dmat[:], AF.Exp), r=[r_dmat], w=[r_decay])
                        V(lambda e: e.tensor_tensor(wT[:].rearrange("p (g h) l -> p g h l", g=8), decay[:].rearrange("p (g h) l -> p g h l", g=8),
                                                    CBm[:].unsqueeze(2).broadcast_to([128, 8, 4, 128]), ALU.mult), r=[r_decay, r_CBm], w=[r_wT])
                        V(lambda e: e.tensor_tensor(xdt[:].rearrange("p (h q) -> p h q", h=32), xt[:].rearrange("p (h q) -> p h q", h=32),
                                                    dt_tm[:, c, hc:hc + 32].unsqueeze(2).broadcast_to([128, 32, 64]), ALU.mult), r=[r_xt, r_tabs], w=[r_xdt])
                        V(lambda e: e.tensor_tensor(xw[:].rearrange("p (h q) -> p h q", h=32), xdt[:].rearrange("p (h q) -> p h q", h=32),
                                                    sml[:, 1, :].unsqueeze(2).broadcast_to([128, 32, 64]), ALU.mult), r=[r_xdt, r_sml], w=[r_xw])
                        A(lambda e: e.copy(prevb[:], ST[:]), r=[r_ST], w=[r_prevb])
                        if dr == 1:
                            yin, r_yin = yin_p.next()
                            S.dma("sp", yin[:], Yacc.ap()[tok, :], r=[r_Y[c]], w=[r_yin])
                        for gh in range(2):
                            cs = slice(gh * 1024, (gh + 1) * 1024)
                            for hl in range(16):
                                h = gh * 16 + hl
                                M(lambda e: e.matmul(ydp[:, hl * 64:(hl + 1) * 64], wT[:, h, :], xdt[:, h * 64:(h + 1) * 64], start=True, stop=True),
                                  r=[r_wT, r_xdt], w=[r_ydp] if hl == 0 else [], wa=[r_ydp] if hl else [], inc=(hl == 15))
                            for gl in range(4):
                                g = gh * 4 + gl
                                M(lambda e: e.matmul(yop[:, gl * 256:(gl + 1) * 256], CTs[:, g, :], prevb[:, g * 256:(g + 1) * 256], start=True, stop=True),
                                  r=[r_CTs, r_prevb], w=[r_yop] if gl == 0 else [], wa=[r_yop] if gl else [], inc=(gl == 3))
                            for gl in range(4):
                                g = gh * 4 + gl
                                M(lambda e: e.matmul(stp[:, gl * 256:(gl + 1) * 256], bt[:, g * 128:(g + 1) * 128], xw[:, g * 256:(g + 1) * 256], start=True, stop=True),
                                  r=[r_bt, r_xw], w=[r_stp] if gl == 0 else [], wa=[r_stp] if gl else [], inc=(gl == 3))
                            if dr == 1:
                                V(lambda e: e.tensor_tensor(ysb[:, cs], ydp[:], yin[:, cs], ALU.add), r=[r_ydp, r_yin], w=[r_ysb] if gh == 0 else [], wa=[r_ysb] if gh else [])
                            else:
                                A(lambda e: e.copy(ysb[:, cs], ydp[:]), r=[r_ydp], w=[r_ysb] if gh == 0 else [], wa=[r_ysb] if gh else [])
                            for hl in range(16):
                                h = gh * 16 + hl
                                A(lambda e: e.activation(eyo[:, hl * 64:(hl + 1) * 64], yop[:, hl * 64:(hl + 1) * 64], AF.Identity, scale=sml[:, 0, h:h + 1]),
                                  r=[r_yop, r_sml], w=[r_eyo] if hl == 0 else [], wa=[r_eyo] if hl else [])
                            V(lambda e: e.tensor_tensor(ysb[:, cs], ysb[:, cs], eyo[:], ALU.add), r=[r_eyo], w=[r_ysb])
                            V(lambda e: e.tensor_tensor(ST[:, cs].rearrange("p (h q) -> p h q", h=16), ST[:, cs].rearrange("p (h q) -> p h q", h=16),
                                                        sml[:, 2, gh * 16:(gh + 1) * 16].unsqueeze(2).broadcast_to([128, 16, 64]), ALU.mult),
                              r=[r_sml, r_prevb], w=[r_ST])
                            V(lambda e: e.tensor_tensor(ST[:, cs], ST[:, cs], stp[:], ALU.add), r=[r_stp], w=[r_ST])
                        S.dma("pool", Yacc.ap()[tok, :], ysb[:], r=[r_ysb], w=[r_Y[c]])
                S.barrier()
            with ExitStack() as ph:
                sb, ps = mk(ph)
                dvec = sb([128, 2048], F32, "dvec")
                mnw = sb([128, 2048], F32, "mnw")
                r_dv = Res()
                S.dma("sp", dvec[:], bass.AP(lay["dvec"], 0, [[0, 128], [1, 2048]]), w=[r_dv])
                S.dma("sp", mnw[:], bass.AP(lay["mnw"], 0, [[0, 128], [1, 2048]]), wa=[r_dv])
                ow = sb([128, 16, D], BF16, "ow")
                r_ow = Res()
                for k4 in range(4):
                    S.dma("pool", ow[:, k4 * 4:(k4 + 1) * 4, :], lay["outw"].ap()[k4 * 512:(k4 + 1) * 512, :].rearrange("(k p) n -> p k n", p=128),
                          w=[r_ow] if k4 == 0 else [], wa=[r_ow] if k4 else [])
                y_p = RR([sb([128, 2048], F32, "ty") for _ in range(2)])
                x_p = RR([sb([128, 2048], BF16, "tx") for _ in range(2)])
                z_p = RR([sb([128, 2048], BF16, "tz") for _ in range(2)])
                g_p = RR([sb([128, 2048], F32, "tg") for _ in range(2)])
                gb_p = RR([sb([128, 2048], BF16, "tgb") for _ in range(2)])
                junk = sb([128, 2048], BF16, "tjunk")
                r_junk = Res()
                ss_p = RR([sb([128, 2], F32, "tss") for _ in range(2)])
                gT_p = RR([sb([128, 16, 512], BF16, "tgT") for _ in range(2)])
                tp_p = RR([ps([128, 512], BF16, "ttp") for _ in range(2)])
                op_p = RR([ps([128, 512], F32, "top") for _ in range(3)])
                ht_p = RR([sb([128, 8, 512], F32, "tht") for _ in range(2)])
                groups = [(0, 2)] + [(2 + 4 * i, 4) for i in range((NTT - 2) // 4)]
                for (tt0, ng) in groups:
                    j = 1 if tt0 < 2 else 0
                    t0, nt = tt0 * 128, ng * 128
                    gT, r_gT = gT_p.next()
                    for q4 in range(ng):
                        tt = tt0 + q4
                        tok = slice(tt * 128, (tt + 1) * 128)
                        y, r_y = y_p.next()
                        S.dma("sp", y[:], Yacc.ap()[tok, :], r=[r_Y[tt]], w=[r_y])
                        xt, r_xt = x_p.next()
                        S.dma("sp", xt[:], x_tm.ap()[tok, :], r=[r_xtm], w=[r_xt])
                        zt, r_zt = z_p.next()
                        S.dma("sp", zt[:], sz_tm.ap()[tok, :], r=[r_sz], w=[r_zt])
                        gt_, r_g = g_p.next()
                        V(lambda e: e.tensor_tensor(gt_[:], xt[:], dvec[:], ALU.mult), r=[r_xt, r_dv], w=[r_g])
                        V(lambda e: e.tensor_tensor(gt_[:], gt_[:], y[:], ALU.add), r=[r_y], w=[r_g])
                        V(lambda e: e.tensor_tensor(gt_[:], gt_[:], zt[:], ALU.mult), r=[r_zt], w=[r_g])
                        ss, r_ss = ss_p.next()
                        A(lambda e: e.activation(junk[:], gt_[:], AF.Square, accum_out=ss[:, 0:1]), r=[r_g], w=[r_junk, r_ss])
                        A(lambda e: e.activation(ss[:, 1:2], ss[:, 0:1], AF.Sqrt, bias=EPS, scale=1.0 / 2048), w=[r_ss])
                        V(lambda e: e.reciprocal(ss[:, 1:2], ss[:, 1:2]), w=[r_ss])
                        gb, r_gb = gb_p.next()
                        V(lambda e: e.scalar_tensor_tensor(gb[:], gt_[:], ss[:, 1:2], mnw[:], ALU.mult, ALU.mult), r=[r_g, r_ss, r_dv], w=[r_gb])
                        for k4 in range(4):
                            tp_, r_tp = tp_p.next()
                            for q in range(4):
                                k = k4 * 4 + q
                                M(lambda e: e.transpose(tp_[:, q * 128:(q + 1) * 128], gb[:, k * 128:(k + 1) * 128], identb[:]),
                                  r=[r_gb, r_const], w=[r_tp] if q == 0 else [], wa=[r_tp] if q else [], inc=(q == 3))
                            A(lambda e: e.copy(gT[:, k4 * 4:(k4 + 1) * 4, q4 * 128:(q4 + 1) * 128], tp_[:].rearrange("p (q t) -> p q t", q=4)), r=[r_tp],
                              w=[r_gT] if (k4 == 0 and q4 == 0) else [], wa=[] if (k4 == 0 and q4 == 0) else [r_gT])
                    ht, r_ht = ht_p.next()
                    hr = [r_hT[(c, tt)] for c in range(8) for tt in range(tt0, tt0 + ng)]
                    S.dma("sp", ht[:, :, 0:nt], hT_ap[:, t0:t0 + nt].rearrange("(c p) t -> p c t", p=128), r=hr, w=[r_ht])
                    for dc in range(8):
                        op, r_op = op_p.next()
                        for k in range(16):
                            M(lambda e: e.matmul(op[:, 0:nt], ow[:, k, dc * 128:(dc + 1) * 128], gT[:, k, 0:nt], start=(k == 0), stop=(k == 15)),
                              r=[r_ow, r_gT], w=[r_op] if k == 0 else [], wa=[r_op] if k else [], inc=(k == 15))
                        V(lambda e: e.scalar_tensor_tensor(ht[:, dc, 0:nt], op[:, 0:nt], mod_gt[:, dc, j:j + 1], ht[:, dc, 0:nt], ALU.mult, ALU.add),
                          r=[r_op, r_mod], w=[r_ht])
                    S.dma("pool", hT_ap[:, t0:t0 + nt].rearrange("(c p) t -> p c t", p=128), ht[:, :, 0:nt], r=[r_ht], wa=hr)
                S.barrier()

    def attn_layer(lay):
        qT_d = dscr("qT_d", [D, T], BF16)
        kT_d = dscr("kT_d", [256, T], BF16)
        v_tm = dscr("v_tm", [T, 256], BF16)
        sgT_d = dscr("sgT_d", [D, T], BF16)
        oT_d = dscr("oT_d", [D, T], BF16)
        r_q, r_k, r_v, r_sg, r_o = Res(), Res(), Res(), Res(), Res()
        with ExitStack() as st1:
            sb1, ps1 = mk(st1)
            inT = sb1([128, 8, T], BF16, "inT")
            r_inT = Res()
            with ExitStack() as ph:
                sb, ps = mk(ph)
                pre_pass(lay, sb, ps, inT, r_inT)
                S.barrier()
            with ExitStack() as ph:
                sb, ps = mk(ph)
                rope = sb([128, 2, nlat], F32, "rope")
                r_cst = Res()
                S.dma("sp", rope[:], lay["rope"].ap().rearrange("a p t -> p a t"), w=[r_cst])
                qkw = sb([128, 2], F32, "qkw")
                S.dma("sp", qkw[:], lay["qkw"].ap(), wa=[r_cst])
                permb = sb([128, 128], BF16, "permb")
                S.dma("pool", permb[:], lay["perm"].ap(), wa=[r_cst])
                bones = sb([128, 128], BF16, "bones")
                S.dma("pool", bones[:], lay["bones"].ap(), wa=[r_cst])
                sq_p = RR([sb([128, 512], BF16, "asq") for _ in range(2)])
                ss_p = RR([ps([128, 512], F32, "ass") for _ in range(2)])
                rs_p = RR([sb([128, 512], F32, "ars") for _ in range(2)])
                qn_p = RR([sb([128, 512], F32, "aqn") for _ in range(2)])
                qb_p = RR([sb([128, 512], BF16, "aqb") for _ in range(2)])
                rot_p = RR([ps([128, 512], F32, "arot") for _ in range(2)])
                t1_p = RR([sb([128, 512], F32, "at1") for _ in range(2)])
                t2_p = RR([sb([128, 512], F32, "at2") for _ in range(2)])
                qo_p = RR([sb([128, 512], BF16, "aqo") for _ in range(3)])

                def epi_qkg(ci, c0, ncol, t0, nt, pt, r_pt):
                    if c0 >= 1536:
                        o, r_o_ = qo_p.next()
                        A(lambda e: e.activation(o[:, 0:nt], pt[:, 0:nt], AF.Silu), r=[r_pt], w=[r_o_])
                        cg = (c0 - 1536) // 128
                        S.dma("sp", sgT_d.ap()[cg * 128:(cg + 1) * 128, t0:t0 + nt], o[:, 0:nt], r=[r_o_], wa=[r_sg])
                        return
                    isq = c0 < 1024
                    wcol = 0 if isq else 1
                    sq, r_sq = sq_p.next()
                    A(lambda e: e.activation(sq[:, 0:nt], pt[:, 0:nt], AF.Square), r=[r_pt], w=[r_sq])
                    ss, r_ss = ss_p.next()
                    M(lambda e: e.matmul(ss[:, 0:nt], bones[:], sq[:, 0:nt], start=True, stop=True), r=[r_sq, r_cst], w=[r_ss])
                    rs, r_rs = rs_p.next()
                    A(lambda e: e.activation(rs[:, 0:nt], ss[:, 0:nt], AF.Sqrt, bias=EPS, scale=1.0 / 64), r=[r_ss], w=[r_rs])
                    V(lambda e: e.reciprocal(rs[:, 0:nt], rs[:, 0:nt]), w=[r_rs])
                    o, r_o_ = qo_p.next()
                    if t0 < NCTX:
                        V(lambda e: e.scalar_tensor_tensor(o[:, 0:nt], pt[:, 0:nt], qkw[:, wcol:wcol + 1], rs[:, 0:nt], ALU.mult, ALU.mult),
                          r=[r_pt, r_rs, r_cst], w=[r_o_])
                    else:
                        qn, r_qn = qn_p.next()
                        V(lambda e: e.scalar_tensor_tensor(qn[:, 0:nt], pt[:, 0:nt], qkw[:, wcol:wcol + 1], rs[:, 0:nt], ALU.mult, ALU.mult),
                          r=[r_pt, r_rs, r_cst], w=[r_qn])
                        qb, r_qb = qb_p.next()
                        A(lambda e: e.copy(qb[:, 0:nt], qn[:, 0:nt]), r=[r_qn], w=[r_qb])
                        rot, r_rot = rot_p.next()
                        M(lambda e: e.matmul(rot[:, 0:nt], permb[:], qb[:, 0:nt], start=True, stop=True), r=[r_qb, r_cst], w=[r_rot])
                        l0 = t0 - NCTX
                        t1, r_t1 = t1_p.next()
                        G(lambda e: e.tensor_tensor(t1[:, 0:nt], qn[:, 0:nt], rope[:, 0, l0:l0 + nt], ALU.mult), r=[r_qn, r_cst], w=[r_t1])
                        t2, r_t2 = t2_p.next()
                        V(lambda e: e.tensor_tensor(t2[:, 0:nt], rot[:, 0:nt], rope[:, 1, l0:l0 + nt], ALU.mult), r=[r_rot, r_cst], w=[r_t2])
                        V(lambda e: e.tensor_tensor(o[:, 0:nt], t1[:, 0:nt], t2[:, 0:nt], ALU.add), r=[r_t1, r_t2], w=[r_o_])
                    if isq:
                        S.dma("sp", qT_d.ap()[c0:c0 + 128, t0:t0 + nt], o[:, 0:nt], r=[r_o_], wa=[r_q])
                    else:
                        S.dma("sp", kT_d.ap()[c0 - 1024:c0 - 1024 + 128, t0:t0 + nt], o[:, 0:nt], r=[r_o_], wa=[r_k])

                cols = [(128 * i, 128) for i in range(10)] + [(1536 + 128 * i, 128) for i in range(8)]
                linear_fm(sb, ps, inT, r_inT, 8, lay["inw"].ap(), cols, epi_qkg)
                vo_p = RR([sb([128, 256], BF16, "avo") for _ in range(3)])

                def epi_v(g0, n, tt, pt, r_pt):
                    o, r_o_ = vo_p.next()
                    V(lambda e: e.tensor_copy(o[:, 0:n], pt[:, 0:n]), r=[r_pt], w=[r_o_])
                    S.dma("sp", v_tm.ap()[tt * 128:(tt + 1) * 128, :], o[:, 0:n], r=[r_o_], wa=[r_v])
                linear_tm(sb, ps, inT, r_inT, 8, lay["inw"].ap(), 1280, 256, epi_v)
                S.barrier()
        with ExitStack() as ph:
            sb, ps = mk(ph)
            onesf = sb([128, 64], F32, "aones")
            r_on = Res()
            G(lambda e: e.memset(onesf[:], 1.0), w=[r_on])
            Vg = sb([128, NTT, 65], BF16, "Vg")
            r_Vg = Res()
            G(lambda e: e.memset(Vg[:], 1.0), w=[r_Vg])
            kk_p = RR([sb([128, T], BF16, "kk") for _ in range(2)])
            qc_p = RR([sb([128, T], BF16, "qc") for _ in range(2)])
            sg_p = RR([sb([64, T], BF16, "sgh") for _ in range(2)])
            sp_p = RR([ps([128, 512], F32, "asp") for _ in range(4)])
            P_p = RR([sb([128, 512], BF16, "aP") for _ in range(4)])
            oa_p = RR([ps([128, 512], F32, "aoa") for _ in range(2)])
            bc_p = RR([ps([64, 512], F32, "abc") for _ in range(2)])
            osb_p = RR([sb([128, 512], F32, "aosb") for _ in range(2)])
            o1_p = RR([sb([64, 512], F32, "ao1") for _ in range(2)])
            og_p = RR([sb([64, 512], BF16, "aog") for _ in range(2)])
            for gk in range(4):
                kk, r_kk = kk_p.next()
                S.dma("sp", kk[0:64, :], kT_d.ap()[gk * 64:(gk + 1) * 64, :], r=[r_k], w=[r_kk])
                S.dma("sp", kk[64:128, :], kT_d.ap()[gk * 64:(gk + 1) * 64, :], r=[r_k], wa=[r_kk])
                S.dma("sp", Vg[:, :, 0:64], v_tm.ap()[:, gk * 64:(gk + 1) * 64].rearrange("(t p) d -> p t d", p=128), r=[r_v], w=[r_Vg])
                for qc in (2 * gk, 2 * gk + 1):
                    qt, r_qt = qc_p.next()
                    S.dma("sp", qt[:], qT_d.ap()[qc * 128:(qc + 1) * 128, :], r=[r_q], w=[r_qt])
                    for hh in range(2):
                        h = 2 * qc + hh
                        pr = slice(64 * hh, 64 * hh + 64)
                        sgh, r_sgh = sg_p.next()
                        S.dma("sp", sgh[:], sgT_d.ap()[h * 64:(h + 1) * 64, :], r=[r_sg], w=[r_sgh])
                        tasks = []
                        for (t0, nt) in BLKS:
                            ktiles = [0, 1] if t0 < NCTX else list(range(NTT))
                            for ki, kt in enumerate(ktiles):
                                tasks.append((t0, nt, ki, kt, len(ktiles)))
                        spq = {}
                        cur = {}
                        deferred = []

                        def emit_qk(ti):
                            t0, nt, ki, kt, nk = tasks[ti]
                            sp_, r_sp = sp_p.next()
                            M(lambda e: e.matmul(sp_[:, 0:nt], kk[pr, kt * 128:(kt + 1) * 128], qt[pr, t0:t0 + nt], start=True, stop=True),
                              r=[r_kk, r_qt], w=[r_sp])
                            spq[ti] = (sp_, r_sp)

                        def finalize_pe(args):
                            (t0, nt, osb, r_osb) = args
                            bc, r_bc = bc_p.next()
                            M(lambda e: e.matmul(bc[:, 0:nt], onesf[64:65, :], osb[64:65, 0:nt], start=True, stop=True), r=[r_osb, r_on], w=[r_bc])
                            o1, r_o1 = o1_p.next()
                            V(lambda e: e.tensor_tensor(o1[:, 0:nt], osb[0:64, 0:nt], bc[:, 0:nt], ALU.mult), r=[r_osb, r_bc], w=[r_o1])
                            og, r_og = og_p.next()
                            G(lambda e: e.tensor_tensor(og[:, 0:nt], o1[:, 0:nt], sgh[:, t0:t0 + nt], ALU.mult), r=[r_o1, r_sgh], w=[r_og])
                            S.dma("sp", oT_d.ap()[h * 64:(h + 1) * 64, t0:t0 + nt], og[:, 0:nt], r=[r_og], wa=[r_o])

                        LOOK = 2
                        for ti in range(min(LOOK, len(tasks))):
                            emit_qk(ti)
                        for ti in range(len(tasks)):
                            t0, nt, ki, kt, nk = tasks[ti]
                            sp_, r_sp = spq.pop(ti)
                            if ki == 0:
                                cur["oa"] = oa_p.next()
                            oa, r_oa = cur["oa"]
                            P, r_P = P_p.next()
                            A(lambda e: e.activation(P[:, 0:nt], sp_[:, 0:nt], AF.Exp, bias=-8.0, scale=0.125), r=[r_sp], w=[r_P])
                            M(lambda e: e.matmul(oa[0:65, 0:nt], Vg[:, kt, :], P[:, 0:nt], start=(ki == 0), stop=(ki == nk - 1)),
                              r=[r_Vg, r_P], w=[r_oa] if ki == 0 else [], wa=[r_oa] if ki else [], inc=(ki == nk - 1))
                            if ti + LOOK < len(tasks):
                                emit_qk(ti + LOOK)
                            deferred = [(n - 1, a) for (n, a) in deferred]
                            while deferred and deferred[0][0] <= 0:
                                finalize_pe(deferred.pop(0)[1])
                            if ki == nk - 1:
                                osb, r_osb = osb_p.next()
                                V(lambda e: e.tensor_copy(osb[0:65, 0:nt], oa[0:65, 0:nt]), r=[r_oa], w=[r_osb])
                                V(lambda e: e.reciprocal(osb[64:65, 0:nt], osb[64:65, 0:nt]), w=[r_osb])
                                deferred.append((4, (t0, nt, osb, r_osb)))
                        for (_, a) in deferred:
                            finalize_pe(a)
            S.barrier()
        with ExitStack() as ph:
            sb, ps = mk(ph)
            oT = sb([128, 8, T], BF16, "oT")
            r_oT = Res()
            S.dma("sp", oT[:], oT_d.ap().rearrange("(c p) t -> p c t", p=128), r=[r_o], w=[r_oT])
            linear_fm(sb, ps, oT, r_oT, 8, lay["outw"].ap(), [(128 * i, 128) for i in range(8)], make_resid_epi(sb))
            S.barrier()

    def s5_layer(lay):
        uT_d = dscr("uT_d", [D, T], BF16)
        szT_d = dscr("szT_d", [D, T], BF16)
        gT_d = dscr("gT_d", [D, T], BF16)
        y2T_d = dscr("y2T_d", [D, T], BF16)
        r_u, r_sz, r_g, r_y2 = Res(), Res(), Res(), Res()
        NLV = 1
        while (1 << (NLV - 1)) < T:
            NLV += 1
        with ExitStack() as st1:
            sb1, ps1 = mk(st1)
            inT = sb1([128, 8, T], BF16, "inT")
            r_inT = Res()
            with ExitStack() as ph:
                sb, ps = mk(ph)
                pre_pass(lay, sb, ps, inT, r_inT)
                S.barrier()
            with ExitStack() as ph:
                sb, ps = mk(ph)
                uo_p = RR([sb([128, 512], BF16, "suo") for _ in range(3)])

                def epi_uz(ci, c0, ncol, t0, nt, pt, r_pt):
                    o, r_o_ = uo_p.next()
                    if c0 < 1024:
                        V(lambda e: e.tensor_copy(o[:, 0:nt], pt[:, 0:nt]), r=[r_pt], w=[r_o_])
                        S.dma("sp", uT_d.ap()[c0:c0 + 128, t0:t0 + nt], o[:, 0:nt], r=[r_o_], wa=[r_u])
                    else:
                        A(lambda e: e.activation(o[:, 0:nt], pt[:, 0:nt], AF.Silu), r=[r_pt], w=[r_o_])
                        S.dma("sp", szT_d.ap()[c0 - 1024:c0 - 1024 + 128, t0:t0 + nt], o[:, 0:nt], r=[r_o_], wa=[r_sz])
                linear_fm(sb, ps, inT, r_inT, 8, lay["inw"].ap(), [(128 * i, 128) for i in range(16)], epi_uz)
                S.barrier()
        with ExitStack() as ph:
            sb, ps = mk(ph)
            lam = sb([128, 3, 64], F32, "lam")
            r_t = Res()
            S.dma("sp", lam[:], lay["lam"].ap(), w=[r_t])
            tb = sb([128, 16, 64], F32, "stb")
            tbi = sb([128, 64], I32, "stbi")
            coef = sb([128, 3, 64], F32, "coef")
            pw = sb([128, 64, NLV, 3], F32, "pw")
            lr, li, ls = lam[:, 0, :], lam[:, 1, :], lam[:, 2, :]
            X = lambda i: tb[:, i, :]

            def vt(fn):
                V(fn, w=[r_t])

            def at(fn):
                A(fn, w=[r_t])
            at(lambda e: e.activation(X(0), ls, AF.Exp))
            vt(lambda e: e.tensor_tensor(X(1), lr, X(0), ALU.mult))
            at(lambda e: e.activation(X(2), X(1), AF.Exp))
            vt(lambda e: e.tensor_tensor(X(3), li, X(0), ALU.mult))

            def sin_of(dst, src, shift):
                vt(lambda e: e.tensor_scalar(X(4), src, shift, 1.0 / (2 * PI), ALU.add, ALU.mult))
                vt(lambda e: e.tensor_copy(tbi[:], X(4)))
                vt(lambda e: e.tensor_copy(X(5), tbi[:]))
                vt(lambda e: e.tensor_scalar(X(4), src, shift, None, ALU.add))
                vt(lambda e: e.scalar_tensor_tensor(X(4), X(5), -2 * PI, X(4), ALU.mult, ALU.add))
                vt(lambda e: e.tensor_single_scalar(X(5), X(4), PI, ALU.is_gt))
                vt(lambda e: e.scalar_tensor_tensor(X(4), X(5), -2 * PI, X(4), ALU.mult, ALU.add))
                vt(lambda e: e.tensor_single_scalar(X(5), X(4), -PI, ALU.is_lt))
                vt(lambda e: e.scalar_tensor_tensor(X(4), X(5), 2 * PI, X(4), ALU.mult, ALU.add))
                at(lambda e: e.activation(dst, X(4), AF.Sin))
            sin_of(X(6), X(3), 0.0)
            sin_of(X(7), X(3), PI / 2)
            vt(lambda e: e.tensor_tensor(X(8), X(2), X(7), ALU.mult))
            vt(lambda e: e.tensor_tensor(X(9), X(2), X(6), ALU.mult))
            vt(lambda e: e.tensor_tensor(X(10), lr, lr, ALU.mult))
            vt(lambda e: e.tensor_tensor(X(11), li, li, ALU.mult))
            vt(lambda e: e.tensor_tensor(X(10), X(10), X(11), ALU.add))
            vt(lambda e: e.reciprocal(X(10), X(10)))
            vt(lambda e: e.tensor_scalar(X(11), X(8), -1.0, None, ALU.add))
            vt(lambda e: e.tensor_tensor(X(12), X(11), lr, ALU.mult))
            vt(lambda e: e.tensor_tensor(X(13), X(9), li, ALU.mult))
            vt(lambda e: e.tensor_tensor(X(12), X(12), X(13), ALU.add))
            vt(lambda e: e.tensor_tensor(coef[:, 0, :], X(12), X(10), ALU.mult))
            vt(lambda e: e.tensor_tensor(X(12), X(9), lr, ALU.mult))
            vt(lambda e: e.tensor_tensor(X(13), X(11), li, ALU.mult))
            vt(lambda e: e.tensor_tensor(X(12), X(12), X(13), ALU.subtract))
            vt(lambda e: e.tensor_tensor(coef[:, 1, :], X(12), X(10), ALU.mult))
            vt(lambda e: e.tensor_scalar(coef[:, 2, :], coef[:, 1, :], -1.0, None, ALU.mult))
            vt(lambda e: e.tensor_copy(pw[:, :, 0, 0], X(8)))
            vt(lambda e: e.tensor_copy(pw[:, :, 0, 1], X(9)))
            for lv in range(NLV):
                vt(lambda e: e.tensor_scalar(pw[:, :, lv, 2], pw[:, :, lv, 1], -1.0, None, ALU.mult))
                if lv + 1 < NLV:
                    vt(lambda e: e.tensor_tensor(X(12), pw[:, :, lv, 0], pw[:, :, lv, 0], ALU.mult))
                    vt(lambda e: e.tensor_tensor(X(13), pw[:, :, lv, 1], pw[:, :, lv, 1], ALU.mult))
                    vt(lambda e: e.tensor_tensor(pw[:, :, lv + 1, 0], X(12), X(13), ALU.subtract))
                    vt(lambda e: e.tensor_tensor(X(12), pw[:, :, lv, 0], pw[:, :, lv, 1], ALU.mult))
                    vt(lambda e: e.tensor_scalar(pw[:, :, lv + 1, 1], X(12), 2.0, None, ALU.mult))
            brt = sb([128, 2, 2, 32, 128], BF16, "brt")
            for q in range(2):
                for k in range(2):
                    for j8 in range(4):
                        S.dma("pool", brt[:, q, k, j8 * 8:(j8 + 1) * 8, :], lay["brt"].ap()[q, :, k, j8 * 8:(j8 + 1) * 8, :], wa=[r_t])
            crp = sb([128, 2, 2, 32, 32], BF16, "crp")
            S.dma("pool", crp[:, 0], lay["crp"].ap()[0], wa=[r_t])
            S.dma("pool", crp[:, 1], lay["crp"].ap()[1], wa=[r_t])
            at(lambda e: e.mul(crp[:, 1], crp[:, 1], -1.0))
            sd = sb([128, 8], F32, "sd")
            S.dma("sp", sd[:], lay["sd"].ap(), wa=[r_t])
            Xr = sb([128, T], F32, "Xr")
            Xi = sb([128, T], F32, "Xi")
            Yr = sb([128, T], F32, "Yr")
            Yi = sb([128, T], F32, "Yi")
            r_X, r_Yb, r_Yi = Res(), Res(), Res()
            xb = [[sb([128, T], BF16, "xb%d%d" % (k, q)) for q in range(2)] for k in range(2)]
            r_xb = [Res(), Res()]
            uc_p = RR([sb([128, T], BF16, "suc") for _ in range(2)])
            p12_p = RR([ps([128, 512], F32, "sp12") for _ in range(4)])
            tmp_p = RR([sb([128, 512], F32, "stmp") for _ in range(2)])
            yp_p = RR([ps([128, 512], F32, "syp") for _ in range(2)])
            yv = sb([128, T], F32, "yv")
            ga = Yr
            r_yv = Res()
            go_p = RR([sb([128, T], BF16, "sgo") for _ in range(1)])

            def sview(t, off, step, a0, cnt, mult):
                s0 = off + a0 * step
                st_ = mult * step
                return t[:, s0:s0 + (cnt - 1) * st_ + 1:st_]

            r_Xr, r_Xi, r_Yr, r_Yi = Res(), Res(), Res(), Res()
            r_or = [Res(), Res()]
            r_oi = [Res(), Res()]

            def scan(col, k):
                outr, outi = xb[k]
                rof = {id(Xr): r_Xr, id(Xi): r_Xi, id(Yr): r_Yr, id(Yi): r_Yi, id(outr): r_or[k], id(outi): r_oi[k]}

                def stt(o_t, o_ap, a_t, a_ap, sc, b_t, b_ap):
                    V(lambda e: e.scalar_tensor_tensor(o_ap, a_ap, sc, b_ap, ALU.mult, ALU.add),
                      r=[rof[id(a_t)], rof[id(b_t)], r_t, r_X, r_Yb], w=[rof[id(o_t)]])

                def cpy(o_t, o_ap, a_t, a_ap):
                    A(lambda e: e.copy(o_ap, a_ap), r=[rof[id(a_t)], r_X, r_Yb], w=[rof[id(o_t)]])

                def rec(tr, ti, off, step, n, lv, yoff, top):
                    if n == 1:
                        if top:
                            cpy(outr, outr[:, 0:1], tr, tr[:, off:off + 1])
                            cpy(outi, outi[:, 0:1], ti, ti[:, off:off + 1])
                        return
                    m = n // 2
                    ne = n - m
                    ar, ai, nai = pw[:, col, lv, 0:1], pw[:, col, lv, 1:2], pw[:, col, lv, 2:3]
                    Ev = lambda t, a0, cnt: sview(t, off, step, 2 * a0, cnt, 2)
                    Ov = lambda t, a0, cnt: sview(t, off, step, 2 * a0 + 1, cnt, 2)
                    yr, yi = Yr[:, yoff:yoff + m], Yi[:, yoff:yoff + m]
                    stt(Yr, yr, tr, Ev(tr, 0, m), ar, tr, Ov(tr, 0, m))
                    stt(Yi, yi, ti, Ev(ti, 0, m), ar, ti, Ov(ti, 0, m))
                    stt(Yr, yr, ti, Ev(ti, 0, m), nai, Yr, yr)
                    stt(Yi, yi, tr, Ev(tr, 0, m), ai, Yi, yi)
                    rec(Yr, Yi, yoff, 1, m, lv + 1, yoff + m, False)
                    ne1 = ne - 1
                    zr, zi = Yr[:, yoff:yoff + ne1], Yi[:, yoff:yoff + ne1]
                    if top:
                        cpy(outr, sview(outr, 0, 1, 1, m, 2), Yr, yr)
                        cpy(outi, sview(outi, 0, 1, 1, m, 2), Yi, yi)
                        cpy(outr, outr[:, 0:1], tr, tr[:, off:off + 1])
                        cpy(outi, outi[:, 0:1], ti, ti[:, off:off + 1])
                        if ne1 > 0:
                            stt(tr, Ev(tr, 1, ne1), Yr, zr, ar, tr, Ev(tr, 1, ne1))
                            stt(ti, Ev(ti, 1, ne1), Yi, zi, ar, ti, Ev(ti, 1, ne1))
                            V(lambda e: e.scalar_tensor_tensor(sview(outr, 0, 1, 2, ne1, 2), zi, nai, Ev(tr, 1, ne1), ALU.mult, ALU.add),
                              r=[r_Yi, r_Xr, r_t], w=[r_or[k]])
                            V(lambda e: e.scalar_tensor_tensor(sview(outi, 0, 1, 2, ne1, 2), zr, ai, Ev(ti, 1, ne1), ALU.mult, ALU.add),
                              r=[r_Yr, r_Xi, r_t], w=[r_oi[k]])
                    else:
                        cpy(tr, Ov(tr, 0, m), Yr, yr)
                        cpy(ti, Ov(ti, 0, m), Yi, yi)
                        if ne1 > 0:
                            stt(tr, Ev(tr, 1, ne1), Yr, zr, ar, tr, Ev(tr, 1, ne1))
                            stt(ti, Ev(ti, 1, ne1), Yi, zi, ar, ti, Ev(ti, 1, ne1))
                            stt(tr, Ev(tr, 1, ne1), Yi, zi, nai, tr, Ev(tr, 1, ne1))
                            stt(ti, Ev(ti, 1, ne1), Yr, zr, ai, ti, Ev(ti, 1, ne1))
                rec(Xr, Xi, 0, 1, T, 0, 0, True)

            def bwd_pos(t0, nt):
                return (NCTX - t0 - nt) if t0 < NCTX else (NCTX + T - t0 - nt)

            uc = None
            SK = os.environ.get("S5_SKIP", "")
            for j in range(32 if "L" not in SK else 0):
                cj, jm = j // 4, j % 4
                pr = slice(32 * jm, 32 * jm + 32)
                if jm == 0:
                    uc, r_uc = uc_p.next()
                    S.dma("sp", uc[:], uT_d.ap()[cj * 128:(cj + 1) * 128, :], r=[r_u], w=[r_uc])
                for k in range(2):
                    col = k * 32 + j
                    for (t0, nt) in BLKS:
                        if k == 0:
                            i0 = t0
                            uv = uc[:, t0:t0 + nt]
                        else:
                            i0 = bwd_pos(t0, nt)
                            uv = uc[:, t0:t0 + nt][:, ::-1]
                        if "m" in SK:
                            continue
                        p1, r_p1 = p12_p.next()
                        p2, r_p2 = p12_p.next()
                        M(lambda e: e.matmul(p1[:, 0:nt], brt[:, 0, k, j, :], uv, start=True, stop=True), r=[r_uc, r_t], w=[r_p1])
                        M(lambda e: e.matmul(p2[:, 0:nt], brt[:, 1, k, j, :], uv, start=True, stop=True), r=[r_uc, r_t], w=[r_p2])
                        if "e" in SK:
                            continue
                        tm, r_tm = tmp_p.next()
                        if "a" not in SK:
                            A(lambda e: e.activation(tm[:, 0:nt], p2[:, 0:nt], AF.Identity, scale=coef[:, 2, col:col + 1]), r=[r_t], w=[r_tm, r_p2])
                        if "v" not in SK:
                            V(lambda e: e.scalar_tensor_tensor(Xr[:, i0:i0 + nt], p1[:, 0:nt], coef[:, 0, col:col + 1], tm[:, 0:nt], ALU.mult, ALU.add),
                              r=[r_tm, r_t], w=[r_Xr, r_p1])
                        tm2, r_tm2 = tmp_p.next()
                        if "a" not in SK:
                            A(lambda e: e.activation(tm2[:, 0:nt], p1[:, 0:nt], AF.Identity, scale=coef[:, 1, col:col + 1]), r=[r_t], w=[r_tm2, r_p1])
                        if "v" not in SK:
                            V(lambda e: e.scalar_tensor_tensor(Xi[:, i0:i0 + nt], p2[:, 0:nt], coef[:, 0, col:col + 1], tm2[:, 0:nt], ALU.mult, ALU.add),
                              r=[r_tm2, r_t], w=[r_Xi, r_p2])
                    if "s" not in SK:
                        scan(col, k)
                for (t0, nt) in (BLKS if "r" not in SK else []):
                    yp, r_yp = yp_p.next()
                    i0 = bwd_pos(t0, nt)
                    rv = (lambda a: a[:, ::-1]) if not os.environ.get("S5_NOREV") else (lambda a: a)
                    ops = [(crp[:, 0, 0, j, :], xb[0][0][:, t0:t0 + nt], r_or[0]), (crp[:, 1, 0, j, :], xb[0][1][:, t0:t0 + nt], r_oi[0]),
                           (crp[:, 0, 1, j, :], rv(xb[1][0][:, i0:i0 + nt]), r_or[1]), (crp[:, 1, 1, j, :], rv(xb[1][1][:, i0:i0 + nt]), r_oi[1])]
                    for qi, (lh, rh, rr) in enumerate(ops):
                        M(lambda e: e.matmul(yp[pr, 0:nt], lh, rh, start=(qi == 0), stop=(qi == 3), tile_position=(0, 32 * jm)), r=[rr, r_t],
                          w=[r_yp] if qi == 0 else [], wa=[r_yp] if qi else [])
                    V(lambda e: e.scalar_tensor_tensor(yv[pr, t0:t0 + nt], uc[pr, t0:t0 + nt], sd[pr, cj:cj + 1], yp[pr, 0:nt], ALU.mult, ALU.add),
                      r=[r_yp, r_uc, r_t], w=[r_yv] if (jm == 0 and t0 == 0) else [], wa=[] if (jm == 0 and t0 == 0) else [r_yv])
                if jm == 3 and "g" not in SK:
                    A(lambda e: e.activation(ga[:], yv[:], AF.Square), r=[r_yv], w=[r_Yr])
                    V(lambda e: e.tensor_scalar(ga[:], ga[:], 0.044715, 1.0, ALU.mult, ALU.add), w=[r_Yr])
                    V(lambda e: e.tensor_tensor(ga[:], ga[:], yv[:], ALU.mult), r=[r_yv], w=[r_Yr])
                    A(lambda e: e.activation(ga[:], ga[:], AF.Sigmoid, scale=1.5957691216057308), w=[r_Yr])
                    go, r_go = go_p.next()
                    V(lambda e: e.tensor_tensor(go[:], ga[:], yv[:], ALU.mult), r=[r_Yr, r_yv], w=[r_go])
                    S.dma("sp", gT_d.ap()[cj * 128:(cj + 1) * 128, :], go[:], r=[r_go], wa=[r_g])
            S.barrier()
        with ExitStack() as ph:
            sb, ps = mk(ph)
            gT = sb([128, 8, T], BF16, "gT")
            r_gT = Res()
            S.dma("sp", gT[:], gT_d.ap().rearrange("(c p) t -> p c t", p=128), r=[r_g], w=[r_gT])
            glub = sb([128, 8], F32, "glub")
            r_gb = Res()
            S.dma("sp", glub[:], lay["glub"].ap(), w=[r_gb])
            sig_p = RR([sb([128, 512], F32, "ssig") for _ in range(2)])
            szt_p = RR([sb([128, 512], BF16, "sszt") for _ in range(2)])
            y2_p = RR([sb([128, 512], BF16, "sy2") for _ in range(3)])

            def epi_glu(ci, c0, ncol, t0, nt, pt, r_pt):
                sg, r_sg_ = sig_p.next()
                A(lambda e: e.activation(sg[:, 0:nt], pt[:, 0:nt], AF.Sigmoid, bias=glub[:, ci:ci + 1], scale=1.0), r=[r_pt, r_gb], w=[r_sg_])
                szt, r_szt = szt_p.next()
                S.dma("sp", szt[:, 0:nt], szT_d.ap()[c0:c0 + 128, t0:t0 + nt], r=[r_sz], w=[r_szt])
                V(lambda e: e.tensor_tensor(sg[:, 0:nt], sg[:, 0:nt], gT[:, ci, t0:t0 + nt], ALU.mult), r=[r_gT], w=[r_sg_])
                y2, r_y2_ = y2_p.next()
                V(lambda e: e.tensor_tensor(y2[:, 0:nt], sg[:, 0:nt], szt[:, 0:nt], ALU.mult), r=[r_sg_, r_szt], w=[r_y2_])
                S.dma("sp", y2T_d.ap()[c0:c0 + 128, t0:t0 + nt], y2[:, 0:nt], r=[r_y2_], wa=[r_y2])
            linear_fm(sb, ps, gT, r_gT, 8, lay["gluw"].ap(), [(128 * i, 128) for i in range(8)], epi_glu)
            S.barrier()
        with ExitStack() as ph:
            sb, ps = mk(ph)
            y2 = sb([128, 8, T], BF16, "y2r")
            r_y2r = Res()
            S.dma("sp", y2[:], y2T_d.ap().rearrange("(c p) t -> p c t", p=128), r=[r_y2], w=[r_y2r])
            linear_fm(sb, ps, y2, r_y2r, 8, lay["outw"].ap(), [(128 * i, 128) for i in range(8)], make_resid_epi(sb))
            S.barrier()

    for i in layers:
        kind = i % 3
        if kind == 0:
            mamba_layer(L[i])
        elif kind == 1:
            attn_layer(L[i])
        else:
            s5_layer(L[i])

    with ExitStack() as ph:
        sb, ps = mk(ph)
        pre_pass(None, sb, ps, None, None, final=True)
    S.barrier()
    es.close()
    nc._ninst = S.ninst
    return nc


def prep_inputs(inputs, b, nlat=NLAT, layers=(0, 1, 2, 3)):
    f = lambda a: np.ascontiguousarray(np.asarray(a, dtype=np.float32))
    chunked = lambda v, n: f(np.asarray(v, np.float32).reshape(n, 128).T)
    m = {}
    m["x"] = f(inputs["x"][b][:nlat])
    m["ctx"] = f(inputs["ctx"][b])
    m["cc"] = f(np.stack([chunked(inputs["c"][b], 8), chunked(inputs["c_ctx"], 8)], axis=-1))
    m["ident"] = np.eye(128, dtype=np.float32)
    m["fnw"] = chunked(inputs["final_norm_w"], 8)
    has_m = False
    for i in layers:
        m["normw%d" % i] = chunked(inputs["norm_w"][i], 8)
        m["modw%d" % i] = f(inputs["mod_w"][i])
        m["modb%d" % i] = chunked(inputs["mod_b"][i], 24)
        kind, j = i % 3, i // 3
        if kind == 0:
            has_m = True
            m["m_in_w%d" % j] = f(inputs["m_in_w"][j])
            cw = np.asarray(inputs["m_conv_w"][j], np.float32)
            m["m_convw%d" % j] = f(cw.reshape(5, 32, 128).transpose(2, 1, 0))
            m["m_convb%d" % j] = chunked(inputs["m_conv_b"][j], 32)
            m["m_alog%d" % j] = f(np.asarray(inputs["m_a_log"][j], np.float32).reshape(64, 1))
            m["m_dtb%d" % j] = f(np.asarray(inputs["m_dt_bias"][j], np.float32).reshape(64, 1))
            m["m_dvec%d" % j] = f(np.repeat(np.asarray(inputs["m_d"][j], np.float32), 64))
            m["m_normw%d" % j] = f(inputs["m_norm_w"][j])
            m["m_out_w%d" % j] = f(inputs["m_out_w"][j])
        elif kind == 1:
            m["a_in_w"] = f(inputs["a_in_w"][0])
            m["a_out_w"] = f(inputs["a_out_w"][0])
            m["a_qkw"] = f(np.stack([np.tile(np.asarray(inputs["a_q_norm"][0], np.float32), 2),
                                     np.tile(np.asarray(inputs["a_k_norm"][0], np.float32), 2)], axis=1))
            grid_w = 64
            pos = np.arange(nlat)
            r_idx, c_idx = (pos // grid_w).astype(np.float32), (pos % grid_w).astype(np.float32)
            inv = (10000.0 ** (-np.arange(0, 32, 2, dtype=np.float32) / 32)).astype(np.float32)
            dd = np.arange(128) % 64
            ax, part, ii = dd // 32, (dd % 32) // 16, dd % 16
            ang = np.where(ax[:, None] == 0, r_idx[None, :], c_idx[None, :]).astype(np.float32) * inv[ii][:, None]
            m["a_rope"] = f(np.stack([np.cos(ang), np.sin(ang)]))
            perm = np.zeros((128, 128), np.float32)
            for dcol in range(128):
                if part[dcol] == 0:
                    perm[dcol + 16, dcol] = -1.0
                else:
                    perm[dcol - 16, dcol] = 1.0
            m["a_perm"] = perm
            bo = np.zeros((128, 128), np.float32)
            bo[:64, :64] = 1.0
            bo[64:, 64:] = 1.0
            m["a_bones"] = bo
        else:
            m["s_in_w"] = f(inputs["s_in_w"][0])
            m["s_glu_w"] = f(inputs["s_glu_w"][0])
            m["s_out_w"] = f(inputs["s_out_w"][0])
            m["s_sd"] = chunked(inputs["s_d"][0], 8)
            m["s_glub"] = chunked(inputs["s_glu_b"][0], 8)
            lre = np.asarray(inputs["s_lambda_re"][0], np.float32)
            lim = np.asarray(inputs["s_lambda_im"][0], np.float32)
            lst = np.asarray(inputs["s_log_step"][0], np.float32)

            def pair_layout(a):
                a = a.reshape(2, 32, 2, 64)
                return a.transpose(2, 3, 0, 1).reshape(128, 64)
            lam = np.stack([pair_layout(lre), pair_layout(lim),
                            pair_layout(np.broadcast_to(lst[:, :, None], (2, 64, 64)))], axis=1)
            m["s_lam"] = f(lam)
            brt = np.zeros((2, 128, 2, 32, 128), np.float32)
            crp = np.zeros((2, 128, 2, 32, 32), np.float32)
            for q, (bsrc, csrc) in enumerate(((inputs["s_b_re"][0], inputs["s_c_re"][0]), (inputs["s_b_im"][0], inputs["s_c_im"][0]))):
                bsrc = np.asarray(bsrc, np.float32)
                csrc = np.asarray(csrc, np.float32)
                for k in range(2):
                    for j in range(32):
                        for gl in range(2):
                            g_ = 2 * j + gl
                            r0 = 32 * (j % 4) + 16 * gl
                            brt[q, r0:r0 + 16, k, j, gl * 64:(gl + 1) * 64] = bsrc[k, g_].T
                            crp[q, gl * 64:(gl + 1) * 64, k, j, 16 * gl:16 * gl + 16] = csrc[k, g_].T
            m["s_brt"] = brt
            m["s_crp"] = crp
    if has_m:
        up = np.triu(np.ones((128, 128), np.float32))
        m["masks"] = f(np.stack([up, up.T]))
    return m


ACTIVE_CORES = (0, 1, 4, 5)


def kernel(**inputs):
    nc = build_program()
    real = [prep_inputs(inputs, b) for b in range(4)]
    big = ("x", "ctx", "cc", "modw", "m_in_w", "m_out_w", "a_in_w", "a_out_w", "s_in_w", "s_glu_w", "s_out_w", "s_brt", "s_crp")
    idle = {k: (np.zeros_like(v) if k.startswith(big) else v) for k, v in real[0].items()}
    in_maps = [idle] * 8
    for b, core in enumerate(ACTIVE_CORES):
        in_maps[core] = real[b]
    res = run_bass_kernel_spmd(nc, in_maps, core_ids=list(range(8)))
    out = np.stack([np.asarray(res.results[core]["out"], dtype=np.float32) for core in ACTIVE_CORES], axis=0)
    return out
```

```python
import os
import numpy as np
from contextlib import ExitStack
import concourse.bass as bass
import concourse.mybir as mybir
from concourse.bass_utils import run_bass_kernel_spmd

F32 = mybir.dt.float32
BF16 = mybir.dt.bfloat16
I32 = mybir.dt.int32
AF = mybir.ActivationFunctionType
ALU = mybir.AluOpType
AX = mybir.AxisListType

D = 1024
NCTX = 256
NLAT = 4096
EPS = 1e-6
M_IN = 6208
PI = float(np.pi)


class Res:
    __slots__ = ("w", "r", "name")

    def __init__(self, name=""):
        self.w = {}
        self.r = {}
        self.name = name


class Sched:
    def __init__(self, nc, es):
        self.nc = nc
        self.eng = {"pe": nc.tensor, "act": nc.scalar, "dve": nc.vector, "pool": nc.gpsimd, "sp": nc.sync}
        self.sem = {}
        self.cnt = {}
        self.known = {e: {} for e in self.eng}
        for e in self.eng:
            self.sem[e] = es.enter_context(nc.semaphore("s_" + e))
            self.cnt[e] = 0
        self.NDS = 8
        self.dslot = {}
        for q in ("sp", "pool"):
            for i in range(self.NDS):
                k = "d_%s%d" % (q, i)
                self.sem[k] = es.enter_context(nc.semaphore(k))
                self.cnt[k] = 0
            self.dslot[q] = 0
        self.ninst = 0

    def _wait(self, e, evs):
        kn = self.known[e]
        for k, v in evs.items():
            if v <= 0 or (e == "pe" and k == "pe") or kn.get(k, 0) >= v:
                continue
            self.eng[e].wait_ge(self.sem[k], v)
            kn[k] = v

    @staticmethod
    def _deps(r, w, wa):
        evs = {}

        def add(d):
            for k, v in d.items():
                if evs.get(k, 0) < v:
                    evs[k] = v
        for x in r:
            add(x.w)
        for x in w:
            add(x.w)
            add(x.r)
        for x in wa:
            add(x.r)
        return evs

    @staticmethod
    def _commit(k, v, r, w, wa):
        for x in r:
            if x.r.get(k, 0) < v:
                x.r[k] = v
        for x in w:
            if x.w.get(k, 0) < v:
                x.w[k] = v
        for x in wa:
            if x.w.get(k, 0) < v:
                x.w[k] = v

    def op(self, e, fn, r=(), w=(), wa=(), inc=True):
        self._wait(e, self._deps(r, w, wa))
        ins = fn(self.eng[e])
        if inc:
            self.cnt[e] += 1
            ins.then_inc(self.sem[e], 1)
            self._commit(e, self.cnt[e], r, w, wa)
        else:
            self._commit(e, self.cnt[e] + 1, r, w, wa)
        self.ninst += 1
        return ins

    def dma(self, q, out, in_, r=(), w=(), wa=(), **kw):
        i = self.dslot[q]
        self.dslot[q] = (i + 1) % self.NDS
        k = "d_%s%d" % (q, i)
        evs = self._deps(r, w, wa)
        evs[k] = max(evs.get(k, 0), self.cnt[k])
        self._wait(q, evs)
        ins = self.eng[q].dma_start(out=out, in_=in_, **kw)
        self.cnt[k] += 16
        ins.then_inc(self.sem[k], 16)
        self._commit(k, self.cnt[k], r, w, wa)
        self.ninst += 1
        return ins

    def barrier(self):
        evs = {k: v for k, v in self.cnt.items() if v > 0}
        for e in self.eng:
            self._wait(e, dict(evs))


class RR:
    def __init__(self, tiles):
        self.t = tiles
        self.r = [Res() for _ in tiles]
        self.i = 0

    def next(self):
        i = self.i
        self.i = (i + 1) % len(self.t)
        return self.t[i], self.r[i]


def build_program(nlat=NLAT, layers=(0, 1, 2, 3)):
    T = NCTX + nlat
    NTT = T // 128
    BLKS = [(0, NCTX)] + [(NCTX + 512 * i, 512) for i in range(nlat // 512)]
    nc = bass.Bass("TRN2", target_bir_lowering=False)
    es = ExitStack()
    es.enter_context(nc.allow_low_precision("bf16 matmul operands, fp32 accumulation"))
    S = Sched(nc, es)
    uid = [0]

    def mk(stack):
        def sb(shape, dt=F32, name="t"):
            uid[0] += 1
            return stack.enter_context(nc.sbuf_tensor("%s_%d" % (name, uid[0]), list(shape), dt))

        def ps(shape, dt=F32, name="p"):
            uid[0] += 1
            return stack.enter_context(nc.psum_tensor("%s_%d" % (name, uid[0]), list(shape), dt))
        return sb, ps

    def din(name, shape, dt=F32):
        return nc.dram_tensor(name, list(shape), dt, kind="ExternalInput")

    def dscr(name, shape, dt=F32):
        return nc.dram_tensor(name, list(shape), dt)

    V = lambda fn, r=(), w=(), wa=(): S.op("dve", fn, r, w, wa)
    A = lambda fn, r=(), w=(), wa=(): S.op("act", fn, r, w, wa)
    G = lambda fn, r=(), w=(), wa=(): S.op("pool", fn, r, w, wa)
    M = lambda fn, r=(), w=(), wa=(), inc=True: S.op("pe", fn, r, w, wa, inc)

    x_in = din("x", [nlat, D])
    ctx_in = din("ctx", [NCTX, D])
    cc_in = din("cc", [128, 8, 2])
    ident_in = din("ident", [128, 128])
    fnw_in = din("fnw", [128, 8])
    out_t = nc.dram_tensor("out", [nlat, D], F32, kind="ExternalOutput")
    L = {}
    for i in layers:
        L[i] = dict(normw=din("normw%d" % i, [128, 8]), modw=din("modw%d" % i, [D, 3 * D]), modb=din("modb%d" % i, [128, 24]))
        kind, j = i % 3, i // 3
        if kind == 0:
            L[i].update(inw=din("m_in_w%d" % j, [D, M_IN]), convw=din("m_convw%d" % j, [128, 32, 5]), convb=din("m_convb%d" % j, [128, 32]),
                        alog=din("m_alog%d" % j, [64, 1]), dtb=din("m_dtb%d" % j, [64, 1]), dvec=din("m_dvec%d" % j, [2048]),
                        mnw=din("m_normw%d" % j, [2048]), outw=din("m_out_w%d" % j, [2048, D]))
        elif kind == 1:
            L[i].update(inw=din("a_in_w", [D, 2560]), qkw=din("a_qkw", [128, 2]), outw=din("a_out_w", [D, D]),
                        rope=din("a_rope", [2, 128, nlat]), perm=din("a_perm", [128, 128]), bones=din("a_bones", [128, 128]))
        else:
            L[i].update(inw=din("s_in_w", [D, 2048]), lam=din("s_lam", [128, 3, 64]), brt=din("s_brt", [2, 128, 2, 32, 128]),
                        crp=din("s_crp", [2, 128, 2, 32, 32]), sd=din("s_sd", [128, 8]), gluw=din("s_glu_w", [D, D]),
                        glub=din("s_glub", [128, 8]), outw=din("s_out_w", [D, D]))
    masks_in = din("masks", [2, 128, 128]) if any(i % 3 == 0 for i in layers) else None

    hT = dscr("hT", [D, T])
    hT_ap = hT.ap()
    r_hT = {(c, tt): Res() for c in range(8) for tt in range(NTT)}

    def hres(c, t0, nt):
        return [r_hT[(c, tt)] for tt in range(t0 // 128, (t0 + nt + 127) // 128)]

    def hres_all(t0, nt):
        out = []
        for c in range(8):
            out += hres(c, t0, nt)
        return out

    gsb, gps = mk(es)
    ident = gsb([128, 128], F32, "ident")
    r_const = Res("const")
    S.dma("sp", ident[:], ident_in.ap(), w=[r_const])
    identb = gsb([128, 128], BF16, "identb")
    ones_bf = gsb([128, 128], BF16, "ones")
    G(lambda e: e.memset(ones_bf[:], 1.0), wa=[r_const])
    V(lambda e: e.tensor_copy(identb[:], ident[:]), r=[r_const], wa=[r_const])
    fnw = gsb([128, 8], F32, "fnw")
    S.dma("sp", fnw[:], fnw_in.ap(), wa=[r_const])
    cc = gsb([128, 8, 2], F32, "cc")
    S.dma("sp", cc[:], cc_in.ap(), wa=[r_const])
    scs = gsb([128, 8, 2], F32, "scs")
    A(lambda e: e.activation(scs[:], cc[:], AF.Silu), r=[r_const], wa=[r_const])
    mod_sc = gsb([128, 8, 2], F32, "mod_sc")
    mod_bi = gsb([128, 8, 2], F32, "mod_bi")
    mod_gt = gsb([128, 8, 2], F32, "mod_gt")
    r_mod = Res("mod")
    S.barrier()

    with ExitStack() as ph:
        sb, ps = mk(ph)
        xin = RR([sb([128, D], F32, "xin") for _ in range(2)])
        tp = RR([ps([128, 512], F32, "tp") for _ in range(2)])
        xo = RR([sb([128, 8, 128], F32, "xo") for _ in range(2)])
        for tt in range(NTT):
            xt, r_xt = xin.next()
            src = ctx_in.ap()[tt * 128:(tt + 1) * 128, :] if tt < 2 else x_in.ap()[(tt - 2) * 128:(tt - 1) * 128, :]
            S.dma("sp", xt[:], src, w=[r_xt])
            ot, r_ot = xo.next()
            for half in range(2):
                pt, r_pt = tp.next()
                for j in range(4):
                    c = half * 4 + j
                    M(lambda e: e.transpose(pt[:, j * 128:(j + 1) * 128], xt[:, c * 128:(c + 1) * 128], ident[:]),
                      r=[r_xt, r_const], w=[r_pt] if j == 0 else [], wa=[r_pt] if j else [])
                dst = ot[:, half * 4:(half + 1) * 4, :]
                if half:
                    A(lambda e: e.copy(dst, pt[:].rearrange("p (j t) -> p j t", j=4)), r=[r_pt], wa=[r_ot])
                else:
                    V(lambda e: e.tensor_copy(dst, pt[:].rearrange("p (j t) -> p j t", j=4)), r=[r_pt], w=[r_ot])
            S.dma("pool", hT_ap[:, tt * 128:(tt + 1) * 128].rearrange("(c p) t -> p c t", p=128), ot[:], r=[r_ot],
                  wa=[r_hT[(c, tt)] for c in range(8)])
        S.barrier()

    def pre_pass(lay, sb, ps, inT, r_inT, final=False):
        if not final:
            mw = RR([sb([128, 8, 512], F32, "modw") for _ in range(2)])
            mp = RR([ps([128, 512], F32, "modp") for _ in range(2)])
            modT = sb([128, 24, 2], F32, "modT")
            r_modT = Res()
            modb = sb([128, 24], F32, "modb")
            normw = sb([128, 8], F32, "normw")
            r_small = Res()
            S.dma("sp", modb[:], lay["modb"].ap(), w=[r_small])
            S.dma("sp", normw[:], lay["normw"].ap(), wa=[r_small])
            for cg in range(6):
                wt, r_wt = mw.next()
                S.dma("sp", wt[:], lay["modw"].ap()[:, cg * 512:(cg + 1) * 512].rearrange("(k p) n -> p k n", p=128), w=[r_wt])
                for c4 in range(4):
                    pt, r_pt = mp.next()
                    for k in range(8):
                        M(lambda e: e.matmul(pt[:, 0:2], wt[:, k, c4 * 128:(c4 + 1) * 128], scs[:, k, :], start=(k == 0), stop=(k == 7)),
                          r=[r_wt, r_const], w=[r_pt] if k == 0 else [], wa=[r_pt] if k else [])
                    col = cg * 4 + c4
                    V(lambda e: e.tensor_scalar(modT[:, col, :], pt[:, 0:2], modb[:, col:col + 1], None, ALU.add),
                      r=[r_pt, r_small], wa=[r_modT])
            V(lambda e: e.tensor_scalar(mod_sc[:], modT[:, 8:16, :], 1.0, None, ALU.add), r=[r_modT], w=[r_mod])
            V(lambda e: e.tensor_tensor(mod_sc[:], mod_sc[:], normw[:].unsqueeze(2).broadcast_to([128, 8, 2]), ALU.mult), r=[r_small], w=[r_mod])
            V(lambda e: e.tensor_copy(mod_bi[:], modT[:, 0:8, :]), r=[r_modT], w=[r_mod])
            V(lambda e: e.tensor_copy(mod_gt[:], modT[:, 16:24, :]), r=[r_modT], w=[r_mod])
        hb = RR([sb([128, 8, 512], F32, "hb") for _ in range(2)])
        sq = RR([sb([128, 8, 512], BF16, "sq") for _ in range(2)])
        ssp = RR([ps([128, 512], F32, "ssp") for _ in range(2)])
        rstd = RR([sb([128, 512], F32, "rstd") for _ in range(2)])
        tmp = RR([sb([128, 512], F32, "ntmp") for _ in range(3)])
        if final:
            hn = RR([sb([128, 8, 512], F32, "hn") for _ in range(2)])
            tp = RR([ps([128, 512], F32, "ftp") for _ in range(2)])
            ot = RR([sb([128, D], F32, "fot") for _ in range(2)])
            r_out = Res()
        for (t0, nt) in BLKS:
            j = 1 if t0 < NCTX else 0
            if final and j == 1:
                continue
            h, r_h = hb.next()
            S.dma("sp", h[:, :, 0:nt], hT_ap[:, t0:t0 + nt].rearrange("(c p) t -> p c t", p=128), r=hres_all(t0, nt), w=[r_h])
            q, r_q = sq.next()
            A(lambda e: e.activation(q[:, :, 0:nt], h[:, :, 0:nt], AF.Square), r=[r_h], w=[r_q])
            sp_, r_sp = ssp.next()
            for c in range(8):
                M(lambda e: e.matmul(sp_[:, 0:nt], ones_bf[:], q[:, c, 0:nt], start=(c == 0), stop=(c == 7)),
                  r=[r_q, r_const], w=[r_sp] if c == 0 else [], wa=[r_sp] if c else [], inc=(c == 7))
            rs, r_rs = rstd.next()
            A(lambda e: e.activation(rs[:, 0:nt], sp_[:, 0:nt], AF.Sqrt, bias=EPS, scale=1.0 / D), r=[r_sp], w=[r_rs])
            V(lambda e: e.reciprocal(rs[:, 0:nt], rs[:, 0:nt]), w=[r_rs])
            if not final:
                for c in range(8):
                    tm, r_tm = tmp.next()
                    V(lambda e: e.tensor_tensor(tm[:, 0:nt], h[:, c, 0:nt], rs[:, 0:nt], ALU.mult), r=[r_h, r_rs], w=[r_tm])
                    A(lambda e: e.activation(inT[:, c, t0:t0 + nt], tm[:, 0:nt], AF.Identity, bias=mod_bi[:, c, j:j + 1], scale=mod_sc[:, c, j:j + 1]),
                      r=[r_tm, r_mod], wa=[r_inT])
            else:
                hn_, r_hn = hn.next()
                for c in range(8):
                    V(lambda e: e.scalar_tensor_tensor(hn_[:, c, 0:nt], h[:, c, 0:nt], fnw[:, c:c + 1], rs[:, 0:nt], ALU.mult, ALU.mult),
                      r=[r_h, r_rs, r_const], w=[r_hn] if c == 0 else [], wa=[r_hn] if c else [])
                for tl in range(nt // 128):
                    o, r_o = ot.next()
                    for half in range(2):
                        pt, r_pt = tp.next()
                        for jj in range(4):
                            c = half * 4 + jj
                            M(lambda e: e.transpose(pt[:, jj * 128:(jj + 1) * 128], hn_[:, c, tl * 128:(tl + 1) * 128], ident[:]),
                              r=[r_hn, r_const], w=[r_pt] if jj == 0 else [], wa=[r_pt] if jj else [])
                        if half:
                            A(lambda e: e.copy(o[:, 512:1024], pt[:]), r=[r_pt], wa=[r_o])
                        else:
                            V(lambda e: e.tensor_copy(o[:, 0:512], pt[:]), r=[r_pt], w=[r_o])
                    row = t0 - NCTX + tl * 128
                    S.dma("pool", out_t.ap()[row:row + 128, :], o[:], r=[r_o], wa=[r_out])
        if final:
            evs = dict(r_out.w)
            S._wait("sp", evs)

    def linear_fm(sb, ps, act, r_act, KC, W_ap, col_chunks, epi, blks=None, t_off=0):
        wts = RR([sb([128, KC, 128], BF16, "lw") for _ in range(3)])
        pts = RR([ps([128, 512], F32, "lp") for _ in range(2)])
        for ci, (c0, ncol) in enumerate(col_chunks):
            wt, r_wt = wts.next()
            S.dma("pool", wt[:, :, 0:ncol], W_ap[:, c0:c0 + ncol].rearrange("(k p) n -> p k n", p=128), w=[r_wt])
            for (t0, nt) in (blks or BLKS):
                pt, r_pt = pts.next()
                for k in range(KC):
                    M(lambda e: e.matmul(pt[0:ncol, 0:nt], wt[:, k, 0:ncol], act[:, k, t0 - t_off:t0 - t_off + nt], start=(k == 0), stop=(k == KC - 1)),
                      r=[r_wt, r_act], w=[r_pt] if k == 0 else [], wa=[r_pt] if k else [], inc=(k == KC - 1))
                epi(ci, c0, ncol, t0, nt, pt, r_pt)

    def linear_tm(sb, ps, act, r_act, KC, W_ap, c0, ncols, epi):
        wts = RR([sb([128, KC, 512], BF16, "lwt") for _ in range(2)])
        pts = RR([ps([128, 512], F32, "lpt") for _ in range(2)])
        for g0 in range(0, ncols, 512):
            n = min(512, ncols - g0)
            wt, r_wt = wts.next()
            S.dma("pool", wt[:, :, 0:n], W_ap[:, c0 + g0:c0 + g0 + n].rearrange("(k p) n -> p k n", p=128), w=[r_wt])
            for tt in range(NTT):
                pt, r_pt = pts.next()
                for k in range(KC):
                    M(lambda e: e.matmul(pt[:, 0:n], act[:, k, tt * 128:(tt + 1) * 128], wt[:, k, 0:n], start=(k == 0), stop=(k == KC - 1)),
                      r=[r_wt, r_act], w=[r_pt] if k == 0 else [], wa=[r_pt] if k else [], inc=(k == KC - 1))
                epi(g0, n, tt, pt, r_pt)

    def make_resid_epi(sb):
        hts = RR([sb([128, 512], F32, "rh") for _ in range(3)])

        def epi(ci, c0, ncol, t0, nt, pt, r_pt):
            c = c0 // 128
            j = 1 if t0 < NCTX else 0
            ht, r_ht = hts.next()
            S.dma("sp", ht[:, 0:nt], hT_ap[c * 128:(c + 1) * 128, t0:t0 + nt], r=hres(c, t0, nt), w=[r_ht])
            V(lambda e: e.scalar_tensor_tensor(ht[:, 0:nt], pt[:, 0:nt], mod_gt[:, c, j:j + 1], ht[:, 0:nt], ALU.mult, ALU.add),
              r=[r_pt, r_mod], w=[r_ht])
            S.dma("pool", hT_ap[c * 128:(c + 1) * 128, t0:t0 + nt], ht[:, 0:nt], r=[r_ht], wa=hres(c, t0, nt))
        return epi

    def mamba_layer(lay):
        x_tm = dscr("x_tm%d" % uid[0], [T, 2048], BF16)
        B_tm = dscr("B_tm%d" % uid[0], [T, 1024], BF16)
        BT_d = dscr("BT_d%d" % uid[0], [8, 128, T], BF16)
        CT_d = dscr("CT_d%d" % uid[0], [8, 128, T], BF16)
        sz_tm = dscr("sz_tm%d" % uid[0], [T, 2048], BF16)
        laT_d = dscr("laT_d%d" % uid[0], [64, T], F32)
        ltot_d = dscr("ltot_d%d" % uid[0], [NTT, 64], F32)
        Yacc = dscr("Yacc%d" % uid[0], [T, 2048], F32)
        uid[0] += 1
        r_xtm, r_Btm, r_BT, r_CT, r_sz, r_laT, r_ltot = Res(), Res(), Res(), Res(), Res(), Res(), Res()
        r_Y = [Res() for _ in range(NTT)]
        with ExitStack() as lst:
            lsb, lps = mk(lst)
            la_tm = lsb([128, NTT, 64], F32, "la_tm")
            dt_tm = lsb([128, NTT, 64], F32, "dt_tm")
            LTB = lsb([128, NTT, 64], F32, "LTB")
            r_tabs = Res()
            with ExitStack() as st1:
                sb1, ps1 = mk(st1)
                inT = sb1([128, 8, T], BF16, "inT")
                r_inT = Res()
                with ExitStack() as ph:
                    sb, ps = mk(ph)
                    pre_pass(lay, sb, ps, inT, r_inT)
                    S.barrier()
                with ExitStack() as ph:
                    sb, ps = mk(ph)
                    convw = sb([128, 32, 5], F32, "convw")
                    convb = sb([128, 32], F32, "convb")
                    r_cv = Res()
                    S.dma("sp", convw[:], lay["convw"].ap(), w=[r_cv])
                    S.dma("sp", convb[:], lay["convb"].ap(), wa=[r_cv])
                    xr = sb([128, T + 8], F32, "xr")
                    r_xr = Res()
                    G(lambda e: e.memset(xr[:], 0.0), w=[r_xr])
                    acc = sb([128, T], F32, "cacc")
                    r_acc = Res()
                    xo = RR([sb([128, T], BF16, "cxo") for _ in range(2)])
                    tps = RR([ps([128, 512], BF16, "ctp") for _ in range(2)])
                    tos = RR([sb([128, 512], BF16, "cto") for _ in range(3)])
                    state = {}

                    def epi_xbc(ci, c0, ncol, t0, nt, pt, r_pt):
                        off = 2 if t0 < NCTX else 6
                        if ci % 2:
                            A(lambda e: e.copy(xr[:, t0 + off:t0 + off + nt], pt[:, 0:nt]), r=[r_pt], wa=[r_xr])
                        else:
                            V(lambda e: e.tensor_copy(xr[:, t0 + off:t0 + off + nt], pt[:, 0:nt]), r=[r_pt], wa=[r_xr])
                        if t0 + nt < T:
                            return
                        segs = [(0, NCTX, 0), (NCTX, nlat, 4)]
                        for (s0, sn, dl) in segs:
                            A(lambda e: e.activation(acc[:, s0:s0 + sn], xr[:, s0 + dl:s0 + dl + sn], AF.Identity,
                                                     bias=convb[:, ci:ci + 1], scale=convw[:, ci, 0:1]), r=[r_xr, r_cv], wa=[r_acc])
                            for k in range(1, 5):
                                V(lambda e: e.scalar_tensor_tensor(acc[:, s0:s0 + sn], xr[:, s0 + dl + k:s0 + dl + k + sn], convw[:, ci, k:k + 1],
                                                                   acc[:, s0:s0 + sn], ALU.mult, ALU.add), r=[r_xr, r_cv], w=[r_acc])
                        o, r_o = xo.next()
                        A(lambda e: e.activation(o[:], acc[:], AF.Silu), r=[r_acc], w=[r_o])
                        if ci < 24:
                            dst, col0, r_d = (x_tm, ci * 128, r_xtm) if ci < 16 else (B_tm, (ci - 16) * 128, r_Btm)
                            for t4 in range(0, NTT, 4):
                                n4 = min(4, NTT - t4)
                                tp_, r_tp = tps.next()
                                for q in range(n4):
                                    M(lambda e: e.transpose(tp_[:, q * 128:(q + 1) * 128], o[:, (t4 + q) * 128:(t4 + q + 1) * 128], identb[:]),
                                      r=[r_o, r_const], w=[r_tp] if q == 0 else [], wa=[r_tp] if q else [])
                                to, r_to = tos.next()
                                A(lambda e: e.copy(to[:, 0:n4 * 128], tp_[:, 0:n4 * 128]), r=[r_tp], w=[r_to])
                                S.dma("sp", dst.ap()[t4 * 128:(t4 + n4) * 128, col0:col0 + 128].rearrange("(q p) c -> p q c", p=128),
                                      to[:, 0:n4 * 128].rearrange("p (q c) -> p q c", q=n4), r=[r_to], wa=[r_d])
                        if ci >= 16:
                            gg = (ci - 16) % 8
                            dd, r_dd = (BT_d, r_BT) if ci < 24 else (CT_d, r_CT)
                            S.dma("sp", dd.ap()[gg], o[:], r=[r_o], wa=[r_dd])

                    linear_fm(sb, ps, inT, r_inT, 8, lay["inw"].ap(), [(2048 + 128 * i, 128) for i in range(32)], epi_xbc)
                    S.barrier()
                with ExitStack() as ph:
                    sb, ps = mk(ph)
                    dtT = sb([64, NTT, 128], F32, "dtT")
                    dA = sb([64, NTT, 128], F32, "dA")
                    laP = sb([64, NTT, 128], F32, "laP")
                    laT = sb([64, NTT, 128], F32, "laT")
                    rp = sb([64, NTT, 128], F32, "rp")
                    r_dt, r_dA, r_laP, r_laTs, r_rp = Res(), Res(), Res(), Res(), Res()
                    sm = sb([64, 4], F32, "dtsm")
                    r_sm = Res()
                    S.dma("sp", sm[:, 0:1], lay["alog"].ap(), w=[r_sm])
                    S.dma("sp", sm[:, 1:2], lay["dtb"].ap(), wa=[r_sm])
                    A(lambda e: e.activation(sm[:, 2:3], sm[:, 0:1], AF.Exp), r=[r_sm], wa=[r_sm])
                    V(lambda e: e.tensor_scalar(sm[:, 3:4], sm[:, 2:3], -1.0, None, ALU.mult), r=[r_sm], wa=[r_sm])
                    G(lambda e: e.memset(rp[:], 1.0), w=[r_rp])
                    G(lambda e: e.memset(rp[:, :, 0:1], 0.0), w=[r_rp])
                    dtf = dtT[:].rearrange("p c l -> p (c l)")

                    def epi_dt(ci, c0, ncol, t0, nt, pt, r_pt):
                        A(lambda e: e.activation(dtf[:, t0:t0 + nt], pt[0:64, 0:nt], AF.Exp, bias=sm[:, 1:2], scale=1.0), r=[r_pt, r_sm], wa=[r_dt])
                    linear_fm(sb, ps, inT, r_inT, 8, lay["inw"].ap(), [(6144, 64)], epi_dt)
                    A(lambda e: e.activation(dtf, dtf, AF.Ln, bias=1.0, scale=1.0), w=[r_dt])
                    V(lambda e: e.tensor_scalar(dA[:], dtT[:], sm[:, 3:4], None, ALU.mult), r=[r_dt, r_sm], w=[r_dA])
                    V(lambda e: e.tensor_tensor_scan(laP[:].rearrange("p c l -> p (c l)"), rp[:].rearrange("p c l -> p (c l)"),
                                                     dA[:].rearrange("p c l -> p (c l)"), 0.0, ALU.mult, ALU.add), r=[r_rp, r_dA], w=[r_laP])
                    V(lambda e: e.tensor_copy(laT[0:32], laP[0:32]), r=[r_laP], w=[r_laTs])
                    V(lambda e: e.tensor_tensor(laT[32:64], dA[32:64], laP[32:64], ALU.subtract), r=[r_laP, r_dA], wa=[r_laTs])
                    V(lambda e: e.tensor_tensor(laT[32:64], laT[32:64], laP[32:64, :, 127:128].broadcast_to([32, NTT, 128]), ALU.add), r=[r_laP], w=[r_laTs])
                    S.dma("sp", laT_d.ap(), laT[:].rearrange("p c l -> p (c l)"), r=[r_laTs], w=[r_laT])
                    tpp = RR([ps([128, 64], F32, "dtp") for _ in range(2)])
                    for tt in range(NTT):
                        for (src, r_src, dst) in ((laT, r_laTs, la_tm), (dtT, r_dt, dt_tm)):
                            tp_, r_tp = tpp.next()
                            M(lambda e: e.transpose(tp_[:], src[:, tt, :], ident[0:64, 0:64]), r=[r_src, r_const], w=[r_tp])
                            V(lambda e: e.tensor_copy(dst[:, tt, :], tp_[:]), r=[r_tp], wa=[r_tabs])
                    S.dma("sp", ltot_d.ap()[:, 0:32], la_tm[127:128, :, 0:32], r=[r_tabs], w=[r_ltot])
                    S.dma("sp", ltot_d.ap()[:, 32:64], la_tm[0:1, :, 32:64], r=[r_tabs], wa=[r_ltot])
                    S.dma("sp", LTB[:].rearrange("p c h -> p (c h)"), bass.AP(ltot_d, 0, [[0, 128], [1, NTT * 64]]), r=[r_ltot], wa=[r_tabs])
                    S.barrier()
                with ExitStack() as ph:
                    sb, ps = mk(ph)
                    zo = RR([sb([128, 512], BF16, "zo") for _ in range(3)])

                    def epi_z(g0, n, tt, pt, r_pt):
                        o, r_o = zo.next()
                        A(lambda e: e.activation(o[:, 0:n], pt[:, 0:n], AF.Silu), r=[r_pt], w=[r_o])
                        S.dma("sp", sz_tm.ap()[tt * 128:(tt + 1) * 128, g0:g0 + n], o[:, 0:n], r=[r_o], wa=[r_sz])
                    linear_tm(sb, ps, inT, r_inT, 8, lay["inw"].ap(), 0, 2048, epi_z)
                    S.barrier()
            with ExitStack() as ph:
                sb, ps = mk(ph)
                masks = sb([128, 2, 128], F32, "masks")
                r_mk = Res()
                S.dma("sp", masks[:], masks_in.ap().rearrange("d s l -> s d l"), w=[r_mk])
                xt_p = RR([sb([128, 2048], BF16, "sx") for _ in range(2)])
                bt_p = RR([sb([128, 1024], BF16, "sB") for _ in range(2)])
                BTs_p = RR([sb([128, 8, 128], BF16, "sBT") for _ in range(2)])
                CTs_p = RR([sb([128, 8, 128], BF16, "sCT") for _ in range(2)])
                LaB_p = RR([sb([128, 32, 128], F32, "sLaB") for _ in range(2)])
                dmat = sb([128, 32, 128], F32, "dmat")
                r_dmat = Res()
                decay = sb([128, 32, 128], BF16, "decay")
                r_decay = Res()
                wT = sb([128, 32, 128], BF16, "wT")
                r_wT = Res()
                CBm = sb([128, 8, 128], BF16, "CBm")
                r_CBm = Res()
                xdt = sb([128, 2048], BF16, "xdt")
                r_xdt = Res()
                xw = sb([128, 2048], BF16, "xw")
                r_xw = Res()
                sml = sb([128, 4, 32], F32, "ssml")
                r_sml = Res()
                ST = sb([128, 2048], F32, "ST")
                r_ST = Res()
                prevb = sb([128, 2048], BF16, "prevb")
                r_prevb = Res()
                ysb = sb([128, 2048], F32, "ysb")
                r_ysb = Res()
                eyo = sb([128, 1024], F32, "eyo")
                r_eyo = Res()
                yin_p = RR([sb([128, 2048], F32, "yin") for _ in range(2)])
                cbp = ps([128, 8, 128], F32, "cbp")
                r_cbp = Res()
                ydp = ps([128, 1024], F32, "ydp")
                r_ydp = Res()
                yop = ps([128, 1024], F32, "yop")
                r_yop = Res()
                stp = ps([128, 1024], F32, "stp")
                r_stp = Res()
                for dr in range(2):
                    order = list(range(NTT)) if dr == 0 else [1, 0] + list(range(NTT - 1, 1, -1))
                    V(lambda e: e.memset(ST[:], 0.0), w=[r_ST])
                    hc = dr * 32
                    for c in order:
                        tok = slice(c * 128, (c + 1) * 128)
                        xt, r_xt = xt_p.next()
                        S.dma("sp", xt[:], x_tm.ap()[tok, :], r=[r_xtm], w=[r_xt])
                        bt, r_bt = bt_p.next()
                        S.dma("sp", bt[:], B_tm.ap()[tok, :], r=[r_Btm], w=[r_bt])
                        BTs, r_BTs = BTs_p.next()
                        S.dma("sp", BTs[:], BT_d.ap()[:, :, tok].rearrange("g n t -> n g t"), r=[r_BT], w=[r_BTs])
                        CTs, r_CTs = CTs_p.next()
                        S.dma("sp", CTs[:], CT_d.ap()[:, :, tok].rearrange("g n t -> n g t"), r=[r_CT], w=[r_CTs])
                        LaB, r_LaB = LaB_p.next()
                        S.dma("sp", LaB[:], bass.AP(laT_d, hc * T + c * 128, [[0, 128], [T, 32], [1, 128]]), r=[r_laT], w=[r_LaB])
                        la_c = la_tm[:, c, hc:hc + 32]
                        A(lambda e: e.activation(sml[:, 0, :], la_c, AF.Exp), r=[r_tabs], w=[r_sml])
                        V(lambda e: e.tensor_tensor(sml[:, 3, :], LTB[:, c, hc:hc + 32], la_c, ALU.subtract), r=[r_tabs], w=[r_sml])
                        V(lambda e: e.tensor_single_scalar(sml[:, 3, :], sml[:, 3, :], 0.0, ALU.min), w=[r_sml])
                        A(lambda e: e.activation(sml[:, 1, :], sml[:, 3, :], AF.Exp), w=[r_sml])
                        A(lambda e: e.activation(sml[:, 2, :], LTB[:, c, hc:hc + 32], AF.Exp), r=[r_tabs], w=[r_sml])
                        for g in range(8):
                            M(lambda e: e.matmul(cbp[:, g, :], BTs[:, g, :], CTs[:, g, :], start=True, stop=True), r=[r_BTs, r_CTs],
                              w=[r_cbp] if g == 0 else [], wa=[r_cbp] if g else [], inc=(g == 7))
                        V(lambda e: e.tensor_tensor(CBm[:], cbp[:], masks[:, dr:dr + 1, :].broadcast_to([128, 8, 128]), ALU.mult),
                          r=[r_cbp, r_mk], w=[r_CBm])
                        for h in range(32):
                            V(lambda e: e.tensor_scalar(dmat[:, h, :], LaB[:, h, :], la_tm[:, c, hc + h:hc + h + 1], 0.0, ALU.subtract, ALU.min),
                              r=[r_LaB, r_tabs], w=[r_dmat] if h == 0 else [], wa=[r_dmat] if h else [])
                        A(lambda e: e.activation(decay[:], dmat[:], AF.Exp), r=[r_dmat], w=[r_decay])
                        V(lambda e: e.tensor_tensor(wT[:].rearrange("p (g h) l -> p g h l", g=8), decay[:].rearrange("p (g h) l -> p g h l", g=8),
                                                    CBm[:].unsqueeze(2).broadcast_to([128, 8, 4, 128]), ALU.mult), r=[r_decay, r_CBm], w=[r_wT])
                        V(lambda e: e.tensor_tensor(xdt[:].rearrange("p (h q) -> p h q", h=32), xt[:].rearrange("p (h q) -> p h q", h=32),
                                                    dt_tm[:, c, hc:hc + 32].unsqueeze(2).broadcast_to([128, 32, 64]), ALU.mult), r=[r_xt, r_tabs], w=[r_xdt])
                        V(lambda e: e.tensor_tensor(xw[:].rearrange("p (h q) -> p h q", h=32), xdt[:].rearrange("p (h q) -> p h q", h=32),
                                                    sml[:, 1, :].unsqueeze(2).broadcast_to([128, 32, 64]), ALU.mult), r=[r_xdt, r_sml], w=[r_xw])
                        A(lambda e: e.copy(prevb[:], ST[:]), r=[r_ST], w=[r_prevb])
                        if dr == 1:
                            yin, r_yin = yin_p.next()
                            S.dma("sp", yin[:], Yacc.ap()[tok, :], r=[r_Y[c]], w=[r_yin])
                        for gh in range(2):
                            cs = slice(gh * 1024, (gh + 1) * 1024)
                            for hl in range(16):
                                h = gh * 16 + hl
                                M(lambda e: e.matmul(ydp[:, hl * 64:(hl + 1) * 64], wT[:, h, :], xdt[:, h * 64:(h + 1) * 64], start=True, stop=True),
                                  r=[r_wT, r_xdt], w=[r_ydp] if hl == 0 else [], wa=[r_ydp] if hl else [], inc=(hl == 15))
                            for gl in range(4):
                                g = gh * 4 + gl
                                M(lambda e: e.matmul(yop[:, gl * 256:(gl + 1) * 256], CTs[:, g, :], prevb[:, g * 256:(g + 1) * 256], start=True, stop=True),
                                  r=[r_CTs, r_prevb], w=[r_yop] if gl == 0 else [], wa=[r_yop] if gl else [], inc=(gl == 3))
                            for gl in range(4):
                                g = gh * 4 + gl
                                M(lambda e: e.matmul(stp[:, gl * 256:(gl + 1) * 256], bt[:, g * 128:(g + 1) * 128], xw[:, g * 256:(g + 1) * 256], start=True, stop=True),
                                  r=[r_bt, r_xw], w=[r_stp] if gl == 0 else [], wa=[r_stp] if gl else [], inc=(gl == 3))
                            if dr == 1:
                                V(lambda e: e.tensor_tensor(ysb[:, cs], ydp[:], yin[:, cs], ALU.add), r=[r_ydp, r_yin], w=[r_ysb] if gh == 0 else [], wa=[r_ysb] if gh else [])
                            else:
                                A(lambda e: e.copy(ysb[:, cs], ydp[:]), r=[r_ydp], w=[r_ysb] if gh == 0 else [], wa=[r_ysb] if gh else [])
                            for hl in range(16):
                                h = gh * 16 + hl
                                A(lambda e: e.activation(eyo[:, hl * 64:(hl + 1) * 64], yop[:, hl * 64:(hl + 1) * 64], AF.Identity, scale=sml[:, 0, h:h + 1]),
                                  r=[r_yop, r_sml], w=[r_eyo] if hl == 0 else [], wa=[r_eyo] if hl else [])
                            V(lambda e: e.tensor_tensor(ysb[:, cs], ysb[:, cs], eyo[:], ALU.add), r=[r_eyo], w=[r_ysb])
                            V(lambda e: e.tensor_tensor(ST[:, cs].rearrange("p (h q) -> p h q", h=16), ST[:, cs].rearrange("p (h q) -> p h q", h=16),
                                                        sml[:, 2, gh * 16:(gh + 1) * 16].unsqueeze(2).broadcast_to([128, 16, 64]), ALU.mult),
                              r=[r_sml, r_prevb], w=[r_ST])
                            V(lambda e: e.tensor_tensor(ST[:, cs], ST[:, cs], stp[:], ALU.add), r=[r_stp], w=[r_ST])
                        S.dma("pool", Yacc.ap()[tok, :], ysb[:], r=[r_ysb], w=[r_Y[c]])
                S.barrier()
            with ExitStack() as ph:
                sb, ps = mk(ph)
                dvec = sb([128, 2048], F32, "dvec")
                mnw = sb([128, 2048], F32, "mnw")
                r_dv = Res()
                S.dma("sp", dvec[:], bass.AP(lay["dvec"], 0, [[0, 128], [1, 2048]]), w=[r_dv])
                S.dma("sp", mnw[:], bass.AP(lay["mnw"], 0, [[0, 128], [1, 2048]]), wa=[r_dv])
                ow = sb([128, 16, D], BF16, "ow")
                r_ow = Res()
                for k4 in range(4):
                    S.dma("pool", ow[:, k4 * 4:(k4 + 1) * 4, :], lay["outw"].ap()[k4 * 512:(k4 + 1) * 512, :].rearrange("(k p) n -> p k n", p=128),
                          w=[r_ow] if k4 == 0 else [], wa=[r_ow] if k4 else [])
                y_p = RR([sb([128, 2048], F32, "ty") for _ in range(2)])
                x_p = RR([sb([128, 2048], BF16, "tx") for _ in range(2)])
                z_p = RR([sb([128, 2048], BF16, "tz") for _ in range(2)])
                g_p = RR([sb([128, 2048], F32, "tg") for _ in range(2)])
                gb_p = RR([sb([128, 2048], BF16, "tgb") for _ in range(2)])
                junk = sb([128, 2048], BF16, "tjunk")
                r_junk = Res()
                ss_p = RR([sb([128, 2], F32, "tss") for _ in range(2)])
                gT_p = RR([sb([128, 16, 512], BF16, "tgT") for _ in range(2)])
                tp_p = RR([ps([128, 512], BF16, "ttp") for _ in range(2)])
                op_p = RR([ps([128, 512], F32, "top") for _ in range(3)])
                ht_p = RR([sb([128, 8, 512], F32, "tht") for _ in range(2)])
                groups = [(0, 2)] + [(2 + 4 * i, 4) for i in range((NTT - 2) // 4)]
                for (tt0, ng) in groups:
                    j = 1 if tt0 < 2 else 0
                    t0, nt = tt0 * 128, ng * 128
                    gT, r_gT = gT_p.next()
                    for q4 in range(ng):
                        tt = tt0 + q4
                        tok = slice(tt * 128, (tt + 1) * 128)
                        y, r_y = y_p.next()
                        S.dma("sp", y[:], Yacc.ap()[tok, :], r=[r_Y[tt]], w=[r_y])
                        xt, r_xt = x_p.next()
                        S.dma("sp", xt[:], x_tm.ap()[tok, :], r=[r_xtm], w=[r_xt])
                        zt, r_zt = z_p.next()
                        S.dma("sp", zt[:], sz_tm.ap()[tok, :], r=[r_sz], w=[r_zt])
                        gt_, r_g = g_p.next()
                        V(lambda e: e.tensor_tensor(gt_[:], xt[:], dvec[:], ALU.mult), r=[r_xt, r_dv], w=[r_g])
                        V(lambda e: e.tensor_tensor(gt_[:], gt_[:], y[:], ALU.add), r=[r_y], w=[r_g])
                        V(lambda e: e.tensor_tensor(gt_[:], gt_[:], zt[:], ALU.mult), r=[r_zt], w=[r_g])
                        ss, r_ss = ss_p.next()
                        A(lambda e: e.activation(junk[:], gt_[:], AF.Square, accum_out=ss[:, 0:1]), r=[r_g], w=[r_junk, r_ss])
                        A(lambda e: e.activation(ss[:, 1:2], ss[:, 0:1], AF.Sqrt, bias=EPS, scale=1.0 / 2048), w=[r_ss])
                        V(lambda e: e.reciprocal(ss[:, 1:2], ss[:, 1:2]), w=[r_ss])
                        gb, r_gb = gb_p.next()
                        V(lambda e: e.scalar_tensor_tensor(gb[:], gt_[:], ss[:, 1:2], mnw[:], ALU.mult, ALU.mult), r=[r_g, r_ss, r_dv], w=[r_gb])
                        for k4 in range(4):
                            tp_, r_tp = tp_p.next()
                            for q in range(4):
                                k = k4 * 4 + q
                                M(lambda e: e.transpose(tp_[:, q * 128:(q + 1) * 128], gb[:, k * 128:(k + 1) * 128], identb[:]),
                                  r=[r_gb, r_const], w=[r_tp] if q == 0 else [], wa=[r_tp] if q else [], inc=(q == 3))
                            A(lambda e: e.copy(gT[:, k4 * 4:(k4 + 1) * 4, q4 * 128:(q4 + 1) * 128], tp_[:].rearrange("p (q t) -> p q t", q=4)), r=[r_tp],
                              w=[r_gT] if (k4 == 0 and q4 == 0) else [], wa=[] if (k4 == 0 and q4 == 0) else [r_gT])
                    ht, r_ht = ht_p.next()
                    hr = [r_hT[(c, tt)] for c in range(8) for tt in range(tt0, tt0 + ng)]
                    S.dma("sp", ht[:, :, 0:nt], hT_ap[:, t0:t0 + nt].rearrange("(c p) t -> p c t", p=128), r=hr, w=[r_ht])
                    for dc in range(8):
                        op, r_op = op_p.next()
                        for k in range(16):
                            M(lambda e: e.matmul(op[:, 0:nt], ow[:, k, dc * 128:(dc + 1) * 128], gT[:, k, 0:nt], start=(k == 0), stop=(k == 15)),
                              r=[r_ow, r_gT], w=[r_op] if k == 0 else [], wa=[r_op] if k else [], inc=(k == 15))
                        V(lambda e: e.scalar_tensor_tensor(ht[:, dc, 0:nt], op[:, 0:nt], mod_gt[:, dc, j:j + 1], ht[:, dc, 0:nt], ALU.mult, ALU.add),
                          r=[r_op, r_mod], w=[r_ht])
                    S.dma("pool", hT_ap[:, t0:t0 + nt].rearrange("(c p) t -> p c t", p=128), ht[:, :, 0:nt], r=[r_ht], wa=hr)
                S.barrier()

    def attn_layer(lay):
        qT_d = dscr("qT_d", [D, T], BF16)
        kT_d = dscr("kT_d", [256, T], BF16)
        v_tm = dscr("v_tm", [T, 256], BF16)
        sgT_d = dscr("sgT_d", [D, T], BF16)
        oT_d = dscr("oT_d", [D, T], BF16)
        r_q, r_k, r_v, r_sg, r_o = Res(), Res(), Res(), Res(), Res()
        with ExitStack() as st1:
            sb1, ps1 = mk(st1)
            inT = sb1([128, 8, T], BF16, "inT")
            r_inT = Res()
            with ExitStack() as ph:
                sb, ps = mk(ph)
                pre_pass(lay, sb, ps, inT, r_inT)
                S.barrier()
            with ExitStack() as ph:
                sb, ps = mk(ph)
                rope = sb([128, 2, nlat], F32, "rope")
                r_cst = Res()
                S.dma("sp", rope[:], lay["rope"].ap().rearrange("a p t -> p a t"), w=[r_cst])
                qkw = sb([128, 2], F32, "qkw")
                S.dma("sp", qkw[:], lay["qkw"].ap(), wa=[r_cst])
                permb = sb([128, 128], BF16, "permb")
                S.dma("pool", permb[:], lay["perm"].ap(), wa=[r_cst])
                bones = sb([128, 128], BF16, "bones")
                S.dma("pool", bones[:], lay["bones"].ap(), wa=[r_cst])
                sq_p = RR([sb([128, 512], BF16, "asq") for _ in range(2)])
                ss_p = RR([ps([128, 512], F32, "ass") for _ in range(2)])
                rs_p = RR([sb([128, 512], F32, "ars") for _ in range(2)])
                qn_p = RR([sb([128, 512], F32, "aqn") for _ in range(2)])
                qb_p = RR([sb([128, 512], BF16, "aqb") for _ in range(2)])
                rot_p = RR([ps([128, 512], F32, "arot") for _ in range(2)])
                t1_p = RR([sb([128, 512], F32, "at1") for _ in range(2)])
                t2_p = RR([sb([128, 512], F32, "at2") for _ in range(2)])
                qo_p = RR([sb([128, 512], BF16, "aqo") for _ in range(3)])

                def epi_qkg(ci, c0, ncol, t0, nt, pt, r_pt):
                    if c0 >= 1536:
                        o, r_o_ = qo_p.next()
                        A(lambda e: e.activation(o[:, 0:nt], pt[:, 0:nt], AF.Silu), r=[r_pt], w=[r_o_])
                        cg = (c0 - 1536) // 128
                        S.dma("sp", sgT_d.ap()[cg * 128:(cg + 1) * 128, t0:t0 + nt], o[:, 0:nt], r=[r_o_], wa=[r_sg])
                        return
                    isq = c0 < 1024
                    wcol = 0 if isq else 1
                    sq, r_sq = sq_p.next()
                    A(lambda e: e.activation(sq[:, 0:nt], pt[:, 0:nt], AF.Square), r=[r_pt], w=[r_sq])
                    ss, r_ss = ss_p.next()
                    M(lambda e: e.matmul(ss[:, 0:nt], bones[:], sq[:, 0:nt], start=True, stop=True), r=[r_sq, r_cst], w=[r_ss])
                    rs, r_rs = rs_p.next()
                    A(lambda e: e.activation(rs[:, 0:nt], ss[:, 0:nt], AF.Sqrt, bias=EPS, scale=1.0 / 64), r=[r_ss], w=[r_rs])
                    V(lambda e: e.reciprocal(rs[:, 0:nt], rs[:, 0:nt]), w=[r_rs])
                    o, r_o_ = qo_p.next()
                    if t0 < NCTX:
                        V(lambda e: e.scalar_tensor_tensor(o[:, 0:nt], pt[:, 0:nt], qkw[:, wcol:wcol + 1], rs[:, 0:nt], ALU.mult, ALU.mult),
                          r=[r_pt, r_rs, r_cst], w=[r_o_])
                    else:
                        qn, r_qn = qn_p.next()
                        V(lambda e: e.scalar_tensor_tensor(qn[:, 0:nt], pt[:, 0:nt], qkw[:, wcol:wcol + 1], rs[:, 0:nt], ALU.mult, ALU.mult),
                          r=[r_pt, r_rs, r_cst], w=[r_qn])
                        qb, r_qb = qb_p.next()
                        A(lambda e: e.copy(qb[:, 0:nt], qn[:, 0:nt]), r=[r_qn], w=[r_qb])
                        rot, r_rot = rot_p.next()
                        M(lambda e: e.matmul(rot[:, 0:nt], permb[:], qb[:, 0:nt], start=True, stop=True), r=[r_qb, r_cst], w=[r_rot])
                        l0 = t0 - NCTX
                        t1, r_t1 = t1_p.next()
                        G(lambda e: e.tensor_tensor(t1[:, 0:nt], qn[:, 0:nt], rope[:, 0, l0:l0 + nt], ALU.mult), r=[r_qn, r_cst], w=[r_t1])
                        t2, r_t2 = t2_p.next()
                        V(lambda e: e.tensor_tensor(t2[:, 0:nt], rot[:, 0:nt], rope[:, 1, l0:l0 + nt], ALU.mult), r=[r_rot, r_cst], w=[r_t2])
                        V(lambda e: e.tensor_tensor(o[:, 0:nt], t1[:, 0:nt], t2[:, 0:nt], ALU.add), r=[r_t1, r_t2], w=[r_o_])
                    if isq:
                        S.dma("sp", qT_d.ap()[c0:c0 + 128, t0:t0 + nt], o[:, 0:nt], r=[r_o_], wa=[r_q])
                    else:
                        S.dma("sp", kT_d.ap()[c0 - 1024:c0 - 1024 + 128, t0:t0 + nt], o[:, 0:nt], r=[r_o_], wa=[r_k])

                cols = [(128 * i, 128) for i in range(10)] + [(1536 + 128 * i, 128) for i in range(8)]
                linear_fm(sb, ps, inT, r_inT, 8, lay["inw"].ap(), cols, epi_qkg)
                vo_p = RR([sb([128, 256], BF16, "avo") for _ in range(3)])

                def epi_v(g0, n, tt, pt, r_pt):
                    o, r_o_ = vo_p.next()
                    V(lambda e: e.tensor_copy(o[:, 0:n], pt[:, 0:n]), r=[r_pt], w=[r_o_])
                    S.dma("sp", v_tm.ap()[tt * 128:(tt + 1) * 128, :], o[:, 0:n], r=[r_o_], wa=[r_v])
                linear_tm(sb, ps, inT, r_inT, 8, lay["inw"].ap(), 1280, 256, epi_v)
                S.barrier()
        with ExitStack() as ph:
            sb, ps = mk(ph)
            onesf = sb([128, 64], F32, "aones")
            r_on = Res()
            G(lambda e: e.memset(onesf[:], 1.0), w=[r_on])
            Vg = sb([128, NTT, 65], BF16, "Vg")
            r_Vg = Res()
            G(lambda e: e.memset(Vg[:], 1.0), w=[r_Vg])
            kk_p = RR([sb([128, T], BF16, "kk") for _ in range(2)])
            qc_p = RR([sb([128, T], BF16, "qc") for _ in range(2)])
            sg_p = RR([sb([64, T], BF16, "sgh") for _ in range(2)])
            sp_p = RR([ps([128, 512], F32, "asp") for _ in range(4)])
            P_p = RR([sb([128, 512], BF16, "aP") for _ in range(4)])
            oa_p = RR([ps([128, 512], F32, "aoa") for _ in range(2)])
            bc_p = RR([ps([64, 512], F32, "abc") for _ in range(2)])
            osb_p = RR([sb([128, 512], F32, "aosb") for _ in range(2)])
            o1_p = RR([sb([64, 512], F32, "ao1") for _ in range(2)])
            og_p = RR([sb([64, 512], BF16, "aog") for _ in range(2)])
            for gk in range(4):
                kk, r_kk = kk_p.next()
                S.dma("sp", kk[0:64, :], kT_d.ap()[gk * 64:(gk + 1) * 64, :], r=[r_k], w=[r_kk])
                S.dma("sp", kk[64:128, :], kT_d.ap()[gk * 64:(gk + 1) * 64, :], r=[r_k], wa=[r_kk])
                S.dma("sp", Vg[:, :, 0:64], v_tm.ap()[:, gk * 64:(gk + 1) * 64].rearrange("(t p) d -> p t d", p=128), r=[r_v], w=[r_Vg])
                for qc in (2 * gk, 2 * gk + 1):
                    qt, r_qt = qc_p.next()
                    S.dma("sp", qt[:], qT_d.ap()[qc * 128:(qc + 1) * 128, :], r=[r_q], w=[r_qt])
                    for hh in range(2):
                        h = 2 * qc + hh
                        pr = slice(64 * hh, 64 * hh + 64)
                        sgh, r_sgh = sg_p.next()
                        S.dma("sp", sgh[:], sgT_d.ap()[h * 64:(h + 1) * 64, :], r=[r_sg], w=[r_sgh])
                        tasks = []
                        for (t0, nt) in BLKS:
                            ktiles = [0, 1] if t0 < NCTX else list(range(NTT))
                            for ki, kt in enumerate(ktiles):
                                tasks.append((t0, nt, ki, kt, len(ktiles)))
                        spq = {}
                        cur = {}
                        deferred = []

                        def emit_qk(ti):
                            t0, nt, ki, kt, nk = tasks[ti]
                            sp_, r_sp = sp_p.next()
                            M(lambda e: e.matmul(sp_[:, 0:nt], kk[pr, kt * 128:(kt + 1) * 128], qt[pr, t0:t0 + nt], start=True, stop=True),
                              r=[r_kk, r_qt], w=[r_sp])
                            spq[ti] = (sp_, r_sp)

                        def finalize_pe(args):
                            (t0, nt, osb, r_osb) = args
                            bc, r_bc = bc_p.next()
                            M(lambda e: e.matmul(bc[:, 0:nt], onesf[64:65, :], osb[64:65, 0:nt], start=True, stop=True), r=[r_osb, r_on], w=[r_bc])
                            o1, r_o1 = o1_p.next()
                            V(lambda e: e.tensor_tensor(o1[:, 0:nt], osb[0:64, 0:nt], bc[:, 0:nt], ALU.mult), r=[r_osb, r_bc], w=[r_o1])
                            og, r_og = og_p.next()
                            G(lambda e: e.tensor_tensor(og[:, 0:nt], o1[:, 0:nt], sgh[:, t0:t0 + nt], ALU.mult), r=[r_o1, r_sgh], w=[r_og])
                            S.dma("sp", oT_d.ap()[h * 64:(h + 1) * 64, t0:t0 + nt], og[:, 0:nt], r=[r_og], wa=[r_o])

                        LOOK = 2
                        for ti in range(min(LOOK, len(tasks))):
                            emit_qk(ti)
                        for ti in range(len(tasks)):
                            t0, nt, ki, kt, nk = tasks[ti]
                            sp_, r_sp = spq.pop(ti)
                            if ki == 0:
                                cur["oa"] = oa_p.next()
                            oa, r_oa = cur["oa"]
                            P, r_P = P_p.next()
                            A(lambda e: e.activation(P[:, 0:nt], sp_[:, 0:nt], AF.Exp, bias=-8.0, scale=0.125), r=[r_sp], w=[r_P])
                            M(lambda e: e.matmul(oa[0:65, 0:nt], Vg[:, kt, :], P[:, 0:nt], start=(ki == 0), stop=(ki == nk - 1)),
                              r=[r_Vg, r_P], w=[r_oa] if ki == 0 else [], wa=[r_oa] if ki else [], inc=(ki == nk - 1))
                            if ti + LOOK < len(tasks):
                                emit_qk(ti + LOOK)
                            deferred = [(n - 1, a) for (n, a) in deferred]
                            while deferred and deferred[0][0] <= 0:
                                finalize_pe(deferred.pop(0)[1])
                            if ki == nk - 1:
                                osb, r_osb = osb_p.next()
                                V(lambda e: e.tensor_copy(osb[0:65, 0:nt], oa[0:65, 0:nt]), r=[r_oa], w=[r_osb])
                                V(lambda e: e.reciprocal(osb[64:65, 0:nt], osb[64:65, 0:nt]), w=[r_osb])
                                deferred.append((4, (t0, nt, osb, r_osb)))
                        for (_, a) in deferred:
                            finalize_pe(a)
            S.barrier()
        with ExitStack() as ph:
            sb, ps = mk(ph)
            oT = sb([128, 8, T], BF16, "oT")
            r_oT = Res()
            S.dma("sp", oT[:], oT_d.ap().rearrange("(c p) t -> p c t", p=128), r=[r_o], w=[r_oT])
            linear_fm(sb, ps, oT, r_oT, 8, lay["outw"].ap(), [(128 * i, 128) for i in range(8)], make_resid_epi(sb))
            S.barrier()

    def s5_layer(lay):
        uT_d = dscr("uT_d", [D, T], BF16)
        szT_d = dscr("szT_d", [D, T], BF16)
        gT_d = dscr("gT_d", [D, T], BF16)
        y2T_d = dscr("y2T_d", [D, T], BF16)
        r_u, r_sz, r_g, r_y2 = Res(), Res(), Res(), Res()
        NLV = 1
        while (1 << (NLV - 1)) < T:
            NLV += 1
        with ExitStack() as st1:
            sb1, ps1 = mk(st1)
            inT = sb1([128, 8, T], BF16, "inT")
            r_inT = Res()
            with ExitStack() as ph:
                sb, ps = mk(ph)
                pre_pass(lay, sb, ps, inT, r_inT)
                S.barrier()
            with ExitStack() as ph:
                sb, ps = mk(ph)
                uo_p = RR([sb([128, 512], BF16, "suo") for _ in range(3)])

                def epi_uz(ci, c0, ncol, t0, nt, pt, r_pt):
                    o, r_o_ = uo_p.next()
                    if c0 < 1024:
                        V(lambda e: e.tensor_copy(o[:, 0:nt], pt[:, 0:nt]), r=[r_pt], w=[r_o_])
                        S.dma("sp", uT_d.ap()[c0:c0 + 128, t0:t0 + nt], o[:, 0:nt], r=[r_o_], wa=[r_u])
                    else:
                        A(lambda e: e.activation(o[:, 0:nt], pt[:, 0:nt], AF.Silu), r=[r_pt], w=[r_o_])
                        S.dma("sp", szT_d.ap()[c0 - 1024:c0 - 1024 + 128, t0:t0 + nt], o[:, 0:nt], r=[r_o_], wa=[r_sz])
                linear_fm(sb, ps, inT, r_inT, 8, lay["inw"].ap(), [(128 * i, 128) for i in range(16)], epi_uz)
                S.barrier()
        with ExitStack() as ph:
            sb, ps = mk(ph)
            lam = sb([128, 3, 64], F32, "lam")
            r_t = Res()
            S.dma("sp", lam[:], lay["lam"].ap(), w=[r_t])
            tb = sb([128, 16, 64], F32, "stb")
            tbi = sb([128, 64], I32, "stbi")
            coef = sb([128, 3, 64], F32, "coef")
            pw = sb([128, 64, NLV, 3], F32, "pw")
            lr, li, ls = lam[:, 0, :], lam[:, 1, :], lam[:, 2, :]
            X = lambda i: tb[:, i, :]

            def vt(fn):
                V(fn, w=[r_t])

            def at(fn):
                A(fn, w=[r_t])
            at(lambda e: e.activation(X(0), ls, AF.Exp))
            vt(lambda e: e.tensor_tensor(X(1), lr, X(0), ALU.mult))
            at(lambda e: e.activation(X(2), X(1), AF.Exp))
            vt(lambda e: e.tensor_tensor(X(3), li, X(0), ALU.mult))

            def sin_of(dst, src, shift):
                vt(lambda e: e.tensor_scalar(X(4), src, shift, 1.0 / (2 * PI), ALU.add, ALU.mult))
                vt(lambda e: e.tensor_copy(tbi[:], X(4)))
                vt(lambda e: e.tensor_copy(X(5), tbi[:]))
                vt(lambda e: e.tensor_scalar(X(4), src, shift, None, ALU.add))
                vt(lambda e: e.scalar_tensor_tensor(X(4), X(5), -2 * PI, X(4), ALU.mult, ALU.add))
                vt(lambda e: e.tensor_single_scalar(X(5), X(4), PI, ALU.is_gt))
                vt(lambda e: e.scalar_tensor_tensor(X(4), X(5), -2 * PI, X(4), ALU.mult, ALU.add))
                vt(lambda e: e.tensor_single_scalar(X(5), X(4), -PI, ALU.is_lt))
                vt(lambda e: e.scalar_tensor_tensor(X(4), X(5), 2 * PI, X(4), ALU.mult, ALU.add))
                at(lambda e: e.activation(dst, X(4), AF.Sin))
            sin_of(X(6), X(3), 0.0)
            sin_of(X(7), X(3), PI / 2)
            vt(lambda e: e.tensor_tensor(X(8), X(2), X(7), ALU.mult))
            vt(lambda e: e.tensor_tensor(X(9), X(2), X(6), ALU.mult))
            vt(lambda e: e.tensor_tensor(X(10), lr, lr, ALU.mult))
            vt(lambda e: e.tensor_tensor(X(11), li, li, ALU.mult))
            vt(lambda e: e.tensor_tensor(X(10), X(10), X(11), ALU.add))
            vt(lambda e: e.reciprocal(X(10), X(10)))
            vt(lambda e: e.tensor_scalar(X(11), X(8), -1.0, None, ALU.add))
            vt(lambda e: e.tensor_tensor(X(12), X(11), lr, ALU.mult))
            vt(lambda e: e.tensor_tensor(X(13), X(9), li, ALU.mult))
            vt(lambda e: e.tensor_tensor(X(12), X(12), X(13), ALU.add))
            vt(lambda e: e.tensor_tensor(coef[:, 0, :], X(12), X(10), ALU.mult))
            vt(lambda e: e.tensor_tensor(X(12), X(9), lr, ALU.mult))
            vt(lambda e: e.tensor_tensor(X(13), X(11), li, ALU.mult))
            vt(lambda e: e.tensor_tensor(X(12), X(12), X(13), ALU.subtract))
            vt(lambda e: e.tensor_tensor(coef[:, 1, :], X(12), X(10), ALU.mult))
            vt(lambda e: e.tensor_scalar(coef[:, 2, :], coef[:, 1, :], -1.0, None, ALU.mult))
            vt(lambda e: e.tensor_copy(pw[:, :, 0, 0], X(8)))
            vt(lambda e: e.tensor_copy(pw[:, :, 0, 1], X(9)))
            for lv in range(NLV):
                vt(lambda e: e.tensor_scalar(pw[:, :, lv, 2], pw[:, :, lv, 1], -1.0, None, ALU.mult))
                if lv + 1 < NLV:
                    vt(lambda e: e.tensor_tensor(X(12), pw[:, :, lv, 0], pw[:, :, lv, 0], ALU.mult))
                    vt(lambda e: e.tensor_tensor(X(13), pw[:, :, lv, 1], pw[:, :, lv, 1], ALU.mult))
                    vt(lambda e: e.tensor_tensor(pw[:, :, lv + 1, 0], X(12), X(13), ALU.subtract))
                    vt(lambda e: e.tensor_tensor(X(12), pw[:, :, lv, 0], pw[:, :, lv, 1], ALU.mult))
                    vt(lambda e: e.tensor_scalar(pw[:, :, lv + 1, 1], X(12), 2.0, None, ALU.mult))
            brt = sb([128, 2, 2, 32, 128], BF16, "brt")
            for q in range(2):
                for k in range(2):
                    for j8 in range(4):
                        S.dma("pool", brt[:, q, k, j8 * 8:(j8 + 1) * 8, :], lay["brt"].ap()[q, :, k, j8 * 8:(j8 + 1) * 8, :], wa=[r_t])
            crp = sb([128, 2, 2, 32, 32], BF16, "crp")
            S.dma("pool", crp[:, 0], lay["crp"].ap()[0], wa=[r_t])
            S.dma("pool", crp[:, 1], lay["crp"].ap()[1], wa=[r_t])
            at(lambda e: e.mul(crp[:, 1], crp[:, 1], -1.0))
            sd = sb([128, 8], F32, "sd")
            S.dma("sp", sd[:], lay["sd"].ap(), wa=[r_t])
            Xr = sb([128, T], F32, "Xr")
            Xi = sb([128, T], F32, "Xi")
            Yr = sb([128, T], F32, "Yr")
            Yi = sb([128, T], F32, "Yi")
            r_X, r_Yb, r_Yi = Res(), Res(), Res()
            xb = [[sb([128, T], BF16, "xb%d%d" % (k, q)) for q in range(2)] for k in range(2)]
            r_xb = [Res(), Res()]
            uc_p = RR([sb([128, T], BF16, "suc") for _ in range(2)])
            p12_p = RR([ps([128, 512], F32, "sp12") for _ in range(4)])
            tmp_p = RR([sb([128, 512], F32, "stmp") for _ in range(2)])
            yp_p = RR([ps([128, 512], F32, "syp") for _ in range(2)])
            yv = sb([128, T], F32, "yv")
            ga = Yr
            r_yv = Res()
            go_p = RR([sb([128, T], BF16, "sgo") for _ in range(1)])

            def sview(t, off, step, a0, cnt, mult):
                s0 = off + a0 * step
                st_ = mult * step
                return t[:, s0:s0 + (cnt - 1) * st_ + 1:st_]

            r_Xr, r_Xi, r_Yr, r_Yi = Res(), Res(), Res(), Res()
            r_or = [Res(), Res()]
            r_oi = [Res(), Res()]

            def scan(col, k):
                outr, outi = xb[k]
                rof = {id(Xr): r_Xr, id(Xi): r_Xi, id(Yr): r_Yr, id(Yi): r_Yi, id(outr): r_or[k], id(outi): r_oi[k]}

                def stt(o_t, o_ap, a_t, a_ap, sc, b_t, b_ap):
                    V(lambda e: e.scalar_tensor_tensor(o_ap, a_ap, sc, b_ap, ALU.mult, ALU.add),
                      r=[rof[id(a_t)], rof[id(b_t)], r_t, r_X, r_Yb], w=[rof[id(o_t)]])

                def cpy(o_t, o_ap, a_t, a_ap):
                    A(lambda e: e.copy(o_ap, a_ap), r=[rof[id(a_t)], r_X, r_Yb], w=[rof[id(o_t)]])

                def rec(tr, ti, off, step, n, lv, yoff, top):
                    if n == 1:
                        if top:
                            cpy(outr, outr[:, 0:1], tr, tr[:, off:off + 1])
                            cpy(outi, outi[:, 0:1], ti, ti[:, off:off + 1])
                        return
                    m = n // 2
                    ne = n - m
                    ar, ai, nai = pw[:, col, lv, 0:1], pw[:, col, lv, 1:2], pw[:, col, lv, 2:3]
                    Ev = lambda t, a0, cnt: sview(t, off, step, 2 * a0, cnt, 2)
                    Ov = lambda t, a0, cnt: sview(t, off, step, 2 * a0 + 1, cnt, 2)
                    yr, yi = Yr[:, yoff:yoff + m], Yi[:, yoff:yoff + m]
                    stt(Yr, yr, tr, Ev(tr, 0, m), ar, tr, Ov(tr, 0, m))
                    stt(Yi, yi, ti, Ev(ti, 0, m), ar, ti, Ov(ti, 0, m))
                    stt(Yr, yr, ti, Ev(ti, 0, m), nai, Yr, yr)
                    stt(Yi, yi, tr, Ev(tr, 0, m), ai, Yi, yi)
                    rec(Yr, Yi, yoff, 1, m, lv + 1, yoff + m, False)
                    ne1 = ne - 1
                    zr, zi = Yr[:, yoff:yoff + ne1], Yi[:, yoff:yoff + ne1]
                    if top:
                        cpy(outr, sview(outr, 0, 1, 1, m, 2), Yr, yr)
                        cpy(outi, sview(outi, 0, 1, 1, m, 2), Yi, yi)
                        cpy(outr, outr[:, 0:1], tr, tr[:, off:off + 1])
                        cpy(outi, outi[:, 0:1], ti, ti[:, off:off + 1])
                        if ne1 > 0:
                            stt(tr, Ev(tr, 1, ne1), Yr, zr, ar, tr, Ev(tr, 1, ne1))
                            stt(ti, Ev(ti, 1, ne1), Yi, zi, ar, ti, Ev(ti, 1, ne1))
                            V(lambda e: e.scalar_tensor_tensor(sview(outr, 0, 1, 2, ne1, 2), zi, nai, Ev(tr, 1, ne1), ALU.mult, ALU.add),
                              r=[r_Yi, r_Xr, r_t], w=[r_or[k]])
                            V(lambda e: e.scalar_tensor_tensor(sview(outi, 0, 1, 2, ne1, 2), zr, ai, Ev(ti, 1, ne1), ALU.mult, ALU.add),
                              r=[r_Yr, r_Xi, r_t], w=[r_oi[k]])
                    else:
                        cpy(tr, Ov(tr, 0, m), Yr, yr)
                        cpy(ti, Ov(ti, 0, m), Yi, yi)
                        if ne1 > 0:
                            stt(tr, Ev(tr, 1, ne1), Yr, zr, ar, tr, Ev(tr, 1, ne1))
                            stt(ti, Ev(ti, 1, ne1), Yi, zi, ar, ti, Ev(ti, 1, ne1))
                            stt(tr, Ev(tr, 1, ne1), Yi, zi, nai, tr, Ev(tr, 1, ne1))
                            stt(ti, Ev(ti, 1, ne1), Yr, zr, ai, ti, Ev(ti, 1, ne1))
                rec(Xr, Xi, 0, 1, T, 0, 0, True)

            def bwd_pos(t0, nt):
                return (NCTX - t0 - nt) if t0 < NCTX else (NCTX + T - t0 - nt)

            uc = None
            SK = ""
            for j in range(32):
                cj, jm = j // 4, j % 4
                pr = slice(32 * jm, 32 * jm + 32)
                if jm == 0:
                    uc, r_uc = uc_p.next()
                    S.dma("sp", uc[:], uT_d.ap()[cj * 128:(cj + 1) * 128, :], r=[r_u], w=[r_uc])
                for k in range(2):
                    col = k * 32 + j
                    for (t0, nt) in BLKS:
                        if k == 0:
                            i0 = t0
                            uv = uc[:, t0:t0 + nt]
                        else:
                            i0 = bwd_pos(t0, nt)
                            uv = uc[:, t0:t0 + nt][:, ::-1]
                        if "m" in SK:
                            continue
                        p1, r_p1 = p12_p.next()
                        p2, r_p2 = p12_p.next()
                        M(lambda e: e.matmul(p1[:, 0:nt], brt[:, 0, k, j, :], uv, start=True, stop=True), r=[r_uc, r_t], w=[r_p1])
                        M(lambda e: e.matmul(p2[:, 0:nt], brt[:, 1, k, j, :], uv, start=True, stop=True), r=[r_uc, r_t], w=[r_p2])
                        if "e" in SK:
                            continue
                        tm, r_tm = tmp_p.next()
                        if "a" not in SK:
                            A(lambda e: e.activation(tm[:, 0:nt], p2[:, 0:nt], AF.Identity, scale=coef[:, 2, col:col + 1]), r=[r_t], w=[r_tm, r_p2])
                        if "v" not in SK:
                            V(lambda e: e.scalar_tensor_tensor(Xr[:, i0:i0 + nt], p1[:, 0:nt], coef[:, 0, col:col + 1], tm[:, 0:nt], ALU.mult, ALU.add),
                              r=[r_tm, r_t], w=[r_Xr, r_p1])
                        tm2, r_tm2 = tmp_p.next()
                        if "a" not in SK:
                            A(lambda e: e.activation(tm2[:, 0:nt], p1[:, 0:nt], AF.Identity, scale=coef[:, 1, col:col + 1]), r=[r_t], w=[r_tm2, r_p1])
                        if "v" not in SK:
                            V(lambda e: e.scalar_tensor_tensor(Xi[:, i0:i0 + nt], p2[:, 0:nt], coef[:, 0, col:col + 1], tm2[:, 0:nt], ALU.mult, ALU.add),
                              r=[r_tm2, r_t], w=[r_Xi, r_p2])
                    if "s" not in SK:
                        scan(col, k)
                for (t0, nt) in (BLKS if "r" not in SK else []):
                    yp, r_yp = yp_p.next()
                    i0 = bwd_pos(t0, nt)
                    rv = lambda a: a[:, ::-1]
                    ops = [(crp[:, 0, 0, j, :], xb[0][0][:, t0:t0 + nt], r_or[0]), (crp[:, 1, 0, j, :], xb[0][1][:, t0:t0 + nt], r_oi[0]),
                           (crp[:, 0, 1, j, :], rv(xb[1][0][:, i0:i0 + nt]), r_or[1]), (crp[:, 1, 1, j, :], rv(xb[1][1][:, i0:i0 + nt]), r_oi[1])]
                    for qi, (lh, rh, rr) in enumerate(ops):
                        M(lambda e: e.matmul(yp[pr, 0:nt], lh, rh, start=(qi == 0), stop=(qi == 3), tile_position=(0, 32 * jm)), r=[rr, r_t],
                          w=[r_yp] if qi == 0 else [], wa=[r_yp] if qi else [])
                    V(lambda e: e.scalar_tensor_tensor(yv[pr, t0:t0 + nt], uc[pr, t0:t0 + nt], sd[pr, cj:cj + 1], yp[pr, 0:nt], ALU.mult, ALU.add),
                      r=[r_yp, r_uc, r_t], w=[r_yv] if (jm == 0 and t0 == 0) else [], wa=[] if (jm == 0 and t0 == 0) else [r_yv])
                if jm == 3 and "g" not in SK:
                    A(lambda e: e.activation(ga[:], yv[:], AF.Square), r=[r_yv], w=[r_Yr])
                    V(lambda e: e.tensor_scalar(ga[:], ga[:], 0.044715, 1.0, ALU.mult, ALU.add), w=[r_Yr])
                    V(lambda e: e.tensor_tensor(ga[:], ga[:], yv[:], ALU.mult), r=[r_yv], w=[r_Yr])
                    A(lambda e: e.activation(ga[:], ga[:], AF.Sigmoid, scale=1.5957691216057308), w=[r_Yr])
                    go, r_go = go_p.next()
                    V(lambda e: e.tensor_tensor(go[:], ga[:], yv[:], ALU.mult), r=[r_Yr, r_yv], w=[r_go])
                    S.dma("sp", gT_d.ap()[cj * 128:(cj + 1) * 128, :], go[:], r=[r_go], wa=[r_g])
            S.barrier()
        with ExitStack() as ph:
            sb, ps = mk(ph)
            gT = sb([128, 8, T], BF16, "gT")
            r_gT = Res()
            S.dma("sp", gT[:], gT_d.ap().rearrange("(c p) t -> p c t", p=128), r=[r_g], w=[r_gT])
            glub = sb([128, 8], F32, "glub")
            r_gb = Res()
            S.dma("sp", glub[:], lay["glub"].ap(), w=[r_gb])
            sig_p = RR([sb([128, 512], F32, "ssig") for _ in range(2)])
            szt_p = RR([sb([128, 512], BF16, "sszt") for _ in range(2)])
            y2_p = RR([sb([128, 512], BF16, "sy2") for _ in range(3)])

            def epi_glu(ci, c0, ncol, t0, nt, pt, r_pt):
                sg, r_sg_ = sig_p.next()
                A(lambda e: e.activation(sg[:, 0:nt], pt[:, 0:nt], AF.Sigmoid, bias=glub[:, ci:ci + 1], scale=1.0), r=[r_pt, r_gb], w=[r_sg_])
                szt, r_szt = szt_p.next()
                S.dma("sp", szt[:, 0:nt], szT_d.ap()[c0:c0 + 128, t0:t0 + nt], r=[r_sz], w=[r_szt])
                V(lambda e: e.tensor_tensor(sg[:, 0:nt], sg[:, 0:nt], gT[:, ci, t0:t0 + nt], ALU.mult), r=[r_gT], w=[r_sg_])
                y2, r_y2_ = y2_p.next()
                V(lambda e: e.tensor_tensor(y2[:, 0:nt], sg[:, 0:nt], szt[:, 0:nt], ALU.mult), r=[r_sg_, r_szt], w=[r_y2_])
                S.dma("sp", y2T_d.ap()[c0:c0 + 128, t0:t0 + nt], y2[:, 0:nt], r=[r_y2_], wa=[r_y2])
            linear_fm(sb, ps, gT, r_gT, 8, lay["gluw"].ap(), [(128 * i, 128) for i in range(8)], epi_glu)
            S.barrier()
        with ExitStack() as ph:
            sb, ps = mk(ph)
            y2 = sb([128, 8, T], BF16, "y2r")
            r_y2r = Res()
            S.dma("sp", y2[:], y2T_d.ap().rearrange("(c p) t -> p c t", p=128), r=[r_y2], w=[r_y2r])
            linear_fm(sb, ps, y2, r_y2r, 8, lay["outw"].ap(), [(128 * i, 128) for i in range(8)], make_resid_epi(sb))
            S.barrier()

    for i in layers:
        kind = i % 3
        if kind == 0:
            mamba_layer(L[i])
        elif kind == 1:
            attn_layer(L[i])
        else:
            s5_layer(L[i])

    with ExitStack() as ph:
        sb, ps = mk(ph)
        pre_pass(None, sb, ps, None, None, final=True)
    S.barrier()
    es.close()
    nc._ninst = S.ninst
    return nc


def prep_inputs(inputs, b, nlat=NLAT, layers=(0, 1, 2, 3)):
    f = lambda a: np.ascontiguousarray(np.asarray(a, dtype=np.float32))
    chunked = lambda v, n: f(np.asarray(v, np.float32).reshape(n, 128).T)
    m = {}
    m["x"] = f(inputs["x"][b][:nlat])
    m["ctx"] = f(inputs["ctx"][b])
    m["cc"] = f(np.stack([chunked(inputs["c"][b], 8), chunked(inputs["c_ctx"], 8)], axis=-1))
    m["ident"] = np.eye(128, dtype=np.float32)
    m["fnw"] = chunked(inputs["final_norm_w"], 8)
    has_m = False
    for i in layers:
        m["normw%d" % i] = chunked(inputs["norm_w"][i], 8)
        m["modw%d" % i] = f(inputs["mod_w"][i])
        m["modb%d" % i] = chunked(inputs["mod_b"][i], 24)
        kind, j = i % 3, i // 3
        if kind == 0:
            has_m = True
            m["m_in_w%d" % j] = f(inputs["m_in_w"][j])
            cw = np.asarray(inputs["m_conv_w"][j], np.float32)
            m["m_convw%d" % j] = f(cw.reshape(5, 32, 128).transpose(2, 1, 0))
            m["m_convb%d" % j] = chunked(inputs["m_conv_b"][j], 32)
            m["m_alog%d" % j] = f(np.asarray(inputs["m_a_log"][j], np.float32).reshape(64, 1))
            m["m_dtb%d" % j] = f(np.asarray(inputs["m_dt_bias"][j], np.float32).reshape(64, 1))
            m["m_dvec%d" % j] = f(np.repeat(np.asarray(inputs["m_d"][j], np.float32), 64))
            m["m_normw%d" % j] = f(inputs["m_norm_w"][j])
            m["m_out_w%d" % j] = f(inputs["m_out_w"][j])
        elif kind == 1:
            m["a_in_w"] = f(inputs["a_in_w"][0])
            m["a_out_w"] = f(inputs["a_out_w"][0])
            m["a_qkw"] = f(np.stack([np.tile(np.asarray(inputs["a_q_norm"][0], np.float32), 2),
                                     np.tile(np.asarray(inputs["a_k_norm"][0], np.float32), 2)], axis=1))
            grid_w = 64
            pos = np.arange(nlat)
            r_idx, c_idx = (pos // grid_w).astype(np.float32), (pos % grid_w).astype(np.float32)
            inv = (10000.0 ** (-np.arange(0, 32, 2, dtype=np.float32) / 32)).astype(np.float32)
            dd = np.arange(128) % 64
            ax, part, ii = dd // 32, (dd % 32) // 16, dd % 16
            ang = np.where(ax[:, None] == 0, r_idx[None, :], c_idx[None, :]).astype(np.float32) * inv[ii][:, None]
            m["a_rope"] = f(np.stack([np.cos(ang), np.sin(ang)]))
            perm = np.zeros((128, 128), np.float32)
            for dcol in range(128):
                if part[dcol] == 0:
                    perm[dcol + 16, dcol] = -1.0
                else:
                    perm[dcol - 16, dcol] = 1.0
            m["a_perm"] = perm
            bo = np.zeros((128, 128), np.float32)
            bo[:64, :64] = 1.0
            bo[64:, 64:] = 1.0
            m["a_bones"] = bo
        else:
            m["s_in_w"] = f(inputs["s_in_w"][0])
            m["s_glu_w"] = f(inputs["s_glu_w"][0])
            m["s_out_w"] = f(inputs["s_out_w"][0])
            m["s_sd"] = chunked(inputs["s_d"][0], 8)
            m["s_glub"] = chunked(inputs["s_glu_b"][0], 8)
            lre = np.asarray(inputs["s_lambda_re"][0], np.float32)
            lim = np.asarray(inputs["s_lambda_im"][0], np.float32)
            lst = np.asarray(inputs["s_log_step"][0], np.float32)

            def pair_layout(a):
                a = a.reshape(2, 32, 2, 64)
                return a.transpose(2, 3, 0, 1).reshape(128, 64)
            lam = np.stack([pair_layout(lre), pair_layout(lim),
                            pair_layout(np.broadcast_to(lst[:, :, None], (2, 64, 64)))], axis=1)
            m["s_lam"] = f(lam)
            brt = np.zeros((2, 128, 2, 32, 128), np.float32)
            crp = np.zeros((2, 128, 2, 32, 32), np.float32)
            for q, (bsrc, csrc) in enumerate(((inputs["s_b_re"][0], inputs["s_c_re"][0]), (inputs["s_b_im"][0], inputs["s_c_im"][0]))):
                bsrc = np.asarray(bsrc, np.float32)
                csrc = np.asarray(csrc, np.float32)
                for k in range(2):
                    for j in range(32):
                        for gl in range(2):
                            g_ = 2 * j + gl
                            r0 = 32 * (j % 4) + 16 * gl
                            brt[q, r0:r0 + 16, k, j, gl * 64:(gl + 1) * 64] = bsrc[k, g_].T
                            crp[q, gl * 64:(gl + 1) * 64, k, j, 16 * gl:16 * gl + 16] = csrc[k, g_].T
            m["s_brt"] = brt
            m["s_crp"] = crp
    if has_m:
        up = np.triu(np.ones((128, 128), np.float32))
        m["masks"] = f(np.stack([up, up.T]))
    return m


ACTIVE_CORES = (0, 1, 4, 5)


def kernel(**inputs):
    nc = build_program()
    real = [prep_inputs(inputs, b) for b in range(4)]
    big = ("x", "ctx", "cc", "modw", "m_in_w", "m_out_w", "a_in_w", "a_out_w", "s_in_w", "s_glu_w", "s_out_w", "s_brt", "s_crp")
    idle = {k: (np.zeros_like(v) if k.startswith(big) else v) for k, v in real[0].items()}
    in_maps = [idle] * 8
    for b, core in enumerate(ACTIVE_CORES):
        in_maps[core] = real[b]
    res = run_bass_kernel_spmd(nc, in_maps, core_ids=list(range(8)))
    out = np.stack([np.asarray(res.results[core]["out"], dtype=np.float32) for core in ACTIVE_CORES], axis=0)
    return out
```

```python
import os
import numpy as np
from contextlib import ExitStack
import concourse.bass as bass
import concourse.mybir as mybir
from concourse.bass_utils import run_bass_kernel_spmd

F32 = mybir.dt.float32
BF16 = mybir.dt.bfloat16
I32 = mybir.dt.int32
AF = mybir.ActivationFunctionType
ALU = mybir.AluOpType
AX = mybir.AxisListType

D = 1024
NCTX = 256
NLAT = 4096
EPS = 1e-6
M_IN = 6208
PI = float(np.pi)


class Res:
    __slots__ = ("w", "r", "name")

    def __init__(self, name=""):
        self.w = {}
        self.r = {}
        self.name = name


class Sched:
    def __init__(self, nc, es):
        self.nc = nc
        self.eng = {"pe": nc.tensor, "act": nc.scalar, "dve": nc.vector, "pool": nc.gpsimd, "sp": nc.sync}
        self.sem = {}
        self.cnt = {}
        self.known = {e: {} for e in self.eng}
        for e in self.eng:
            self.sem[e] = es.enter_context(nc.semaphore("s_" + e))
            self.cnt[e] = 0
        self.NDS = 8
        self.dslot = {}
        for q in ("sp", "pool"):
            for i in range(self.NDS):
                k = "d_%s%d" % (q, i)
                self.sem[k] = es.enter_context(nc.semaphore(k))
                self.cnt[k] = 0
            self.dslot[q] = 0
        self.ninst = 0

    def _wait(self, e, evs):
        kn = self.known[e]
        for k, v in evs.items():
            if v <= 0 or (e == "pe" and k == "pe") or kn.get(k, 0) >= v:
                continue
            self.eng[e].wait_ge(self.sem[k], v)
            kn[k] = v

    @staticmethod
    def _deps(r, w, wa):
        evs = {}

        def add(d):
            for k, v in d.items():
                if evs.get(k, 0) < v:
                    evs[k] = v
        for x in r:
            add(x.w)
        for x in w:
            add(x.w)
            add(x.r)
        for x in wa:
            add(x.r)
        return evs

    @staticmethod
    def _commit(k, v, r, w, wa):
        for x in r:
            if x.r.get(k, 0) < v:
                x.r[k] = v
        for x in w:
            if x.w.get(k, 0) < v:
                x.w[k] = v
        for x in wa:
            if x.w.get(k, 0) < v:
                x.w[k] = v

    def op(self, e, fn, r=(), w=(), wa=(), inc=True):
        self._wait(e, self._deps(r, w, wa))
        ins = fn(self.eng[e])
        if inc:
            self.cnt[e] += 1
            ins.then_inc(self.sem[e], 1)
            self._commit(e, self.cnt[e], r, w, wa)
        else:
            self._commit(e, self.cnt[e] + 1, r, w, wa)
        self.ninst += 1
        return ins

    def dma(self, q, out, in_, r=(), w=(), wa=(), **kw):
        i = self.dslot[q]
        self.dslot[q] = (i + 1) % self.NDS
        k = "d_%s%d" % (q, i)
        evs = self._deps(r, w, wa)
        evs[k] = max(evs.get(k, 0), self.cnt[k])
        self._wait(q, evs)
        ins = self.eng[q].dma_start(out=out, in_=in_, **kw)
        self.cnt[k] += 16
        ins.then_inc(self.sem[k], 16)
        self._commit(k, self.cnt[k], r, w, wa)
        self.ninst += 1
        return ins

    def barrier(self):
        evs = {k: v for k, v in self.cnt.items() if v > 0}
        for e in self.eng:
            self._wait(e, dict(evs))


class RR:
    def __init__(self, tiles):
        self.t = tiles
        self.r = [Res() for _ in tiles]
        self.i = 0

    def next(self):
        i = self.i
        self.i = (i + 1) % len(self.t)
        return self.t[i], self.r[i]


def build_program(nlat=NLAT, layers=(0, 1, 2, 3)):
    T = NCTX + nlat
    NTT = T // 128
    BLKS = [(0, NCTX)] + [(NCTX + 512 * i, 512) for i in range(nlat // 512)]
    nc = bass.Bass("TRN2", target_bir_lowering=False)
    es = ExitStack()
    es.enter_context(nc.allow_low_precision("bf16 matmul operands, fp32 accumulation"))
    S = Sched(nc, es)
    uid = [0]

    def mk(stack):
        def sb(shape, dt=F32, name="t"):
            uid[0] += 1
            return stack.enter_context(nc.sbuf_tensor("%s_%d" % (name, uid[0]), list(shape), dt))

        def ps(shape, dt=F32, name="p"):
            uid[0] += 1
            return stack.enter_context(nc.psum_tensor("%s_%d" % (name, uid[0]), list(shape), dt))
        return sb, ps

    def din(name, shape, dt=F32):
        return nc.dram_tensor(name, list(shape), dt, kind="ExternalInput")

    def dscr(name, shape, dt=F32):
        return nc.dram_tensor(name, list(shape), dt)

    V = lambda fn, r=(), w=(), wa=(): S.op("dve", fn, r, w, wa)
    A = lambda fn, r=(), w=(), wa=(): S.op("act", fn, r, w, wa)
    G = lambda fn, r=(), w=(), wa=(): S.op("pool", fn, r, w, wa)
    M = lambda fn, r=(), w=(), wa=(), inc=True: S.op("pe", fn, r, w, wa, inc)

    x_in = din("x", [nlat, D])
    ctx_in = din("ctx", [NCTX, D])
    cc_in = din("cc", [128, 8, 2])
    ident_in = din("ident", [128, 128])
    fnw_in = din("fnw", [128, 8])
    out_t = nc.dram_tensor("out", [nlat, D], F32, kind="ExternalOutput")
    L = {}
    for i in layers:
        L[i] = dict(normw=din("normw%d" % i, [128, 8]), modw=din("modw%d" % i, [D, 3 * D]), modb=din("modb%d" % i, [128, 24]))
        kind, j = i % 3, i // 3
        if kind == 0:
            L[i].update(inw=din("m_in_w%d" % j, [D, M_IN]), convw=din("m_convw%d" % j, [128, 32, 5]), convb=din("m_convb%d" % j, [128, 32]),
                        alog=din("m_alog%d" % j, [64, 1]), dtb=din("m_dtb%d" % j, [64, 1]), dvec=din("m_dvec%d" % j, [2048]),
                        mnw=din("m_normw%d" % j, [2048]), outw=din("m_out_w%d" % j, [2048, D]))
        elif kind == 1:
            L[i].update(inw=din("a_in_w", [D, 2560]), qkw=din("a_qkw", [128, 2]), outw=din("a_out_w", [D, D]),
                        rope=din("a_rope", [2, 128, nlat]), perm=din("a_perm", [128, 128]), bones=din("a_bones", [128, 128]))
        else:
            L[i].update(inw=din("s_in_w", [D, 2048]), lam=din("s_lam", [128, 3, 64]), brt=din("s_brt", [2, 128, 2, 32, 128]),
                        crp=din("s_crp", [2, 128, 2, 32, 32]), sd=din("s_sd", [128, 8]), gluw=din("s_glu_w", [D, D]),
                        glub=din("s_glub", [128, 8]), outw=din("s_out_w", [D, D]))
    masks_in = din("masks", [2, 128, 128]) if any(i % 3 == 0 for i in layers) else None

    hT = dscr("hT", [D, T])
    hT_ap = hT.ap()
    r_hT = {(c, tt): Res() for c in range(8) for tt in range(NTT)}

    def hres(c, t0, nt):
        return [r_hT[(c, tt)] for tt in range(t0 // 128, (t0 + nt + 127) // 128)]

    def hres_all(t0, nt):
        out = []
        for c in range(8):
            out += hres(c, t0, nt)
        return out

    gsb, gps = mk(es)
    ident = gsb([128, 128], F32, "ident")
    r_const = Res("const")
    S.dma("sp", ident[:], ident_in.ap(), w=[r_const])
    identb = gsb([128, 128], BF16, "identb")
    ones_bf = gsb([128, 128], BF16, "ones")
    G(lambda e: e.memset(ones_bf[:], 1.0), wa=[r_const])
    V(lambda e: e.tensor_copy(identb[:], ident[:]), r=[r_const], wa=[r_const])
    fnw = gsb([128, 8], F32, "fnw")
    S.dma("sp", fnw[:], fnw_in.ap(), wa=[r_const])
    cc = gsb([128, 8, 2], F32, "cc")
    S.dma("sp", cc[:], cc_in.ap(), wa=[r_const])
    scs = gsb([128, 8, 2], F32, "scs")
    A(lambda e: e.activation(scs[:], cc[:], AF.Silu), r=[r_const], wa=[r_const])
    mod_sc = gsb([128, 8, 2], F32, "mod_sc")
    mod_bi = gsb([128, 8, 2], F32, "mod_bi")
    mod_gt = gsb([128, 8, 2], F32, "mod_gt")
    r_mod = Res("mod")
    S.barrier()

    with ExitStack() as ph:
        sb, ps = mk(ph)
        xin = RR([sb([128, D], F32, "xin") for _ in range(2)])
        tp = RR([ps([128, 512], F32, "tp") for _ in range(2)])
        xo = RR([sb([128, 8, 128], F32, "xo") for _ in range(2)])
        for tt in range(NTT):
            xt, r_xt = xin.next()
            src = ctx_in.ap()[tt * 128:(tt + 1) * 128, :] if tt < 2 else x_in.ap()[(tt - 2) * 128:(tt - 1) * 128, :]
            S.dma("sp", xt[:], src, w=[r_xt])
            ot, r_ot = xo.next()
            for half in range(2):
                pt, r_pt = tp.next()
                for j in range(4):
                    c = half * 4 + j
                    M(lambda e: e.transpose(pt[:, j * 128:(j + 1) * 128], xt[:, c * 128:(c + 1) * 128], ident[:]),
                      r=[r_xt, r_const], w=[r_pt] if j == 0 else [], wa=[r_pt] if j else [])
                dst = ot[:, half * 4:(half + 1) * 4, :]
                if half:
                    A(lambda e: e.copy(dst, pt[:].rearrange("p (j t) -> p j t", j=4)), r=[r_pt], wa=[r_ot])
                else:
                    V(lambda e: e.tensor_copy(dst, pt[:].rearrange("p (j t) -> p j t", j=4)), r=[r_pt], w=[r_ot])
            S.dma("pool", hT_ap[:, tt * 128:(tt + 1) * 128].rearrange("(c p) t -> p c t", p=128), ot[:], r=[r_ot],
                  wa=[r_hT[(c, tt)] for c in range(8)])
        S.barrier()

    def pre_pass(lay, sb, ps, inT, r_inT, final=False):
        if not final:
            mw = RR([sb([128, 8, 512], F32, "modw") for _ in range(2)])
            mp = RR([ps([128, 512], F32, "modp") for _ in range(2)])
            modT = sb([128, 24, 2], F32, "modT")
            r_modT = Res()
            modb = sb([128, 24], F32, "modb")
            normw = sb([128, 8], F32, "normw")
            r_small = Res()
            S.dma("sp", modb[:], lay["modb"].ap(), w=[r_small])
            S.dma("sp", normw[:], lay["normw"].ap(), wa=[r_small])
            for cg in range(6):
                wt, r_wt = mw.next()
                S.dma("sp", wt[:], lay["modw"].ap()[:, cg * 512:(cg + 1) * 512].rearrange("(k p) n -> p k n", p=128), w=[r_wt])
                for c4 in range(4):
                    pt, r_pt = mp.next()
                    for k in range(8):
                        M(lambda e: e.matmul(pt[:, 0:2], wt[:, k, c4 * 128:(c4 + 1) * 128], scs[:, k, :], start=(k == 0), stop=(k == 7)),
                          r=[r_wt, r_const], w=[r_pt] if k == 0 else [], wa=[r_pt] if k else [])
                    col = cg * 4 + c4
                    V(lambda e: e.tensor_scalar(modT[:, col, :], pt[:, 0:2], modb[:, col:col + 1], None, ALU.add),
                      r=[r_pt, r_small], wa=[r_modT])
            V(lambda e: e.tensor_scalar(mod_sc[:], modT[:, 8:16, :], 1.0, None, ALU.add), r=[r_modT], w=[r_mod])
            V(lambda e: e.tensor_tensor(mod_sc[:], mod_sc[:], normw[:].unsqueeze(2).broadcast_to([128, 8, 2]), ALU.mult), r=[r_small], w=[r_mod])
            V(lambda e: e.tensor_copy(mod_bi[:], modT[:, 0:8, :]), r=[r_modT], w=[r_mod])
            V(lambda e: e.tensor_copy(mod_gt[:], modT[:, 16:24, :]), r=[r_modT], w=[r_mod])
        hb = RR([sb([128, 8, 512], F32, "hb") for _ in range(2)])
        sq = RR([sb([128, 8, 512], BF16, "sq") for _ in range(2)])
        ssp = RR([ps([128, 512], F32, "ssp") for _ in range(2)])
        rstd = RR([sb([128, 512], F32, "rstd") for _ in range(2)])
        tmp = RR([sb([128, 512], F32, "ntmp") for _ in range(3)])
        if final:
            hn = RR([sb([128, 8, 512], F32, "hn") for _ in range(2)])
            tp = RR([ps([128, 512], F32, "ftp") for _ in range(2)])
            ot = RR([sb([128, D], F32, "fot") for _ in range(2)])
            r_out = Res()
        for (t0, nt) in BLKS:
            j = 1 if t0 < NCTX else 0
            if final and j == 1:
                continue
            h, r_h = hb.next()
            S.dma("sp", h[:, :, 0:nt], hT_ap[:, t0:t0 + nt].rearrange("(c p) t -> p c t", p=128), r=hres_all(t0, nt), w=[r_h])
            q, r_q = sq.next()
            A(lambda e: e.activation(q[:, :, 0:nt], h[:, :, 0:nt], AF.Square), r=[r_h], w=[r_q])
            sp_, r_sp = ssp.next()
            for c in range(8):
                M(lambda e: e.matmul(sp_[:, 0:nt], ones_bf[:], q[:, c, 0:nt], start=(c == 0), stop=(c == 7)),
                  r=[r_q, r_const], w=[r_sp] if c == 0 else [], wa=[r_sp] if c else [], inc=(c == 7))
            rs, r_rs = rstd.next()
            A(lambda e: e.activation(rs[:, 0:nt], sp_[:, 0:nt], AF.Sqrt, bias=EPS, scale=1.0 / D), r=[r_sp], w=[r_rs])
            V(lambda e: e.reciprocal(rs[:, 0:nt], rs[:, 0:nt]), w=[r_rs])
            if not final:
                for c in range(8):
                    tm, r_tm = tmp.next()
                    V(lambda e: e.tensor_tensor(tm[:, 0:nt], h[:, c, 0:nt], rs[:, 0:nt], ALU.mult), r=[r_h, r_rs], w=[r_tm])
                    A(lambda e: e.activation(inT[:, c, t0:t0 + nt], tm[:, 0:nt], AF.Identity, bias=mod_bi[:, c, j:j + 1], scale=mod_sc[:, c, j:j + 1]),
                      r=[r_tm, r_mod], wa=[r_inT])
            else:
                hn_, r_hn = hn.next()
                for c in range(8):
                    V(lambda e: e.scalar_tensor_tensor(hn_[:, c, 0:nt], h[:, c, 0:nt], fnw[:, c:c + 1], rs[:, 0:nt], ALU.mult, ALU.mult),
                      r=[r_h, r_rs, r_const], w=[r_hn] if c == 0 else [], wa=[r_hn] if c else [])
                for tl in range(nt // 128):
                    o, r_o = ot.next()
                    for half in range(2):
                        pt, r_pt = tp.next()
                        for jj in range(4):
                            c = half * 4 + jj
                            M(lambda e: e.transpose(pt[:, jj * 128:(jj + 1) * 128], hn_[:, c, tl * 128:(tl + 1) * 128], ident[:]),
                              r=[r_hn, r_const], w=[r_pt] if jj == 0 else [], wa=[r_pt] if jj else [])
                        if half:
                            A(lambda e: e.copy(o[:, 512:1024], pt[:]), r=[r_pt], wa=[r_o])
                        else:
                            V(lambda e: e.tensor_copy(o[:, 0:512], pt[:]), r=[r_pt], w=[r_o])
                    row = t0 - NCTX + tl * 128
                    S.dma("pool", out_t.ap()[row:row + 128, :], o[:], r=[r_o], wa=[r_out])
        if final:
            evs = dict(r_out.w)
            S._wait("sp", evs)

    def linear_fm(sb, ps, act, r_act, KC, W_ap, col_chunks, epi, blks=None, t_off=0):
        wts = RR([sb([128, KC, 128], BF16, "lw") for _ in range(3)])
        pts = RR([ps([128, 512], F32, "lp") for _ in range(2)])
        for ci, (c0, ncol) in enumerate(col_chunks):
            wt, r_wt = wts.next()
            S.dma("pool", wt[:, :, 0:ncol], W_ap[:, c0:c0 + ncol].rearrange("(k p) n -> p k n", p=128), w=[r_wt])
            for (t0, nt) in (blks or BLKS):
                pt, r_pt = pts.next()
                for k in range(KC):
                    M(lambda e: e.matmul(pt[0:ncol, 0:nt], wt[:, k, 0:ncol], act[:, k, t0 - t_off:t0 - t_off + nt], start=(k == 0), stop=(k == KC - 1)),
                      r=[r_wt, r_act], w=[r_pt] if k == 0 else [], wa=[r_pt] if k else [], inc=(k == KC - 1))
                epi(ci, c0, ncol, t0, nt, pt, r_pt)

    def linear_tm(sb, ps, act, r_act, KC, W_ap, c0, ncols, epi):
        wts = RR([sb([128, KC, 512], BF16, "lwt") for _ in range(2)])
        pts = RR([ps([128, 512], F32, "lpt") for _ in range(2)])
        for g0 in range(0, ncols, 512):
            n = min(512, ncols - g0)
            wt, r_wt = wts.next()
            S.dma("pool", wt[:, :, 0:n], W_ap[:, c0 + g0:c0 + g0 + n].rearrange("(k p) n -> p k n", p=128), w=[r_wt])
            for tt in range(NTT):
                pt, r_pt = pts.next()
                for k in range(KC):
                    M(lambda e: e.matmul(pt[:, 0:n], act[:, k, tt * 128:(tt + 1) * 128], wt[:, k, 0:n], start=(k == 0), stop=(k == KC - 1)),
                      r=[r_wt, r_act], w=[r_pt] if k == 0 else [], wa=[r_pt] if k else [], inc=(k == KC - 1))
                epi(g0, n, tt, pt, r_pt)

    def make_resid_epi(sb):
        hts = RR([sb([128, 512], F32, "rh") for _ in range(3)])

        def epi(ci, c0, ncol, t0, nt, pt, r_pt):
            c = c0 // 128
            j = 1 if t0 < NCTX else 0
            ht, r_ht = hts.next()
            S.dma("sp", ht[:, 0:nt], hT_ap[c * 128:(c + 1) * 128, t0:t0 + nt], r=hres(c, t0, nt), w=[r_ht])
            V(lambda e: e.scalar_tensor_tensor(ht[:, 0:nt], pt[:, 0:nt], mod_gt[:, c, j:j + 1], ht[:, 0:nt], ALU.mult, ALU.add),
              r=[r_pt, r_mod], w=[r_ht])
            S.dma("pool", hT_ap[c * 128:(c + 1) * 128, t0:t0 + nt], ht[:, 0:nt], r=[r_ht], wa=hres(c, t0, nt))
        return epi

    def mamba_layer(lay):
        x_tm = dscr("x_tm%d" % uid[0], [T, 2048], BF16)
        B_tm = dscr("B_tm%d" % uid[0], [T, 1024], BF16)
        BT_d = dscr("BT_d%d" % uid[0], [8, 128, T], BF16)
        CT_d = dscr("CT_d%d" % uid[0], [8, 128, T], BF16)
        sz_tm = dscr("sz_tm%d" % uid[0], [T, 2048], BF16)
        laT_d = dscr("laT_d%d" % uid[0], [64, T], F32)
        ltot_d = dscr("ltot_d%d" % uid[0], [NTT, 64], F32)
        Yacc = dscr("Yacc%d" % uid[0], [T, 2048], F32)
        uid[0] += 1
        r_xtm, r_Btm, r_BT, r_CT, r_sz, r_laT, r_ltot = Res(), Res(), Res(), Res(), Res(), Res(), Res()
        r_Y = [Res() for _ in range(NTT)]
        with ExitStack() as lst:
            lsb, lps = mk(lst)
            la_tm = lsb([128, NTT, 64], F32, "la_tm")
            dt_tm = lsb([128, NTT, 64], F32, "dt_tm")
            LTB = lsb([128, NTT, 64], F32, "LTB")
            r_tabs = Res()
            with ExitStack() as st1:
                sb1, ps1 = mk(st1)
                inT = sb1([128, 8, T], BF16, "inT")
                r_inT = Res()
                with ExitStack() as ph:
                    sb, ps = mk(ph)
                    pre_pass(lay, sb, ps, inT, r_inT)
                    S.barrier()
                with ExitStack() as ph:
                    sb, ps = mk(ph)
                    convw = sb([128, 32, 5], F32, "convw")
                    convb = sb([128, 32], F32, "convb")
                    r_cv = Res()
                    S.dma("sp", convw[:], lay["convw"].ap(), w=[r_cv])
                    S.dma("sp", convb[:], lay["convb"].ap(), wa=[r_cv])
                    xr = sb([128, T + 8], F32, "xr")
                    r_xr = Res()
                    G(lambda e: e.memset(xr[:], 0.0), w=[r_xr])
                    acc = sb([128, T], F32, "cacc")
                    r_acc = Res()
                    xo = RR([sb([128, T], BF16, "cxo") for _ in range(2)])
                    tps = RR([ps([128, 512], BF16, "ctp") for _ in range(2)])
                    tos = RR([sb([128, 512], BF16, "cto") for _ in range(3)])
                    state = {}

                    def epi_xbc(ci, c0, ncol, t0, nt, pt, r_pt):
                        off = 2 if t0 < NCTX else 6
                        if ci % 2:
                            A(lambda e: e.copy(xr[:, t0 + off:t0 + off + nt], pt[:, 0:nt]), r=[r_pt], wa=[r_xr])
                        else:
                            V(lambda e: e.tensor_copy(xr[:, t0 + off:t0 + off + nt], pt[:, 0:nt]), r=[r_pt], wa=[r_xr])
                        if t0 + nt < T:
                            return
                        segs = [(0, NCTX, 0), (NCTX, nlat, 4)]
                        for (s0, sn, dl) in segs:
                            A(lambda e: e.activation(acc[:, s0:s0 + sn], xr[:, s0 + dl:s0 + dl + sn], AF.Identity,
                                                     bias=convb[:, ci:ci + 1], scale=convw[:, ci, 0:1]), r=[r_xr, r_cv], wa=[r_acc])
                            for k in range(1, 5):
                                V(lambda e: e.scalar_tensor_tensor(acc[:, s0:s0 + sn], xr[:, s0 + dl + k:s0 + dl + k + sn], convw[:, ci, k:k + 1],
                                                                   acc[:, s0:s0 + sn], ALU.mult, ALU.add), r=[r_xr, r_cv], w=[r_acc])
                        o, r_o = xo.next()
                        A(lambda e: e.activation(o[:], acc[:], AF.Silu), r=[r_acc], w=[r_o])
                        if ci < 24:
                            dst, col0, r_d = (x_tm, ci * 128, r_xtm) if ci < 16 else (B_tm, (ci - 16) * 128, r_Btm)
                            for t4 in range(0, NTT, 4):
                                n4 = min(4, NTT - t4)
                                tp_, r_tp = tps.next()
                                for q in range(n4):
                                    M(lambda e: e.transpose(tp_[:, q * 128:(q + 1) * 128], o[:, (t4 + q) * 128:(t4 + q + 1) * 128], identb[:]),
                                      r=[r_o, r_const], w=[r_tp] if q == 0 else [], wa=[r_tp] if q else [])
                                to, r_to = tos.next()
                                A(lambda e: e.copy(to[:, 0:n4 * 128], tp_[:, 0:n4 * 128]), r=[r_tp], w=[r_to])
                                S.dma("sp", dst.ap()[t4 * 128:(t4 + n4) * 128, col0:col0 + 128].rearrange("(q p) c -> p q c", p=128),
                                      to[:, 0:n4 * 128].rearrange("p (q c) -> p q c", q=n4), r=[r_to], wa=[r_d])
                        if ci >= 16:
                            gg = (ci - 16) % 8
                            dd, r_dd = (BT_d, r_BT) if ci < 24 else (CT_d, r_CT)
                            S.dma("sp", dd.ap()[gg], o[:], r=[r_o], wa=[r_dd])

                    linear_fm(sb, ps, inT, r_inT, 8, lay["inw"].ap(), [(2048 + 128 * i, 128) for i in range(32)], epi_xbc)
                    S.barrier()
                with ExitStack() as ph:
                    sb, ps = mk(ph)
                    dtT = sb([64, NTT, 128], F32, "dtT")
                    dA = sb([64, NTT, 128], F32, "dA")
                    laP = sb([64, NTT, 128], F32, "laP")
                    laT = sb([64, NTT, 128], F32, "laT")
                    rp = sb([64, NTT, 128], F32, "rp")
                    r_dt, r_dA, r_laP, r_laTs, r_rp = Res(), Res(), Res(), Res(), Res()
                    sm = sb([64, 4], F32, "dtsm")
                    r_sm = Res()
                    S.dma("sp", sm[:, 0:1], lay["alog"].ap(), w=[r_sm])
                    S.dma("sp", sm[:, 1:2], lay["dtb"].ap(), wa=[r_sm])
                    A(lambda e: e.activation(sm[:, 2:3], sm[:, 0:1], AF.Exp), r=[r_sm], wa=[r_sm])
                    V(lambda e: e.tensor_scalar(sm[:, 3:4], sm[:, 2:3], -1.0, None, ALU.mult), r=[r_sm], wa=[r_sm])
                    G(lambda e: e.memset(rp[:], 1.0), w=[r_rp])
                    G(lambda e: e.memset(rp[:, :, 0:1], 0.0), w=[r_rp])
                    dtf = dtT[:].rearrange("p c l -> p (c l)")

                    def epi_dt(ci, c0, ncol, t0, nt, pt, r_pt):
                        A(lambda e: e.activation(dtf[:, t0:t0 + nt], pt[0:64, 0:nt], AF.Exp, bias=sm[:, 1:2], scale=1.0), r=[r_pt, r_sm], wa=[r_dt])
                    linear_fm(sb, ps, inT, r_inT, 8, lay["inw"].ap(), [(6144, 64)], epi_dt)
                    A(lambda e: e.activation(dtf, dtf, AF.Ln, bias=1.0, scale=1.0), w=[r_dt])
                    V(lambda e: e.tensor_scalar(dA[:], dtT[:], sm[:, 3:4], None, ALU.mult), r=[r_dt, r_sm], w=[r_dA])
                    V(lambda e: e.tensor_tensor_scan(laP[:].rearrange("p c l -> p (c l)"), rp[:].rearrange("p c l -> p (c l)"),
                                                     dA[:].rearrange("p c l -> p (c l)"), 0.0, ALU.mult, ALU.add), r=[r_rp, r_dA], w=[r_laP])
                    V(lambda e: e.tensor_copy(laT[0:32], laP[0:32]), r=[r_laP], w=[r_laTs])
                    V(lambda e: e.tensor_tensor(laT[32:64], dA[32:64], laP[32:64], ALU.subtract), r=[r_laP, r_dA], wa=[r_laTs])
                    V(lambda e: e.tensor_tensor(laT[32:64], laT[32:64], laP[32:64, :, 127:128].broadcast_to([32, NTT, 128]), ALU.add), r=[r_laP], w=[r_laTs])
                    S.dma("sp", laT_d.ap(), laT[:].rearrange("p c l -> p (c l)"), r=[r_laTs], w=[r_laT])
                    tpp = RR([ps([128, 64], F32, "dtp") for _ in range(2)])
                    for tt in range(NTT):
                        for (src, r_src, dst) in ((laT, r_laTs, la_tm), (dtT, r_dt, dt_tm)):
                            tp_, r_tp = tpp.next()
                            M(lambda e: e.transpose(tp_[:], src[:, tt, :], ident[0:64, 0:64]), r=[r_src, r_const], w=[r_tp])
                            V(lambda e: e.tensor_copy(dst[:, tt, :], tp_[:]), r=[r_tp], wa=[r_tabs])
                    S.dma("sp", ltot_d.ap()[:, 0:32], la_tm[127:128, :, 0:32], r=[r_tabs], w=[r_ltot])
                    S.dma("sp", ltot_d.ap()[:, 32:64], la_tm[0:1, :, 32:64], r=[r_tabs], wa=[r_ltot])
                    S.dma("sp", LTB[:].rearrange("p c h -> p (c h)"), bass.AP(ltot_d, 0, [[0, 128], [1, NTT * 64]]), r=[r_ltot], wa=[r_tabs])
                    S.barrier()
                with ExitStack() as ph:
                    sb, ps = mk(ph)
                    zo = RR([sb([128, 512], BF16, "zo") for _ in range(3)])

                    def epi_z(g0, n, tt, pt, r_pt):
                        o, r_o = zo.next()
                        A(lambda e: e.activation(o[:, 0:n], pt[:, 0:n], AF.Silu), r=[r_pt], w=[r_o])
                        S.dma("sp", sz_tm.ap()[tt * 128:(tt + 1) * 128, g0:g0 + n], o[:, 0:n], r=[r_o], wa=[r_sz])
                    linear_tm(sb, ps, inT, r_inT, 8, lay["inw"].ap(), 0, 2048, epi_z)
                    S.barrier()
            with ExitStack() as ph:
                sb, ps = mk(ph)
                masks = sb([128, 2, 128], F32, "masks")
                r_mk = Res()
                S.dma("sp", masks[:], masks_in.ap().rearrange("d s l -> s d l"), w=[r_mk])
                xt_p = RR([sb([128, 2048], BF16, "sx") for _ in range(2)])
                bt_p = RR([sb([128, 1024], BF16, "sB") for _ in range(2)])
                BTs_p = RR([sb([128, 8, 128], BF16, "sBT") for _ in range(2)])
                CTs_p = RR([sb([128, 8, 128], BF16, "sCT") for _ in range(2)])
                LaB_p = RR([sb([128, 32, 128], F32, "sLaB") for _ in range(2)])
                dmat_p = RR([sb([128, 32, 128], F32, "dmat") for _ in range(2)])
                decay_p = RR([sb([128, 32, 128], BF16, "decay") for _ in range(2)])
                wT_p = RR([sb([128, 32, 128], BF16, "wT") for _ in range(2)])
                CBm_p = RR([sb([128, 8, 128], BF16, "CBm") for _ in range(2)])
                xdt_p = RR([sb([128, 2048], BF16, "xdt") for _ in range(2)])
                xw_p = RR([sb([128, 2048], BF16, "xw") for _ in range(2)])
                sml = sb([128, 4, 32], F32, "ssml")
                r_sml = Res()
                ST = sb([128, 2048], F32, "ST")
                r_ST = Res()
                prevb = sb([128, 2048], BF16, "prevb")
                r_prevb = Res()
                ysb = sb([128, 2048], F32, "ysb")
                r_ysb = Res()
                eyo = sb([128, 1024], F32, "eyo")
                r_eyo = Res()
                yin_p = RR([sb([128, 2048], F32, "yin") for _ in range(2)])
                cbp = ps([128, 8, 128], F32, "cbp")
                r_cbp = Res()
                ydp = ps([128, 1024], F32, "ydp")
                r_ydp = Res()
                yop = ps([128, 1024], F32, "yop")
                r_yop = Res()
                stp = ps([128, 1024], F32, "stp")
                r_stp = Res()
                for dr in range(2):
                    order = list(range(NTT)) if dr == 0 else [1, 0] + list(range(NTT - 1, 1, -1))
                    V(lambda e: e.memset(ST[:], 0.0), w=[r_ST])
                    hc = dr * 32
                    for c in order:
                        tok = slice(c * 128, (c + 1) * 128)
                        xt, r_xt = xt_p.next()
                        S.dma("sp", xt[:], x_tm.ap()[tok, :], r=[r_xtm], w=[r_xt])
                        bt, r_bt = bt_p.next()
                        S.dma("sp", bt[:], B_tm.ap()[tok, :], r=[r_Btm], w=[r_bt])
                        BTs, r_BTs = BTs_p.next()
                        S.dma("sp", BTs[:], BT_d.ap()[:, :, tok].rearrange("g n t -> n g t"), r=[r_BT], w=[r_BTs])
                        CTs, r_CTs = CTs_p.next()
                        S.dma("sp", CTs[:], CT_d.ap()[:, :, tok].rearrange("g n t -> n g t"), r=[r_CT], w=[r_CTs])
                        LaB, r_LaB = LaB_p.next()
                        S.dma("sp", LaB[:], bass.AP(laT_d, hc * T + c * 128, [[0, 128], [T, 32], [1, 128]]), r=[r_laT], w=[r_LaB])
                        la_c = la_tm[:, c, hc:hc + 32]
                        dmat, r_dmat = dmat_p.next()
                        decay, r_decay = decay_p.next()
                        wT, r_wT = wT_p.next()
                        CBm, r_CBm = CBm_p.next()
                        xdt, r_xdt = xdt_p.next()
                        xw, r_xw = xw_p.next()
                        A(lambda e: e.activation(sml[:, 0, :], la_c, AF.Exp), r=[r_tabs], w=[r_sml])
                        V(lambda e: e.tensor_tensor(sml[:, 3, :], LTB[:, c, hc:hc + 32], la_c, ALU.subtract), r=[r_tabs], w=[r_sml])
                        V(lambda e: e.tensor_single_scalar(sml[:, 3, :], sml[:, 3, :], 0.0, ALU.min), w=[r_sml])
                        A(lambda e: e.activation(sml[:, 1, :], sml[:, 3, :], AF.Exp), w=[r_sml])
                        A(lambda e: e.activation(sml[:, 2, :], LTB[:, c, hc:hc + 32], AF.Exp), r=[r_tabs], w=[r_sml])
                        for g in range(8):
                            M(lambda e: e.matmul(cbp[:, g, :], BTs[:, g, :], CTs[:, g, :], start=True, stop=True), r=[r_BTs, r_CTs],
                              w=[r_cbp] if g == 0 else [], wa=[r_cbp] if g else [], inc=(g == 7))
                        V(lambda e: e.tensor_tensor(CBm[:], cbp[:], masks[:, dr:dr + 1, :].broadcast_to([128, 8, 128]), ALU.mult),
                          r=[r_cbp, r_mk], w=[r_CBm])
                        for h in range(32):
                            V(lambda e: e.tensor_scalar(dmat[:, h, :], LaB[:, h, :], la_tm[:, c, hc + h:hc + h + 1], 0.0, ALU.subtract, ALU.min),
                              r=[r_LaB, r_tabs], w=[r_dmat] if h == 0 else [], wa=[r_dmat] if h else [])
                        A(lambda e: e.activation(decay[:], dmat[:], AF.Exp), r=[r_dmat], w=[r_decay])
                        V(lambda e: e.tensor_tensor(wT[:].rearrange("p (g h) l -> p g h l", g=8), decay[:].rearrange("p (g h) l -> p g h l", g=8),
                                                    CBm[:].unsqueeze(2).broadcast_to([128, 8, 4, 128]), ALU.mult), r=[r_decay, r_CBm], w=[r_wT])
                        V(lambda e: e.tensor_tensor(xdt[:].rearrange("p (h q) -> p h q", h=32), xt[:].rearrange("p (h q) -> p h q", h=32),
                                                    dt_tm[:, c, hc:hc + 32].unsqueeze(2).broadcast_to([128, 32, 64]), ALU.mult), r=[r_xt, r_tabs], w=[r_xdt])
                        V(lambda e: e.tensor_tensor(xw[:].rearrange("p (h q) -> p h q", h=32), xdt[:].rearrange("p (h q) -> p h q", h=32),
                                                    sml[:, 1, :].unsqueeze(2).broadcast_to([128, 32, 64]), ALU.mult), r=[r_xdt, r_sml], w=[r_xw])
                        A(lambda e: e.copy(prevb[:], ST[:]), r=[r_ST], w=[r_prevb])
                        if dr == 1:
                            yin, r_yin = yin_p.next()
                            S.dma("sp", yin[:], Yacc.ap()[tok, :], r=[r_Y[c]], w=[r_yin])
                        for gh in range(2):
                            cs = slice(gh * 1024, (gh + 1) * 1024)
                            for hl in range(16):
                                h = gh * 16 + hl
                                M(lambda e: e.matmul(ydp[:, hl * 64:(hl + 1) * 64], wT[:, h, :], xdt[:, h * 64:(h + 1) * 64], start=True, stop=True),
                                  r=[r_wT, r_xdt], w=[r_ydp] if hl == 0 else [], wa=[r_ydp] if hl else [], inc=(hl == 15))
                            for gl in range(4):
                                g = gh * 4 + gl
                                M(lambda e: e.matmul(yop[:, gl * 256:(gl + 1) * 256], CTs[:, g, :], prevb[:, g * 256:(g + 1) * 256], start=True, stop=True),
                                  r=[r_CTs, r_prevb], w=[r_yop] if gl == 0 else [], wa=[r_yop] if gl else [], inc=(gl == 3))
                            for gl in range(4):
                                g = gh * 4 + gl
                                M(lambda e: e.matmul(stp[:, gl * 256:(gl + 1) * 256], bt[:, g * 128:(g + 1) * 128], xw[:, g * 256:(g + 1) * 256], start=True, stop=True),
                                  r=[r_bt, r_xw], w=[r_stp] if gl == 0 else [], wa=[r_stp] if gl else [], inc=(gl == 3))
                            if dr == 1:
                                V(lambda e: e.tensor_tensor(ysb[:, cs], ydp[:], yin[:, cs], ALU.add), r=[r_ydp, r_yin], w=[r_ysb] if gh == 0 else [], wa=[r_ysb] if gh else [])
                            else:
                                A(lambda e: e.copy(ysb[:, cs], ydp[:]), r=[r_ydp], w=[r_ysb] if gh == 0 else [], wa=[r_ysb] if gh else [])
                            for hl in range(16):
                                h = gh * 16 + hl
                                A(lambda e: e.activation(eyo[:, hl * 64:(hl + 1) * 64], yop[:, hl * 64:(hl + 1) * 64], AF.Identity, scale=sml[:, 0, h:h + 1]),
                                  r=[r_yop, r_sml], w=[r_eyo] if hl == 0 else [], wa=[r_eyo] if hl else [])
                            V(lambda e: e.tensor_tensor(ysb[:, cs], ysb[:, cs], eyo[:], ALU.add), r=[r_eyo], w=[r_ysb])
                            V(lambda e: e.tensor_tensor(ST[:, cs].rearrange("p (h q) -> p h q", h=16), ST[:, cs].rearrange("p (h q) -> p h q", h=16),
                                                        sml[:, 2, gh * 16:(gh + 1) * 16].unsqueeze(2).broadcast_to([128, 16, 64]), ALU.mult),
                              r=[r_sml, r_prevb], w=[r_ST])
                            V(lambda e: e.tensor_tensor(ST[:, cs], ST[:, cs], stp[:], ALU.add), r=[r_stp], w=[r_ST])
                        S.dma("pool", Yacc.ap()[tok, :], ysb[:], r=[r_ysb], w=[r_Y[c]])
                S.barrier()
            with ExitStack() as ph:
                sb, ps = mk(ph)
                dvec = sb([128, 2048], F32, "dvec")
                mnw = sb([128, 2048], F32, "mnw")
                r_dv = Res()
                S.dma("sp", dvec[:], bass.AP(lay["dvec"], 0, [[0, 128], [1, 2048]]), w=[r_dv])
                S.dma("sp", mnw[:], bass.AP(lay["mnw"], 0, [[0, 128], [1, 2048]]), wa=[r_dv])
                ow = sb([128, 16, D], BF16, "ow")
                r_ow = Res()
                for k4 in range(4):
                    S.dma("pool", ow[:, k4 * 4:(k4 + 1) * 4, :], lay["outw"].ap()[k4 * 512:(k4 + 1) * 512, :].rearrange("(k p) n -> p k n", p=128),
                          w=[r_ow] if k4 == 0 else [], wa=[r_ow] if k4 else [])
                y_p = RR([sb([128, 2048], F32, "ty") for _ in range(2)])
                x_p = RR([sb([128, 2048], BF16, "tx") for _ in range(2)])
                z_p = RR([sb([128, 2048], BF16, "tz") for _ in range(2)])
                g_p = RR([sb([128, 2048], F32, "tg") for _ in range(2)])
                gb_p = RR([sb([128, 2048], BF16, "tgb") for _ in range(2)])
                junk = sb([128, 2048], BF16, "tjunk")
                r_junk = Res()
                ss_p = RR([sb([128, 2], F32, "tss") for _ in range(2)])
                gT_p = RR([sb([128, 16, 512], BF16, "tgT") for _ in range(2)])
                tp_p = RR([ps([128, 512], BF16, "ttp") for _ in range(2)])
                op_p = RR([ps([128, 512], F32, "top") for _ in range(3)])
                ht_p = RR([sb([128, 8, 512], F32, "tht") for _ in range(2)])
                groups = [(0, 2)] + [(2 + 4 * i, 4) for i in range((NTT - 2) // 4)]
                for (tt0, ng) in groups:
                    j = 1 if tt0 < 2 else 0
                    t0, nt = tt0 * 128, ng * 128
                    gT, r_gT = gT_p.next()
                    for q4 in range(ng):
                        tt = tt0 + q4
                        tok = slice(tt * 128, (tt + 1) * 128)
                        y, r_y = y_p.next()
                        S.dma("sp", y[:], Yacc.ap()[tok, :], r=[r_Y[tt]], w=[r_y])
                        xt, r_xt = x_p.next()
                        S.dma("sp", xt[:], x_tm.ap()[tok, :], r=[r_xtm], w=[r_xt])
                        zt, r_zt = z_p.next()
                        S.dma("sp", zt[:], sz_tm.ap()[tok, :], r=[r_sz], w=[r_zt])
                        gt_, r_g = g_p.next()
                        V(lambda e: e.tensor_tensor(gt_[:], xt[:], dvec[:], ALU.mult), r=[r_xt, r_dv], w=[r_g])
                        V(lambda e: e.tensor_tensor(gt_[:], gt_[:], y[:], ALU.add), r=[r_y], w=[r_g])
                        V(lambda e: e.tensor_tensor(gt_[:], gt_[:], zt[:], ALU.mult), r=[r_zt], w=[r_g])
                        ss, r_ss = ss_p.next()
                        A(lambda e: e.activation(junk[:], gt_[:], AF.Square, accum_out=ss[:, 0:1]), r=[r_g], w=[r_junk, r_ss])
                        A(lambda e: e.activation(ss[:, 1:2], ss[:, 0:1], AF.Sqrt, bias=EPS, scale=1.0 / 2048), w=[r_ss])
                        V(lambda e: e.reciprocal(ss[:, 1:2], ss[:, 1:2]), w=[r_ss])
                        gb, r_gb = gb_p.next()
                        V(lambda e: e.scalar_tensor_tensor(gb[:], gt_[:], ss[:, 1:2], mnw[:], ALU.mult, ALU.mult), r=[r_g, r_ss, r_dv], w=[r_gb])
                        for k4 in range(4):
                            tp_, r_tp = tp_p.next()
                            for q in range(4):
                                k = k4 * 4 + q
                                M(lambda e: e.transpose(tp_[:, q * 128:(q + 1) * 128], gb[:, k * 128:(k + 1) * 128], identb[:]),
                                  r=[r_gb, r_const], w=[r_tp] if q == 0 else [], wa=[r_tp] if q else [], inc=(q == 3))
                            A(lambda e: e.copy(gT[:, k4 * 4:(k4 + 1) * 4, q4 * 128:(q4 + 1) * 128], tp_[:].rearrange("p (q t) -> p q t", q=4)), r=[r_tp],
                              w=[r_gT] if (k4 == 0 and q4 == 0) else [], wa=[] if (k4 == 0 and q4 == 0) else [r_gT])
                    ht, r_ht = ht_p.next()
                    hr = [r_hT[(c, tt)] for c in range(8) for tt in range(tt0, tt0 + ng)]
                    S.dma("sp", ht[:, :, 0:nt], hT_ap[:, t0:t0 + nt].rearrange("(c p) t -> p c t", p=128), r=hr, w=[r_ht])
                    for dc in range(8):
                        op, r_op = op_p.next()
                        for k in range(16):
                            M(lambda e: e.matmul(op[:, 0:nt], ow[:, k, dc * 128:(dc + 1) * 128], gT[:, k, 0:nt], start=(k == 0), stop=(k == 15)),
                              r=[r_ow, r_gT], w=[r_op] if k == 0 else [], wa=[r_op] if k else [], inc=(k == 15))
                        V(lambda e: e.scalar_tensor_tensor(ht[:, dc, 0:nt], op[:, 0:nt], mod_gt[:, dc, j:j + 1], ht[:, dc, 0:nt], ALU.mult, ALU.add),
                          r=[r_op, r_mod], w=[r_ht])
                    S.dma("pool", hT_ap[:, t0:t0 + nt].rearrange("(c p) t -> p c t", p=128), ht[:, :, 0:nt], r=[r_ht], wa=hr)
                S.barrier()

    def attn_layer(lay):
        qT_d = dscr("qT_d", [D, T], BF16)
        kT_d = dscr("kT_d", [256, T], BF16)
        v_tm = dscr("v_tm", [T, 256], BF16)
        sgT_d = dscr("sgT_d", [D, T], BF16)
        oT_d = dscr("oT_d", [D, T], BF16)
        r_q, r_k, r_v, r_sg, r_o = Res(), Res(), Res(), Res(), Res()
        with ExitStack() as st1:
            sb1, ps1 = mk(st1)
            inT = sb1([128, 8, T], BF16, "inT")
            r_inT = Res()
            with ExitStack() as ph:
                sb, ps = mk(ph)
                pre_pass(lay, sb, ps, inT, r_inT)
                S.barrier()
            with ExitStack() as ph:
                sb, ps = mk(ph)
                rope = sb([128, 2, nlat], F32, "rope")
                r_cst = Res()
                S.dma("sp", rope[:], lay["rope"].ap().rearrange("a p t -> p a t"), w=[r_cst])
                qkw = sb([128, 2], F32, "qkw")
                S.dma("sp", qkw[:], lay["qkw"].ap(), wa=[r_cst])
                permb = sb([128, 128], BF16, "permb")
                S.dma("pool", permb[:], lay["perm"].ap(), wa=[r_cst])
                bones = sb([128, 128], BF16, "bones")
                S.dma("pool", bones[:], lay["bones"].ap(), wa=[r_cst])
                sq_p = RR([sb([128, 512], BF16, "asq") for _ in range(2)])
                ss_p = RR([ps([128, 512], F32, "ass") for _ in range(2)])
                rs_p = RR([sb([128, 512], F32, "ars") for _ in range(2)])
                qn_p = RR([sb([128, 512], F32, "aqn") for _ in range(2)])
                qb_p = RR([sb([128, 512], BF16, "aqb") for _ in range(2)])
                rot_p = RR([ps([128, 512], F32, "arot") for _ in range(2)])
                t1_p = RR([sb([128, 512], F32, "at1") for _ in range(2)])
                t2_p = RR([sb([128, 512], F32, "at2") for _ in range(2)])
                qo_p = RR([sb([128, 512], BF16, "aqo") for _ in range(3)])

                def epi_qkg(ci, c0, ncol, t0, nt, pt, r_pt):
                    if c0 >= 1536:
                        o, r_o_ = qo_p.next()
                        A(lambda e: e.activation(o[:, 0:nt], pt[:, 0:nt], AF.Silu), r=[r_pt], w=[r_o_])
                        cg = (c0 - 1536) // 128
                        S.dma("sp", sgT_d.ap()[cg * 128:(cg + 1) * 128, t0:t0 + nt], o[:, 0:nt], r=[r_o_], wa=[r_sg])
                        return
                    isq = c0 < 1024
                    wcol = 0 if isq else 1
                    sq, r_sq = sq_p.next()
                    A(lambda e: e.activation(sq[:, 0:nt], pt[:, 0:nt], AF.Square), r=[r_pt], w=[r_sq])
                    ss, r_ss = ss_p.next()
                    M(lambda e: e.matmul(ss[:, 0:nt], bones[:], sq[:, 0:nt], start=True, stop=True), r=[r_sq, r_cst], w=[r_ss])
                    rs, r_rs = rs_p.next()
                    A(lambda e: e.activation(rs[:, 0:nt], ss[:, 0:nt], AF.Sqrt, bias=EPS, scale=1.0 / 64), r=[r_ss], w=[r_rs])
                    V(lambda e: e.reciprocal(rs[:, 0:nt], rs[:, 0:nt]), w=[r_rs])
                    o, r_o_ = qo_p.next()
                    if t0 < NCTX:
                        V(lambda e: e.scalar_tensor_tensor(o[:, 0:nt], pt[:, 0:nt], qkw[:, wcol:wcol + 1], rs[:, 0:nt], ALU.mult, ALU.mult),
                          r=[r_pt, r_rs, r_cst], w=[r_o_])
                    else:
                        qn, r_qn = qn_p.next()
                        V(lambda e: e.scalar_tensor_tensor(qn[:, 0:nt], pt[:, 0:nt], qkw[:, wcol:wcol + 1], rs[:, 0:nt], ALU.mult, ALU.mult),
                          r=[r_pt, r_rs, r_cst], w=[r_qn])
                        qb, r_qb = qb_p.next()
                        A(lambda e: e.copy(qb[:, 0:nt], qn[:, 0:nt]), r=[r_qn], w=[r_qb])
                        rot, r_rot = rot_p.next()
                        M(lambda e: e.matmul(rot[:, 0:nt], permb[:], qb[:, 0:nt], start=True, stop=True), r=[r_qb, r_cst], w=[r_rot])
                        l0 = t0 - NCTX
                        t1, r_t1 = t1_p.next()
                        G(lambda e: e.tensor_tensor(t1[:, 0:nt], qn[:, 0:nt], rope[:, 0, l0:l0 + nt], ALU.mult), r=[r_qn, r_cst], w=[r_t1])
                        t2, r_t2 = t2_p.next()
                        V(lambda e: e.tensor_tensor(t2[:, 0:nt], rot[:, 0:nt], rope[:, 1, l0:l0 + nt], ALU.mult), r=[r_rot, r_cst], w=[r_t2])
                        V(lambda e: e.tensor_tensor(o[:, 0:nt], t1[:, 0:nt], t2[:, 0:nt], ALU.add), r=[r_t1, r_t2], w=[r_o_])
                    if isq:
                        S.dma("sp", qT_d.ap()[c0:c0 + 128, t0:t0 + nt], o[:, 0:nt], r=[r_o_], wa=[r_q])
                    else:
                        S.dma("sp", kT_d.ap()[c0 - 1024:c0 - 1024 + 128, t0:t0 + nt], o[:, 0:nt], r=[r_o_], wa=[r_k])

                cols = [(128 * i, 128) for i in range(10)] + [(1536 + 128 * i, 128) for i in range(8)]
                linear_fm(sb, ps, inT, r_inT, 8, lay["inw"].ap(), cols, epi_qkg)
                vo_p = RR([sb([128, 256], BF16, "avo") for _ in range(3)])

                def epi_v(g0, n, tt, pt, r_pt):
                    o, r_o_ = vo_p.next()
                    V(lambda e: e.tensor_copy(o[:, 0:n], pt[:, 0:n]), r=[r_pt], w=[r_o_])
                    S.dma("sp", v_tm.ap()[tt * 128:(tt + 1) * 128, :], o[:, 0:n], r=[r_o_], wa=[r_v])
                linear_tm(sb, ps, inT, r_inT, 8, lay["inw"].ap(), 1280, 256, epi_v)
                S.barrier()
        with ExitStack() as ph:
            sb, ps = mk(ph)
            onesf = sb([128, 64], F32, "aones")
            r_on = Res()
            G(lambda e: e.memset(onesf[:], 1.0), w=[r_on])
            Vg = sb([128, NTT, 65], BF16, "Vg")
            r_Vg = Res()
            G(lambda e: e.memset(Vg[:], 1.0), w=[r_Vg])
            kk_p = RR([sb([128, T], BF16, "kk") for _ in range(2)])
            qc_p = RR([sb([128, T], BF16, "qc") for _ in range(2)])
            sg_p = RR([sb([64, T], BF16, "sgh") for _ in range(2)])
            sp_p = RR([ps([128, 512], F32, "asp") for _ in range(4)])
            P_p = RR([sb([128, 512], BF16, "aP") for _ in range(4)])
            oa_p = RR([ps([128, 512], F32, "aoa") for _ in range(2)])
            bc_p = RR([ps([64, 512], F32, "abc") for _ in range(2)])
            osb_p = RR([sb([128, 512], F32, "aosb") for _ in range(2)])
            o1_p = RR([sb([64, 512], F32, "ao1") for _ in range(2)])
            og_p = RR([sb([64, 512], BF16, "aog") for _ in range(2)])
            for gk in range(4):
                kk, r_kk = kk_p.next()
                S.dma("sp", kk[0:64, :], kT_d.ap()[gk * 64:(gk + 1) * 64, :], r=[r_k], w=[r_kk])
                S.dma("sp", kk[64:128, :], kT_d.ap()[gk * 64:(gk + 1) * 64, :], r=[r_k], wa=[r_kk])
                S.dma("sp", Vg[:, :, 0:64], v_tm.ap()[:, gk * 64:(gk + 1) * 64].rearrange("(t p) d -> p t d", p=128), r=[r_v], w=[r_Vg])
                for qc in (2 * gk, 2 * gk + 1):
                    qt, r_qt = qc_p.next()
                    S.dma("sp", qt[:], qT_d.ap()[qc * 128:(qc + 1) * 128, :], r=[r_q], w=[r_qt])
                    for hh in range(2):
                        h = 2 * qc + hh
                        pr = slice(64 * hh, 64 * hh + 64)
                        sgh, r_sgh = sg_p.next()
                        S.dma("sp", sgh[:], sgT_d.ap()[h * 64:(h + 1) * 64, :], r=[r_sg], w=[r_sgh])
                        tasks = []
                        for (t0, nt) in BLKS:
                            ktiles = [0, 1] if t0 < NCTX else list(range(NTT))
                            for ki, kt in enumerate(ktiles):
                                tasks.append((t0, nt, ki, kt, len(ktiles)))
                        spq = {}
                        cur = {}
                        deferred = []

                        def emit_qk(ti):
                            t0, nt, ki, kt, nk = tasks[ti]
                            sp_, r_sp = sp_p.next()
                            M(lambda e: e.matmul(sp_[:, 0:nt], kk[pr, kt * 128:(kt + 1) * 128], qt[pr, t0:t0 + nt], start=True, stop=True),
                              r=[r_kk, r_qt], w=[r_sp])
                            spq[ti] = (sp_, r_sp)

                        def finalize_pe(args):
                            (t0, nt, osb, r_osb) = args
                            bc, r_bc = bc_p.next()
                            M(lambda e: e.matmul(bc[:, 0:nt], onesf[64:65, :], osb[64:65, 0:nt], start=True, stop=True), r=[r_osb, r_on], w=[r_bc])
                            o1, r_o1 = o1_p.next()
                            V(lambda e: e.tensor_tensor(o1[:, 0:nt], osb[0:64, 0:nt], bc[:, 0:nt], ALU.mult), r=[r_osb, r_bc], w=[r_o1])
                            og, r_og = og_p.next()
                            G(lambda e: e.tensor_tensor(og[:, 0:nt], o1[:, 0:nt], sgh[:, t0:t0 + nt], ALU.mult), r=[r_o1, r_sgh], w=[r_og])
                            S.dma("sp", oT_d.ap()[h * 64:(h + 1) * 64, t0:t0 + nt], og[:, 0:nt], r=[r_og], wa=[r_o])

                        LOOK = 2
                        for ti in range(min(LOOK, len(tasks))):
                            emit_qk(ti)
                        for ti in range(len(tasks)):
                            t0, nt, ki, kt, nk = tasks[ti]
                            sp_, r_sp = spq.pop(ti)
                            if ki == 0:
                                cur["oa"] = oa_p.next()
                            oa, r_oa = cur["oa"]
                            P, r_P = P_p.next()
                            A(lambda e: e.activation(P[:, 0:nt], sp_[:, 0:nt], AF.Exp, bias=-8.0, scale=0.125), r=[r_sp], w=[r_P])
                            M(lambda e: e.matmul(oa[0:65, 0:nt], Vg[:, kt, :], P[:, 0:nt], start=(ki == 0), stop=(ki == nk - 1)),
                              r=[r_Vg, r_P], w=[r_oa] if ki == 0 else [], wa=[r_oa] if ki else [], inc=(ki == nk - 1))
                            if ti + LOOK < len(tasks):
                                emit_qk(ti + LOOK)
                            deferred = [(n - 1, a) for (n, a) in deferred]
                            while deferred and deferred[0][0] <= 0:
                                finalize_pe(deferred.pop(0)[1])
                            if ki == nk - 1:
                                osb, r_osb = osb_p.next()
                                V(lambda e: e.tensor_copy(osb[0:65, 0:nt], oa[0:65, 0:nt]), r=[r_oa], w=[r_osb])
                                V(lambda e: e.reciprocal(osb[64:65, 0:nt], osb[64:65, 0:nt]), w=[r_osb])
                                deferred.append((4, (t0, nt, osb, r_osb)))
                        for (_, a) in deferred:
                            finalize_pe(a)
            S.barrier()
        with ExitStack() as ph:
            sb, ps = mk(ph)
            oT = sb([128, 8, T], BF16, "oT")
            r_oT = Res()
            S.dma("sp", oT[:], oT_d.ap().rearrange("(c p) t -> p c t", p=128), r=[r_o], w=[r_oT])
            linear_fm(sb, ps, oT, r_oT, 8, lay["outw"].ap(), [(128 * i, 128) for i in range(8)], make_resid_epi(sb))
            S.barrier()

    def s5_layer(lay):
        uT_d = dscr("uT_d", [D, T], BF16)
        szT_d = dscr("szT_d", [D, T], BF16)
        gT_d = dscr("gT_d", [D, T], BF16)
        y2T_d = dscr("y2T_d", [D, T], BF16)
        r_u, r_sz, r_g, r_y2 = Res(), Res(), Res(), Res()
        NLV = 1
        while (1 << (NLV - 1)) < T:
            NLV += 1
        with ExitStack() as st1:
            sb1, ps1 = mk(st1)
            inT = sb1([128, 8, T], BF16, "inT")
            r_inT = Res()
            with ExitStack() as ph:
                sb, ps = mk(ph)
                pre_pass(lay, sb, ps, inT, r_inT)
                S.barrier()
            with ExitStack() as ph:
                sb, ps = mk(ph)
                uo_p = RR([sb([128, 512], BF16, "suo") for _ in range(3)])

                def epi_uz(ci, c0, ncol, t0, nt, pt, r_pt):
                    o, r_o_ = uo_p.next()
                    if c0 < 1024:
                        V(lambda e: e.tensor_copy(o[:, 0:nt], pt[:, 0:nt]), r=[r_pt], w=[r_o_])
                        S.dma("sp", uT_d.ap()[c0:c0 + 128, t0:t0 + nt], o[:, 0:nt], r=[r_o_], wa=[r_u])
                    else:
                        A(lambda e: e.activation(o[:, 0:nt], pt[:, 0:nt], AF.Silu), r=[r_pt], w=[r_o_])
                        S.dma("sp", szT_d.ap()[c0 - 1024:c0 - 1024 + 128, t0:t0 + nt], o[:, 0:nt], r=[r_o_], wa=[r_sz])
                linear_fm(sb, ps, inT, r_inT, 8, lay["inw"].ap(), [(128 * i, 128) for i in range(16)], epi_uz)
                S.barrier()
        with ExitStack() as ph:
            sb, ps = mk(ph)
            lam = sb([128, 3, 64], F32, "lam")
            r_t = Res()
            S.dma("sp", lam[:], lay["lam"].ap(), w=[r_t])
            tb = sb([128, 16, 64], F32, "stb")
            tbi = sb([128, 64], I32, "stbi")
            coef = sb([128, 3, 64], F32, "coef")
            pw = sb([128, 64, NLV, 3], F32, "pw")
            lr, li, ls = lam[:, 0, :], lam[:, 1, :], lam[:, 2, :]
            X = lambda i: tb[:, i, :]

            def vt(fn):
                V(fn, w=[r_t])

            def at(fn):
                A(fn, w=[r_t])
            at(lambda e: e.activation(X(0), ls, AF.Exp))
            vt(lambda e: e.tensor_tensor(X(1), lr, X(0), ALU.mult))
            at(lambda e: e.activation(X(2), X(1), AF.Exp))
            vt(lambda e: e.tensor_tensor(X(3), li, X(0), ALU.mult))

            def sin_of(dst, src, shift):
                vt(lambda e: e.tensor_scalar(X(4), src, shift, 1.0 / (2 * PI), ALU.add, ALU.mult))
                vt(lambda e: e.tensor_copy(tbi[:], X(4)))
                vt(lambda e: e.tensor_copy(X(5), tbi[:]))
                vt(lambda e: e.tensor_scalar(X(4), src, shift, None, ALU.add))
                vt(lambda e: e.scalar_tensor_tensor(X(4), X(5), -2 * PI, X(4), ALU.mult, ALU.add))
                vt(lambda e: e.tensor_single_scalar(X(5), X(4), PI, ALU.is_gt))
                vt(lambda e: e.scalar_tensor_tensor(X(4), X(5), -2 * PI, X(4), ALU.mult, ALU.add))
                vt(lambda e: e.tensor_single_scalar(X(5), X(4), -PI, ALU.is_lt))
                vt(lambda e: e.scalar_tensor_tensor(X(4), X(5), 2 * PI, X(4), ALU.mult, ALU.add))
                at(lambda e: e.activation(dst, X(4), AF.Sin))
            sin_of(X(6), X(3), 0.0)
            sin_of(X(7), X(3), PI / 2)
            vt(lambda e: e.tensor_tensor(X(8), X(2), X(7), ALU.mult))
            vt(lambda e: e.tensor_tensor(X(9), X(2), X(6), ALU.mult))
            vt(lambda e: e.tensor_tensor(X(10), lr, lr, ALU.mult))
            vt(lambda e: e.tensor_tensor(X(11), li, li, ALU.mult))
            vt(lambda e: e.tensor_tensor(X(10), X(10), X(11), ALU.add))
            vt(lambda e: e.reciprocal(X(10), X(10)))
            vt(lambda e: e.tensor_scalar(X(11), X(8), -1.0, None, ALU.add))
            vt(lambda e: e.tensor_tensor(X(12), X(11), lr, ALU.mult))
            vt(lambda e: e.tensor_tensor(X(13), X(9), li, ALU.mult))
            vt(lambda e: e.tensor_tensor(X(12), X(12), X(13), ALU.add))
            vt(lambda e: e.tensor_tensor(coef[:, 0, :], X(12), X(10), ALU.mult))
            vt(lambda e: e.tensor_tensor(X(12), X(9), lr, ALU.mult))
            vt(lambda e: e.tensor_tensor(X(13), X(11), li, ALU.mult))
            vt(lambda e: e.tensor_tensor(X(12), X(12), X(13), ALU.subtract))
            vt(lambda e: e.tensor_tensor(coef[:, 1, :], X(12), X(10), ALU.mult))
            vt(lambda e: e.tensor_scalar(coef[:, 2, :], coef[:, 1, :], -1.0, None, ALU.mult))
            vt(lambda e: e.tensor_copy(pw[:, :, 0, 0], X(8)))
            vt(lambda e: e.tensor_copy(pw[:, :, 0, 1], X(9)))
            for lv in range(NLV):
                vt(lambda e: e.tensor_scalar(pw[:, :, lv, 2], pw[:, :, lv, 1], -1.0, None, ALU.mult))
                if lv + 1 < NLV:
                    vt(lambda e: e.tensor_tensor(X(12), pw[:, :, lv, 0], pw[:, :, lv, 0], ALU.mult))
                    vt(lambda e: e.tensor_tensor(X(13), pw[:, :, lv, 1], pw[:, :, lv, 1], ALU.mult))
                    vt(lambda e: e.tensor_tensor(pw[:, :, lv + 1, 0], X(12), X(13), ALU.subtract))
                    vt(lambda e: e.tensor_tensor(X(12), pw[:, :, lv, 0], pw[:, :, lv, 1], ALU.mult))
                    vt(lambda e: e.tensor_scalar(pw[:, :, lv + 1, 1], X(12), 2.0, None, ALU.mult))
            brt = sb([128, 2, 2, 32, 128], BF16, "brt")
            for q in range(2):
                for k in range(2):
                    for j8 in range(4):
                        S.dma("pool", brt[:, q, k, j8 * 8:(j8 + 1) * 8, :], lay["brt"].ap()[q, :, k, j8 * 8:(j8 + 1) * 8, :], wa=[r_t])
            crp = sb([128, 2, 2, 32, 32], BF16, "crp")
            S.dma("pool", crp[:, 0], lay["crp"].ap()[0], wa=[r_t])
            S.dma("pool", crp[:, 1], lay["crp"].ap()[1], wa=[r_t])
            at(lambda e: e.mul(crp[:, 1], crp[:, 1], -1.0))
            sd = sb([128, 8], F32, "sd")
            S.dma("sp", sd[:], lay["sd"].ap(), wa=[r_t])
            Xr = sb([128, T], F32, "Xr")
            Xi = sb([128, T], F32, "Xi")
            Yr = sb([128, T], F32, "Yr")
            Yi = sb([128, T], F32, "Yi")
            r_X, r_Yb, r_Yi = Res(), Res(), Res()
            xb = [[sb([128, T], BF16, "xb%d%d" % (k, q)) for q in range(2)] for k in range(2)]
            r_xb = [Res(), Res()]
            uc_p = RR([sb([128, T], BF16, "suc") for _ in range(2)])
            p12_p = RR([ps([128, 512], F32, "sp12") for _ in range(4)])
            tmp_p = RR([sb([128, 512], F32, "stmp") for _ in range(2)])
            yp_p = RR([ps([128, 512], F32, "syp") for _ in range(2)])
            yv = sb([128, T], F32, "yv")
            ga = Yr
            r_yv = Res()
            go_p = RR([sb([128, T], BF16, "sgo") for _ in range(1)])

            def sview(t, off, step, a0, cnt, mult):
                s0 = off + a0 * step
                st_ = mult * step
                return t[:, s0:s0 + (cnt - 1) * st_ + 1:st_]

            r_Xr, r_Xi, r_Yr, r_Yi = Res(), Res(), Res(), Res()
            r_or = [Res(), Res()]
            r_oi = [Res(), Res()]

            def scan(col, k):
                outr, outi = xb[k]
                rof = {id(Xr): r_Xr, id(Xi): r_Xi, id(Yr): r_Yr, id(Yi): r_Yi, id(outr): r_or[k], id(outi): r_oi[k]}

                def stt(o_t, o_ap, a_t, a_ap, sc, b_t, b_ap):
                    V(lambda e: e.scalar_tensor_tensor(o_ap, a_ap, sc, b_ap, ALU.mult, ALU.add),
                      r=[rof[id(a_t)], rof[id(b_t)], r_t, r_X, r_Yb], w=[rof[id(o_t)]])

                def cpy(o_t, o_ap, a_t, a_ap):
                    A(lambda e: e.copy(o_ap, a_ap), r=[rof[id(a_t)], r_X, r_Yb], w=[rof[id(o_t)]])

                def rec(tr, ti, off, step, n, lv, yoff, top):
                    if n == 1:
                        if top:
                            cpy(outr, outr[:, 0:1], tr, tr[:, off:off + 1])
                            cpy(outi, outi[:, 0:1], ti, ti[:, off:off + 1])
                        return
                    m = n // 2
                    ne = n - m
                    ar, ai, nai = pw[:, col, lv, 0:1], pw[:, col, lv, 1:2], pw[:, col, lv, 2:3]
                    Ev = lambda t, a0, cnt: sview(t, off, step, 2 * a0, cnt, 2)
                    Ov = lambda t, a0, cnt: sview(t, off, step, 2 * a0 + 1, cnt, 2)
                    yr, yi = Yr[:, yoff:yoff + m], Yi[:, yoff:yoff + m]
                    stt(Yr, yr, tr, Ev(tr, 0, m), ar, tr, Ov(tr, 0, m))
                    stt(Yi, yi, ti, Ev(ti, 0, m), ar, ti, Ov(ti, 0, m))
                    stt(Yr, yr, ti, Ev(ti, 0, m), nai, Yr, yr)
                    stt(Yi, yi, tr, Ev(tr, 0, m), ai, Yi, yi)
                    rec(Yr, Yi, yoff, 1, m, lv + 1, yoff + m, False)
                    ne1 = ne - 1
                    zr, zi = Yr[:, yoff:yoff + ne1], Yi[:, yoff:yoff + ne1]
                    if top:
                        cpy(outr, sview(outr, 0, 1, 1, m, 2), Yr, yr)
                        cpy(outi, sview(outi, 0, 1, 1, m, 2), Yi, yi)
                        cpy(outr, outr[:, 0:1], tr, tr[:, off:off + 1])
                        cpy(outi, outi[:, 0:1], ti, ti[:, off:off + 1])
                        if ne1 > 0:
                            stt(tr, Ev(tr, 1, ne1), Yr, zr, ar, tr, Ev(tr, 1, ne1))
                            stt(ti, Ev(ti, 1, ne1), Yi, zi, ar, ti, Ev(ti, 1, ne1))
                            V(lambda e: e.scalar_tensor_tensor(sview(outr, 0, 1, 2, ne1, 2), zi, nai, Ev(tr, 1, ne1), ALU.mult, ALU.add),
                              r=[r_Yi, r_Xr, r_t], w=[r_or[k]])
                            V(lambda e: e.scalar_tensor_tensor(sview(outi, 0, 1, 2, ne1, 2), zr, ai, Ev(ti, 1, ne1), ALU.mult, ALU.add),
                              r=[r_Yr, r_Xi, r_t], w=[r_oi[k]])
                    else:
                        cpy(tr, Ov(tr, 0, m), Yr, yr)
                        cpy(ti, Ov(ti, 0, m), Yi, yi)
                        if ne1 > 0:
                            stt(tr, Ev(tr, 1, ne1), Yr, zr, ar, tr, Ev(tr, 1, ne1))
                            stt(ti, Ev(ti, 1, ne1), Yi, zi, ar, ti, Ev(ti, 1, ne1))
                            stt(tr, Ev(tr, 1, ne1), Yi, zi, nai, tr, Ev(tr, 1, ne1))
                            stt(ti, Ev(ti, 1, ne1), Yr, zr, ai, ti, Ev(ti, 1, ne1))
                rec(Xr, Xi, 0, 1, T, 0, 0, True)

            def bwd_pos(t0, nt):
                return (NCTX - t0 - nt) if t0 < NCTX else (NCTX + T - t0 - nt)

            uc = None
            SK = ""
            for j in range(32):
                cj, jm = j // 4, j % 4
                pr = slice(32 * jm, 32 * jm + 32)
                if jm == 0:
                    uc, r_uc = uc_p.next()
                    S.dma("sp", uc[:], uT_d.ap()[cj * 128:(cj + 1) * 128, :], r=[r_u], w=[r_uc])
                for k in range(2):
                    col = k * 32 + j
                    for (t0, nt) in BLKS:
                        if k == 0:
                            i0 = t0
                            uv = uc[:, t0:t0 + nt]
                        else:
                            i0 = bwd_pos(t0, nt)
                            uv = uc[:, t0:t0 + nt][:, ::-1]
                        if "m" in SK:
                            continue
                        p1, r_p1 = p12_p.next()
                        p2, r_p2 = p12_p.next()
                        M(lambda e: e.matmul(p1[:, 0:nt], brt[:, 0, k, j, :], uv, start=True, stop=True), r=[r_uc, r_t], w=[r_p1])
                        M(lambda e: e.matmul(p2[:, 0:nt], brt[:, 1, k, j, :], uv, start=True, stop=True), r=[r_uc, r_t], w=[r_p2])
                        if "e" in SK:
                            continue
                        tm, r_tm = tmp_p.next()
                        if "a" not in SK:
                            A(lambda e: e.activation(tm[:, 0:nt], p2[:, 0:nt], AF.Identity, scale=coef[:, 2, col:col + 1]), r=[r_t], w=[r_tm, r_p2])
                        if "v" not in SK:
                            V(lambda e: e.scalar_tensor_tensor(Xr[:, i0:i0 + nt], p1[:, 0:nt], coef[:, 0, col:col + 1], tm[:, 0:nt], ALU.mult, ALU.add),
                              r=[r_tm, r_t], w=[r_Xr, r_p1])
                        tm2, r_tm2 = tmp_p.next()
                        if "a" not in SK:
                            A(lambda e: e.activation(tm2[:, 0:nt], p1[:, 0:nt], AF.Identity, scale=coef[:, 1, col:col + 1]), r=[r_t], w=[r_tm2, r_p1])
                        if "v" not in SK:
                            V(lambda e: e.scalar_tensor_tensor(Xi[:, i0:i0 + nt], p2[:, 0:nt], coef[:, 0, col:col + 1], tm2[:, 0:nt], ALU.mult, ALU.add),
                              r=[r_tm2, r_t], w=[r_Xi, r_p2])
                    if "s" not in SK:
                        scan(col, k)
                for (t0, nt) in (BLKS if "r" not in SK else []):
                    yp, r_yp = yp_p.next()
                    i0 = bwd_pos(t0, nt)
                    rv = lambda a: a[:, ::-1]
                    ops = [(crp[:, 0, 0, j, :], xb[0][0][:, t0:t0 + nt], r_or[0]), (crp[:, 1, 0, j, :], xb[0][1][:, t0:t0 + nt], r_oi[0]),
                           (crp[:, 0, 1, j, :], rv(xb[1][0][:, i0:i0 + nt]), r_or[1]), (crp[:, 1, 1, j, :], rv(xb[1][1][:, i0:i0 + nt]), r_oi[1])]
                    for qi, (lh, rh, rr) in enumerate(ops):
                        M(lambda e: e.matmul(yp[pr, 0:nt], lh, rh, start=(qi == 0), stop=(qi == 3), tile_position=(0, 32 * jm)), r=[rr, r_t],
                          w=[r_yp] if qi == 0 else [], wa=[r_yp] if qi else [])
                    V(lambda e: e.scalar_tensor_tensor(yv[pr, t0:t0 + nt], uc[pr, t0:t0 + nt], sd[pr, cj:cj + 1], yp[pr, 0:nt], ALU.mult, ALU.add),
                      r=[r_yp, r_uc, r_t], w=[r_yv] if (jm == 0 and t0 == 0) else [], wa=[] if (jm == 0 and t0 == 0) else [r_yv])
                if jm == 3 and "g" not in SK:
                    A(lambda e: e.activation(ga[:], yv[:], AF.Square), r=[r_yv], w=[r_Yr])
                    V(lambda e: e.tensor_scalar(ga[:], ga[:], 0.044715, 1.0, ALU.mult, ALU.add), w=[r_Yr])
                    V(lambda e: e.tensor_tensor(ga[:], ga[:], yv[:], ALU.mult), r=[r_yv], w=[r_Yr])
                    A(lambda e: e.activation(ga[:], ga[:], AF.Sigmoid, scale=1.5957691216057308), w=[r_Yr])
                    go, r_go = go_p.next()
                    V(lambda e: e.tensor_tensor(go[:], ga[:], yv[:], ALU.mult), r=[r_Yr, r_yv], w=[r_go])
                    S.dma("sp", gT_d.ap()[cj * 128:(cj + 1) * 128, :], go[:], r=[r_go], wa=[r_g])
            S.barrier()
        with ExitStack() as ph:
            sb, ps = mk(ph)
            gT = sb([128, 8, T], BF16, "gT")
            r_gT = Res()
            S.dma("sp", gT[:], gT_d.ap().rearrange("(c p) t -> p c t", p=128), r=[r_g], w=[r_gT])
            glub = sb([128, 8], F32, "glub")
            r_gb = Res()
            S.dma("sp", glub[:], lay["glub"].ap(), w=[r_gb])
            sig_p = RR([sb([128, 512], F32, "ssig") for _ in range(2)])
            szt_p = RR([sb([128, 512], BF16, "sszt") for _ in range(2)])
            y2_p = RR([sb([128, 512], BF16, "sy2") for _ in range(3)])

            def epi_glu(ci, c0, ncol, t0, nt, pt, r_pt):
                sg, r_sg_ = sig_p.next()
                A(lambda e: e.activation(sg[:, 0:nt], pt[:, 0:nt], AF.Sigmoid, bias=glub[:, ci:ci + 1], scale=1.0), r=[r_pt, r_gb], w=[r_sg_])
                szt, r_szt = szt_p.next()
                S.dma("sp", szt[:, 0:nt], szT_d.ap()[c0:c0 + 128, t0:t0 + nt], r=[r_sz], w=[r_szt])
                V(lambda e: e.tensor_tensor(sg[:, 0:nt], sg[:, 0:nt], gT[:, ci, t0:t0 + nt], ALU.mult), r=[r_gT], w=[r_sg_])
                y2, r_y2_ = y2_p.next()
                V(lambda e: e.tensor_tensor(y2[:, 0:nt], sg[:, 0:nt], szt[:, 0:nt], ALU.mult), r=[r_sg_, r_szt], w=[r_y2_])
                S.dma("sp", y2T_d.ap()[c0:c0 + 128, t0:t0 + nt], y2[:, 0:nt], r=[r_y2_], wa=[r_y2])
            linear_fm(sb, ps, gT, r_gT, 8, lay["gluw"].ap(), [(128 * i, 128) for i in range(8)], epi_glu)
            S.barrier()
        with ExitStack() as ph:
            sb, ps = mk(ph)
            y2 = sb([128, 8, T], BF16, "y2r")
            r_y2r = Res()
            S.dma("sp", y2[:], y2T_d.ap().rearrange("(c p) t -> p c t", p=128), r=[r_y2], w=[r_y2r])
            linear_fm(sb, ps, y2, r_y2r, 8, lay["outw"].ap(), [(128 * i, 128) for i in range(8)], make_resid_epi(sb))
            S.barrier()

    for i in layers:
        kind = i % 3
        if kind == 0:
            mamba_layer(L[i])
        elif kind == 1:
            attn_layer(L[i])
        else:
            s5_layer(L[i])

    with ExitStack() as ph:
        sb, ps = mk(ph)
        pre_pass(None, sb, ps, None, None, final=True)
    S.barrier()
    es.close()
    nc._ninst = S.ninst
    return nc


def prep_inputs(inputs, b, nlat=NLAT, layers=(0, 1, 2, 3)):
    f = lambda a: np.ascontiguousarray(np.asarray(a, dtype=np.float32))
    chunked = lambda v, n: f(np.asarray(v, np.float32).reshape(n, 128).T)
    m = {}
    m["x"] = f(inputs["x"][b][:nlat])
    m["ctx"] = f(inputs["ctx"][b])
    m["cc"] = f(np.stack([chunked(inputs["c"][b], 8), chunked(inputs["c_ctx"], 8)], axis=-1))
    m["ident"] = np.eye(128, dtype=np.float32)
    m["fnw"] = chunked(inputs["final_norm_w"], 8)
    has_m = False
    for i in layers:
        m["normw%d" % i] = chunked(inputs["norm_w"][i], 8)
        m["modw%d" % i] = f(inputs["mod_w"][i])
        m["modb%d" % i] = chunked(inputs["mod_b"][i], 24)
        kind, j = i % 3, i // 3
        if kind == 0:
            has_m = True
            m["m_in_w%d" % j] = f(inputs["m_in_w"][j])
            cw = np.asarray(inputs["m_conv_w"][j], np.float32)
            m["m_convw%d" % j] = f(cw.reshape(5, 32, 128).transpose(2, 1, 0))
            m["m_convb%d" % j] = chunked(inputs["m_conv_b"][j], 32)
            m["m_alog%d" % j] = f(np.asarray(inputs["m_a_log"][j], np.float32).reshape(64, 1))
            m["m_dtb%d" % j] = f(np.asarray(inputs["m_dt_bias"][j], np.float32).reshape(64, 1))
            m["m_dvec%d" % j] = f(np.repeat(np.asarray(inputs["m_d"][j], np.float32), 64))
            m["m_normw%d" % j] = f(inputs["m_norm_w"][j])
            m["m_out_w%d" % j] = f(inputs["m_out_w"][j])
        elif kind == 1:
            m["a_in_w"] = f(inputs["a_in_w"][0])
            m["a_out_w"] = f(inputs["a_out_w"][0])
            m["a_qkw"] = f(np.stack([np.tile(np.asarray(inputs["a_q_norm"][0], np.float32), 2),
                                     np.tile(np.asarray(inputs["a_k_norm"][0], np.float32), 2)], axis=1))
            grid_w = 64
            pos = np.arange(nlat)
            r_idx, c_idx = (pos // grid_w).astype(np.float32), (pos % grid_w).astype(np.float32)
            inv = (10000.0 ** (-np.arange(0, 32, 2, dtype=np.float32) / 32)).astype(np.float32)
            dd = np.arange(128) % 64
            ax, part, ii = dd // 32, (dd % 32) // 16, dd % 16
            ang = np.where(ax[:, None] == 0, r_idx[None, :], c_idx[None, :]).astype(np.float32) * inv[ii][:, None]
            m["a_rope"] = f(np.stack([np.cos(ang), np.sin(ang)]))
            perm = np.zeros((128, 128), np.float32)
            for dcol in range(128):
                if part[dcol] == 0:
                    perm[dcol + 16, dcol] = -1.0
                else:
                    perm[dcol - 16, dcol] = 1.0
            m["a_perm"] = perm
            bo = np.zeros((128, 128), np.float32)
            bo[:64, :64] = 1.0
            bo[64:, 64:] = 1.0
            m["a_bones"] = bo
        else:
            m["s_in_w"] = f(inputs["s_in_w"][0])
            m["s_glu_w"] = f(inputs["s_glu_w"][0])
            m["s_out_w"] = f(inputs["s_out_w"][0])
            m["s_sd"] = chunked(inputs["s_d"][0], 8)
            m["s_glub"] = chunked(inputs["s_glu_b"][0], 8)
            lre = np.asarray(inputs["s_lambda_re"][0], np.float32)
            lim = np.asarray(inputs["s_lambda_im"][0], np.float32)
            lst = np.asarray(inputs["s_log_step"][0], np.float32)

            def pair_layout(a):
                a = a.reshape(2, 32, 2, 64)
                return a.transpose(2, 3, 0, 1).reshape(128, 64)
            lam = np.stack([pair_layout(lre), pair_layout(lim),
                            pair_layout(np.broadcast_to(lst[:, :, None], (2, 64, 64)))], axis=1)
            m["s_lam"] = f(lam)
            brt = np.zeros((2, 128, 2, 32, 128), np.float32)
            crp = np.zeros((2, 128, 2, 32, 32), np.float32)
            for q, (bsrc, csrc) in enumerate(((inputs["s_b_re"][0], inputs["s_c_re"][0]), (inputs["s_b_im"][0], inputs["s_c_im"][0]))):
                bsrc = np.asarray(bsrc, np.float32)
                csrc = np.asarray(csrc, np.float32)
                for k in range(2):
                    for j in range(32):
                        for gl in range(2):
                            g_ = 2 * j + gl
                            r0 = 32 * (j % 4) + 16 * gl
                            brt[q, r0:r0 + 16, k, j, gl * 64:(gl + 1) * 64] = bsrc[k, g_].T
                            crp[q, gl * 64:(gl + 1) * 64, k, j, 16 * gl:16 * gl + 16] = csrc[k, g_].T
            m["s_brt"] = brt
            m["s_crp"] = crp
    if has_m:
        up = np.triu(np.ones((128, 128), np.float32))
        m["masks"] = f(np.stack([up, up.T]))
    return m


ACTIVE_CORES = (0, 1, 4, 5)


def kernel(**inputs):
    nc = build_program()
    real = [prep_inputs(inputs, b) for b in range(4)]
    big = ("x", "ctx", "cc", "modw", "m_in_w", "m_out_w", "a_in_w", "a_out_w", "s_in_w", "s_glu_w", "s_out_w", "s_brt", "s_crp")
    idle = {k: (np.zeros_like(v) if k.startswith(big) else v) for k, v in real[0].items()}
    in_maps = [idle] * 8
    for b, core in enumerate(ACTIVE_CORES):
        in_maps[core] = real[b]
    res = run_bass_kernel_spmd(nc, in_maps, core_ids=list(range(8)))
    out = np.stack([np.asarray(res.results[core]["out"], dtype=np.float32) for core in ACTIVE_CORES], axis=0)
    return out
```

```python
import os
import numpy as np
from contextlib import ExitStack
import concourse.bass as bass
import concourse.mybir as mybir
from concourse.bass_utils import run_bass_kernel_spmd

F32 = mybir.dt.float32
BF16 = mybir.dt.bfloat16
I32 = mybir.dt.int32
AF = mybir.ActivationFunctionType
ALU = mybir.AluOpType
AX = mybir.AxisListType

D = 1024
NCTX = 256
NLAT = 4096
EPS = 1e-6
M_IN = 6208
PI = float(np.pi)


class Res:
    __slots__ = ("w", "r", "name")

    def __init__(self, name=""):
        self.w = {}
        self.r = {}
        self.name = name


class Sched:
    def __init__(self, nc, es):
        self.nc = nc
        self.eng = {"pe": nc.tensor, "act": nc.scalar, "dve": nc.vector, "pool": nc.gpsimd, "sp": nc.sync}
        self.sem = {}
        self.cnt = {}
        self.known = {e: {} for e in self.eng}
        for e in self.eng:
            self.sem[e] = es.enter_context(nc.semaphore("s_" + e))
            self.cnt[e] = 0
        self.NDS = 8
        self.dslot = {}
        for q in ("sp", "pool"):
            for i in range(self.NDS):
                k = "d_%s%d" % (q, i)
                self.sem[k] = es.enter_context(nc.semaphore(k))
                self.cnt[k] = 0
            self.dslot[q] = 0
        self.ninst = 0

    def _wait(self, e, evs):
        kn = self.known[e]
        for k, v in evs.items():
            if v <= 0 or (e == "pe" and k == "pe") or kn.get(k, 0) >= v:
                continue
            self.eng[e].wait_ge(self.sem[k], v)
            kn[k] = v

    @staticmethod
    def _deps(r, w, wa):
        evs = {}

        def add(d):
            for k, v in d.items():
                if evs.get(k, 0) < v:
                    evs[k] = v
        for x in r:
            add(x.w)
        for x in w:
            add(x.w)
            add(x.r)
        for x in wa:
            add(x.r)
        return evs

    @staticmethod
    def _commit(k, v, r, w, wa):
        for x in r:
            if x.r.get(k, 0) < v:
                x.r[k] = v
        for x in w:
            if x.w.get(k, 0) < v:
                x.w[k] = v
        for x in wa:
            if x.w.get(k, 0) < v:
                x.w[k] = v

    def op(self, e, fn, r=(), w=(), wa=(), inc=True):
        self._wait(e, self._deps(r, w, wa))
        ins = fn(self.eng[e])
        if inc:
            self.cnt[e] += 1
            ins.then_inc(self.sem[e], 1)
            self._commit(e, self.cnt[e], r, w, wa)
        else:
            self._commit(e, self.cnt[e] + 1, r, w, wa)
        self.ninst += 1
        return ins

    def dma(self, q, out, in_, r=(), w=(), wa=(), **kw):
        i = self.dslot[q]
        self.dslot[q] = (i + 1) % self.NDS
        k = "d_%s%d" % (q, i)
        evs = self._deps(r, w, wa)
        evs[k] = max(evs.get(k, 0), self.cnt[k])
        self._wait(q, evs)
        ins = self.eng[q].dma_start(out=out, in_=in_, **kw)
        self.cnt[k] += 16
        ins.then_inc(self.sem[k], 16)
        self._commit(k, self.cnt[k], r, w, wa)
        self.ninst += 1
        return ins

    def barrier(self):
        evs = {k: v for k, v in self.cnt.items() if v > 0}
        for e in self.eng:
            self._wait(e, dict(evs))


class RR:
    def __init__(self, tiles):
        self.t = tiles
        self.r = [Res() for _ in tiles]
        self.i = 0

    def next(self):
        i = self.i
        self.i = (i + 1) % len(self.t)
        return self.t[i], self.r[i]


def build_program(nlat=NLAT, layers=(0, 1, 2, 3)):
    T = NCTX + nlat
    NTT = T // 128
    BLKS = [(0, NCTX)] + [(NCTX + 512 * i, 512) for i in range(nlat // 512)]
    nc = bass.Bass("TRN2", target_bir_lowering=False)
    es = ExitStack()
    es.enter_context(nc.allow_low_precision("bf16 matmul operands, fp32 accumulation"))
    S = Sched(nc, es)
    uid = [0]

    def mk(stack):
        def sb(shape, dt=F32, name="t"):
            uid[0] += 1
            return stack.enter_context(nc.sbuf_tensor("%s_%d" % (name, uid[0]), list(shape), dt))

        def ps(shape, dt=F32, name="p"):
            uid[0] += 1
            return stack.enter_context(nc.psum_tensor("%s_%d" % (name, uid[0]), list(shape), dt))
        return sb, ps

    def din(name, shape, dt=F32):
        return nc.dram_tensor(name, list(shape), dt, kind="ExternalInput")

    def dscr(name, shape, dt=F32):
        return nc.dram_tensor(name, list(shape), dt)

    V = lambda fn, r=(), w=(), wa=(): S.op("dve", fn, r, w, wa)
    A = lambda fn, r=(), w=(), wa=(): S.op("act", fn, r, w, wa)
    G = lambda fn, r=(), w=(), wa=(): S.op("pool", fn, r, w, wa)
    M = lambda fn, r=(), w=(), wa=(), inc=True: S.op("pe", fn, r, w, wa, inc)

    x_in = din("x", [nlat, D])
    ctx_in = din("ctx", [NCTX, D])
    cc_in = din("cc", [128, 8, 2])
    ident_in = din("ident", [128, 128])
    fnw_in = din("fnw", [128, 8])
    out_t = nc.dram_tensor("out", [nlat, D], F32, kind="ExternalOutput")
    L = {}
    for i in layers:
        L[i] = dict(normw=din("normw%d" % i, [128, 8]), modw=din("modw%d" % i, [D, 3 * D]), modb=din("modb%d" % i, [128, 24]))
        kind, j = i % 3, i // 3
        if kind == 0:
            L[i].update(inw=din("m_in_w%d" % j, [D, M_IN]), convw=din("m_convw%d" % j, [128, 32, 5]), convb=din("m_convb%d" % j, [128, 32]),
                        alog=din("m_alog%d" % j, [64, 1]), dtb=din("m_dtb%d" % j, [64, 1]), dvec=din("m_dvec%d" % j, [2048]),
                        mnw=din("m_normw%d" % j, [2048]), outw=din("m_out_w%d" % j, [2048, D]))
        elif kind == 1:
            L[i].update(inw=din("a_in_w", [D, 2560]), qkw=din("a_qkw", [128, 2]), outw=din("a_out_w", [D, D]),
                        rope=din("a_rope", [2, 128, nlat]), perm=din("a_perm", [128, 128]), bones=din("a_bones", [128, 128]))
        else:
            L[i].update(inw=din("s_in_w", [D, 2048]), lam=din("s_lam", [128, 3, 64]), brt=din("s_brt", [2, 128, 2, 32, 128]),
                        crp=din("s_crp", [2, 128, 2, 32, 32]), sd=din("s_sd", [128, 8]), gluw=din("s_glu_w", [D, D]),
                        glub=din("s_glub", [128, 8]), outw=din("s_out_w", [D, D]))
    masks_in = din("masks", [2, 128, 128]) if any(i % 3 == 0 for i in layers) else None

    hT = dscr("hT", [D, T])
    hT_ap = hT.ap()
    r_hT = {(c, tt): Res() for c in range(8) for tt in range(NTT)}

    def hres(c, t0, nt):
        return [r_hT[(c, tt)] for tt in range(t0 // 128, (t0 + nt + 127) // 128)]

    def hres_all(t0, nt):
        out = []
        for c in range(8):
            out += hres(c, t0, nt)
        return out

    gsb, gps = mk(es)
    ident = gsb([128, 128], F32, "ident")
    r_const = Res("const")
    S.dma("sp", ident[:], ident_in.ap(), w=[r_const])
    identb = gsb([128, 128], BF16, "identb")
    ones_bf = gsb([128, 128], BF16, "ones")
    G(lambda e: e.memset(ones_bf[:], 1.0), wa=[r_const])
    V(lambda e: e.tensor_copy(identb[:], ident[:]), r=[r_const], wa=[r_const])
    fnw = gsb([128, 8], F32, "fnw")
    S.dma("sp", fnw[:], fnw_in.ap(), wa=[r_const])
    cc = gsb([128, 8, 2], F32, "cc")
    S.dma("sp", cc[:], cc_in.ap(), wa=[r_const])
    scs = gsb([128, 8, 2], F32, "scs")
    A(lambda e: e.activation(scs[:], cc[:], AF.Silu), r=[r_const], wa=[r_const])
    mod_sc = gsb([128, 8, 2], F32, "mod_sc")
    mod_bi = gsb([128, 8, 2], F32, "mod_bi")
    mod_gt = gsb([128, 8, 2], F32, "mod_gt")
    r_mod = Res("mod")
    S.barrier()

    with ExitStack() as ph:
        sb, ps = mk(ph)
        xin = RR([sb([128, D], F32, "xin") for _ in range(2)])
        tp = RR([ps([128, 512], F32, "tp") for _ in range(2)])
        xo = RR([sb([128, 8, 128], F32, "xo") for _ in range(2)])
        for tt in range(NTT):
            xt, r_xt = xin.next()
            src = ctx_in.ap()[tt * 128:(tt + 1) * 128, :] if tt < 2 else x_in.ap()[(tt - 2) * 128:(tt - 1) * 128, :]
            S.dma("sp", xt[:], src, w=[r_xt])
            ot, r_ot = xo.next()
            for half in range(2):
                pt, r_pt = tp.next()
                for j in range(4):
                    c = half * 4 + j
                    M(lambda e: e.transpose(pt[:, j * 128:(j + 1) * 128], xt[:, c * 128:(c + 1) * 128], ident[:]),
                      r=[r_xt, r_const], w=[r_pt] if j == 0 else [], wa=[r_pt] if j else [])
                dst = ot[:, half * 4:(half + 1) * 4, :]
                if half:
                    A(lambda e: e.copy(dst, pt[:].rearrange("p (j t) -> p j t", j=4)), r=[r_pt], wa=[r_ot])
                else:
                    V(lambda e: e.tensor_copy(dst, pt[:].rearrange("p (j t) -> p j t", j=4)), r=[r_pt], w=[r_ot])
            S.dma("pool", hT_ap[:, tt * 128:(tt + 1) * 128].rearrange("(c p) t -> p c t", p=128), ot[:], r=[r_ot],
                  wa=[r_hT[(c, tt)] for c in range(8)])
        S.barrier()

    def pre_pass(lay, sb, ps, inT, r_inT, final=False):
        if not final:
            mw = RR([sb([128, 8, 512], F32, "modw") for _ in range(2)])
            mp = RR([ps([128, 512], F32, "modp") for _ in range(2)])
            modT = sb([128, 24, 2], F32, "modT")
            r_modT = Res()
            modb = sb([128, 24], F32, "modb")
            normw = sb([128, 8], F32, "normw")
            r_small = Res()
            S.dma("sp", modb[:], lay["modb"].ap(), w=[r_small])
            S.dma("sp", normw[:], lay["normw"].ap(), wa=[r_small])
            for cg in range(6):
                wt, r_wt = mw.next()
                S.dma("sp", wt[:], lay["modw"].ap()[:, cg * 512:(cg + 1) * 512].rearrange("(k p) n -> p k n", p=128), w=[r_wt])
                for c4 in range(4):
                    pt, r_pt = mp.next()
                    for k in range(8):
                        M(lambda e: e.matmul(pt[:, 0:2], wt[:, k, c4 * 128:(c4 + 1) * 128], scs[:, k, :], start=(k == 0), stop=(k == 7)),
                          r=[r_wt, r_const], w=[r_pt] if k == 0 else [], wa=[r_pt] if k else [])
                    col = cg * 4 + c4
                    V(lambda e: e.tensor_scalar(modT[:, col, :], pt[:, 0:2], modb[:, col:col + 1], None, ALU.add),
                      r=[r_pt, r_small], wa=[r_modT])
            V(lambda e: e.tensor_scalar(mod_sc[:], modT[:, 8:16, :], 1.0, None, ALU.add), r=[r_modT], w=[r_mod])
            V(lambda e: e.tensor_tensor(mod_sc[:], mod_sc[:], normw[:].unsqueeze(2).broadcast_to([128, 8, 2]), ALU.mult), r=[r_small], w=[r_mod])
            V(lambda e: e.tensor_copy(mod_bi[:], modT[:, 0:8, :]), r=[r_modT], w=[r_mod])
            V(lambda e: e.tensor_copy(mod_gt[:], modT[:, 16:24, :]), r=[r_modT], w=[r_mod])
        hb = RR([sb([128, 8, 512], F32, "hb") for _ in range(2)])
        sq = RR([sb([128, 8, 512], BF16, "sq") for _ in range(2)])
        ssp = RR([ps([128, 512], F32, "ssp") for _ in range(2)])
        rstd = RR([sb([128, 512], F32, "rstd") for _ in range(2)])
        tmp = RR([sb([128, 512], F32, "ntmp") for _ in range(3)])
        if final:
            hn = RR([sb([128, 8, 512], F32, "hn") for _ in range(2)])
            tp = RR([ps([128, 512], F32, "ftp") for _ in range(2)])
            ot = RR([sb([128, D], F32, "fot") for _ in range(2)])
            r_out = Res()
        for (t0, nt) in BLKS:
            j = 1 if t0 < NCTX else 0
            if final and j == 1:
                continue
            h, r_h = hb.next()
            S.dma("sp", h[:, :, 0:nt], hT_ap[:, t0:t0 + nt].rearrange("(c p) t -> p c t", p=128), r=hres_all(t0, nt), w=[r_h])
            q, r_q = sq.next()
            A(lambda e: e.activation(q[:, :, 0:nt], h[:, :, 0:nt], AF.Square), r=[r_h], w=[r_q])
            sp_, r_sp = ssp.next()
            for c in range(8):
                M(lambda e: e.matmul(sp_[:, 0:nt], ones_bf[:], q[:, c, 0:nt], start=(c == 0), stop=(c == 7)),
                  r=[r_q, r_const], w=[r_sp] if c == 0 else [], wa=[r_sp] if c else [], inc=(c == 7))
            rs, r_rs = rstd.next()
            A(lambda e: e.activation(rs[:, 0:nt], sp_[:, 0:nt], AF.Sqrt, bias=EPS, scale=1.0 / D), r=[r_sp], w=[r_rs])
            V(lambda e: e.reciprocal(rs[:, 0:nt], rs[:, 0:nt]), w=[r_rs])
            if not final:
                for c in range(8):
                    tm, r_tm = tmp.next()
                    V(lambda e: e.tensor_tensor(tm[:, 0:nt], h[:, c, 0:nt], rs[:, 0:nt], ALU.mult), r=[r_h, r_rs], w=[r_tm])
                    A(lambda e: e.activation(inT[:, c, t0:t0 + nt], tm[:, 0:nt], AF.Identity, bias=mod_bi[:, c, j:j + 1], scale=mod_sc[:, c, j:j + 1]),
                      r=[r_tm, r_mod], wa=[r_inT])
            else:
                hn_, r_hn = hn.next()
                for c in range(8):
                    V(lambda e: e.scalar_tensor_tensor(hn_[:, c, 0:nt], h[:, c, 0:nt], fnw[:, c:c + 1], rs[:, 0:nt], ALU.mult, ALU.mult),
                      r=[r_h, r_rs, r_const], w=[r_hn] if c == 0 else [], wa=[r_hn] if c else [])
                for tl in range(nt // 128):
                    o, r_o = ot.next()
                    for half in range(2):
                        pt, r_pt = tp.next()
                        for jj in range(4):
                            c = half * 4 + jj
                            M(lambda e: e.transpose(pt[:, jj * 128:(jj + 1) * 128], hn_[:, c, tl * 128:(tl + 1) * 128], ident[:]),
                              r=[r_hn, r_const], w=[r_pt] if jj == 0 else [], wa=[r_pt] if jj else [])
                        if half:
                            A(lambda e: e.copy(o[:, 512:1024], pt[:]), r=[r_pt], wa=[r_o])
                        else:
                            V(lambda e: e.tensor_copy(o[:, 0:512], pt[:]), r=[r_pt], w=[r_o])
                    row = t0 - NCTX + tl * 128
                    S.dma("pool", out_t.ap()[row:row + 128, :], o[:], r=[r_o], wa=[r_out])
        if final:
            evs = dict(r_out.w)
            S._wait("sp", evs)

    def linear_fm(sb, ps, act, r_act, KC, W_ap, col_chunks, epi, blks=None, t_off=0):
        wts = RR([sb([128, KC, 128], BF16, "lw") for _ in range(3)])
        pts = RR([ps([128, 512], F32, "lp") for _ in range(2)])
        for ci, (c0, ncol) in enumerate(col_chunks):
            wt, r_wt = wts.next()
            S.dma("pool", wt[:, :, 0:ncol], W_ap[:, c0:c0 + ncol].rearrange("(k p) n -> p k n", p=128), w=[r_wt])
            for (t0, nt) in (blks or BLKS):
                pt, r_pt = pts.next()
                for k in range(KC):
                    M(lambda e: e.matmul(pt[0:ncol, 0:nt], wt[:, k, 0:ncol], act[:, k, t0 - t_off:t0 - t_off + nt], start=(k == 0), stop=(k == KC - 1)),
                      r=[r_wt, r_act], w=[r_pt] if k == 0 else [], wa=[r_pt] if k else [], inc=(k == KC - 1))
                epi(ci, c0, ncol, t0, nt, pt, r_pt)

    def linear_tm(sb, ps, act, r_act, KC, W_ap, c0, ncols, epi):
        wts = RR([sb([128, KC, 512], BF16, "lwt") for _ in range(2)])
        pts = RR([ps([128, 512], F32, "lpt") for _ in range(2)])
        for g0 in range(0, ncols, 512):
            n = min(512, ncols - g0)
            wt, r_wt = wts.next()
            S.dma("pool", wt[:, :, 0:n], W_ap[:, c0 + g0:c0 + g0 + n].rearrange("(k p) n -> p k n", p=128), w=[r_wt])
            for tt in range(NTT):
                pt, r_pt = pts.next()
                for k in range(KC):
                    M(lambda e: e.matmul(pt[:, 0:n], act[:, k, tt * 128:(tt + 1) * 128], wt[:, k, 0:n], start=(k == 0), stop=(k == KC - 1)),
                      r=[r_wt, r_act], w=[r_pt] if k == 0 else [], wa=[r_pt] if k else [], inc=(k == KC - 1))
                epi(g0, n, tt, pt, r_pt)

    def make_resid_epi(sb):
        hts = RR([sb([128, 512], F32, "rh") for _ in range(3)])

        def epi(ci, c0, ncol, t0, nt, pt, r_pt):
            c = c0 // 128
            j = 1 if t0 < NCTX else 0
            ht, r_ht = hts.next()
            S.dma("sp", ht[:, 0:nt], hT_ap[c * 128:(c + 1) * 128, t0:t0 + nt], r=hres(c, t0, nt), w=[r_ht])
            V(lambda e: e.scalar_tensor_tensor(ht[:, 0:nt], pt[:, 0:nt], mod_gt[:, c, j:j + 1], ht[:, 0:nt], ALU.mult, ALU.add),
              r=[r_pt, r_mod], w=[r_ht])
            S.dma("pool", hT_ap[c * 128:(c + 1) * 128, t0:t0 + nt], ht[:, 0:nt], r=[r_ht], wa=hres(c, t0, nt))
        return epi

    def mamba_layer(lay):
        x_tm = dscr("x_tm%d" % uid[0], [T, 2048], BF16)
        B_tm = dscr("B_tm%d" % uid[0], [T, 1024], BF16)
        BT_d = dscr("BT_d%d" % uid[0], [8, 128, T], BF16)
        CT_d = dscr("CT_d%d" % uid[0], [8, 128, T], BF16)
        sz_tm = dscr("sz_tm%d" % uid[0], [T, 2048], BF16)
        laT_d = dscr("laT_d%d" % uid[0], [64, T], F32)
        ltot_d = dscr("ltot_d%d" % uid[0], [NTT, 64], F32)
        Yacc = dscr("Yacc%d" % uid[0], [T, 2048], F32)
        uid[0] += 1
        r_xtm, r_Btm, r_BT, r_CT, r_sz, r_laT, r_ltot = Res(), Res(), Res(), Res(), Res(), Res(), Res()
        r_Y = [Res() for _ in range(NTT)]
        with ExitStack() as lst:
            lsb, lps = mk(lst)
            la_tm = lsb([128, NTT, 64], F32, "la_tm")
            dt_tm = lsb([128, NTT, 64], F32, "dt_tm")
            LTB = lsb([128, NTT, 64], F32, "LTB")
            r_tabs = Res()
            with ExitStack() as st1:
                sb1, ps1 = mk(st1)
                inT = sb1([128, 8, T], BF16, "inT")
                r_inT = Res()
                with ExitStack() as ph:
                    sb, ps = mk(ph)
                    pre_pass(lay, sb, ps, inT, r_inT)
                    S.barrier()
                with ExitStack() as ph:
                    sb, ps = mk(ph)
                    convw = sb([128, 32, 5], F32, "convw")
                    convb = sb([128, 32], F32, "convb")
                    r_cv = Res()
                    S.dma("sp", convw[:], lay["convw"].ap(), w=[r_cv])
                    S.dma("sp", convb[:], lay["convb"].ap(), wa=[r_cv])
                    xr = sb([128, T + 8], F32, "xr")
                    r_xr = Res()
                    G(lambda e: e.memset(xr[:], 0.0), w=[r_xr])
                    acc = sb([128, T], F32, "cacc")
                    r_acc = Res()
                    xo = RR([sb([128, T], BF16, "cxo") for _ in range(2)])
                    tps = RR([ps([128, 512], BF16, "ctp") for _ in range(2)])
                    tos = RR([sb([128, 512], BF16, "cto") for _ in range(3)])
                    state = {}

                    def epi_xbc(ci, c0, ncol, t0, nt, pt, r_pt):
                        off = 2 if t0 < NCTX else 6
                        if ci % 2:
                            A(lambda e: e.copy(xr[:, t0 + off:t0 + off + nt], pt[:, 0:nt]), r=[r_pt], wa=[r_xr])
                        else:
                            V(lambda e: e.tensor_copy(xr[:, t0 + off:t0 + off + nt], pt[:, 0:nt]), r=[r_pt], wa=[r_xr])
                        if t0 + nt < T:
                            return
                        segs = [(0, NCTX, 0), (NCTX, nlat, 4)]
                        for (s0, sn, dl) in segs:
                            A(lambda e: e.activation(acc[:, s0:s0 + sn], xr[:, s0 + dl:s0 + dl + sn], AF.Identity,
                                                     bias=convb[:, ci:ci + 1], scale=convw[:, ci, 0:1]), r=[r_xr, r_cv], wa=[r_acc])
                            for k in range(1, 5):
                                V(lambda e: e.scalar_tensor_tensor(acc[:, s0:s0 + sn], xr[:, s0 + dl + k:s0 + dl + k + sn], convw[:, ci, k:k + 1],
                                                                   acc[:, s0:s0 + sn], ALU.mult, ALU.add), r=[r_xr, r_cv], w=[r_acc])
                        o, r_o = xo.next()
                        A(lambda e: e.activation(o[:], acc[:], AF.Silu), r=[r_acc], w=[r_o])
                        if ci < 24:
                            dst, col0, r_d = (x_tm, ci * 128, r_xtm) if ci < 16 else (B_tm, (ci - 16) * 128, r_Btm)
                            for t4 in range(0, NTT, 4):
                                n4 = min(4, NTT - t4)
                                tp_, r_tp = tps.next()
                                for q in range(n4):
                                    M(lambda e: e.transpose(tp_[:, q * 128:(q + 1) * 128], o[:, (t4 + q) * 128:(t4 + q + 1) * 128], identb[:]),
                                      r=[r_o, r_const], w=[r_tp] if q == 0 else [], wa=[r_tp] if q else [])
                                to, r_to = tos.next()
                                A(lambda e: e.copy(to[:, 0:n4 * 128], tp_[:, 0:n4 * 128]), r=[r_tp], w=[r_to])
                                S.dma("sp", dst.ap()[t4 * 128:(t4 + n4) * 128, col0:col0 + 128].rearrange("(q p) c -> p q c", p=128),
                                      to[:, 0:n4 * 128].rearrange("p (q c) -> p q c", q=n4), r=[r_to], wa=[r_d])
                        if ci >= 16:
                            gg = (ci - 16) % 8
                            dd, r_dd = (BT_d, r_BT) if ci < 24 else (CT_d, r_CT)
                            S.dma("sp", dd.ap()[gg], o[:], r=[r_o], wa=[r_dd])

                    linear_fm(sb, ps, inT, r_inT, 8, lay["inw"].ap(), [(2048 + 128 * i, 128) for i in range(32)], epi_xbc)
                    S.barrier()
                with ExitStack() as ph:
                    sb, ps = mk(ph)
                    dtT = sb([64, NTT, 128], F32, "dtT")
                    dA = sb([64, NTT, 128], F32, "dA")
                    laP = sb([64, NTT, 128], F32, "laP")
                    laT = sb([64, NTT, 128], F32, "laT")
                    rp = sb([64, NTT, 128], F32, "rp")
                    r_dt, r_dA, r_laP, r_laTs, r_rp = Res(), Res(), Res(), Res(), Res()
                    sm = sb([64, 4], F32, "dtsm")
                    r_sm = Res()
                    S.dma("sp", sm[:, 0:1], lay["alog"].ap(), w=[r_sm])
                    S.dma("sp", sm[:, 1:2], lay["dtb"].ap(), wa=[r_sm])
                    A(lambda e: e.activation(sm[:, 2:3], sm[:, 0:1], AF.Exp), r=[r_sm], wa=[r_sm])
                    V(lambda e: e.tensor_scalar(sm[:, 3:4], sm[:, 2:3], -1.0, None, ALU.mult), r=[r_sm], wa=[r_sm])
                    G(lambda e: e.memset(rp[:], 1.0), w=[r_rp])
                    G(lambda e: e.memset(rp[:, :, 0:1], 0.0), w=[r_rp])
                    dtf = dtT[:].rearrange("p c l -> p (c l)")

                    def epi_dt(ci, c0, ncol, t0, nt, pt, r_pt):
                        A(lambda e: e.activation(dtf[:, t0:t0 + nt], pt[0:64, 0:nt], AF.Exp, bias=sm[:, 1:2], scale=1.0), r=[r_pt, r_sm], wa=[r_dt])
                    linear_fm(sb, ps, inT, r_inT, 8, lay["inw"].ap(), [(6144, 64)], epi_dt)
                    A(lambda e: e.activation(dtf, dtf, AF.Ln, bias=1.0, scale=1.0), w=[r_dt])
                    V(lambda e: e.tensor_scalar(dA[:], dtT[:], sm[:, 3:4], None, ALU.mult), r=[r_dt, r_sm], w=[r_dA])
                    V(lambda e: e.tensor_tensor_scan(laP[:].rearrange("p c l -> p (c l)"), rp[:].rearrange("p c l -> p (c l)"),
                                                     dA[:].rearrange("p c l -> p (c l)"), 0.0, ALU.mult, ALU.add), r=[r_rp, r_dA], w=[r_laP])
                    V(lambda e: e.tensor_copy(laT[0:32], laP[0:32]), r=[r_laP], w=[r_laTs])
                    V(lambda e: e.tensor_tensor(laT[32:64], dA[32:64], laP[32:64], ALU.subtract), r=[r_laP, r_dA], wa=[r_laTs])
                    V(lambda e: e.tensor_tensor(laT[32:64], laT[32:64], laP[32:64, :, 127:128].broadcast_to([32, NTT, 128]), ALU.add), r=[r_laP], w=[r_laTs])
                    S.dma("sp", laT_d.ap(), laT[:].rearrange("p c l -> p (c l)"), r=[r_laTs], w=[r_laT])
                    tpp = RR([ps([128, 64], F32, "dtp") for _ in range(2)])
                    for tt in range(NTT):
                        for (src, r_src, dst) in ((laT, r_laTs, la_tm), (dtT, r_dt, dt_tm)):
                            tp_, r_tp = tpp.next()
                            M(lambda e: e.transpose(tp_[:], src[:, tt, :], ident[0:64, 0:64]), r=[r_src, r_const], w=[r_tp])
                            V(lambda e: e.tensor_copy(dst[:, tt, :], tp_[:]), r=[r_tp], wa=[r_tabs])
                    S.dma("sp", ltot_d.ap()[:, 0:32], la_tm[127:128, :, 0:32], r=[r_tabs], w=[r_ltot])
                    S.dma("sp", ltot_d.ap()[:, 32:64], la_tm[0:1, :, 32:64], r=[r_tabs], wa=[r_ltot])
                    S.dma("sp", LTB[:].rearrange("p c h -> p (c h)"), bass.AP(ltot_d, 0, [[0, 128], [1, NTT * 64]]), r=[r_ltot], wa=[r_tabs])
                    S.barrier()
                with ExitStack() as ph:
                    sb, ps = mk(ph)
                    zo = RR([sb([128, 512], BF16, "zo") for _ in range(3)])

                    def epi_z(g0, n, tt, pt, r_pt):
                        o, r_o = zo.next()
                        A(lambda e: e.activation(o[:, 0:n], pt[:, 0:n], AF.Silu), r=[r_pt], w=[r_o])
                        S.dma("sp", sz_tm.ap()[tt * 128:(tt + 1) * 128, g0:g0 + n], o[:, 0:n], r=[r_o], wa=[r_sz])
                    linear_tm(sb, ps, inT, r_inT, 8, lay["inw"].ap(), 0, 2048, epi_z)
                    S.barrier()
            with ExitStack() as ph:
                sb, ps = mk(ph)
                masks = sb([128, 2, 128], F32, "masks")
                r_mk = Res()
                S.dma("sp", masks[:], masks_in.ap().rearrange("d s l -> s d l"), w=[r_mk])
                xt_p = RR([sb([128, 2048], BF16, "sx") for _ in range(2)])
                bt_p = RR([sb([128, 1024], BF16, "sB") for _ in range(2)])
                BTs_p = RR([sb([128, 8, 128], BF16, "sBT") for _ in range(2)])
                CTs_p = RR([sb([128, 8, 128], BF16, "sCT") for _ in range(2)])
                LaB_p = RR([sb([128, 32, 128], F32, "sLaB") for _ in range(2)])
                dmat_p = RR([sb([128, 32, 128], F32, "dmat") for _ in range(2)])
                decay_p = RR([sb([128, 32, 128], BF16, "decay") for _ in range(2)])
                wT_p = RR([sb([128, 32, 128], BF16, "wT") for _ in range(2)])
                CBm_p = RR([sb([128, 8, 128], BF16, "CBm") for _ in range(2)])
                xdt_p = RR([sb([128, 2048], BF16, "xdt") for _ in range(2)])
                xw_p = RR([sb([128, 2048], BF16, "xw") for _ in range(2)])
                sml = sb([128, 4, 32], F32, "ssml")
                r_sml = Res()
                ST = sb([128, 2048], F32, "ST")
                r_ST = Res()
                prevb = sb([128, 2048], BF16, "prevb")
                r_prevb = Res()
                ysb = sb([128, 2048], F32, "ysb")
                r_ysb = Res()
                eyo = sb([128, 1024], F32, "eyo")
                r_eyo = Res()
                yin_p = RR([sb([128, 2048], F32, "yin") for _ in range(2)])
                cbp = ps([128, 8, 128], F32, "cbp")
                r_cbp = Res()
                ydp = ps([128, 1024], F32, "ydp")
                r_ydp = Res()
                yop = ps([128, 1024], F32, "yop")
                r_yop = Res()
                stp = ps([128, 1024], F32, "stp")
                r_stp = Res()
                for dr in range(2):
                    order = list(range(NTT)) if dr == 0 else [1, 0] + list(range(NTT - 1, 1, -1))
                    V(lambda e: e.memset(ST[:], 0.0), w=[r_ST])
                    hc = dr * 32
                    for c in order:
                        tok = slice(c * 128, (c + 1) * 128)
                        xt, r_xt = xt_p.next()
                        S.dma("sp", xt[:], x_tm.ap()[tok, :], r=[r_xtm], w=[r_xt])
                        bt, r_bt = bt_p.next()
                        S.dma("sp", bt[:], B_tm.ap()[tok, :], r=[r_Btm], w=[r_bt])
                        BTs, r_BTs = BTs_p.next()
                        S.dma("sp", BTs[:], BT_d.ap()[:, :, tok].rearrange("g n t -> n g t"), r=[r_BT], w=[r_BTs])
                        CTs, r_CTs = CTs_p.next()
                        S.dma("sp", CTs[:], CT_d.ap()[:, :, tok].rearrange("g n t -> n g t"), r=[r_CT], w=[r_CTs])
                        LaB, r_LaB = LaB_p.next()
                        S.dma("sp", LaB[:], bass.AP(laT_d, hc * T + c * 128, [[0, 128], [T, 32], [1, 128]]), r=[r_laT], w=[r_LaB])
                        la_c = la_tm[:, c, hc:hc + 32]
                        dmat, r_dmat = dmat_p.next()
                        decay, r_decay = decay_p.next()
                        wT, r_wT = wT_p.next()
                        CBm, r_CBm = CBm_p.next()
                        xdt, r_xdt = xdt_p.next()
                        xw, r_xw = xw_p.next()
                        A(lambda e: e.activation(sml[:, 0, :], la_c, AF.Exp), r=[r_tabs], w=[r_sml])
                        V(lambda e: e.tensor_tensor(sml[:, 3, :], LTB[:, c, hc:hc + 32], la_c, ALU.subtract), r=[r_tabs], w=[r_sml])
                        V(lambda e: e.tensor_single_scalar(sml[:, 3, :], sml[:, 3, :], 0.0, ALU.min), w=[r_sml])
                        A(lambda e: e.activation(sml[:, 1, :], sml[:, 3, :], AF.Exp), w=[r_sml])
                        A(lambda e: e.activation(sml[:, 2, :], LTB[:, c, hc:hc + 32], AF.Exp), r=[r_tabs], w=[r_sml])
                        for g in range(8):
                            M(lambda e: e.matmul(cbp[:, g, :], BTs[:, g, :], CTs[:, g, :], start=True, stop=True), r=[r_BTs, r_CTs],
                              w=[r_cbp] if g == 0 else [], wa=[r_cbp] if g else [], inc=(g == 7))
                        V(lambda e: e.tensor_tensor(CBm[:], cbp[:], masks[:, dr:dr + 1, :].broadcast_to([128, 8, 128]), ALU.mult),
                          r=[r_cbp, r_mk], w=[r_CBm])
                        for h in range(32):
                            V(lambda e: e.tensor_scalar(dmat[:, h, :], LaB[:, h, :], la_tm[:, c, hc + h:hc + h + 1], 0.0, ALU.subtract, ALU.min),
                              r=[r_LaB, r_tabs], w=[r_dmat] if h == 0 else [], wa=[r_dmat] if h else [])
                        A(lambda e: e.activation(decay[:], dmat[:], AF.Exp), r=[r_dmat], w=[r_decay])
                        V(lambda e: e.tensor_tensor(wT[:].rearrange("p (g h) l -> p g h l", g=8), decay[:].rearrange("p (g h) l -> p g h l", g=8),
                                                    CBm[:].unsqueeze(2).broadcast_to([128, 8, 4, 128]), ALU.mult), r=[r_decay, r_CBm], w=[r_wT])
                        V(lambda e: e.tensor_tensor(xdt[:].rearrange("p (h q) -> p h q", h=32), xt[:].rearrange("p (h q) -> p h q", h=32),
                                                    dt_tm[:, c, hc:hc + 32].unsqueeze(2).broadcast_to([128, 32, 64]), ALU.mult), r=[r_xt, r_tabs], w=[r_xdt])
                        V(lambda e: e.tensor_tensor(xw[:].rearrange("p (h q) -> p h q", h=32), xdt[:].rearrange("p (h q) -> p h q", h=32),
                                                    sml[:, 1, :].unsqueeze(2).broadcast_to([128, 32, 64]), ALU.mult), r=[r_xdt, r_sml], w=[r_xw])
                        A(lambda e: e.copy(prevb[:], ST[:]), r=[r_ST], w=[r_prevb])
                        if dr == 1:
                            yin, r_yin = yin_p.next()
                            S.dma("sp", yin[:], Yacc.ap()[tok, :], r=[r_Y[c]], w=[r_yin])
                        for gh in range(2):
                            cs = slice(gh * 1024, (gh + 1) * 1024)
                            for hl in range(16):
                                h = gh * 16 + hl
                                M(lambda e: e.matmul(ydp[:, hl * 64:(hl + 1) * 64], wT[:, h, :], xdt[:, h * 64:(h + 1) * 64], start=True, stop=True),
                                  r=[r_wT, r_xdt], w=[r_ydp] if hl == 0 else [], wa=[r_ydp] if hl else [], inc=(hl == 15))
                            for gl in range(4):
                                g = gh * 4 + gl
                                M(lambda e: e.matmul(yop[:, gl * 256:(gl + 1) * 256], CTs[:, g, :], prevb[:, g * 256:(g + 1) * 256], start=True, stop=True),
                                  r=[r_CTs, r_prevb], w=[r_yop] if gl == 0 else [], wa=[r_yop] if gl else [], inc=(gl == 3))
                            for gl in range(4):
                                g = gh * 4 + gl
                                M(lambda e: e.matmul(stp[:, gl * 256:(gl + 1) * 256], bt[:, g * 128:(g + 1) * 128], xw[:, g * 256:(g + 1) * 256], start=True, stop=True),
                                  r=[r_bt, r_xw], w=[r_stp] if gl == 0 else [], wa=[r_stp] if gl else [], inc=(gl == 3))
                            if dr == 1:
                                V(lambda e: e.tensor_tensor(ysb[:, cs], ydp[:], yin[:, cs], ALU.add), r=[r_ydp, r_yin], w=[r_ysb] if gh == 0 else [], wa=[r_ysb] if gh else [])
                            else:
                                A(lambda e: e.copy(ysb[:, cs], ydp[:]), r=[r_ydp], w=[r_ysb] if gh == 0 else [], wa=[r_ysb] if gh else [])
                            for hl in range(16):
                                h = gh * 16 + hl
                                A(lambda e: e.activation(eyo[:, hl * 64:(hl + 1) * 64], yop[:, hl * 64:(hl + 1) * 64], AF.Identity, scale=sml[:, 0, h:h + 1]),
                                  r=[r_yop, r_sml], w=[r_eyo] if hl == 0 else [], wa=[r_eyo] if hl else [])
                            V(lambda e: e.tensor_tensor(ysb[:, cs], ysb[:, cs], eyo[:], ALU.add), r=[r_eyo], w=[r_ysb])
                            V(lambda e: e.tensor_tensor(ST[:, cs].rearrange("p (h q) -> p h q", h=16), ST[:, cs].rearrange("p (h q) -> p h q", h=16),
                                                        sml[:, 2, gh * 16:(gh + 1) * 16].unsqueeze(2).broadcast_to([128, 16, 64]), ALU.mult),
                              r=[r_sml, r_prevb], w=[r_ST])
                            V(lambda e: e.tensor_tensor(ST[:, cs], ST[:, cs], stp[:], ALU.add), r=[r_stp], w=[r_ST])
                        S.dma("pool", Yacc.ap()[tok, :], ysb[:], r=[r_ysb], w=[r_Y[c]])
                S.barrier()
            with ExitStack() as ph:
                sb, ps = mk(ph)
                dvec = sb([128, 2048], F32, "dvec")
                mnw = sb([128, 2048], F32, "mnw")
                r_dv = Res()
                S.dma("sp", dvec[:], bass.AP(lay["dvec"], 0, [[0, 128], [1, 2048]]), w=[r_dv])
                S.dma("sp", mnw[:], bass.AP(lay["mnw"], 0, [[0, 128], [1, 2048]]), wa=[r_dv])
                ow = sb([128, 16, D], BF16, "ow")
                r_ow = Res()
                for k4 in range(4):
                    S.dma("pool", ow[:, k4 * 4:(k4 + 1) * 4, :], lay["outw"].ap()[k4 * 512:(k4 + 1) * 512, :].rearrange("(k p) n -> p k n", p=128),
                          w=[r_ow] if k4 == 0 else [], wa=[r_ow] if k4 else [])
                y_p = RR([sb([128, 2048], F32, "ty") for _ in range(2)])
                x_p = RR([sb([128, 2048], BF16, "tx") for _ in range(2)])
                z_p = RR([sb([128, 2048], BF16, "tz") for _ in range(2)])
                g_p = RR([sb([128, 2048], F32, "tg") for _ in range(2)])
                gb_p = RR([sb([128, 2048], BF16, "tgb") for _ in range(2)])
                junk = sb([128, 2048], BF16, "tjunk")
                r_junk = Res()
                ss_p = RR([sb([128, 2], F32, "tss") for _ in range(2)])
                gT_p = RR([sb([128, 16, 512], BF16, "tgT") for _ in range(2)])
                tp_p = RR([ps([128, 512], BF16, "ttp") for _ in range(2)])
                op_p = RR([ps([128, 512], F32, "top") for _ in range(3)])
                ht_p = RR([sb([128, 8, 512], F32, "tht") for _ in range(2)])
                groups = [(0, 2)] + [(2 + 4 * i, 4) for i in range((NTT - 2) // 4)]
                for (tt0, ng) in groups:
                    j = 1 if tt0 < 2 else 0
                    t0, nt = tt0 * 128, ng * 128
                    gT, r_gT = gT_p.next()
                    for q4 in range(ng):
                        tt = tt0 + q4
                        tok = slice(tt * 128, (tt + 1) * 128)
                        y, r_y = y_p.next()
                        S.dma("sp", y[:], Yacc.ap()[tok, :], r=[r_Y[tt]], w=[r_y])
                        xt, r_xt = x_p.next()
                        S.dma("sp", xt[:], x_tm.ap()[tok, :], r=[r_xtm], w=[r_xt])
                        zt, r_zt = z_p.next()
                        S.dma("sp", zt[:], sz_tm.ap()[tok, :], r=[r_sz], w=[r_zt])
                        gt_, r_g = g_p.next()
                        V(lambda e: e.tensor_tensor(gt_[:], xt[:], dvec[:], ALU.mult), r=[r_xt, r_dv], w=[r_g])
                        V(lambda e: e.tensor_tensor(gt_[:], gt_[:], y[:], ALU.add), r=[r_y], w=[r_g])
                        V(lambda e: e.tensor_tensor(gt_[:], gt_[:], zt[:], ALU.mult), r=[r_zt], w=[r_g])
                        ss, r_ss = ss_p.next()
                        A(lambda e: e.activation(junk[:], gt_[:], AF.Square, accum_out=ss[:, 0:1]), r=[r_g], w=[r_junk, r_ss])
                        A(lambda e: e.activation(ss[:, 1:2], ss[:, 0:1], AF.Sqrt, bias=EPS, scale=1.0 / 2048), w=[r_ss])
                        V(lambda e: e.reciprocal(ss[:, 1:2], ss[:, 1:2]), w=[r_ss])
                        gb, r_gb = gb_p.next()
                        V(lambda e: e.scalar_tensor_tensor(gb[:], gt_[:], ss[:, 1:2], mnw[:], ALU.mult, ALU.mult), r=[r_g, r_ss, r_dv], w=[r_gb])
                        for k4 in range(4):
                            tp_, r_tp = tp_p.next()
                            for q in range(4):
                                k = k4 * 4 + q
                                M(lambda e: e.transpose(tp_[:, q * 128:(q + 1) * 128], gb[:, k * 128:(k + 1) * 128], identb[:]),
                                  r=[r_gb, r_const], w=[r_tp] if q == 0 else [], wa=[r_tp] if q else [], inc=(q == 3))
                            A(lambda e: e.copy(gT[:, k4 * 4:(k4 + 1) * 4, q4 * 128:(q4 + 1) * 128], tp_[:].rearrange("p (q t) -> p q t", q=4)), r=[r_tp],
                              w=[r_gT] if (k4 == 0 and q4 == 0) else [], wa=[] if (k4 == 0 and q4 == 0) else [r_gT])
                    ht, r_ht = ht_p.next()
                    hr = [r_hT[(c, tt)] for c in range(8) for tt in range(tt0, tt0 + ng)]
                    S.dma("sp", ht[:, :, 0:nt], hT_ap[:, t0:t0 + nt].rearrange("(c p) t -> p c t", p=128), r=hr, w=[r_ht])
                    for dc in range(8):
                        op, r_op = op_p.next()
                        for k in range(16):
                            M(lambda e: e.matmul(op[:, 0:nt], ow[:, k, dc * 128:(dc + 1) * 128], gT[:, k, 0:nt], start=(k == 0), stop=(k == 15)),
                              r=[r_ow, r_gT], w=[r_op] if k == 0 else [], wa=[r_op] if k else [], inc=(k == 15))
                        V(lambda e: e.scalar_tensor_tensor(ht[:, dc, 0:nt], op[:, 0:nt], mod_gt[:, dc, j:j + 1], ht[:, dc, 0:nt], ALU.mult, ALU.add),
                          r=[r_op, r_mod], w=[r_ht])
                    S.dma("pool", hT_ap[:, t0:t0 + nt].rearrange("(c p) t -> p c t", p=128), ht[:, :, 0:nt], r=[r_ht], wa=hr)
                S.barrier()

    def attn_layer(lay):
        qT_d = dscr("qT_d", [D, T], BF16)
        kT_d = dscr("kT_d", [256, T], BF16)
        v_tm = dscr("v_tm", [T, 256], BF16)
        sgT_d = dscr("sgT_d", [D, T], BF16)
        oT_d = dscr("oT_d", [D, T], BF16)
        r_q, r_k, r_v, r_sg, r_o = Res(), Res(), Res(), Res(), Res()
        with ExitStack() as st1:
            sb1, ps1 = mk(st1)
            inT = sb1([128, 8, T], BF16, "inT")
            r_inT = Res()
            with ExitStack() as ph:
                sb, ps = mk(ph)
                pre_pass(lay, sb, ps, inT, r_inT)
                S.barrier()
            with ExitStack() as ph:
                sb, ps = mk(ph)
                rope = sb([128, 2, nlat], F32, "rope")
                r_cst = Res()
                S.dma("sp", rope[:], lay["rope"].ap().rearrange("a p t -> p a t"), w=[r_cst])
                qkw = sb([128, 2], F32, "qkw")
                S.dma("sp", qkw[:], lay["qkw"].ap(), wa=[r_cst])
                permb = sb([128, 128], BF16, "permb")
                S.dma("pool", permb[:], lay["perm"].ap(), wa=[r_cst])
                bones = sb([128, 128], BF16, "bones")
                S.dma("pool", bones[:], lay["bones"].ap(), wa=[r_cst])
                sq_p = RR([sb([128, 512], BF16, "asq") for _ in range(2)])
                ss_p = RR([ps([128, 512], F32, "ass") for _ in range(2)])
                rs_p = RR([sb([128, 512], F32, "ars") for _ in range(2)])
                qn_p = RR([sb([128, 512], F32, "aqn") for _ in range(2)])
                qb_p = RR([sb([128, 512], BF16, "aqb") for _ in range(2)])
                rot_p = RR([ps([128, 512], F32, "arot") for _ in range(2)])
                t1_p = RR([sb([128, 512], F32, "at1") for _ in range(2)])
                t2_p = RR([sb([128, 512], F32, "at2") for _ in range(2)])
                qo_p = RR([sb([128, 512], BF16, "aqo") for _ in range(3)])

                def epi_qkg(ci, c0, ncol, t0, nt, pt, r_pt):
                    if c0 >= 1536:
                        o, r_o_ = qo_p.next()
                        A(lambda e: e.activation(o[:, 0:nt], pt[:, 0:nt], AF.Silu), r=[r_pt], w=[r_o_])
                        cg = (c0 - 1536) // 128
                        S.dma("sp", sgT_d.ap()[cg * 128:(cg + 1) * 128, t0:t0 + nt], o[:, 0:nt], r=[r_o_], wa=[r_sg])
                        return
                    isq = c0 < 1024
                    wcol = 0 if isq else 1
                    sq, r_sq = sq_p.next()
                    A(lambda e: e.activation(sq[:, 0:nt], pt[:, 0:nt], AF.Square), r=[r_pt], w=[r_sq])
                    ss, r_ss = ss_p.next()
                    M(lambda e: e.matmul(ss[:, 0:nt], bones[:], sq[:, 0:nt], start=True, stop=True), r=[r_sq, r_cst], w=[r_ss])
                    rs, r_rs = rs_p.next()
                    A(lambda e: e.activation(rs[:, 0:nt], ss[:, 0:nt], AF.Sqrt, bias=EPS, scale=1.0 / 64), r=[r_ss], w=[r_rs])
                    V(lambda e: e.reciprocal(rs[:, 0:nt], rs[:, 0:nt]), w=[r_rs])
                    o, r_o_ = qo_p.next()
                    if t0 < NCTX:
                        V(lambda e: e.scalar_tensor_tensor(o[:, 0:nt], pt[:, 0:nt], qkw[:, wcol:wcol + 1], rs[:, 0:nt], ALU.mult, ALU.mult),
                          r=[r_pt, r_rs, r_cst], w=[r_o_])
                    else:
                        qn, r_qn = qn_p.next()
                        V(lambda e: e.scalar_tensor_tensor(qn[:, 0:nt], pt[:, 0:nt], qkw[:, wcol:wcol + 1], rs[:, 0:nt], ALU.mult, ALU.mult),
                          r=[r_pt, r_rs, r_cst], w=[r_qn])
                        qb, r_qb = qb_p.next()
                        A(lambda e: e.copy(qb[:, 0:nt], qn[:, 0:nt]), r=[r_qn], w=[r_qb])
                        rot, r_rot = rot_p.next()
                        M(lambda e: e.matmul(rot[:, 0:nt], permb[:], qb[:, 0:nt], start=True, stop=True), r=[r_qb, r_cst], w=[r_rot])
                        l0 = t0 - NCTX
                        t1, r_t1 = t1_p.next()
                        G(lambda e: e.tensor_tensor(t1[:, 0:nt], qn[:, 0:nt], rope[:, 0, l0:l0 + nt], ALU.mult), r=[r_qn, r_cst], w=[r_t1])
                        t2, r_t2 = t2_p.next()
                        V(lambda e: e.tensor_tensor(t2[:, 0:nt], rot[:, 0:nt], rope[:, 1, l0:l0 + nt], ALU.mult), r=[r_rot, r_cst], w=[r_t2])
                        V(lambda e: e.tensor_tensor(o[:, 0:nt], t1[:, 0:nt], t2[:, 0:nt], ALU.add), r=[r_t1, r_t2], w=[r_o_])
                    if isq:
                        S.dma("sp", qT_d.ap()[c0:c0 + 128, t0:t0 + nt], o[:, 0:nt], r=[r_o_], wa=[r_q])
                    else:
                        S.dma("sp", kT_d.ap()[c0 - 1024:c0 - 1024 + 128, t0:t0 + nt], o[:, 0:nt], r=[r_o_], wa=[r_k])

                cols = [(128 * i, 128) for i in range(10)] + [(1536 + 128 * i, 128) for i in range(8)]
                linear_fm(sb, ps, inT, r_inT, 8, lay["inw"].ap(), cols, epi_qkg)
                vo_p = RR([sb([128, 256], BF16, "avo") for _ in range(3)])

                def epi_v(g0, n, tt, pt, r_pt):
                    o, r_o_ = vo_p.next()
                    V(lambda e: e.tensor_copy(o[:, 0:n], pt[:, 0:n]), r=[r_pt], w=[r_o_])
                    S.dma("sp", v_tm.ap()[tt * 128:(tt + 1) * 128, :], o[:, 0:n], r=[r_o_], wa=[r_v])
                linear_tm(sb, ps, inT, r_inT, 8, lay["inw"].ap(), 1280, 256, epi_v)
                S.barrier()
        with ExitStack() as ph:
            sb, ps = mk(ph)
            onesf = sb([128, 64], F32, "aones")
            r_on = Res()
            G(lambda e: e.memset(onesf[:], 1.0), w=[r_on])
            Vg = sb([128, NTT, 65], BF16, "Vg")
            r_Vg = Res()
            G(lambda e: e.memset(Vg[:], 1.0), w=[r_Vg])
            kk_p = RR([sb([128, T], BF16, "kk") for _ in range(2)])
            qc_p = RR([sb([128, T], BF16, "qc") for _ in range(2)])
            sg_p = RR([sb([64, T], BF16, "sgh") for _ in range(2)])
            sp_p = RR([ps([128, 512], F32, "asp") for _ in range(4)])
            P_p = RR([sb([128, 512], BF16, "aP") for _ in range(4)])
            oa_p = RR([ps([128, 512], F32, "aoa") for _ in range(2)])
            bc_p = RR([ps([64, 512], F32, "abc") for _ in range(2)])
            osb_p = RR([sb([128, 512], F32, "aosb") for _ in range(2)])
            o1_p = RR([sb([64, 512], F32, "ao1") for _ in range(2)])
            og_p = RR([sb([64, 512], BF16, "aog") for _ in range(2)])
            for gk in range(4):
                kk, r_kk = kk_p.next()
                S.dma("sp", kk[0:64, :], kT_d.ap()[gk * 64:(gk + 1) * 64, :], r=[r_k], w=[r_kk])
                S.dma("sp", kk[64:128, :], kT_d.ap()[gk * 64:(gk + 1) * 64, :], r=[r_k], wa=[r_kk])
                S.dma("sp", Vg[:, :, 0:64], v_tm.ap()[:, gk * 64:(gk + 1) * 64].rearrange("(t p) d -> p t d", p=128), r=[r_v], w=[r_Vg])
                for qc in (2 * gk, 2 * gk + 1):
                    qt, r_qt = qc_p.next()
                    S.dma("sp", qt[:], qT_d.ap()[qc * 128:(qc + 1) * 128, :], r=[r_q], w=[r_qt])
                    for hh in range(2):
                        h = 2 * qc + hh
                        pr = slice(64 * hh, 64 * hh + 64)
                        sgh, r_sgh = sg_p.next()
                        S.dma("sp", sgh[:], sgT_d.ap()[h * 64:(h + 1) * 64, :], r=[r_sg], w=[r_sgh])
                        tasks = []
                        for (t0, nt) in BLKS:
                            ktiles = [0, 1] if t0 < NCTX else list(range(NTT))
                            for ki, kt in enumerate(ktiles):
                                tasks.append((t0, nt, ki, kt, len(ktiles)))
                        spq = {}
                        cur = {}
                        deferred = []

                        def emit_qk(ti):
                            t0, nt, ki, kt, nk = tasks[ti]
                            sp_, r_sp = sp_p.next()
                            M(lambda e: e.matmul(sp_[:, 0:nt], kk[pr, kt * 128:(kt + 1) * 128], qt[pr, t0:t0 + nt], start=True, stop=True),
                              r=[r_kk, r_qt], w=[r_sp])
                            spq[ti] = (sp_, r_sp)

                        def finalize_pe(args):
                            (t0, nt, osb, r_osb) = args
                            bc, r_bc = bc_p.next()
                            M(lambda e: e.matmul(bc[:, 0:nt], onesf[64:65, :], osb[64:65, 0:nt], start=True, stop=True), r=[r_osb, r_on], w=[r_bc])
                            o1, r_o1 = o1_p.next()
                            V(lambda e: e.tensor_tensor(o1[:, 0:nt], osb[0:64, 0:nt], bc[:, 0:nt], ALU.mult), r=[r_osb, r_bc], w=[r_o1])
                            og, r_og = og_p.next()
                            G(lambda e: e.tensor_tensor(og[:, 0:nt], o1[:, 0:nt], sgh[:, t0:t0 + nt], ALU.mult), r=[r_o1, r_sgh], w=[r_og])
                            S.dma("sp", oT_d.ap()[h * 64:(h + 1) * 64, t0:t0 + nt], og[:, 0:nt], r=[r_og], wa=[r_o])

                        LOOK = 3
                        for ti in range(min(LOOK, len(tasks))):
                            emit_qk(ti)
                        for ti in range(len(tasks)):
                            t0, nt, ki, kt, nk = tasks[ti]
                            sp_, r_sp = spq.pop(ti)
                            if ki == 0:
                                cur["oa"] = oa_p.next()
                            oa, r_oa = cur["oa"]
                            P, r_P = P_p.next()
                            A(lambda e: e.activation(P[:, 0:nt], sp_[:, 0:nt], AF.Exp, bias=-8.0, scale=0.125), r=[r_sp], w=[r_P])
                            M(lambda e: e.matmul(oa[0:65, 0:nt], Vg[:, kt, :], P[:, 0:nt], start=(ki == 0), stop=(ki == nk - 1)),
                              r=[r_Vg, r_P], w=[r_oa] if ki == 0 else [], wa=[r_oa] if ki else [], inc=(ki == nk - 1))
                            if ti + LOOK < len(tasks):
                                emit_qk(ti + LOOK)
                            deferred = [(n - 1, a) for (n, a) in deferred]
                            while deferred and deferred[0][0] <= 0:
                                finalize_pe(deferred.pop(0)[1])
                            if ki == nk - 1:
                                osb, r_osb = osb_p.next()
                                V(lambda e: e.tensor_copy(osb[0:65, 0:nt], oa[0:65, 0:nt]), r=[r_oa], w=[r_osb])
                                V(lambda e: e.reciprocal(osb[64:65, 0:nt], osb[64:65, 0:nt]), w=[r_osb])
                                deferred.append((4, (t0, nt, osb, r_osb)))
                        for (_, a) in deferred:
                            finalize_pe(a)
            S.barrier()
        with ExitStack() as ph:
            sb, ps = mk(ph)
            oT = sb([128, 8, T], BF16, "oT")
            r_oT = Res()
            S.dma("sp", oT[:], oT_d.ap().rearrange("(c p) t -> p c t", p=128), r=[r_o], w=[r_oT])
            linear_fm(sb, ps, oT, r_oT, 8, lay["outw"].ap(), [(128 * i, 128) for i in range(8)], make_resid_epi(sb))
            S.barrier()

    def s5_layer(lay):
        uT_d = dscr("uT_d", [D, T], BF16)
        szT_d = dscr("szT_d", [D, T], BF16)
        gT_d = dscr("gT_d", [D, T], BF16)
        y2T_d = dscr("y2T_d", [D, T], BF16)
        r_u, r_sz, r_g, r_y2 = Res(), Res(), Res(), Res()
        NLV = 1
        while (1 << (NLV - 1)) < T:
            NLV += 1
        with ExitStack() as st1:
            sb1, ps1 = mk(st1)
            inT = sb1([128, 8, T], BF16, "inT")
            r_inT = Res()
            with ExitStack() as ph:
                sb, ps = mk(ph)
                pre_pass(lay, sb, ps, inT, r_inT)
                S.barrier()
            with ExitStack() as ph:
                sb, ps = mk(ph)
                uo_p = RR([sb([128, 512], BF16, "suo") for _ in range(3)])

                def epi_uz(ci, c0, ncol, t0, nt, pt, r_pt):
                    o, r_o_ = uo_p.next()
                    if c0 < 1024:
                        V(lambda e: e.tensor_copy(o[:, 0:nt], pt[:, 0:nt]), r=[r_pt], w=[r_o_])
                        S.dma("sp", uT_d.ap()[c0:c0 + 128, t0:t0 + nt], o[:, 0:nt], r=[r_o_], wa=[r_u])
                    else:
                        A(lambda e: e.activation(o[:, 0:nt], pt[:, 0:nt], AF.Silu), r=[r_pt], w=[r_o_])
                        S.dma("sp", szT_d.ap()[c0 - 1024:c0 - 1024 + 128, t0:t0 + nt], o[:, 0:nt], r=[r_o_], wa=[r_sz])
                linear_fm(sb, ps, inT, r_inT, 8, lay["inw"].ap(), [(128 * i, 128) for i in range(16)], epi_uz)
                S.barrier()
        with ExitStack() as ph:
            sb, ps = mk(ph)
            lam = sb([128, 3, 64], F32, "lam")
            r_t = Res()
            S.dma("sp", lam[:], lay["lam"].ap(), w=[r_t])
            tb = sb([128, 16, 64], F32, "stb")
            tbi = sb([128, 64], I32, "stbi")
            coef = sb([128, 3, 64], F32, "coef")
            pw = sb([128, 64, NLV, 3], F32, "pw")
            lr, li, ls = lam[:, 0, :], lam[:, 1, :], lam[:, 2, :]
            X = lambda i: tb[:, i, :]

            def vt(fn):
                V(fn, w=[r_t])

            def at(fn):
                A(fn, w=[r_t])
            at(lambda e: e.activation(X(0), ls, AF.Exp))
            vt(lambda e: e.tensor_tensor(X(1), lr, X(0), ALU.mult))
            at(lambda e: e.activation(X(2), X(1), AF.Exp))
            vt(lambda e: e.tensor_tensor(X(3), li, X(0), ALU.mult))

            def sin_of(dst, src, shift):
                vt(lambda e: e.tensor_scalar(X(4), src, shift, 1.0 / (2 * PI), ALU.add, ALU.mult))
                vt(lambda e: e.tensor_copy(tbi[:], X(4)))
                vt(lambda e: e.tensor_copy(X(5), tbi[:]))
                vt(lambda e: e.tensor_scalar(X(4), src, shift, None, ALU.add))
                vt(lambda e: e.scalar_tensor_tensor(X(4), X(5), -2 * PI, X(4), ALU.mult, ALU.add))
                vt(lambda e: e.tensor_single_scalar(X(5), X(4), PI, ALU.is_gt))
                vt(lambda e: e.scalar_tensor_tensor(X(4), X(5), -2 * PI, X(4), ALU.mult, ALU.add))
                vt(lambda e: e.tensor_single_scalar(X(5), X(4), -PI, ALU.is_lt))
                vt(lambda e: e.scalar_tensor_tensor(X(4), X(5), 2 * PI, X(4), ALU.mult, ALU.add))
                at(lambda e: e.activation(dst, X(4), AF.Sin))
            sin_of(X(6), X(3), 0.0)
            sin_of(X(7), X(3), PI / 2)
            vt(lambda e: e.tensor_tensor(X(8), X(2), X(7), ALU.mult))
            vt(lambda e: e.tensor_tensor(X(9), X(2), X(6), ALU.mult))
            vt(lambda e: e.tensor_tensor(X(10), lr, lr, ALU.mult))
            vt(lambda e: e.tensor_tensor(X(11), li, li, ALU.mult))
            vt(lambda e: e.tensor_tensor(X(10), X(10), X(11), ALU.add))
            vt(lambda e: e.reciprocal(X(10), X(10)))
            vt(lambda e: e.tensor_scalar(X(11), X(8), -1.0, None, ALU.add))
            vt(lambda e: e.tensor_tensor(X(12), X(11), lr, ALU.mult))
            vt(lambda e: e.tensor_tensor(X(13), X(9), li, ALU.mult))
            vt(lambda e: e.tensor_tensor(X(12), X(12), X(13), ALU.add))
            vt(lambda e: e.tensor_tensor(coef[:, 0, :], X(12), X(10), ALU.mult))
            vt(lambda e: e.tensor_tensor(X(12), X(9), lr, ALU.mult))
            vt(lambda e: e.tensor_tensor(X(13), X(11), li, ALU.mult))
            vt(lambda e: e.tensor_tensor(X(12), X(12), X(13), ALU.subtract))
            vt(lambda e: e.tensor_tensor(coef[:, 1, :], X(12), X(10), ALU.mult))
            vt(lambda e: e.tensor_scalar(coef[:, 2, :], coef[:, 1, :], -1.0, None, ALU.mult))
            vt(lambda e: e.tensor_copy(pw[:, :, 0, 0], X(8)))
            vt(lambda e: e.tensor_copy(pw[:, :, 0, 1], X(9)))
            for lv in range(NLV):
                vt(lambda e: e.tensor_scalar(pw[:, :, lv, 2], pw[:, :, lv, 1], -1.0, None, ALU.mult))
                if lv + 1 < NLV:
                    vt(lambda e: e.tensor_tensor(X(12), pw[:, :, lv, 0], pw[:, :, lv, 0], ALU.mult))
                    vt(lambda e: e.tensor_tensor(X(13), pw[:, :, lv, 1], pw[:, :, lv, 1], ALU.mult))
                    vt(lambda e: e.tensor_tensor(pw[:, :, lv + 1, 0], X(12), X(13), ALU.subtract))
                    vt(lambda e: e.tensor_tensor(X(12), pw[:, :, lv, 0], pw[:, :, lv, 1], ALU.mult))
                    vt(lambda e: e.tensor_scalar(pw[:, :, lv + 1, 1], X(12), 2.0, None, ALU.mult))
            brt = sb([128, 2, 2, 32, 128], BF16, "brt")
            for q in range(2):
                for k in range(2):
                    for j8 in range(4):
                        S.dma("pool", brt[:, q, k, j8 * 8:(j8 + 1) * 8, :], lay["brt"].ap()[q, :, k, j8 * 8:(j8 + 1) * 8, :], wa=[r_t])
            crp = sb([128, 2, 2, 32, 32], BF16, "crp")
            S.dma("pool", crp[:, 0], lay["crp"].ap()[0], wa=[r_t])
            S.dma("pool", crp[:, 1], lay["crp"].ap()[1], wa=[r_t])
            at(lambda e: e.mul(crp[:, 1], crp[:, 1], -1.0))
            sd = sb([128, 8], F32, "sd")
            S.dma("sp", sd[:], lay["sd"].ap(), wa=[r_t])
            Xr = sb([128, T], F32, "Xr")
            Xi = sb([128, T], F32, "Xi")
            Yr = sb([128, T], F32, "Yr")
            Yi = sb([128, T], F32, "Yi")
            r_X, r_Yb, r_Yi = Res(), Res(), Res()
            xb = [[sb([128, T], BF16, "xb%d%d" % (k, q)) for q in range(2)] for k in range(2)]
            r_xb = [Res(), Res()]
            uc_p = RR([sb([128, T], BF16, "suc") for _ in range(2)])
            p12_p = RR([ps([128, 512], F32, "sp12") for _ in range(4)])
            tmp_p = RR([sb([128, 512], F32, "stmp") for _ in range(2)])
            yp_p = RR([ps([128, 512], F32, "syp") for _ in range(2)])
            yv = sb([128, T], F32, "yv")
            ga = Yr
            r_yv = Res()
            go_p = RR([sb([128, T], BF16, "sgo") for _ in range(1)])

            def sview(t, off, step, a0, cnt, mult):
                s0 = off + a0 * step
                st_ = mult * step
                return t[:, s0:s0 + (cnt - 1) * st_ + 1:st_]

            r_Xr, r_Xi, r_Yr, r_Yi = Res(), Res(), Res(), Res()
            r_or = [Res(), Res()]
            r_oi = [Res(), Res()]

            def scan(col, k):
                outr, outi = xb[k]
                rof = {id(Xr): r_Xr, id(Xi): r_Xi, id(Yr): r_Yr, id(Yi): r_Yi, id(outr): r_or[k], id(outi): r_oi[k]}

                def stt(o_t, o_ap, a_t, a_ap, sc, b_t, b_ap):
                    V(lambda e: e.scalar_tensor_tensor(o_ap, a_ap, sc, b_ap, ALU.mult, ALU.add),
                      r=[rof[id(a_t)], rof[id(b_t)], r_t, r_X, r_Yb], w=[rof[id(o_t)]])

                def cpy(o_t, o_ap, a_t, a_ap):
                    A(lambda e: e.copy(o_ap, a_ap), r=[rof[id(a_t)], r_X, r_Yb], w=[rof[id(o_t)]])

                def rec(tr, ti, off, step, n, lv, yoff, top):
                    if n == 1:
                        if top:
                            cpy(outr, outr[:, 0:1], tr, tr[:, off:off + 1])
                            cpy(outi, outi[:, 0:1], ti, ti[:, off:off + 1])
                        return
                    m = n // 2
                    ne = n - m
                    ar, ai, nai = pw[:, col, lv, 0:1], pw[:, col, lv, 1:2], pw[:, col, lv, 2:3]
                    Ev = lambda t, a0, cnt: sview(t, off, step, 2 * a0, cnt, 2)
                    Ov = lambda t, a0, cnt: sview(t, off, step, 2 * a0 + 1, cnt, 2)
                    yr, yi = Yr[:, yoff:yoff + m], Yi[:, yoff:yoff + m]
                    stt(Yr, yr, tr, Ev(tr, 0, m), ar, tr, Ov(tr, 0, m))
                    stt(Yi, yi, ti, Ev(ti, 0, m), ar, ti, Ov(ti, 0, m))
                    stt(Yr, yr, ti, Ev(ti, 0, m), nai, Yr, yr)
                    stt(Yi, yi, tr, Ev(tr, 0, m), ai, Yi, yi)
                    rec(Yr, Yi, yoff, 1, m, lv + 1, yoff + m, False)
                    ne1 = ne - 1
                    zr, zi = Yr[:, yoff:yoff + ne1], Yi[:, yoff:yoff + ne1]
                    if top:
                        cpy(outr, sview(outr, 0, 1, 1, m, 2), Yr, yr)
                        cpy(outi, sview(outi, 0, 1, 1, m, 2), Yi, yi)
                        cpy(outr, outr[:, 0:1], tr, tr[:, off:off + 1])
                        cpy(outi, outi[:, 0:1], ti, ti[:, off:off + 1])
                        if ne1 > 0:
                            stt(tr, Ev(tr, 1, ne1), Yr, zr, ar, tr, Ev(tr, 1, ne1))
                            stt(ti, Ev(ti, 1, ne1), Yi, zi, ar, ti, Ev(ti, 1, ne1))
                            V(lambda e: e.scalar_tensor_tensor(sview(outr, 0, 1, 2, ne1, 2), zi, nai, Ev(tr, 1, ne1), ALU.mult, ALU.add),
                              r=[r_Yi, r_Xr, r_t], w=[r_or[k]])
                            V(lambda e: e.scalar_tensor_tensor(sview(outi, 0, 1, 2, ne1, 2), zr, ai, Ev(ti, 1, ne1), ALU.mult, ALU.add),
                              r=[r_Yr, r_Xi, r_t], w=[r_oi[k]])
                    else:
                        cpy(tr, Ov(tr, 0, m), Yr, yr)
                        cpy(ti, Ov(ti, 0, m), Yi, yi)
                        if ne1 > 0:
                            stt(tr, Ev(tr, 1, ne1), Yr, zr, ar, tr, Ev(tr, 1, ne1))
                            stt(ti, Ev(ti, 1, ne1), Yi, zi, ar, ti, Ev(ti, 1, ne1))
                            stt(tr, Ev(tr, 1, ne1), Yi, zi, nai, tr, Ev(tr, 1, ne1))
                            stt(ti, Ev(ti, 1, ne1), Yr, zr, ai, ti, Ev(ti, 1, ne1))
                rec(Xr, Xi, 0, 1, T, 0, 0, True)

            def bwd_pos(t0, nt):
                return (NCTX - t0 - nt) if t0 < NCTX else (NCTX + T - t0 - nt)

            uc = None
            SK = ""
            for j in range(32):
                cj, jm = j // 4, j % 4
                pr = slice(32 * jm, 32 * jm + 32)
                if jm == 0:
                    uc, r_uc = uc_p.next()
                    S.dma("sp", uc[:], uT_d.ap()[cj * 128:(cj + 1) * 128, :], r=[r_u], w=[r_uc])
                for k in range(2):
                    col = k * 32 + j
                    for (t0, nt) in BLKS:
                        if k == 0:
                            i0 = t0
                            uv = uc[:, t0:t0 + nt]
                        else:
                            i0 = bwd_pos(t0, nt)
                            uv = uc[:, t0:t0 + nt][:, ::-1]
                        if "m" in SK:
                            continue
                        p1, r_p1 = p12_p.next()
                        p2, r_p2 = p12_p.next()
                        M(lambda e: e.matmul(p1[:, 0:nt], brt[:, 0, k, j, :], uv, start=True, stop=True), r=[r_uc, r_t], w=[r_p1])
                        M(lambda e: e.matmul(p2[:, 0:nt], brt[:, 1, k, j, :], uv, start=True, stop=True), r=[r_uc, r_t], w=[r_p2])
                        if "e" in SK:
                            continue
                        tm, r_tm = tmp_p.next()
                        if "a" not in SK:
                            A(lambda e: e.activation(tm[:, 0:nt], p2[:, 0:nt], AF.Identity, scale=coef[:, 2, col:col + 1]), r=[r_t], w=[r_tm, r_p2])
                        if "v" not in SK:
                            V(lambda e: e.scalar_tensor_tensor(Xr[:, i0:i0 + nt], p1[:, 0:nt], coef[:, 0, col:col + 1], tm[:, 0:nt], ALU.mult, ALU.add),
                              r=[r_tm, r_t], w=[r_Xr, r_p1])
                        tm2, r_tm2 = tmp_p.next()
                        if "a" not in SK:
                            A(lambda e: e.activation(tm2[:, 0:nt], p1[:, 0:nt], AF.Identity, scale=coef[:, 1, col:col + 1]), r=[r_t], w=[r_tm2, r_p1])
                        if "v" not in SK:
                            V(lambda e: e.scalar_tensor_tensor(Xi[:, i0:i0 + nt], p2[:, 0:nt], coef[:, 0, col:col + 1], tm2[:, 0:nt], ALU.mult, ALU.add),
                              r=[r_tm2, r_t], w=[r_Xi, r_p2])
                    if "s" not in SK:
                        scan(col, k)
                for (t0, nt) in (BLKS if "r" not in SK else []):
                    yp, r_yp = yp_p.next()
                    i0 = bwd_pos(t0, nt)
                    rv = lambda a: a[:, ::-1]
                    ops = [(crp[:, 0, 0, j, :], xb[0][0][:, t0:t0 + nt], r_or[0]), (crp[:, 1, 0, j, :], xb[0][1][:, t0:t0 + nt], r_oi[0]),
                           (crp[:, 0, 1, j, :], rv(xb[1][0][:, i0:i0 + nt]), r_or[1]), (crp[:, 1, 1, j, :], rv(xb[1][1][:, i0:i0 + nt]), r_oi[1])]
                    for qi, (lh, rh, rr) in enumerate(ops):
                        M(lambda e: e.matmul(yp[pr, 0:nt], lh, rh, start=(qi == 0), stop=(qi == 3), tile_position=(0, 32 * jm)), r=[rr, r_t],
                          w=[r_yp] if qi == 0 else [], wa=[r_yp] if qi else [])
                    V(lambda e: e.scalar_tensor_tensor(yv[pr, t0:t0 + nt], uc[pr, t0:t0 + nt], sd[pr, cj:cj + 1], yp[pr, 0:nt], ALU.mult, ALU.add),
                      r=[r_yp, r_uc, r_t], w=[r_yv] if (jm == 0 and t0 == 0) else [], wa=[] if (jm == 0 and t0 == 0) else [r_yv])
                if jm == 3 and "g" not in SK:
                    A(lambda e: e.activation(ga[:], yv[:], AF.Square), r=[r_yv], w=[r_Yr])
                    V(lambda e: e.tensor_scalar(ga[:], ga[:], 0.044715, 1.0, ALU.mult, ALU.add), w=[r_Yr])
                    V(lambda e: e.tensor_tensor(ga[:], ga[:], yv[:], ALU.mult), r=[r_yv], w=[r_Yr])
                    A(lambda e: e.activation(ga[:], ga[:], AF.Sigmoid, scale=1.5957691216057308), w=[r_Yr])
                    go, r_go = go_p.next()
                    V(lambda e: e.tensor_tensor(go[:], ga[:], yv[:], ALU.mult), r=[r_Yr, r_yv], w=[r_go])
                    S.dma("sp", gT_d.ap()[cj * 128:(cj + 1) * 128, :], go[:], r=[r_go], wa=[r_g])
            S.barrier()
        with ExitStack() as ph:
            sb, ps = mk(ph)
            gT = sb([128, 8, T], BF16, "gT")
            r_gT = Res()
            S.dma("sp", gT[:], gT_d.ap().rearrange("(c p) t -> p c t", p=128), r=[r_g], w=[r_gT])
            glub = sb([128, 8], F32, "glub")
            r_gb = Res()
            S.dma("sp", glub[:], lay["glub"].ap(), w=[r_gb])
            sig_p = RR([sb([128, 512], F32, "ssig") for _ in range(2)])
            szt_p = RR([sb([128, 512], BF16, "sszt") for _ in range(2)])
            y2_p = RR([sb([128, 512], BF16, "sy2") for _ in range(3)])

            def epi_glu(ci, c0, ncol, t0, nt, pt, r_pt):
                sg, r_sg_ = sig_p.next()
                A(lambda e: e.activation(sg[:, 0:nt], pt[:, 0:nt], AF.Sigmoid, bias=glub[:, ci:ci + 1], scale=1.0), r=[r_pt, r_gb], w=[r_sg_])
                szt, r_szt = szt_p.next()
                S.dma("sp", szt[:, 0:nt], szT_d.ap()[c0:c0 + 128, t0:t0 + nt], r=[r_sz], w=[r_szt])
                V(lambda e: e.tensor_tensor(sg[:, 0:nt], sg[:, 0:nt], gT[:, ci, t0:t0 + nt], ALU.mult), r=[r_gT], w=[r_sg_])
                y2, r_y2_ = y2_p.next()
                V(lambda e: e.tensor_tensor(y2[:, 0:nt], sg[:, 0:nt], szt[:, 0:nt], ALU.mult), r=[r_sg_, r_szt], w=[r_y2_])
                S.dma("sp", y2T_d.ap()[c0:c0 + 128, t0:t0 + nt], y2[:, 0:nt], r=[r_y2_], wa=[r_y2])
            linear_fm(sb, ps, gT, r_gT, 8, lay["gluw"].ap(), [(128 * i, 128) for i in range(8)], epi_glu)
            S.barrier()
        with ExitStack() as ph:
            sb, ps = mk(ph)
            y2 = sb([128, 8, T], BF16, "y2r")
            r_y2r = Res()
            S.dma("sp", y2[:], y2T_d.ap().rearrange("(c p) t -> p c t", p=128), r=[r_y2], w=[r_y2r])
            linear_fm(sb, ps, y2, r_y2r, 8, lay["outw"].ap(), [(128 * i, 128) for i in range(8)], make_resid_epi(sb))
            S.barrier()

    for i in layers:
        kind = i % 3
        if kind == 0:
            mamba_layer(L[i])
        elif kind == 1:
            attn_layer(L[i])
        else:
            s5_layer(L[i])

    with ExitStack() as ph:
        sb, ps = mk(ph)
        pre_pass(None, sb, ps, None, None, final=True)
    S.barrier()
    es.close()
    nc._ninst = S.ninst
    return nc


def prep_inputs(inputs, b, nlat=NLAT, layers=(0, 1, 2, 3)):
    f = lambda a: np.ascontiguousarray(np.asarray(a, dtype=np.float32))
    chunked = lambda v, n: f(np.asarray(v, np.float32).reshape(n, 128).T)
    m = {}
    m["x"] = f(inputs["x"][b][:nlat])
    m["ctx"] = f(inputs["ctx"][b])
    m["cc"] = f(np.stack([chunked(inputs["c"][b], 8), chunked(inputs["c_ctx"], 8)], axis=-1))
    m["ident"] = np.eye(128, dtype=np.float32)
    m["fnw"] = chunked(inputs["final_norm_w"], 8)
    has_m = False
    for i in layers:
        m["normw%d" % i] = chunked(inputs["norm_w"][i], 8)
        m["modw%d" % i] = f(inputs["mod_w"][i])
        m["modb%d" % i] = chunked(inputs["mod_b"][i], 24)
        kind, j = i % 3, i // 3
        if kind == 0:
            has_m = True
            m["m_in_w%d" % j] = f(inputs["m_in_w"][j])
            cw = np.asarray(inputs["m_conv_w"][j], np.float32)
            m["m_convw%d" % j] = f(cw.reshape(5, 32, 128).transpose(2, 1, 0))
            m["m_convb%d" % j] = chunked(inputs["m_conv_b"][j], 32)
            m["m_alog%d" % j] = f(np.asarray(inputs["m_a_log"][j], np.float32).reshape(64, 1))
            m["m_dtb%d" % j] = f(np.asarray(inputs["m_dt_bias"][j], np.float32).reshape(64, 1))
            m["m_dvec%d" % j] = f(np.repeat(np.asarray(inputs["m_d"][j], np.float32), 64))
            m["m_normw%d" % j] = f(inputs["m_norm_w"][j])
            m["m_out_w%d" % j] = f(inputs["m_out_w"][j])
        elif kind == 1:
            m["a_in_w"] = f(inputs["a_in_w"][0])
            m["a_out_w"] = f(inputs["a_out_w"][0])
            m["a_qkw"] = f(np.stack([np.tile(np.asarray(inputs["a_q_norm"][0], np.float32), 2),
                                     np.tile(np.asarray(inputs["a_k_norm"][0], np.float32), 2)], axis=1))
            grid_w = 64
            pos = np.arange(nlat)
            r_idx, c_idx = (pos // grid_w).astype(np.float32), (pos % grid_w).astype(np.float32)
            inv = (10000.0 ** (-np.arange(0, 32, 2, dtype=np.float32) / 32)).astype(np.float32)
            dd = np.arange(128) % 64
            ax, part, ii = dd // 32, (dd % 32) // 16, dd % 16
            ang = np.where(ax[:, None] == 0, r_idx[None, :], c_idx[None, :]).astype(np.float32) * inv[ii][:, None]
            m["a_rope"] = f(np.stack([np.cos(ang), np.sin(ang)]))
            perm = np.zeros((128, 128), np.float32)
            for dcol in range(128):
                if part[dcol] == 0:
                    perm[dcol + 16, dcol] = -1.0
                else:
                    perm[dcol - 16, dcol] = 1.0
            m["a_perm"] = perm
            bo = np.zeros((128, 128), np.float32)
            bo[:64, :64] = 1.0
            bo[64:, 64:] = 1.0
            m["a_bones"] = bo
        else:
            m["s_in_w"] = f(inputs["s_in_w"][0])
            m["s_glu_w"] = f(inputs["s_glu_w"][0])
            m["s_out_w"] = f(inputs["s_out_w"][0])
            m["s_sd"] = chunked(inputs["s_d"][0], 8)
            m["s_glub"] = chunked(inputs["s_glu_b"][0], 8)
            lre = np.asarray(inputs["s_lambda_re"][0], np.float32)
            lim = np.asarray(inputs["s_lambda_im"][0], np.float32)
            lst = np.asarray(inputs["s_log_step"][0], np.float32)

            def pair_layout(a):
                a = a.reshape(2, 32, 2, 64)
                return a.transpose(2, 3, 0, 1).reshape(128, 64)
            lam = np.stack([pair_layout(lre), pair_layout(lim),
                            pair_layout(np.broadcast_to(lst[:, :, None], (2, 64, 64)))], axis=1)
            m["s_lam"] = f(lam)
            brt = np.zeros((2, 128, 2, 32, 128), np.float32)
            crp = np.zeros((2, 128, 2, 32, 32), np.float32)
            for q, (bsrc, csrc) in enumerate(((inputs["s_b_re"][0], inputs["s_c_re"][0]), (inputs["s_b_im"][0], inputs["s_c_im"][0]))):
                bsrc = np.asarray(bsrc, np.float32)
                csrc = np.asarray(csrc, np.float32)
                for k in range(2):
                    for j in range(32):
                        for gl in range(2):
                            g_ = 2 * j + gl
                            r0 = 32 * (j % 4) + 16 * gl
                            brt[q, r0:r0 + 16, k, j, gl * 64:(gl + 1) * 64] = bsrc[k, g_].T
                            crp[q, gl * 64:(gl + 1) * 64, k, j, 16 * gl:16 * gl + 16] = csrc[k, g_].T
            m["s_brt"] = brt
            m["s_crp"] = crp
    if has_m:
        up = np.triu(np.ones((128, 128), np.float32))
        m["masks"] = f(np.stack([up, up.T]))
    return m


ACTIVE_CORES = (0, 1, 4, 5)


def kernel(**inputs):
    nc = build_program()
    real = [prep_inputs(inputs, b) for b in range(4)]
    big = ("x", "ctx", "cc", "modw", "m_in_w", "m_out_w", "a_in_w", "a_out_w", "s_in_w", "s_glu_w", "s_out_w", "s_brt", "s_crp")
    idle = {k: (np.zeros_like(v) if k.startswith(big) else v) for k, v in real[0].items()}
    in_maps = [idle] * 8
    for b, core in enumerate(ACTIVE_CORES):
        in_maps[core] = real[b]
    res = run_bass_kernel_spmd(nc, in_maps, core_ids=list(range(8)))
    out = np.stack([np.asarray(res.results[core]["out"], dtype=np.float32) for core in ACTIVE_CORES], axis=0)
    return out
```

```python
import os
import numpy as np
from contextlib import ExitStack
import concourse.bass as bass
import concourse.mybir as mybir
from concourse.bass_utils import run_bass_kernel_spmd

F32 = mybir.dt.float32
BF16 = mybir.dt.bfloat16
I32 = mybir.dt.int32
AF = mybir.ActivationFunctionType
ALU = mybir.AluOpType
AX = mybir.AxisListType

D = 1024
NCTX = 256
NLAT = 4096
EPS = 1e-6
M_IN = 6208
PI = float(np.pi)


class Res:
    __slots__ = ("w", "r", "name")

    def __init__(self, name=""):
        self.w = {}
        self.r = {}
        self.name = name


class Sched:
    def __init__(self, nc, es):
        self.nc = nc
        self.eng = {"pe": nc.tensor, "act": nc.scalar, "dve": nc.vector, "pool": nc.gpsimd, "sp": nc.sync}
        self.sem = {}
        self.cnt = {}
        self.known = {e: {} for e in self.eng}
        for e in self.eng:
            self.sem[e] = es.enter_context(nc.semaphore("s_" + e))
            self.cnt[e] = 0
        self.NDS = 8
        self.dslot = {}
        for q in ("sp", "pool"):
            for i in range(self.NDS):
                k = "d_%s%d" % (q, i)
                self.sem[k] = es.enter_context(nc.semaphore(k))
                self.cnt[k] = 0
            self.dslot[q] = 0
        self.ninst = 0

    def _wait(self, e, evs):
        kn = self.known[e]
        for k, v in evs.items():
            if v <= 0 or (e == "pe" and k == "pe") or kn.get(k, 0) >= v:
                continue
            self.eng[e].wait_ge(self.sem[k], v)
            kn[k] = v

    @staticmethod
    def _deps(r, w, wa):
        evs = {}

        def add(d):
            for k, v in d.items():
                if evs.get(k, 0) < v:
                    evs[k] = v
        for x in r:
            add(x.w)
        for x in w:
            add(x.w)
            add(x.r)
        for x in wa:
            add(x.r)
        return evs

    @staticmethod
    def _commit(k, v, r, w, wa):
        for x in r:
            if x.r.get(k, 0) < v:
                x.r[k] = v
        for x in w:
            if x.w.get(k, 0) < v:
                x.w[k] = v
        for x in wa:
            if x.w.get(k, 0) < v:
                x.w[k] = v

    def op(self, e, fn, r=(), w=(), wa=(), inc=True):
        self._wait(e, self._deps(r, w, wa))
        ins = fn(self.eng[e])
        if inc:
            self.cnt[e] += 1
            ins.then_inc(self.sem[e], 1)
            self._commit(e, self.cnt[e], r, w, wa)
        else:
            self._commit(e, self.cnt[e] + 1, r, w, wa)
        self.ninst += 1
        return ins

    def dma(self, q, out, in_, r=(), w=(), wa=(), **kw):
        i = self.dslot[q]
        self.dslot[q] = (i + 1) % self.NDS
        k = "d_%s%d" % (q, i)
        evs = self._deps(r, w, wa)
        evs[k] = max(evs.get(k, 0), self.cnt[k])
        self._wait(q, evs)
        ins = self.eng[q].dma_start(out=out, in_=in_, **kw)
        self.cnt[k] += 16
        ins.then_inc(self.sem[k], 16)
        self._commit(k, self.cnt[k], r, w, wa)
        self.ninst += 1
        return ins

    def barrier(self):
        evs = {k: v for k, v in self.cnt.items() if v > 0}
        for e in self.eng:
            self._wait(e, dict(evs))


class RR:
    def __init__(self, tiles):
        self.t = tiles
        self.r = [Res() for _ in tiles]
        self.i = 0

    def next(self):
        i = self.i
        self.i = (i + 1) % len(self.t)
        return self.t[i], self.r[i]


def build_program(nlat=NLAT, layers=(0, 1, 2, 3)):
    T = NCTX + nlat
    NTT = T // 128
    BLKS = [(0, NCTX)] + [(NCTX + 512 * i, 512) for i in range(nlat // 512)]
    nc = bass.Bass("TRN2", target_bir_lowering=False)
    es = ExitStack()
    es.enter_context(nc.allow_low_precision("bf16 matmul operands, fp32 accumulation"))
    S = Sched(nc, es)
    uid = [0]

    def mk(stack):
        def sb(shape, dt=F32, name="t"):
            uid[0] += 1
            return stack.enter_context(nc.sbuf_tensor("%s_%d" % (name, uid[0]), list(shape), dt))

        def ps(shape, dt=F32, name="p"):
            uid[0] += 1
            return stack.enter_context(nc.psum_tensor("%s_%d" % (name, uid[0]), list(shape), dt))
        return sb, ps

    def din(name, shape, dt=F32):
        return nc.dram_tensor(name, list(shape), dt, kind="ExternalInput")

    def dscr(name, shape, dt=F32):
        return nc.dram_tensor(name, list(shape), dt)

    V = lambda fn, r=(), w=(), wa=(): S.op("dve", fn, r, w, wa)
    A = lambda fn, r=(), w=(), wa=(): S.op("act", fn, r, w, wa)
    G = lambda fn, r=(), w=(), wa=(): S.op("pool", fn, r, w, wa)
    M = lambda fn, r=(), w=(), wa=(), inc=True: S.op("pe", fn, r, w, wa, inc)

    x_in = din("x", [nlat, D])
    ctx_in = din("ctx", [NCTX, D])
    cc_in = din("cc", [128, 8, 2])
    ident_in = din("ident", [128, 128])
    fnw_in = din("fnw", [128, 8])
    out_t = nc.dram_tensor("out", [nlat, D], F32, kind="ExternalOutput")
    L = {}
    for i in layers:
        L[i] = dict(normw=din("normw%d" % i, [128, 8]), modw=din("modw%d" % i, [D, 3 * D]), modb=din("modb%d" % i, [128, 24]))
        kind, j = i % 3, i // 3
        if kind == 0:
            L[i].update(inw=din("m_in_w%d" % j, [D, M_IN]), convw=din("m_convw%d" % j, [128, 32, 5]), convb=din("m_convb%d" % j, [128, 32]),
                        alog=din("m_alog%d" % j, [64, 1]), dtb=din("m_dtb%d" % j, [64, 1]), dvec=din("m_dvec%d" % j, [2048]),
                        mnw=din("m_normw%d" % j, [2048]), outw=din("m_out_w%d" % j, [2048, D]))
        elif kind == 1:
            L[i].update(inw=din("a_in_w", [D, 2560]), qkw=din("a_qkw", [128, 2]), outw=din("a_out_w", [D, D]),
                        rope=din("a_rope", [2, 128, nlat]), perm=din("a_perm", [128, 128]), bones=din("a_bones", [128, 128]))
        else:
            L[i].update(inw=din("s_in_w", [D, 2048]), lam=din("s_lam", [128, 3, 64]), brt=din("s_brt", [2, 128, 2, 32, 128]),
                        crp=din("s_crp", [2, 128, 2, 32, 32]), sd=din("s_sd", [128, 8]), gluw=din("s_glu_w", [D, D]),
                        glub=din("s_glub", [128, 8]), outw=din("s_out_w", [D, D]))
    masks_in = din("masks", [2, 128, 128]) if any(i % 3 == 0 for i in layers) else None

    hT = dscr("hT", [D, T])
    hT_ap = hT.ap()
    r_hT = {(c, tt): Res() for c in range(8) for tt in range(NTT)}

    def hres(c, t0, nt):
        return [r_hT[(c, tt)] for tt in range(t0 // 128, (t0 + nt + 127) // 128)]

    def hres_all(t0, nt):
        out = []
        for c in range(8):
            out += hres(c, t0, nt)
        return out

    gsb, gps = mk(es)
    ident = gsb([128, 128], F32, "ident")
    r_const = Res("const")
    S.dma("sp", ident[:], ident_in.ap(), w=[r_const])
    identb = gsb([128, 128], BF16, "identb")
    ones_bf = gsb([128, 128], BF16, "ones")
    G(lambda e: e.memset(ones_bf[:], 1.0), wa=[r_const])
    V(lambda e: e.tensor_copy(identb[:], ident[:]), r=[r_const], wa=[r_const])
    fnw = gsb([128, 8], F32, "fnw")
    S.dma("sp", fnw[:], fnw_in.ap(), wa=[r_const])
    cc = gsb([128, 8, 2], F32, "cc")
    S.dma("sp", cc[:], cc_in.ap(), wa=[r_const])
    scs = gsb([128, 8, 2], F32, "scs")
    A(lambda e: e.activation(scs[:], cc[:], AF.Silu), r=[r_const], wa=[r_const])
    mod_sc = gsb([128, 8, 2], F32, "mod_sc")
    mod_bi = gsb([128, 8, 2], F32, "mod_bi")
    mod_gt = gsb([128, 8, 2], F32, "mod_gt")
    r_mod = Res("mod")
    S.barrier()

    with ExitStack() as ph:
        sb, ps = mk(ph)
        xin = RR([sb([128, D], F32, "xin") for _ in range(2)])
        tp = RR([ps([128, 512], F32, "tp") for _ in range(2)])
        xo = RR([sb([128, 8, 128], F32, "xo") for _ in range(2)])
        for tt in range(NTT):
            xt, r_xt = xin.next()
            src = ctx_in.ap()[tt * 128:(tt + 1) * 128, :] if tt < 2 else x_in.ap()[(tt - 2) * 128:(tt - 1) * 128, :]
            S.dma("sp", xt[:], src, w=[r_xt])
            ot, r_ot = xo.next()
            for half in range(2):
                pt, r_pt = tp.next()
                for j in range(4):
                    c = half * 4 + j
                    M(lambda e: e.transpose(pt[:, j * 128:(j + 1) * 128], xt[:, c * 128:(c + 1) * 128], ident[:]),
                      r=[r_xt, r_const], w=[r_pt] if j == 0 else [], wa=[r_pt] if j else [])
                dst = ot[:, half * 4:(half + 1) * 4, :]
                if half:
                    A(lambda e: e.copy(dst, pt[:].rearrange("p (j t) -> p j t", j=4)), r=[r_pt], wa=[r_ot])
                else:
                    V(lambda e: e.tensor_copy(dst, pt[:].rearrange("p (j t) -> p j t", j=4)), r=[r_pt], w=[r_ot])
            S.dma("pool", hT_ap[:, tt * 128:(tt + 1) * 128].rearrange("(c p) t -> p c t", p=128), ot[:], r=[r_ot],
                  wa=[r_hT[(c, tt)] for c in range(8)])
        S.barrier()

    def pre_pass(lay, sb, ps, inT, r_inT, final=False):
        if not final:
            mw = RR([sb([128, 8, 512], F32, "modw") for _ in range(2)])
            mp = RR([ps([128, 512], F32, "modp") for _ in range(2)])
            modT = sb([128, 24, 2], F32, "modT")
            r_modT = Res()
            modb = sb([128, 24], F32, "modb")
            normw = sb([128, 8], F32, "normw")
            r_small = Res()
            S.dma("sp", modb[:], lay["modb"].ap(), w=[r_small])
            S.dma("sp", normw[:], lay["normw"].ap(), wa=[r_small])
            for cg in range(6):
                wt, r_wt = mw.next()
                S.dma("sp", wt[:], lay["modw"].ap()[:, cg * 512:(cg + 1) * 512].rearrange("(k p) n -> p k n", p=128), w=[r_wt])
                for c4 in range(4):
                    pt, r_pt = mp.next()
                    for k in range(8):
                        M(lambda e: e.matmul(pt[:, 0:2], wt[:, k, c4 * 128:(c4 + 1) * 128], scs[:, k, :], start=(k == 0), stop=(k == 7)),
                          r=[r_wt, r_const], w=[r_pt] if k == 0 else [], wa=[r_pt] if k else [])
                    col = cg * 4 + c4
                    V(lambda e: e.tensor_scalar(modT[:, col, :], pt[:, 0:2], modb[:, col:col + 1], None, ALU.add),
                      r=[r_pt, r_small], wa=[r_modT])
            V(lambda e: e.tensor_scalar(mod_sc[:], modT[:, 8:16, :], 1.0, None, ALU.add), r=[r_modT], w=[r_mod])
            V(lambda e: e.tensor_tensor(mod_sc[:], mod_sc[:], normw[:].unsqueeze(2).broadcast_to([128, 8, 2]), ALU.mult), r=[r_small], w=[r_mod])
            V(lambda e: e.tensor_copy(mod_bi[:], modT[:, 0:8, :]), r=[r_modT], w=[r_mod])
            V(lambda e: e.tensor_copy(mod_gt[:], modT[:, 16:24, :]), r=[r_modT], w=[r_mod])
        hb = RR([sb([128, 8, 512], F32, "hb") for _ in range(2)])
        sq = RR([sb([128, 8, 512], BF16, "sq") for _ in range(2)])
        ssp = RR([ps([128, 512], F32, "ssp") for _ in range(2)])
        rstd = RR([sb([128, 512], F32, "rstd") for _ in range(2)])
        tmp = RR([sb([128, 512], F32, "ntmp") for _ in range(3)])
        if final:
            hn = RR([sb([128, 8, 512], F32, "hn") for _ in range(2)])
            tp = RR([ps([128, 512], F32, "ftp") for _ in range(2)])
            ot = RR([sb([128, D], F32, "fot") for _ in range(2)])
            r_out = Res()
        for (t0, nt) in BLKS:
            j = 1 if t0 < NCTX else 0
            if final and j == 1:
                continue
            h, r_h = hb.next()
            S.dma("sp", h[:, :, 0:nt], hT_ap[:, t0:t0 + nt].rearrange("(c p) t -> p c t", p=128), r=hres_all(t0, nt), w=[r_h])
            q, r_q = sq.next()
            A(lambda e: e.activation(q[:, :, 0:nt], h[:, :, 0:nt], AF.Square), r=[r_h], w=[r_q])
            sp_, r_sp = ssp.next()
            for c in range(8):
                M(lambda e: e.matmul(sp_[:, 0:nt], ones_bf[:], q[:, c, 0:nt], start=(c == 0), stop=(c == 7)),
                  r=[r_q, r_const], w=[r_sp] if c == 0 else [], wa=[r_sp] if c else [], inc=(c == 7))
            rs, r_rs = rstd.next()
            A(lambda e: e.activation(rs[:, 0:nt], sp_[:, 0:nt], AF.Sqrt, bias=EPS, scale=1.0 / D), r=[r_sp], w=[r_rs])
            V(lambda e: e.reciprocal(rs[:, 0:nt], rs[:, 0:nt]), w=[r_rs])
            if not final:
                for c in range(8):
                    tm, r_tm = tmp.next()
                    V(lambda e: e.tensor_tensor(tm[:, 0:nt], h[:, c, 0:nt], rs[:, 0:nt], ALU.mult), r=[r_h, r_rs], w=[r_tm])
                    A(lambda e: e.activation(inT[:, c, t0:t0 + nt], tm[:, 0:nt], AF.Identity, bias=mod_bi[:, c, j:j + 1], scale=mod_sc[:, c, j:j + 1]),
                      r=[r_tm, r_mod], wa=[r_inT])
            else:
                hn_, r_hn = hn.next()
                for c in range(8):
                    V(lambda e: e.scalar_tensor_tensor(hn_[:, c, 0:nt], h[:, c, 0:nt], fnw[:, c:c + 1], rs[:, 0:nt], ALU.mult, ALU.mult),
                      r=[r_h, r_rs, r_const], w=[r_hn] if c == 0 else [], wa=[r_hn] if c else [])
                for tl in range(nt // 128):
                    o, r_o = ot.next()
                    for half in range(2):
                        pt, r_pt = tp.next()
                        for jj in range(4):
                            c = half * 4 + jj
                            M(lambda e: e.transpose(pt[:, jj * 128:(jj + 1) * 128], hn_[:, c, tl * 128:(tl + 1) * 128], ident[:]),
                              r=[r_hn, r_const], w=[r_pt] if jj == 0 else [], wa=[r_pt] if jj else [])
                        if half:
                            A(lambda e: e.copy(o[:, 512:1024], pt[:]), r=[r_pt], wa=[r_o])
                        else:
                            V(lambda e: e.tensor_copy(o[:, 0:512], pt[:]), r=[r_pt], w=[r_o])
                    row = t0 - NCTX + tl * 128
                    S.dma("pool", out_t.ap()[row:row + 128, :], o[:], r=[r_o], wa=[r_out])
        if final:
            evs = dict(r_out.w)
            S._wait("sp", evs)

    def linear_fm(sb, ps, act, r_act, KC, W_ap, col_chunks, epi, blks=None, t_off=0, npt=4):
        wts = RR([sb([128, KC, 128], BF16, "lw") for _ in range(3)])
        pts = RR([ps([128, 512], F32, "lp") for _ in range(npt)])
        for ci, (c0, ncol) in enumerate(col_chunks):
            wt, r_wt = wts.next()
            S.dma("pool", wt[:, :, 0:ncol], W_ap[:, c0:c0 + ncol].rearrange("(k p) n -> p k n", p=128), w=[r_wt])
            for (t0, nt) in (blks or BLKS):
                pt, r_pt = pts.next()
                for k in range(KC):
                    M(lambda e: e.matmul(pt[0:ncol, 0:nt], wt[:, k, 0:ncol], act[:, k, t0 - t_off:t0 - t_off + nt], start=(k == 0), stop=(k == KC - 1)),
                      r=[r_wt, r_act], w=[r_pt] if k == 0 else [], wa=[r_pt] if k else [], inc=(k == KC - 1))
                epi(ci, c0, ncol, t0, nt, pt, r_pt)

    def linear_tm(sb, ps, act, r_act, KC, W_ap, c0, ncols, epi):
        wts = RR([sb([128, KC, 512], BF16, "lwt") for _ in range(2)])
        pts = RR([ps([128, 512], F32, "lpt") for _ in range(2)])
        for g0 in range(0, ncols, 512):
            n = min(512, ncols - g0)
            wt, r_wt = wts.next()
            S.dma("pool", wt[:, :, 0:n], W_ap[:, c0 + g0:c0 + g0 + n].rearrange("(k p) n -> p k n", p=128), w=[r_wt])
            for tt in range(NTT):
                pt, r_pt = pts.next()
                for k in range(KC):
                    M(lambda e: e.matmul(pt[:, 0:n], act[:, k, tt * 128:(tt + 1) * 128], wt[:, k, 0:n], start=(k == 0), stop=(k == KC - 1)),
                      r=[r_wt, r_act], w=[r_pt] if k == 0 else [], wa=[r_pt] if k else [], inc=(k == KC - 1))
                epi(g0, n, tt, pt, r_pt)

    def make_resid_epi(sb):
        hts = RR([sb([128, 512], F32, "rh") for _ in range(3)])

        def epi(ci, c0, ncol, t0, nt, pt, r_pt):
            c = c0 // 128
            j = 1 if t0 < NCTX else 0
            ht, r_ht = hts.next()
            S.dma("sp", ht[:, 0:nt], hT_ap[c * 128:(c + 1) * 128, t0:t0 + nt], r=hres(c, t0, nt), w=[r_ht])
            V(lambda e: e.scalar_tensor_tensor(ht[:, 0:nt], pt[:, 0:nt], mod_gt[:, c, j:j + 1], ht[:, 0:nt], ALU.mult, ALU.add),
              r=[r_pt, r_mod], w=[r_ht])
            S.dma("pool", hT_ap[c * 128:(c + 1) * 128, t0:t0 + nt], ht[:, 0:nt], r=[r_ht], wa=hres(c, t0, nt))
        return epi

    def mamba_layer(lay):
        x_tm = dscr("x_tm%d" % uid[0], [T, 2048], BF16)
        B_tm = dscr("B_tm%d" % uid[0], [T, 1024], BF16)
        BT_d = dscr("BT_d%d" % uid[0], [8, 128, T], BF16)
        CT_d = dscr("CT_d%d" % uid[0], [8, 128, T], BF16)
        sz_tm = dscr("sz_tm%d" % uid[0], [T, 2048], BF16)
        laT_d = dscr("laT_d%d" % uid[0], [64, T], F32)
        ltot_d = dscr("ltot_d%d" % uid[0], [NTT, 64], F32)
        Yacc = dscr("Yacc%d" % uid[0], [T, 2048], F32)
        uid[0] += 1
        r_xtm, r_Btm, r_BT, r_CT, r_sz, r_laT, r_ltot = Res(), Res(), Res(), Res(), Res(), Res(), Res()
        r_Y = [Res() for _ in range(NTT)]
        with ExitStack() as lst:
            lsb, lps = mk(lst)
            la_tm = lsb([128, NTT, 64], F32, "la_tm")
            dt_tm = lsb([128, NTT, 64], F32, "dt_tm")
            LTB = lsb([128, NTT, 64], F32, "LTB")
            r_tabs = Res()
            with ExitStack() as st1:
                sb1, ps1 = mk(st1)
                inT = sb1([128, 8, T], BF16, "inT")
                r_inT = Res()
                with ExitStack() as ph:
                    sb, ps = mk(ph)
                    pre_pass(lay, sb, ps, inT, r_inT)
                    S.barrier()
                with ExitStack() as ph:
                    sb, ps = mk(ph)
                    convw = sb([128, 32, 5], F32, "convw")
                    convb = sb([128, 32], F32, "convb")
                    r_cv = Res()
                    S.dma("sp", convw[:], lay["convw"].ap(), w=[r_cv])
                    S.dma("sp", convb[:], lay["convb"].ap(), wa=[r_cv])
                    xr = sb([128, T + 8], F32, "xr")
                    r_xr = Res()
                    G(lambda e: e.memset(xr[:], 0.0), w=[r_xr])
                    acc = sb([128, T], F32, "cacc")
                    r_acc = Res()
                    xo = RR([sb([128, T], BF16, "cxo") for _ in range(2)])
                    tps = RR([ps([128, 512], BF16, "ctp") for _ in range(2)])
                    tos = RR([sb([128, 512], BF16, "cto") for _ in range(3)])
                    state = {}

                    def epi_xbc(ci, c0, ncol, t0, nt, pt, r_pt):
                        off = 2 if t0 < NCTX else 6
                        if ci % 2:
                            A(lambda e: e.copy(xr[:, t0 + off:t0 + off + nt], pt[:, 0:nt]), r=[r_pt], wa=[r_xr])
                        else:
                            V(lambda e: e.tensor_copy(xr[:, t0 + off:t0 + off + nt], pt[:, 0:nt]), r=[r_pt], wa=[r_xr])
                        if t0 + nt < T:
                            return
                        segs = [(0, NCTX, 0), (NCTX, nlat, 4)]
                        for (s0, sn, dl) in segs:
                            A(lambda e: e.activation(acc[:, s0:s0 + sn], xr[:, s0 + dl:s0 + dl + sn], AF.Identity,
                                                     bias=convb[:, ci:ci + 1], scale=convw[:, ci, 0:1]), r=[r_xr, r_cv], wa=[r_acc])
                            for k in range(1, 5):
                                V(lambda e: e.scalar_tensor_tensor(acc[:, s0:s0 + sn], xr[:, s0 + dl + k:s0 + dl + k + sn], convw[:, ci, k:k + 1],
                                                                   acc[:, s0:s0 + sn], ALU.mult, ALU.add), r=[r_xr, r_cv], w=[r_acc])
                        o, r_o = xo.next()
                        A(lambda e: e.activation(o[:], acc[:], AF.Silu), r=[r_acc], w=[r_o])
                        if ci < 24:
                            dst, col0, r_d = (x_tm, ci * 128, r_xtm) if ci < 16 else (B_tm, (ci - 16) * 128, r_Btm)
                            for t4 in range(0, NTT, 4):
                                n4 = min(4, NTT - t4)
                                tp_, r_tp = tps.next()
                                for q in range(n4):
                                    M(lambda e: e.transpose(tp_[:, q * 128:(q + 1) * 128], o[:, (t4 + q) * 128:(t4 + q + 1) * 128], identb[:]),
                                      r=[r_o, r_const], w=[r_tp] if q == 0 else [], wa=[r_tp] if q else [])
                                to, r_to = tos.next()
                                A(lambda e: e.copy(to[:, 0:n4 * 128], tp_[:, 0:n4 * 128]), r=[r_tp], w=[r_to])
                                S.dma("sp", dst.ap()[t4 * 128:(t4 + n4) * 128, col0:col0 + 128].rearrange("(q p) c -> p q c", p=128),
                                      to[:, 0:n4 * 128].rearrange("p (q c) -> p q c", q=n4), r=[r_to], wa=[r_d])
                        if ci >= 16:
                            gg = (ci - 16) % 8
                            dd, r_dd = (BT_d, r_BT) if ci < 24 else (CT_d, r_CT)
                            S.dma("sp", dd.ap()[gg], o[:], r=[r_o], wa=[r_dd])

                    linear_fm(sb, ps, inT, r_inT, 8, lay["inw"].ap(), [(2048 + 128 * i, 128) for i in range(32)], epi_xbc)
                    S.barrier()
                with ExitStack() as ph:
                    sb, ps = mk(ph)
                    dtT = sb([64, NTT, 128], F32, "dtT")
                    dA = sb([64, NTT, 128], F32, "dA")
                    laP = sb([64, NTT, 128], F32, "laP")
                    laT = sb([64, NTT, 128], F32, "laT")
                    rp = sb([64, NTT, 128], F32, "rp")
                    r_dt, r_dA, r_laP, r_laTs, r_rp = Res(), Res(), Res(), Res(), Res()
                    sm = sb([64, 4], F32, "dtsm")
                    r_sm = Res()
                    S.dma("sp", sm[:, 0:1], lay["alog"].ap(), w=[r_sm])
                    S.dma("sp", sm[:, 1:2], lay["dtb"].ap(), wa=[r_sm])
                    A(lambda e: e.activation(sm[:, 2:3], sm[:, 0:1], AF.Exp), r=[r_sm], wa=[r_sm])
                    V(lambda e: e.tensor_scalar(sm[:, 3:4], sm[:, 2:3], -1.0, None, ALU.mult), r=[r_sm], wa=[r_sm])
                    G(lambda e: e.memset(rp[:], 1.0), w=[r_rp])
                    G(lambda e: e.memset(rp[:, :, 0:1], 0.0), w=[r_rp])
                    dtf = dtT[:].rearrange("p c l -> p (c l)")

                    def epi_dt(ci, c0, ncol, t0, nt, pt, r_pt):
                        A(lambda e: e.activation(dtf[:, t0:t0 + nt], pt[0:64, 0:nt], AF.Exp, bias=sm[:, 1:2], scale=1.0), r=[r_pt, r_sm], wa=[r_dt])
                    linear_fm(sb, ps, inT, r_inT, 8, lay["inw"].ap(), [(6144, 64)], epi_dt)
                    A(lambda e: e.activation(dtf, dtf, AF.Ln, bias=1.0, scale=1.0), w=[r_dt])
                    V(lambda e: e.tensor_scalar(dA[:], dtT[:], sm[:, 3:4], None, ALU.mult), r=[r_dt, r_sm], w=[r_dA])
                    V(lambda e: e.tensor_tensor_scan(laP[:].rearrange("p c l -> p (c l)"), rp[:].rearrange("p c l -> p (c l)"),
                                                     dA[:].rearrange("p c l -> p (c l)"), 0.0, ALU.mult, ALU.add), r=[r_rp, r_dA], w=[r_laP])
                    V(lambda e: e.tensor_copy(laT[0:32], laP[0:32]), r=[r_laP], w=[r_laTs])
                    V(lambda e: e.tensor_tensor(laT[32:64], dA[32:64], laP[32:64], ALU.subtract), r=[r_laP, r_dA], wa=[r_laTs])
                    V(lambda e: e.tensor_tensor(laT[32:64], laT[32:64], laP[32:64, :, 127:128].broadcast_to([32, NTT, 128]), ALU.add), r=[r_laP], w=[r_laTs])
                    S.dma("sp", laT_d.ap(), laT[:].rearrange("p c l -> p (c l)"), r=[r_laTs], w=[r_laT])
                    tpp = RR([ps([128, 64], F32, "dtp") for _ in range(2)])
                    for tt in range(NTT):
                        for (src, r_src, dst) in ((laT, r_laTs, la_tm), (dtT, r_dt, dt_tm)):
                            tp_, r_tp = tpp.next()
                            M(lambda e: e.transpose(tp_[:], src[:, tt, :], ident[0:64, 0:64]), r=[r_src, r_const], w=[r_tp])
                            V(lambda e: e.tensor_copy(dst[:, tt, :], tp_[:]), r=[r_tp], wa=[r_tabs])
                    S.dma("sp", ltot_d.ap()[:, 0:32], la_tm[127:128, :, 0:32], r=[r_tabs], w=[r_ltot])
                    S.dma("sp", ltot_d.ap()[:, 32:64], la_tm[0:1, :, 32:64], r=[r_tabs], wa=[r_ltot])
                    S.dma("sp", LTB[:].rearrange("p c h -> p (c h)"), bass.AP(ltot_d, 0, [[0, 128], [1, NTT * 64]]), r=[r_ltot], wa=[r_tabs])
                    S.barrier()
                with ExitStack() as ph:
                    sb, ps = mk(ph)
                    zo = RR([sb([128, 512], BF16, "zo") for _ in range(3)])

                    def epi_z(g0, n, tt, pt, r_pt):
                        o, r_o = zo.next()
                        A(lambda e: e.activation(o[:, 0:n], pt[:, 0:n], AF.Silu), r=[r_pt], w=[r_o])
                        S.dma("sp", sz_tm.ap()[tt * 128:(tt + 1) * 128, g0:g0 + n], o[:, 0:n], r=[r_o], wa=[r_sz])
                    linear_tm(sb, ps, inT, r_inT, 8, lay["inw"].ap(), 0, 2048, epi_z)
                    S.barrier()
            with ExitStack() as ph:
                sb, ps = mk(ph)
                masks = sb([128, 2, 128], F32, "masks")
                r_mk = Res()
                S.dma("sp", masks[:], masks_in.ap().rearrange("d s l -> s d l"), w=[r_mk])
                xt_p = RR([sb([128, 2048], BF16, "sx") for _ in range(2)])
                bt_p = RR([sb([128, 1024], BF16, "sB") for _ in range(2)])
                BTs_p = RR([sb([128, 8, 128], BF16, "sBT") for _ in range(2)])
                CTs_p = RR([sb([128, 8, 128], BF16, "sCT") for _ in range(2)])
                LaB_p = RR([sb([128, 32, 128], F32, "sLaB") for _ in range(2)])
                dmat_p = RR([sb([128, 32, 128], F32, "dmat") for _ in range(2)])
                decay_p = RR([sb([128, 32, 128], BF16, "decay") for _ in range(2)])
                wT_p = RR([sb([128, 32, 128], BF16, "wT") for _ in range(2)])
                CBm_p = RR([sb([128, 8, 128], BF16, "CBm") for _ in range(2)])
                xdt_p = RR([sb([128, 2048], BF16, "xdt") for _ in range(2)])
                xw_p = RR([sb([128, 2048], BF16, "xw") for _ in range(2)])
                sml = sb([128, 4, 32], F32, "ssml")
                r_sml = Res()
                ST = sb([128, 2048], F32, "ST")
                r_ST = Res()
                prevb = sb([128, 2048], BF16, "prevb")
                r_prevb = Res()
                ysb = sb([128, 2048], F32, "ysb")
                r_ysb = Res()
                eyo = sb([128, 1024], F32, "eyo")
                r_eyo = Res()
                yin_p = RR([sb([128, 2048], F32, "yin") for _ in range(2)])
                cbp = ps([128, 8, 128], F32, "cbp")
                r_cbp = Res()
                ydp = ps([128, 1024], F32, "ydp")
                r_ydp = Res()
                yop = ps([128, 1024], F32, "yop")
                r_yop = Res()
                stp = ps([128, 1024], F32, "stp")
                r_stp = Res()
                for dr in range(2):
                    order = list(range(NTT)) if dr == 0 else [1, 0] + list(range(NTT - 1, 1, -1))
                    V(lambda e: e.memset(ST[:], 0.0), w=[r_ST])
                    hc = dr * 32
                    for c in order:
                        tok = slice(c * 128, (c + 1) * 128)
                        xt, r_xt = xt_p.next()
                        S.dma("sp", xt[:], x_tm.ap()[tok, :], r=[r_xtm], w=[r_xt])
                        bt, r_bt = bt_p.next()
                        S.dma("sp", bt[:], B_tm.ap()[tok, :], r=[r_Btm], w=[r_bt])
                        BTs, r_BTs = BTs_p.next()
                        S.dma("sp", BTs[:], BT_d.ap()[:, :, tok].rearrange("g n t -> n g t"), r=[r_BT], w=[r_BTs])
                        CTs, r_CTs = CTs_p.next()
                        S.dma("sp", CTs[:], CT_d.ap()[:, :, tok].rearrange("g n t -> n g t"), r=[r_CT], w=[r_CTs])
                        LaB, r_LaB = LaB_p.next()
                        S.dma("sp", LaB[:], bass.AP(laT_d, hc * T + c * 128, [[0, 128], [T, 32], [1, 128]]), r=[r_laT], w=[r_LaB])
                        la_c = la_tm[:, c, hc:hc + 32]
                        dmat, r_dmat = dmat_p.next()
                        decay, r_decay = decay_p.next()
                        wT, r_wT = wT_p.next()
                        CBm, r_CBm = CBm_p.next()
                        xdt, r_xdt = xdt_p.next()
                        xw, r_xw = xw_p.next()
                        A(lambda e: e.activation(sml[:, 0, :], la_c, AF.Exp), r=[r_tabs], w=[r_sml])
                        V(lambda e: e.tensor_tensor(sml[:, 3, :], LTB[:, c, hc:hc + 32], la_c, ALU.subtract), r=[r_tabs], w=[r_sml])
                        V(lambda e: e.tensor_single_scalar(sml[:, 3, :], sml[:, 3, :], 0.0, ALU.min), w=[r_sml])
                        A(lambda e: e.activation(sml[:, 1, :], sml[:, 3, :], AF.Exp), w=[r_sml])
                        A(lambda e: e.activation(sml[:, 2, :], LTB[:, c, hc:hc + 32], AF.Exp), r=[r_tabs], w=[r_sml])
                        for g in range(8):
                            M(lambda e: e.matmul(cbp[:, g, :], BTs[:, g, :], CTs[:, g, :], start=True, stop=True), r=[r_BTs, r_CTs],
                              w=[r_cbp] if g == 0 else [], wa=[r_cbp] if g else [], inc=(g == 7))
                        V(lambda e: e.tensor_tensor(CBm[:], cbp[:], masks[:, dr:dr + 1, :].broadcast_to([128, 8, 128]), ALU.mult),
                          r=[r_cbp, r_mk], w=[r_CBm])
                        for h in range(32):
                            V(lambda e: e.tensor_scalar(dmat[:, h, :], LaB[:, h, :], la_tm[:, c, hc + h:hc + h + 1], 0.0, ALU.subtract, ALU.min),
                              r=[r_LaB, r_tabs], w=[r_dmat] if h == 0 else [], wa=[r_dmat] if h else [])
                        A(lambda e: e.activation(decay[:], dmat[:], AF.Exp), r=[r_dmat], w=[r_decay])
                        V(lambda e: e.tensor_tensor(wT[:].rearrange("p (g h) l -> p g h l", g=8), decay[:].rearrange("p (g h) l -> p g h l", g=8),
                                                    CBm[:].unsqueeze(2).broadcast_to([128, 8, 4, 128]), ALU.mult), r=[r_decay, r_CBm], w=[r_wT])
                        V(lambda e: e.tensor_tensor(xdt[:].rearrange("p (h q) -> p h q", h=32), xt[:].rearrange("p (h q) -> p h q", h=32),
                                                    dt_tm[:, c, hc:hc + 32].unsqueeze(2).broadcast_to([128, 32, 64]), ALU.mult), r=[r_xt, r_tabs], w=[r_xdt])
                        V(lambda e: e.tensor_tensor(xw[:].rearrange("p (h q) -> p h q", h=32), xdt[:].rearrange("p (h q) -> p h q", h=32),
                                                    sml[:, 1, :].unsqueeze(2).broadcast_to([128, 32, 64]), ALU.mult), r=[r_xdt, r_sml], w=[r_xw])
                        A(lambda e: e.copy(prevb[:], ST[:]), r=[r_ST], w=[r_prevb])
                        if dr == 1:
                            yin, r_yin = yin_p.next()
                            S.dma("sp", yin[:], Yacc.ap()[tok, :], r=[r_Y[c]], w=[r_yin])
                        for gh in range(2):
                            cs = slice(gh * 1024, (gh + 1) * 1024)
                            for hl in range(16):
                                h = gh * 16 + hl
                                M(lambda e: e.matmul(ydp[:, hl * 64:(hl + 1) * 64], wT[:, h, :], xdt[:, h * 64:(h + 1) * 64], start=True, stop=True),
                                  r=[r_wT, r_xdt], w=[r_ydp] if hl == 0 else [], wa=[r_ydp] if hl else [], inc=(hl == 15))
                            for gl in range(4):
                                g = gh * 4 + gl
                                M(lambda e: e.matmul(yop[:, gl * 256:(gl + 1) * 256], CTs[:, g, :], prevb[:, g * 256:(g + 1) * 256], start=True, stop=True),
                                  r=[r_CTs, r_prevb], w=[r_yop] if gl == 0 else [], wa=[r_yop] if gl else [], inc=(gl == 3))
                            for gl in range(4):
                                g = gh * 4 + gl
                                M(lambda e: e.matmul(stp[:, gl * 256:(gl + 1) * 256], bt[:, g * 128:(g + 1) * 128], xw[:, g * 256:(g + 1) * 256], start=True, stop=True),
                                  r=[r_bt, r_xw], w=[r_stp] if gl == 0 else [], wa=[r_stp] if gl else [], inc=(gl == 3))
                            if dr == 1:
                                V(lambda e: e.tensor_tensor(ysb[:, cs], ydp[:], yin[:, cs], ALU.add), r=[r_ydp, r_yin], w=[r_ysb] if gh == 0 else [], wa=[r_ysb] if gh else [])
                            else:
                                A(lambda e: e.copy(ysb[:, cs], ydp[:]), r=[r_ydp], w=[r_ysb] if gh == 0 else [], wa=[r_ysb] if gh else [])
                            for hl in range(16):
                                h = gh * 16 + hl
                                A(lambda e: e.activation(eyo[:, hl * 64:(hl + 1) * 64], yop[:, hl * 64:(hl + 1) * 64], AF.Identity, scale=sml[:, 0, h:h + 1]),
                                  r=[r_yop, r_sml], w=[r_eyo] if hl == 0 else [], wa=[r_eyo] if hl else [])
                            V(lambda e: e.tensor_tensor(ysb[:, cs], ysb[:, cs], eyo[:], ALU.add), r=[r_eyo], w=[r_ysb])
                            V(lambda e: e.tensor_tensor(ST[:, cs].rearrange("p (h q) -> p h q", h=16), ST[:, cs].rearrange("p (h q) -> p h q", h=16),
                                                        sml[:, 2, gh * 16:(gh + 1) * 16].unsqueeze(2).broadcast_to([128, 16, 64]), ALU.mult),
                              r=[r_sml, r_prevb], w=[r_ST])
                            V(lambda e: e.tensor_tensor(ST[:, cs], ST[:, cs], stp[:], ALU.add), r=[r_stp], w=[r_ST])
                        S.dma("pool", Yacc.ap()[tok, :], ysb[:], r=[r_ysb], w=[r_Y[c]])
                S.barrier()
            with ExitStack() as ph:
                sb, ps = mk(ph)
                dvec = sb([128, 2048], F32, "dvec")
                mnw = sb([128, 2048], F32, "mnw")
                r_dv = Res()
                S.dma("sp", dvec[:], bass.AP(lay["dvec"], 0, [[0, 128], [1, 2048]]), w=[r_dv])
                S.dma("sp", mnw[:], bass.AP(lay["mnw"], 0, [[0, 128], [1, 2048]]), wa=[r_dv])
                ow = sb([128, 16, D], BF16, "ow")
                r_ow = Res()
                for k4 in range(4):
                    S.dma("pool", ow[:, k4 * 4:(k4 + 1) * 4, :], lay["outw"].ap()[k4 * 512:(k4 + 1) * 512, :].rearrange("(k p) n -> p k n", p=128),
                          w=[r_ow] if k4 == 0 else [], wa=[r_ow] if k4 else [])
                y_p = RR([sb([128, 2048], F32, "ty") for _ in range(2)])
                x_p = RR([sb([128, 2048], BF16, "tx") for _ in range(2)])
                z_p = RR([sb([128, 2048], BF16, "tz") for _ in range(2)])
                g_p = RR([sb([128, 2048], F32, "tg") for _ in range(2)])
                gb_p = RR([sb([128, 2048], BF16, "tgb") for _ in range(2)])
                junk = sb([128, 2048], BF16, "tjunk")
                r_junk = Res()
                ss_p = RR([sb([128, 2], F32, "tss") for _ in range(2)])
                gT_p = RR([sb([128, 16, 512], BF16, "tgT") for _ in range(2)])
                tp_p = RR([ps([128, 512], BF16, "ttp") for _ in range(2)])
                op_p = RR([ps([128, 512], F32, "top") for _ in range(3)])
                ht_p = RR([sb([128, 8, 512], F32, "tht") for _ in range(2)])
                groups = [(0, 2)] + [(2 + 4 * i, 4) for i in range((NTT - 2) // 4)]
                for (tt0, ng) in groups:
                    j = 1 if tt0 < 2 else 0
                    t0, nt = tt0 * 128, ng * 128
                    gT, r_gT = gT_p.next()
                    for q4 in range(ng):
                        tt = tt0 + q4
                        tok = slice(tt * 128, (tt + 1) * 128)
                        y, r_y = y_p.next()
                        S.dma("sp", y[:], Yacc.ap()[tok, :], r=[r_Y[tt]], w=[r_y])
                        xt, r_xt = x_p.next()
                        S.dma("sp", xt[:], x_tm.ap()[tok, :], r=[r_xtm], w=[r_xt])
                        zt, r_zt = z_p.next()
                        S.dma("sp", zt[:], sz_tm.ap()[tok, :], r=[r_sz], w=[r_zt])
                        gt_, r_g = g_p.next()
                        V(lambda e: e.tensor_tensor(gt_[:], xt[:], dvec[:], ALU.mult), r=[r_xt, r_dv], w=[r_g])
                        V(lambda e: e.tensor_tensor(gt_[:], gt_[:], y[:], ALU.add), r=[r_y], w=[r_g])
                        V(lambda e: e.tensor_tensor(gt_[:], gt_[:], zt[:], ALU.mult), r=[r_zt], w=[r_g])
                        ss, r_ss = ss_p.next()
                        A(lambda e: e.activation(junk[:], gt_[:], AF.Square, accum_out=ss[:, 0:1]), r=[r_g], w=[r_junk, r_ss])
                        A(lambda e: e.activation(ss[:, 1:2], ss[:, 0:1], AF.Sqrt, bias=EPS, scale=1.0 / 2048), w=[r_ss])
                        V(lambda e: e.reciprocal(ss[:, 1:2], ss[:, 1:2]), w=[r_ss])
                        gb, r_gb = gb_p.next()
                        V(lambda e: e.scalar_tensor_tensor(gb[:], gt_[:], ss[:, 1:2], mnw[:], ALU.mult, ALU.mult), r=[r_g, r_ss, r_dv], w=[r_gb])
                        for k4 in range(4):
                            tp_, r_tp = tp_p.next()
                            for q in range(4):
                                k = k4 * 4 + q
                                M(lambda e: e.transpose(tp_[:, q * 128:(q + 1) * 128], gb[:, k * 128:(k + 1) * 128], identb[:]),
                                  r=[r_gb, r_const], w=[r_tp] if q == 0 else [], wa=[r_tp] if q else [], inc=(q == 3))
                            A(lambda e: e.copy(gT[:, k4 * 4:(k4 + 1) * 4, q4 * 128:(q4 + 1) * 128], tp_[:].rearrange("p (q t) -> p q t", q=4)), r=[r_tp],
                              w=[r_gT] if (k4 == 0 and q4 == 0) else [], wa=[] if (k4 == 0 and q4 == 0) else [r_gT])
                    ht, r_ht = ht_p.next()
                    hr = [r_hT[(c, tt)] for c in range(8) for tt in range(tt0, tt0 + ng)]
                    S.dma("sp", ht[:, :, 0:nt], hT_ap[:, t0:t0 + nt].rearrange("(c p) t -> p c t", p=128), r=hr, w=[r_ht])
                    for dc in range(8):
                        op, r_op = op_p.next()
                        for k in range(16):
                            M(lambda e: e.matmul(op[:, 0:nt], ow[:, k, dc * 128:(dc + 1) * 128], gT[:, k, 0:nt], start=(k == 0), stop=(k == 15)),
                              r=[r_ow, r_gT], w=[r_op] if k == 0 else [], wa=[r_op] if k else [], inc=(k == 15))
                        V(lambda e: e.scalar_tensor_tensor(ht[:, dc, 0:nt], op[:, 0:nt], mod_gt[:, dc, j:j + 1], ht[:, dc, 0:nt], ALU.mult, ALU.add),
                          r=[r_op, r_mod], w=[r_ht])
                    S.dma("pool", hT_ap[:, t0:t0 + nt].rearrange("(c p) t -> p c t", p=128), ht[:, :, 0:nt], r=[r_ht], wa=hr)
                S.barrier()

    def attn_layer(lay):
        qT_d = dscr("qT_d", [D, T], BF16)
        kT_d = dscr("kT_d", [256, T], BF16)
        v_tm = dscr("v_tm", [T, 256], BF16)
        sgT_d = dscr("sgT_d", [D, T], BF16)
        oT_d = dscr("oT_d", [D, T], BF16)
        r_q, r_k, r_v, r_sg, r_o = Res(), Res(), Res(), Res(), Res()
        with ExitStack() as st1:
            sb1, ps1 = mk(st1)
            inT = sb1([128, 8, T], BF16, "inT")
            r_inT = Res()
            with ExitStack() as ph:
                sb, ps = mk(ph)
                pre_pass(lay, sb, ps, inT, r_inT)
                S.barrier()
            with ExitStack() as ph:
                sb, ps = mk(ph)
                rope = sb([128, 2, nlat], F32, "rope")
                r_cst = Res()
                S.dma("sp", rope[:], lay["rope"].ap().rearrange("a p t -> p a t"), w=[r_cst])
                qkw = sb([128, 2], F32, "qkw")
                S.dma("sp", qkw[:], lay["qkw"].ap(), wa=[r_cst])
                permb = sb([128, 128], BF16, "permb")
                S.dma("pool", permb[:], lay["perm"].ap(), wa=[r_cst])
                bones = sb([128, 128], BF16, "bones")
                S.dma("pool", bones[:], lay["bones"].ap(), wa=[r_cst])
                sq_p = RR([sb([128, 512], BF16, "asq") for _ in range(2)])
                ss_p = RR([ps([128, 512], F32, "ass") for _ in range(2)])
                rs_p = RR([sb([128, 512], F32, "ars") for _ in range(2)])
                qn_p = RR([sb([128, 512], F32, "aqn") for _ in range(2)])
                qb_p = RR([sb([128, 512], BF16, "aqb") for _ in range(2)])
                rot_p = RR([ps([128, 512], F32, "arot") for _ in range(2)])
                t1_p = RR([sb([128, 512], F32, "at1") for _ in range(2)])
                t2_p = RR([sb([128, 512], F32, "at2") for _ in range(2)])
                qo_p = RR([sb([128, 512], BF16, "aqo") for _ in range(3)])

                def epi_qkg(ci, c0, ncol, t0, nt, pt, r_pt):
                    if c0 >= 1536:
                        o, r_o_ = qo_p.next()
                        A(lambda e: e.activation(o[:, 0:nt], pt[:, 0:nt], AF.Silu), r=[r_pt], w=[r_o_])
                        cg = (c0 - 1536) // 128
                        S.dma("sp", sgT_d.ap()[cg * 128:(cg + 1) * 128, t0:t0 + nt], o[:, 0:nt], r=[r_o_], wa=[r_sg])
                        return
                    isq = c0 < 1024
                    wcol = 0 if isq else 1
                    sq, r_sq = sq_p.next()
                    A(lambda e: e.activation(sq[:, 0:nt], pt[:, 0:nt], AF.Square), r=[r_pt], w=[r_sq])
                    ss, r_ss = ss_p.next()
                    M(lambda e: e.matmul(ss[:, 0:nt], bones[:], sq[:, 0:nt], start=True, stop=True), r=[r_sq, r_cst], w=[r_ss])
                    rs, r_rs = rs_p.next()
                    A(lambda e: e.activation(rs[:, 0:nt], ss[:, 0:nt], AF.Sqrt, bias=EPS, scale=1.0 / 64), r=[r_ss], w=[r_rs])
                    V(lambda e: e.reciprocal(rs[:, 0:nt], rs[:, 0:nt]), w=[r_rs])
                    o, r_o_ = qo_p.next()
                    if t0 < NCTX:
                        V(lambda e: e.scalar_tensor_tensor(o[:, 0:nt], pt[:, 0:nt], qkw[:, wcol:wcol + 1], rs[:, 0:nt], ALU.mult, ALU.mult),
                          r=[r_pt, r_rs, r_cst], w=[r_o_])
                    else:
                        qn, r_qn = qn_p.next()
                        V(lambda e: e.scalar_tensor_tensor(qn[:, 0:nt], pt[:, 0:nt], qkw[:, wcol:wcol + 1], rs[:, 0:nt], ALU.mult, ALU.mult),
                          r=[r_pt, r_rs, r_cst], w=[r_qn])
                        qb, r_qb = qb_p.next()
                        A(lambda e: e.copy(qb[:, 0:nt], qn[:, 0:nt]), r=[r_qn], w=[r_qb])
                        rot, r_rot = rot_p.next()
                        M(lambda e: e.matmul(rot[:, 0:nt], permb[:], qb[:, 0:nt], start=True, stop=True), r=[r_qb, r_cst], w=[r_rot])
                        l0 = t0 - NCTX
                        t1, r_t1 = t1_p.next()
                        G(lambda e: e.tensor_tensor(t1[:, 0:nt], qn[:, 0:nt], rope[:, 0, l0:l0 + nt], ALU.mult), r=[r_qn, r_cst], w=[r_t1])
                        t2, r_t2 = t2_p.next()
                        V(lambda e: e.tensor_tensor(t2[:, 0:nt], rot[:, 0:nt], rope[:, 1, l0:l0 + nt], ALU.mult), r=[r_rot, r_cst], w=[r_t2])
                        V(lambda e: e.tensor_tensor(o[:, 0:nt], t1[:, 0:nt], t2[:, 0:nt], ALU.add), r=[r_t1, r_t2], w=[r_o_])
                    if isq:
                        S.dma("sp", qT_d.ap()[c0:c0 + 128, t0:t0 + nt], o[:, 0:nt], r=[r_o_], wa=[r_q])
                    else:
                        S.dma("sp", kT_d.ap()[c0 - 1024:c0 - 1024 + 128, t0:t0 + nt], o[:, 0:nt], r=[r_o_], wa=[r_k])

                cols = [(128 * i, 128) for i in range(10)] + [(1536 + 128 * i, 128) for i in range(8)]
                linear_fm(sb, ps, inT, r_inT, 8, lay["inw"].ap(), cols, epi_qkg, npt=2)
                vo_p = RR([sb([128, 256], BF16, "avo") for _ in range(3)])

                def epi_v(g0, n, tt, pt, r_pt):
                    o, r_o_ = vo_p.next()
                    V(lambda e: e.tensor_copy(o[:, 0:n], pt[:, 0:n]), r=[r_pt], w=[r_o_])
                    S.dma("sp", v_tm.ap()[tt * 128:(tt + 1) * 128, :], o[:, 0:n], r=[r_o_], wa=[r_v])
                linear_tm(sb, ps, inT, r_inT, 8, lay["inw"].ap(), 1280, 256, epi_v)
                S.barrier()
        with ExitStack() as ph:
            sb, ps = mk(ph)
            onesf = sb([128, 64], F32, "aones")
            r_on = Res()
            G(lambda e: e.memset(onesf[:], 1.0), w=[r_on])
            Vg = sb([128, NTT, 65], BF16, "Vg")
            r_Vg = Res()
            G(lambda e: e.memset(Vg[:], 1.0), w=[r_Vg])
            kk_p = RR([sb([128, T], BF16, "kk") for _ in range(2)])
            qc_p = RR([sb([128, T], BF16, "qc") for _ in range(2)])
            sg_p = RR([sb([64, T], BF16, "sgh") for _ in range(2)])
            sp_p = RR([ps([128, 512], F32, "asp") for _ in range(4)])
            P_p = RR([sb([128, 512], BF16, "aP") for _ in range(4)])
            oa_p = RR([ps([128, 512], F32, "aoa") for _ in range(2)])
            bc_p = RR([ps([64, 512], F32, "abc") for _ in range(2)])
            osb_p = RR([sb([128, 512], F32, "aosb") for _ in range(2)])
            o1_p = RR([sb([64, 512], F32, "ao1") for _ in range(2)])
            og_p = RR([sb([64, 512], BF16, "aog") for _ in range(2)])
            for gk in range(4):
                kk, r_kk = kk_p.next()
                S.dma("sp", kk[0:64, :], kT_d.ap()[gk * 64:(gk + 1) * 64, :], r=[r_k], w=[r_kk])
                S.dma("sp", kk[64:128, :], kT_d.ap()[gk * 64:(gk + 1) * 64, :], r=[r_k], wa=[r_kk])
                S.dma("sp", Vg[:, :, 0:64], v_tm.ap()[:, gk * 64:(gk + 1) * 64].rearrange("(t p) d -> p t d", p=128), r=[r_v], w=[r_Vg])
                for qc in (2 * gk, 2 * gk + 1):
                    qt, r_qt = qc_p.next()
                    S.dma("sp", qt[:], qT_d.ap()[qc * 128:(qc + 1) * 128, :], r=[r_q], w=[r_qt])
                    for hh in range(2):
                        h = 2 * qc + hh
                        pr = slice(64 * hh, 64 * hh + 64)
                        sgh, r_sgh = sg_p.next()
                        S.dma("sp", sgh[:], sgT_d.ap()[h * 64:(h + 1) * 64, :], r=[r_sg], w=[r_sgh])
                        tasks = []
                        for (t0, nt) in BLKS:
                            ktiles = [0, 1] if t0 < NCTX else list(range(NTT))
                            for ki, kt in enumerate(ktiles):
                                tasks.append((t0, nt, ki, kt, len(ktiles)))
                        spq = {}
                        cur = {}
                        deferred = []

                        def emit_qk(ti):
                            t0, nt, ki, kt, nk = tasks[ti]
                            sp_, r_sp = sp_p.next()
                            M(lambda e: e.matmul(sp_[:, 0:nt], kk[pr, kt * 128:(kt + 1) * 128], qt[pr, t0:t0 + nt], start=True, stop=True),
                              r=[r_kk, r_qt], w=[r_sp])
                            spq[ti] = (sp_, r_sp)

                        def finalize_pe(args):
                            (t0, nt, osb, r_osb) = args
                            bc, r_bc = bc_p.next()
                            M(lambda e: e.matmul(bc[:, 0:nt], onesf[64:65, :], osb[64:65, 0:nt], start=True, stop=True), r=[r_osb, r_on], w=[r_bc])
                            o1, r_o1 = o1_p.next()
                            V(lambda e: e.tensor_tensor(o1[:, 0:nt], osb[0:64, 0:nt], bc[:, 0:nt], ALU.mult), r=[r_osb, r_bc], w=[r_o1])
                            og, r_og = og_p.next()
                            G(lambda e: e.tensor_tensor(og[:, 0:nt], o1[:, 0:nt], sgh[:, t0:t0 + nt], ALU.mult), r=[r_o1, r_sgh], w=[r_og])
                            S.dma("sp", oT_d.ap()[h * 64:(h + 1) * 64, t0:t0 + nt], og[:, 0:nt], r=[r_og], wa=[r_o])

                        LOOK = 3
                        for ti in range(min(LOOK, len(tasks))):
                            emit_qk(ti)
                        for ti in range(len(tasks)):
                            t0, nt, ki, kt, nk = tasks[ti]
                            sp_, r_sp = spq.pop(ti)
                            if ki == 0:
                                cur["oa"] = oa_p.next()
                            oa, r_oa = cur["oa"]
                            P, r_P = P_p.next()
                            A(lambda e: e.activation(P[:, 0:nt], sp_[:, 0:nt], AF.Exp, bias=-8.0, scale=0.125), r=[r_sp], w=[r_P])
                            M(lambda e: e.matmul(oa[0:65, 0:nt], Vg[:, kt, :], P[:, 0:nt], start=(ki == 0), stop=(ki == nk - 1)),
                              r=[r_Vg, r_P], w=[r_oa] if ki == 0 else [], wa=[r_oa] if ki else [], inc=(ki == nk - 1))
                            if ti + LOOK < len(tasks):
                                emit_qk(ti + LOOK)
                            deferred = [(n - 1, a) for (n, a) in deferred]
                            while deferred and deferred[0][0] <= 0:
                                finalize_pe(deferred.pop(0)[1])
                            if ki == nk - 1:
                                osb, r_osb = osb_p.next()
                                V(lambda e: e.tensor_copy(osb[0:65, 0:nt], oa[0:65, 0:nt]), r=[r_oa], w=[r_osb])
                                V(lambda e: e.reciprocal(osb[64:65, 0:nt], osb[64:65, 0:nt]), w=[r_osb])
                                deferred.append((4, (t0, nt, osb, r_osb)))
                        for (_, a) in deferred:
                            finalize_pe(a)
            S.barrier()
        with ExitStack() as ph:
            sb, ps = mk(ph)
            oT = sb([128, 8, T], BF16, "oT")
            r_oT = Res()
            S.dma("sp", oT[:], oT_d.ap().rearrange("(c p) t -> p c t", p=128), r=[r_o], w=[r_oT])
            linear_fm(sb, ps, oT, r_oT, 8, lay["outw"].ap(), [(128 * i, 128) for i in range(8)], make_resid_epi(sb))
            S.barrier()

    def s5_layer(lay):
        uT_d = dscr("uT_d", [D, T], BF16)
        szT_d = dscr("szT_d", [D, T], BF16)
        gT_d = dscr("gT_d", [D, T], BF16)
        y2T_d = dscr("y2T_d", [D, T], BF16)
        r_u, r_sz, r_g, r_y2 = Res(), Res(), Res(), Res()
        NLV = 1
        while (1 << (NLV - 1)) < T:
            NLV += 1
        with ExitStack() as st1:
            sb1, ps1 = mk(st1)
            inT = sb1([128, 8, T], BF16, "inT")
            r_inT = Res()
            with ExitStack() as ph:
                sb, ps = mk(ph)
                pre_pass(lay, sb, ps, inT, r_inT)
                S.barrier()
            with ExitStack() as ph:
                sb, ps = mk(ph)
                uo_p = RR([sb([128, 512], BF16, "suo") for _ in range(3)])

                def epi_uz(ci, c0, ncol, t0, nt, pt, r_pt):
                    o, r_o_ = uo_p.next()
                    if c0 < 1024:
                        V(lambda e: e.tensor_copy(o[:, 0:nt], pt[:, 0:nt]), r=[r_pt], w=[r_o_])
                        S.dma("sp", uT_d.ap()[c0:c0 + 128, t0:t0 + nt], o[:, 0:nt], r=[r_o_], wa=[r_u])
                    else:
                        A(lambda e: e.activation(o[:, 0:nt], pt[:, 0:nt], AF.Silu), r=[r_pt], w=[r_o_])
                        S.dma("sp", szT_d.ap()[c0 - 1024:c0 - 1024 + 128, t0:t0 + nt], o[:, 0:nt], r=[r_o_], wa=[r_sz])
                linear_fm(sb, ps, inT, r_inT, 8, lay["inw"].ap(), [(128 * i, 128) for i in range(16)], epi_uz)
                S.barrier()
        with ExitStack() as ph:
            sb, ps = mk(ph)
            lam = sb([128, 3, 64], F32, "lam")
            r_t = Res()
            S.dma("sp", lam[:], lay["lam"].ap(), w=[r_t])
            tb = sb([128, 16, 64], F32, "stb")
            tbi = sb([128, 64], I32, "stbi")
            coef = sb([128, 3, 64], F32, "coef")
            pw = sb([128, 64, NLV, 3], F32, "pw")
            lr, li, ls = lam[:, 0, :], lam[:, 1, :], lam[:, 2, :]
            X = lambda i: tb[:, i, :]

            def vt(fn):
                V(fn, w=[r_t])

            def at(fn):
                A(fn, w=[r_t])
            at(lambda e: e.activation(X(0), ls, AF.Exp))
            vt(lambda e: e.tensor_tensor(X(1), lr, X(0), ALU.mult))
            at(lambda e: e.activation(X(2), X(1), AF.Exp))
            vt(lambda e: e.tensor_tensor(X(3), li, X(0), ALU.mult))

            def sin_of(dst, src, shift):
                vt(lambda e: e.tensor_scalar(X(4), src, shift, 1.0 / (2 * PI), ALU.add, ALU.mult))
                vt(lambda e: e.tensor_copy(tbi[:], X(4)))
                vt(lambda e: e.tensor_copy(X(5), tbi[:]))
                vt(lambda e: e.tensor_scalar(X(4), src, shift, None, ALU.add))
                vt(lambda e: e.scalar_tensor_tensor(X(4), X(5), -2 * PI, X(4), ALU.mult, ALU.add))
                vt(lambda e: e.tensor_single_scalar(X(5), X(4), PI, ALU.is_gt))
                vt(lambda e: e.scalar_tensor_tensor(X(4), X(5), -2 * PI, X(4), ALU.mult, ALU.add))
                vt(lambda e: e.tensor_single_scalar(X(5), X(4), -PI, ALU.is_lt))
                vt(lambda e: e.scalar_tensor_tensor(X(4), X(5), 2 * PI, X(4), ALU.mult, ALU.add))
                at(lambda e: e.activation(dst, X(4), AF.Sin))
            sin_of(X(6), X(3), 0.0)
            sin_of(X(7), X(3), PI / 2)
            vt(lambda e: e.tensor_tensor(X(8), X(2), X(7), ALU.mult))
            vt(lambda e: e.tensor_tensor(X(9), X(2), X(6), ALU.mult))
            vt(lambda e: e.tensor_tensor(X(10), lr, lr, ALU.mult))
            vt(lambda e: e.tensor_tensor(X(11), li, li, ALU.mult))
            vt(lambda e: e.tensor_tensor(X(10), X(10), X(11), ALU.add))
            vt(lambda e: e.reciprocal(X(10), X(10)))
            vt(lambda e: e.tensor_scalar(X(11), X(8), -1.0, None, ALU.add))
            vt(lambda e: e.tensor_tensor(X(12), X(11), lr, ALU.mult))
            vt(lambda e: e.tensor_tensor(X(13), X(9), li, ALU.mult))
            vt(lambda e: e.tensor_tensor(X(12), X(12), X(13), ALU.add))
            vt(lambda e: e.tensor_tensor(coef[:, 0, :], X(12), X(10), ALU.mult))
            vt(lambda e: e.tensor_tensor(X(12), X(9), lr, ALU.mult))
            vt(lambda e: e.tensor_tensor(X(13), X(11), li, ALU.mult))
            vt(lambda e: e.tensor_tensor(X(12), X(12), X(13), ALU.subtract))
            vt(lambda e: e.tensor_tensor(coef[:, 1, :], X(12), X(10), ALU.mult))
            vt(lambda e: e.tensor_scalar(coef[:, 2, :], coef[:, 1, :], -1.0, None, ALU.mult))
            vt(lambda e: e.tensor_copy(pw[:, :, 0, 0], X(8)))
            vt(lambda e: e.tensor_copy(pw[:, :, 0, 1], X(9)))
            for lv in range(NLV):
                vt(lambda e: e.tensor_scalar(pw[:, :, lv, 2], pw[:, :, lv, 1], -1.0, None, ALU.mult))
                if lv + 1 < NLV:
                    vt(lambda e: e.tensor_tensor(X(12), pw[:, :, lv, 0], pw[:, :, lv, 0], ALU.mult))
                    vt(lambda e: e.tensor_tensor(X(13), pw[:, :, lv, 1], pw[:, :, lv, 1], ALU.mult))
                    vt(lambda e: e.tensor_tensor(pw[:, :, lv + 1, 0], X(12), X(13), ALU.subtract))
                    vt(lambda e: e.tensor_tensor(X(12), pw[:, :, lv, 0], pw[:, :, lv, 1], ALU.mult))
                    vt(lambda e: e.tensor_scalar(pw[:, :, lv + 1, 1], X(12), 2.0, None, ALU.mult))
            brt = sb([128, 2, 2, 32, 128], BF16, "brt")
            for q in range(2):
                for k in range(2):
                    for j8 in range(4):
                        S.dma("pool", brt[:, q, k, j8 * 8:(j8 + 1) * 8, :], lay["brt"].ap()[q, :, k, j8 * 8:(j8 + 1) * 8, :], wa=[r_t])
            crp = sb([128, 2, 2, 32, 32], BF16, "crp")
            S.dma("pool", crp[:, 0], lay["crp"].ap()[0], wa=[r_t])
            S.dma("pool", crp[:, 1], lay["crp"].ap()[1], wa=[r_t])
            at(lambda e: e.mul(crp[:, 1], crp[:, 1], -1.0))
            sd = sb([128, 8], F32, "sd")
            S.dma("sp", sd[:], lay["sd"].ap(), wa=[r_t])
            Xr = sb([128, T], F32, "Xr")
            Xi = sb([128, T], F32, "Xi")
            Yr = sb([128, T], F32, "Yr")
            Yi = sb([128, T], F32, "Yi")
            r_X, r_Yb, r_Yi = Res(), Res(), Res()
            xb = [[sb([128, T], BF16, "xb%d%d" % (k, q)) for q in range(2)] for k in range(2)]
            r_xb = [Res(), Res()]
            uc_p = RR([sb([128, T], BF16, "suc") for _ in range(2)])
            p12_p = RR([ps([128, 512], F32, "sp12") for _ in range(4)])
            tmp_p = RR([sb([128, 512], F32, "stmp") for _ in range(2)])
            yp_p = RR([ps([128, 512], F32, "syp") for _ in range(2)])
            yv = sb([128, T], F32, "yv")
            ga = Yr
            r_yv = Res()
            go_p = RR([sb([128, T], BF16, "sgo") for _ in range(1)])

            def sview(t, off, step, a0, cnt, mult):
                s0 = off + a0 * step
                st_ = mult * step
                return t[:, s0:s0 + (cnt - 1) * st_ + 1:st_]

            r_Xr, r_Xi, r_Yr, r_Yi = Res(), Res(), Res(), Res()
            r_or = [Res(), Res()]
            r_oi = [Res(), Res()]

            def scan(col, k):
                outr, outi = xb[k]
                rof = {id(Xr): r_Xr, id(Xi): r_Xi, id(Yr): r_Yr, id(Yi): r_Yi, id(outr): r_or[k], id(outi): r_oi[k]}

                def stt(o_t, o_ap, a_t, a_ap, sc, b_t, b_ap):
                    V(lambda e: e.scalar_tensor_tensor(o_ap, a_ap, sc, b_ap, ALU.mult, ALU.add),
                      r=[rof[id(a_t)], rof[id(b_t)], r_t, r_X, r_Yb], w=[rof[id(o_t)]])

                def cpy(o_t, o_ap, a_t, a_ap):
                    A(lambda e: e.copy(o_ap, a_ap), r=[rof[id(a_t)], r_X, r_Yb], w=[rof[id(o_t)]])

                def rec(tr, ti, off, step, n, lv, yoff, top):
                    if n == 1:
                        if top:
                            cpy(outr, outr[:, 0:1], tr, tr[:, off:off + 1])
                            cpy(outi, outi[:, 0:1], ti, ti[:, off:off + 1])
                        return
                    m = n // 2
                    ne = n - m
                    ar, ai, nai = pw[:, col, lv, 0:1], pw[:, col, lv, 1:2], pw[:, col, lv, 2:3]
                    Ev = lambda t, a0, cnt: sview(t, off, step, 2 * a0, cnt, 2)
                    Ov = lambda t, a0, cnt: sview(t, off, step, 2 * a0 + 1, cnt, 2)
                    yr, yi = Yr[:, yoff:yoff + m], Yi[:, yoff:yoff + m]
                    stt(Yr, yr, tr, Ev(tr, 0, m), ar, tr, Ov(tr, 0, m))
                    stt(Yi, yi, ti, Ev(ti, 0, m), ar, ti, Ov(ti, 0, m))
                    stt(Yr, yr, ti, Ev(ti, 0, m), nai, Yr, yr)
                    stt(Yi, yi, tr, Ev(tr, 0, m), ai, Yi, yi)
                    rec(Yr, Yi, yoff, 1, m, lv + 1, yoff + m, False)
                    ne1 = ne - 1
                    zr, zi = Yr[:, yoff:yoff + ne1], Yi[:, yoff:yoff + ne1]
                    if top:
                        cpy(outr, sview(outr, 0, 1, 1, m, 2), Yr, yr)
                        cpy(outi, sview(outi, 0, 1, 1, m, 2), Yi, yi)
                        cpy(outr, outr[:, 0:1], tr, tr[:, off:off + 1])
                        cpy(outi, outi[:, 0:1], ti, ti[:, off:off + 1])
                        if ne1 > 0:
                            stt(tr, Ev(tr, 1, ne1), Yr, zr, ar, tr, Ev(tr, 1, ne1))
                            stt(ti, Ev(ti, 1, ne1), Yi, zi, ar, ti, Ev(ti, 1, ne1))
                            V(lambda e: e.scalar_tensor_tensor(sview(outr, 0, 1, 2, ne1, 2), zi, nai, Ev(tr, 1, ne1), ALU.mult, ALU.add),
                              r=[r_Yi, r_Xr, r_t], w=[r_or[k]])
                            V(lambda e: e.scalar_tensor_tensor(sview(outi, 0, 1, 2, ne1, 2), zr, ai, Ev(ti, 1, ne1), ALU.mult, ALU.add),
                              r=[r_Yr, r_Xi, r_t], w=[r_oi[k]])
                    else:
                        cpy(tr, Ov(tr, 0, m), Yr, yr)
                        cpy(ti, Ov(ti, 0, m), Yi, yi)
                        if ne1 > 0:
                            stt(tr, Ev(tr, 1, ne1), Yr, zr, ar, tr, Ev(tr, 1, ne1))
                            stt(ti, Ev(ti, 1, ne1), Yi, zi, ar, ti, Ev(ti, 1, ne1))
                            stt(tr, Ev(tr, 1, ne1), Yi, zi, nai, tr, Ev(tr, 1, ne1))
                            stt(ti, Ev(ti, 1, ne1), Yr, zr, ai, ti, Ev(ti, 1, ne1))
                rec(Xr, Xi, 0, 1, T, 0, 0, True)

            def bwd_pos(t0, nt):
                return (NCTX - t0 - nt) if t0 < NCTX else (NCTX + T - t0 - nt)

            uc = None
            SK = ""
            for j in range(32):
                cj, jm = j // 4, j % 4
                pr = slice(32 * jm, 32 * jm + 32)
                if jm == 0:
                    uc, r_uc = uc_p.next()
                    S.dma("sp", uc[:], uT_d.ap()[cj * 128:(cj + 1) * 128, :], r=[r_u], w=[r_uc])
                for k in range(2):
                    col = k * 32 + j
                    for (t0, nt) in BLKS:
                        if k == 0:
                            i0 = t0
                            uv = uc[:, t0:t0 + nt]
                        else:
                            i0 = bwd_pos(t0, nt)
                            uv = uc[:, t0:t0 + nt][:, ::-1]
                        if "m" in SK:
                            continue
                        p1, r_p1 = p12_p.next()
                        p2, r_p2 = p12_p.next()
                        M(lambda e: e.matmul(p1[:, 0:nt], brt[:, 0, k, j, :], uv, start=True, stop=True), r=[r_uc, r_t], w=[r_p1])
                        M(lambda e: e.matmul(p2[:, 0:nt], brt[:, 1, k, j, :], uv, start=True, stop=True), r=[r_uc, r_t], w=[r_p2])
                        if "e" in SK:
                            continue
                        tm, r_tm = tmp_p.next()
                        if "a" not in SK:
                            A(lambda e: e.activation(tm[:, 0:nt], p2[:, 0:nt], AF.Identity, scale=coef[:, 2, col:col + 1]), r=[r_t], w=[r_tm, r_p2])
                        if "v" not in SK:
                            V(lambda e: e.scalar_tensor_tensor(Xr[:, i0:i0 + nt], p1[:, 0:nt], coef[:, 0, col:col + 1], tm[:, 0:nt], ALU.mult, ALU.add),
                              r=[r_tm, r_t], w=[r_Xr, r_p1])
                        tm2, r_tm2 = tmp_p.next()
                        if "a" not in SK:
                            A(lambda e: e.activation(tm2[:, 0:nt], p1[:, 0:nt], AF.Identity, scale=coef[:, 1, col:col + 1]), r=[r_t], w=[r_tm2, r_p1])
                        if "v" not in SK:
                            V(lambda e: e.scalar_tensor_tensor(Xi[:, i0:i0 + nt], p2[:, 0:nt], coef[:, 0, col:col + 1], tm2[:, 0:nt], ALU.mult, ALU.add),
                              r=[r_tm2, r_t], w=[r_Xi, r_p2])
                    if "s" not in SK:
                        scan(col, k)
                for (t0, nt) in (BLKS if "r" not in SK else []):
                    yp, r_yp = yp_p.next()
                    i0 = bwd_pos(t0, nt)
                    rv = lambda a: a[:, ::-1]
                    ops = [(crp[:, 0, 0, j, :], xb[0][0][:, t0:t0 + nt], r_or[0]), (crp[:, 1, 0, j, :], xb[0][1][:, t0:t0 + nt], r_oi[0]),
                           (crp[:, 0, 1, j, :], rv(xb[1][0][:, i0:i0 + nt]), r_or[1]), (crp[:, 1, 1, j, :], rv(xb[1][1][:, i0:i0 + nt]), r_oi[1])]
                    for qi, (lh, rh, rr) in enumerate(ops):
                        M(lambda e: e.matmul(yp[pr, 0:nt], lh, rh, start=(qi == 0), stop=(qi == 3), tile_position=(0, 32 * jm)), r=[rr, r_t],
                          w=[r_yp] if qi == 0 else [], wa=[r_yp] if qi else [])
                    V(lambda e: e.scalar_tensor_tensor(yv[pr, t0:t0 + nt], uc[pr, t0:t0 + nt], sd[pr, cj:cj + 1], yp[pr, 0:nt], ALU.mult, ALU.add),
                      r=[r_yp, r_uc, r_t], w=[r_yv] if (jm == 0 and t0 == 0) else [], wa=[] if (jm == 0 and t0 == 0) else [r_yv])
                if jm == 3 and "g" not in SK:
                    A(lambda e: e.activation(ga[:], yv[:], AF.Square), r=[r_yv], w=[r_Yr])
                    V(lambda e: e.tensor_scalar(ga[:], ga[:], 0.044715, 1.0, ALU.mult, ALU.add), w=[r_Yr])
                    V(lambda e: e.tensor_tensor(ga[:], ga[:], yv[:], ALU.mult), r=[r_yv], w=[r_Yr])
                    A(lambda e: e.activation(ga[:], ga[:], AF.Sigmoid, scale=1.5957691216057308), w=[r_Yr])
                    go, r_go = go_p.next()
                    V(lambda e: e.tensor_tensor(go[:], ga[:], yv[:], ALU.mult), r=[r_Yr, r_yv], w=[r_go])
                    S.dma("sp", gT_d.ap()[cj * 128:(cj + 1) * 128, :], go[:], r=[r_go], wa=[r_g])
            S.barrier()
        with ExitStack() as ph:
            sb, ps = mk(ph)
            gT = sb([128, 8, T], BF16, "gT")
            r_gT = Res()
            S.dma("sp", gT[:], gT_d.ap().rearrange("(c p) t -> p c t", p=128), r=[r_g], w=[r_gT])
            glub = sb([128, 8], F32, "glub")
            r_gb = Res()
            S.dma("sp", glub[:], lay["glub"].ap(), w=[r_gb])
            sig_p = RR([sb([128, 512], F32, "ssig") for _ in range(2)])
            szt_p = RR([sb([128, 512], BF16, "sszt") for _ in range(2)])
            y2_p = RR([sb([128, 512], BF16, "sy2") for _ in range(3)])

            def epi_glu(ci, c0, ncol, t0, nt, pt, r_pt):
                sg, r_sg_ = sig_p.next()
                A(lambda e: e.activation(sg[:, 0:nt], pt[:, 0:nt], AF.Sigmoid, bias=glub[:, ci:ci + 1], scale=1.0), r=[r_pt, r_gb], w=[r_sg_])
                szt, r_szt = szt_p.next()
                S.dma("sp", szt[:, 0:nt], szT_d.ap()[c0:c0 + 128, t0:t0 + nt], r=[r_sz], w=[r_szt])
                V(lambda e: e.tensor_tensor(sg[:, 0:nt], sg[:, 0:nt], gT[:, ci, t0:t0 + nt], ALU.mult), r=[r_gT], w=[r_sg_])
                y2, r_y2_ = y2_p.next()
                V(lambda e: e.tensor_tensor(y2[:, 0:nt], sg[:, 0:nt], szt[:, 0:nt], ALU.mult), r=[r_sg_, r_szt], w=[r_y2_])
                S.dma("sp", y2T_d.ap()[c0:c0 + 128, t0:t0 + nt], y2[:, 0:nt], r=[r_y2_], wa=[r_y2])
            linear_fm(sb, ps, gT, r_gT, 8, lay["gluw"].ap(), [(128 * i, 128) for i in range(8)], epi_glu)
            S.barrier()
        with ExitStack() as ph:
            sb, ps = mk(ph)
            y2 = sb([128, 8, T], BF16, "y2r")
            r_y2r = Res()
            S.dma("sp", y2[:], y2T_d.ap().rearrange("(c p) t -> p c t", p=128), r=[r_y2], w=[r_y2r])
            linear_fm(sb, ps, y2, r_y2r, 8, lay["outw"].ap(), [(128 * i, 128) for i in range(8)], make_resid_epi(sb))
            S.barrier()

    for i in layers:
        kind = i % 3
        if kind == 0:
            mamba_layer(L[i])
        elif kind == 1:
            attn_layer(L[i])
        else:
            s5_layer(L[i])

    with ExitStack() as ph:
        sb, ps = mk(ph)
        pre_pass(None, sb, ps, None, None, final=True)
    S.barrier()
    es.close()
    nc._ninst = S.ninst
    return nc


def prep_inputs(inputs, b, nlat=NLAT, layers=(0, 1, 2, 3)):
    f = lambda a: np.ascontiguousarray(np.asarray(a, dtype=np.float32))
    chunked = lambda v, n: f(np.asarray(v, np.float32).reshape(n, 128).T)
    m = {}
    m["x"] = f(inputs["x"][b][:nlat])
    m["ctx"] = f(inputs["ctx"][b])
    m["cc"] = f(np.stack([chunked(inputs["c"][b], 8), chunked(inputs["c_ctx"], 8)], axis=-1))
    m["ident"] = np.eye(128, dtype=np.float32)
    m["fnw"] = chunked(inputs["final_norm_w"], 8)
    has_m = False
    for i in layers:
        m["normw%d" % i] = chunked(inputs["norm_w"][i], 8)
        m["modw%d" % i] = f(inputs["mod_w"][i])
        m["modb%d" % i] = chunked(inputs["mod_b"][i], 24)
        kind, j = i % 3, i // 3
        if kind == 0:
            has_m = True
            m["m_in_w%d" % j] = f(inputs["m_in_w"][j])
            cw = np.asarray(inputs["m_conv_w"][j], np.float32)
            m["m_convw%d" % j] = f(cw.reshape(5, 32, 128).transpose(2, 1, 0))
            m["m_convb%d" % j] = chunked(inputs["m_conv_b"][j], 32)
            m["m_alog%d" % j] = f(np.asarray(inputs["m_a_log"][j], np.float32).reshape(64, 1))
            m["m_dtb%d" % j] = f(np.asarray(inputs["m_dt_bias"][j], np.float32).reshape(64, 1))
            m["m_dvec%d" % j] = f(np.repeat(np.asarray(inputs["m_d"][j], np.float32), 64))
            m["m_normw%d" % j] = f(inputs["m_norm_w"][j])
            m["m_out_w%d" % j] = f(inputs["m_out_w"][j])
        elif kind == 1:
            m["a_in_w"] = f(inputs["a_in_w"][0])
            m["a_out_w"] = f(inputs["a_out_w"][0])
            m["a_qkw"] = f(np.stack([np.tile(np.asarray(inputs["a_q_norm"][0], np.float32), 2),
                                     np.tile(np.asarray(inputs["a_k_norm"][0], np.float32), 2)], axis=1))
            grid_w = 64
            pos = np.arange(nlat)
            r_idx, c_idx = (pos // grid_w).astype(np.float32), (pos % grid_w).astype(np.float32)
            inv = (10000.0 ** (-np.arange(0, 32, 2, dtype=np.float32) / 32)).astype(np.float32)
            dd = np.arange(128) % 64
            ax, part, ii = dd // 32, (dd % 32) // 16, dd % 16
            ang = np.where(ax[:, None] == 0, r_idx[None, :], c_idx[None, :]).astype(np.float32) * inv[ii][:, None]
            m["a_rope"] = f(np.stack([np.cos(ang), np.sin(ang)]))
            perm = np.zeros((128, 128), np.float32)
            for dcol in range(128):
                if part[dcol] == 0:
                    perm[dcol + 16, dcol] = -1.0
                else:
                    perm[dcol - 16, dcol] = 1.0
            m["a_perm"] = perm
            bo = np.zeros((128, 128), np.float32)
            bo[:64, :64] = 1.0
            bo[64:, 64:] = 1.0
            m["a_bones"] = bo
        else:
            m["s_in_w"] = f(inputs["s_in_w"][0])
            m["s_glu_w"] = f(inputs["s_glu_w"][0])
            m["s_out_w"] = f(inputs["s_out_w"][0])
            m["s_sd"] = chunked(inputs["s_d"][0], 8)
            m["s_glub"] = chunked(inputs["s_glu_b"][0], 8)
            lre = np.asarray(inputs["s_lambda_re"][0], np.float32)
            lim = np.asarray(inputs["s_lambda_im"][0], np.float32)
            lst = np.asarray(inputs["s_log_step"][0], np.float32)

            def pair_layout(a):
                a = a.reshape(2, 32, 2, 64)
                return a.transpose(2, 3, 0, 1).reshape(128, 64)
            lam = np.stack([pair_layout(lre), pair_layout(lim),
                            pair_layout(np.broadcast_to(lst[:, :, None], (2, 64, 64)))], axis=1)
            m["s_lam"] = f(lam)
            brt = np.zeros((2, 128, 2, 32, 128), np.float32)
            crp = np.zeros((2, 128, 2, 32, 32), np.float32)
            for q, (bsrc, csrc) in enumerate(((inputs["s_b_re"][0], inputs["s_c_re"][0]), (inputs["s_b_im"][0], inputs["s_c_im"][0]))):
                bsrc = np.asarray(bsrc, np.float32)
                csrc = np.asarray(csrc, np.float32)
                for k in range(2):
                    for j in range(32):
                        for gl in range(2):
                            g_ = 2 * j + gl
                            r0 = 32 * (j % 4) + 16 * gl
                            brt[q, r0:r0 + 16, k, j, gl * 64:(gl + 1) * 64] = bsrc[k, g_].T
                            crp[q, gl * 64:(gl + 1) * 64, k, j, 16 * gl:16 * gl + 16] = csrc[k, g_].T
            m["s_brt"] = brt
            m["s_crp"] = crp
    if has_m:
        up = np.triu(np.ones((128, 128), np.float32))
        m["masks"] = f(np.stack([up, up.T]))
    return m


ACTIVE_CORES = (0, 1, 4, 5)


def kernel(**inputs):
    nc = build_program()
    real = [prep_inputs(inputs, b) for b in range(4)]
    big = ("x", "ctx", "cc", "modw", "m_in_w", "m_out_w", "a_in_w", "a_out_w", "s_in_w", "s_glu_w", "s_out_w", "s_brt", "s_crp")
    idle = {k: (np.zeros_like(v) if k.startswith(big) else v) for k, v in real[0].items()}
    in_maps = [idle] * 8
    for b, core in enumerate(ACTIVE_CORES):
        in_maps[core] = real[b]
    res = run_bass_kernel_spmd(nc, in_maps, core_ids=list(range(8)))
    out = np.stack([np.asarray(res.results[core]["out"], dtype=np.float32) for core in ACTIVE_CORES], axis=0)
    return out
```
